# Optimizing a Trainium2 kernel written in Bass

```python
import math
import jax, jax.numpy as jnp
from jax import lax
import numpy as np

D_MODEL = 1024
BATCH = 8
SEQ = 4096
DEPTH = 4

HEAD_DIM = 64
NSA_HEADS = 8
NSA_GROUPS = 2
CMP_BLOCK = 32
CMP_STRIDE = 16
CMP_HIDDEN = 256
SEL_BLOCK = 64
SEL_TOPK = 8
NSA_WINDOW = 512
SWA_HEADS = 8
SWA_GROUPS = 2
SWA_WINDOW = 128
FOX_HEADS = 8
FOX_GATE_BIAS = 3.0
Q_BLOCK = 128
REL_BUCKETS = 32
REL_MAX_DIST = 128
D_FF = 2816
N_BRANCH = 3
LN_EPS = 1e-5
DN_ALPHA = (2 * DEPTH) ** 0.25
DN_BETA = (8 * DEPTH) ** -0.25

W_A = NSA_HEADS * HEAD_DIM
KV_A = NSA_GROUPS * HEAD_DIM
W_B = SWA_HEADS * HEAD_DIM
KV_B = SWA_GROUPS * HEAD_DIM
W_C = FOX_HEADS * HEAD_DIM
D_IN = W_A + 6 * KV_A + 3 * NSA_HEADS + W_B + 2 * KV_B + 3 * W_C + FOX_HEADS

kernel_name = 'hybrid_nsa_swa_sink_fox_macaron_deepnorm'


def _layer_norm(x, g, b):
    xf = x.astype(jnp.float32)
    mu = jnp.mean(xf, axis=-1, keepdims=True)
    var = jnp.mean(jnp.square(xf - mu), axis=-1, keepdims=True)
    return ((xf - mu) * lax.rsqrt(var + LN_EPS) * g + b).astype(x.dtype)


def _swiglu(x, w1, w2):
    gt, up = jnp.split(x @ w1, 2, axis=-1)
    return (jax.nn.silu(gt) * up) @ w2


def _t5_bucket(dist):
    n = jnp.maximum(dist, 0)
    exact = REL_BUCKETS // 2
    log_ratio = jnp.log(jnp.maximum(n, 1).astype(jnp.float32) / exact) / math.log(REL_MAX_DIST / exact)
    large = exact + (log_ratio * (REL_BUCKETS - exact)).astype(jnp.int32)
    return jnp.where(n < exact, n, jnp.minimum(large, REL_BUCKETS - 1))


def _masked_softmax(logits, mask):
    logits = jnp.where(mask, logits.astype(jnp.float32), -jnp.inf)
    m = jnp.max(logits, axis=-1, keepdims=True)
    m = jnp.where(jnp.isfinite(m), m, 0.0)
    p = jnp.exp(logits - m)
    return p / jnp.maximum(jnp.sum(p, axis=-1, keepdims=True), 1e-30)


def _split_cols(z):
    sizes = [W_A, 6 * KV_A, 3 * NSA_HEADS, W_B, KV_B, KV_B, W_C, W_C, W_C, FOX_HEADS]
    return jnp.split(z, np.cumsum(sizes)[:-1].tolist(), axis=-1)


def _compress(z, pe, w1, w2):
    B, S, G, dh = z.shape
    r = CMP_BLOCK // CMP_STRIDE
    n_chunk = S // CMP_STRIDE
    nc = n_chunk - r + 1
    zr = z.reshape(B, n_chunk, CMP_STRIDE, G, dh)
    blocks = jnp.concatenate([zr[:, j:j + nc] for j in range(r)], axis=2)
    blocks = blocks + pe[:, None, :]
    flat = blocks.transpose(0, 3, 1, 2, 4).reshape(B, G, nc, CMP_BLOCK * dh)
    return jax.nn.gelu(flat @ w1) @ w2


def _nsa(q_a, kv_a, g_a, pe_k, pe_v, ck_w1, ck_w2, cv_w1, cv_w2, rel_a):
    B, S, _ = q_a.shape
    G, HPG, dh = NSA_GROUPS, NSA_HEADS // NSA_GROUPS, HEAD_DIM
    nb = S // Q_BLOCK
    scale = HEAD_DIM ** -0.5
    q = q_a.reshape(B, S, G, HPG, dh).transpose(0, 2, 3, 1, 4)
    k_cmp, v_cmp, k_slc, v_slc, k_win, v_win = [z.reshape(B, S, G, dh) for z in jnp.split(kv_a, 6, axis=-1)]
    kc = _compress(k_cmp, pe_k, ck_w1, ck_w2)
    vc = _compress(v_cmp, pe_v, cv_w1, cv_w2)
    nc = kc.shape[2]
    ns = S // SEL_BLOCK
    n_sel = min(SEL_TOPK, ns)
    ks = k_slc.reshape(B, ns, SEL_BLOCK, G, dh).transpose(0, 3, 1, 2, 4)
    vs = v_slc.reshape(B, ns, SEL_BLOCK, G, dh).transpose(0, 3, 1, 2, 4)
    pad = ((0, 0), (0, 0), (NSA_WINDOW, 0), (0, 0))
    kw = jnp.pad(k_win.transpose(0, 2, 1, 3), pad)
    vw = jnp.pad(v_win.transpose(0, 2, 1, 3), pad)
    gates = jax.nn.sigmoid(g_a.astype(jnp.float32)).reshape(B, S, G, HPG, 3).transpose(0, 2, 3, 1, 4)
    rel_g = rel_a.reshape(G, HPG, REL_BUCKETS)
    c_start = jnp.arange(nc) * CMP_STRIDE
    c_end = c_start + (CMP_BLOCK - 1)
    s_start = jnp.arange(ns) * SEL_BLOCK
    overlap = ((c_start[:, None] < s_start[None] + SEL_BLOCK) & (c_end[:, None] >= s_start[None])).astype(jnp.float32)
    i_w = jnp.arange(Q_BLOCK)
    j_w = jnp.arange(Q_BLOCK + NSA_WINDOW)
    dist_w = NSA_WINDOW + i_w[:, None] - j_w[None]
    band_w = (dist_w >= 0) & (dist_w < NSA_WINDOW)
    bias_w = rel_g[:, :, _t5_bucket(dist_w)]
    blk = jnp.arange(ns)
    sel_off = jnp.arange(SEL_BLOCK)
    gather = jax.vmap(jax.vmap(lambda kb, ix: kb[ix]))
    lookup = jax.vmap(lambda tab, bk: jnp.moveaxis(tab[:, bk], 0, 1), in_axes=(0, 1), out_axes=1)

    def block(qb):
        q0 = qb * Q_BLOCK
        t = q0 + jnp.arange(Q_BLOCK)
        qq = lax.dynamic_slice_in_dim(q, q0, Q_BLOCK, axis=3)
        gg = lax.dynamic_slice_in_dim(gates, q0, Q_BLOCK, axis=3)
        s_c = jnp.einsum('bghqd,bgkd->bghqk', qq, kc, preferred_element_type=jnp.float32) * scale
        s_c = s_c + rel_g[:, :, _t5_bucket(t[:, None] - c_end[None])]
        p_c = _masked_softmax(s_c, c_end[None] <= t[:, None])
        o_c = jnp.einsum('bghqk,bgkd->bghqd', p_c.astype(vc.dtype), vc)
        imp = jnp.einsum('bghqk,kj->bgqj', p_c, overlap)
        cur = (t // SEL_BLOCK)[:, None]
        forced = (blk[None] == 0) | (blk[None] == cur) | (blk[None] == cur - 1)
        valid = blk[None] * SEL_BLOCK <= t[:, None]
        score = jnp.where(forced, jnp.inf, jnp.where(valid, imp, -jnp.inf))
        _, idx = lax.top_k(score, n_sel)
        k_sel = gather(ks, idx).reshape(B, G, Q_BLOCK, n_sel * SEL_BLOCK, dh)
        v_sel = gather(vs, idx).reshape(B, G, Q_BLOCK, n_sel * SEL_BLOCK, dh)
        pos = (idx[..., None] * SEL_BLOCK + sel_off).reshape(B, G, Q_BLOCK, n_sel * SEL_BLOCK)
        dist = t[:, None] - pos
        s_s = jnp.einsum('bghqd,bgqkd->bghqk', qq, k_sel, preferred_element_type=jnp.float32) * scale
        s_s = s_s + lookup(rel_g, _t5_bucket(dist))
        p_s = _masked_softmax(s_s, (dist >= 0)[:, :, None])
        o_s = jnp.einsum('bghqk,bgqkd->bghqd', p_s.astype(v_sel.dtype), v_sel)
        kwb = lax.dynamic_slice_in_dim(kw, q0, Q_BLOCK + NSA_WINDOW, axis=2)
        vwb = lax.dynamic_slice_in_dim(vw, q0, Q_BLOCK + NSA_WINDOW, axis=2)
        s_w = jnp.einsum('bghqd,bgkd->bghqk', qq, kwb, preferred_element_type=jnp.float32) * scale + bias_w
        mask_w = band_w & ((q0 - NSA_WINDOW + j_w) >= 0)[None]
        p_w = _masked_softmax(s_w, mask_w)
        o_w = jnp.einsum('bghqk,bgkd->bghqd', p_w.astype(vwb.dtype), vwb)
        return gg[..., 0:1] * o_c + gg[..., 1:2] * o_s + gg[..., 2:3] * o_w

    out = lax.map(block, jnp.arange(nb))
    return out.transpose(1, 0, 4, 2, 3, 5).reshape(B, S, W_A).astype(q_a.dtype)


def _swa_sink(q_b, k_b, v_b, sinks, rel_b):
    B, S, _ = q_b.shape
    G, HPG, dh, W = SWA_GROUPS, SWA_HEADS // SWA_GROUPS, HEAD_DIM, SWA_WINDOW
    nb = S // W
    scale = HEAD_DIM ** -0.5
    q = q_b.reshape(B, nb, W, G, HPG, dh)
    k = k_b.reshape(B, nb, W, G, dh)
    v = v_b.reshape(B, nb, W, G, dh)

    def with_prev(z):
        prev = jnp.pad(z[:, :-1], ((0, 0), (1, 0), (0, 0), (0, 0), (0, 0)))
        return jnp.concatenate([prev, z], axis=2)

    kk, vv = with_prev(k), with_prev(v)
    i = jnp.arange(W)
    j = jnp.arange(2 * W)
    dist = W + i[:, None] - j[None]
    band = (dist >= 0) & (dist < W)
    first = (jnp.arange(nb)[:, None] * W - W + j[None]) >= 0
    mask = band[None] & first[:, None, :]
    bias = rel_b.reshape(G, HPG, REL_BUCKETS)[:, :, _t5_bucket(dist)]
    logits = jnp.einsum('bnqghd,bnkgd->bghnqk', q, kk, preferred_element_type=jnp.float32) * scale
    logits = jnp.where(mask, logits + bias[:, :, None], -jnp.inf)
    sink = jnp.broadcast_to(sinks.astype(jnp.float32).reshape(1, G, HPG, 1, 1, 1), logits.shape[:-1] + (1,))
    p = jax.nn.softmax(jnp.concatenate([logits, sink], axis=-1), axis=-1)[..., :-1]
    o = jnp.einsum('bghnqk,bnkgd->bnqghd', p.astype(vv.dtype), vv)
    return o.reshape(B, S, W_B)


def _fox(q_c, k_c, v_c, f_c, b_f):
    B, S, _ = q_c.shape
    H, dh = FOX_HEADS, HEAD_DIM
    nb = S // Q_BLOCK
    scale = HEAD_DIM ** -0.5
    q = q_c.reshape(B, S, H, dh).transpose(0, 2, 1, 3)
    k = k_c.reshape(B, S, H, dh).transpose(0, 2, 1, 3)
    v = v_c.reshape(B, S, H, dh).transpose(0, 2, 1, 3)
    log_f = jax.nn.log_sigmoid((f_c + b_f).astype(jnp.float32))
    c = jnp.cumsum(log_f, axis=1).transpose(0, 2, 1)
    s_idx = jnp.arange(S)

    def block(qb):
        q0 = qb * Q_BLOCK
        t = q0 + jnp.arange(Q_BLOCK)
        qq = lax.dynamic_slice_in_dim(q, q0, Q_BLOCK, axis=2)
        cq = lax.dynamic_slice_in_dim(c, q0, Q_BLOCK, axis=2)
        logits = jnp.einsum('bhqd,bhkd->bhqk', qq, k, preferred_element_type=jnp.float32) * scale
        logits = logits + cq[..., None] - c[:, :, None, :]
        logits = jnp.where(s_idx[None] <= t[:, None], logits, -jnp.inf)
        p = jax.nn.softmax(logits, axis=-1)
        return jnp.einsum('bhqk,bhkd->bhqd', p.astype(v.dtype), v)

    out = lax.map(block, jnp.arange(nb))
    return out.transpose(1, 0, 3, 2, 4).reshape(B, S, W_C)


def _token_mix(h, w_in, pe_k, pe_v, ck_w1, ck_w2, cv_w1, cv_w2, sinks, b_f, rel_bias,
               w_br_a, w_br_b, w_br_c, w_gate, b_gate, w_out):
    B, S, D = h.shape
    q_a, kv_a, g_a, q_b, k_b, v_b, q_c, k_c, v_c, f_c = _split_cols(h @ w_in)
    rel_t = rel_bias.T
    o_a = _nsa(q_a, kv_a, g_a, pe_k, pe_v, ck_w1, ck_w2, cv_w1, cv_w2, rel_t[:NSA_HEADS])
    o_b = _swa_sink(q_b, k_b, v_b, sinks, rel_t[NSA_HEADS:])
    o_c = _fox(q_c, k_c, v_c, f_c, b_f)
    g = jax.nn.sigmoid(h @ w_gate + b_gate).reshape(B, S, N_BRANCH, D)
    merged = g[:, :, 0] * (o_a @ w_br_a) + g[:, :, 1] * (o_b @ w_br_b) + g[:, :, 2] * (o_c @ w_br_c)
    return merged @ w_out


def setup_inputs(seed: int = 0) -> dict:
    key = jax.random.key(seed)
    keys = iter(jax.random.split(key, 32))
    L, D, F = DEPTH, D_MODEL, D_FF

    def nrm(shape, scale):
        return jax.random.normal(next(keys), shape, jnp.float32) * scale

    return {
        'x': nrm((BATCH, SEQ, D), 1.0),
        'rel_bias': nrm((REL_BUCKETS, NSA_HEADS + SWA_HEADS), 0.5),
        'ln1_g': 1.0 + nrm((L, D), 0.02),
        'ln1_b': nrm((L, D), 0.02),
        'ffn1_w1': nrm((L, D, 2 * F), D ** -0.5),
        'ffn1_w2': nrm((L, F, D), DN_BETA * F ** -0.5),
        'w_in': nrm((L, D, D_IN), D ** -0.5),
        'cmp_pe_k': nrm((L, CMP_BLOCK, HEAD_DIM), 0.1),
        'cmp_pe_v': nrm((L, CMP_BLOCK, HEAD_DIM), 0.1),
        'cmp_k_w1': nrm((L, CMP_BLOCK * HEAD_DIM, CMP_HIDDEN), (CMP_BLOCK * HEAD_DIM) ** -0.5),
        'cmp_k_w2': nrm((L, CMP_HIDDEN, HEAD_DIM), CMP_HIDDEN ** -0.5),
        'cmp_v_w1': nrm((L, CMP_BLOCK * HEAD_DIM, CMP_HIDDEN), (CMP_BLOCK * HEAD_DIM) ** -0.5),
        'cmp_v_w2': nrm((L, CMP_HIDDEN, HEAD_DIM), CMP_HIDDEN ** -0.5),
        'swa_sinks': nrm((L, SWA_HEADS), 1.0),
        'fox_b_f': FOX_GATE_BIAS + nrm((L, FOX_HEADS), 0.1),
        'w_br_a': nrm((L, W_A, D), W_A ** -0.5),
        'w_br_b': nrm((L, W_B, D), W_B ** -0.5),
        'w_br_c': nrm((L, W_C, D), W_C ** -0.5),
        'w_gate': nrm((L, D, N_BRANCH * D), D ** -0.5),
        'b_gate': nrm((L, N_BRANCH * D), 0.01),
        'w_out': nrm((L, D, D), DN_BETA * D ** -0.5),
        'ln2_g': 1.0 + nrm((L, D), 0.02),
        'ln2_b': nrm((L, D), 0.02),
        'ffn2_w1': nrm((L, D, 2 * F), D ** -0.5),
        'ffn2_w2': nrm((L, F, D), DN_BETA * F ** -0.5),
        'ln3_g': 1.0 + nrm((L, D), 0.02),
        'ln3_b': nrm((L, D), 0.02),
    }


def reference(x, rel_bias, ln1_g, ln1_b, ffn1_w1, ffn1_w2, w_in, cmp_pe_k, cmp_pe_v,
              cmp_k_w1, cmp_k_w2, cmp_v_w1, cmp_v_w2, swa_sinks, fox_b_f,
              w_br_a, w_br_b, w_br_c, w_gate, b_gate, w_out, ln2_g, ln2_b,
              ffn2_w1, ffn2_w2, ln3_g, ln3_b):
    for l in range(DEPTH):
        x = _layer_norm(DN_ALPHA * x + 0.5 * _swiglu(x, ffn1_w1[l], ffn1_w2[l]), ln1_g[l], ln1_b[l])
        mix = _token_mix(x, w_in[l], cmp_pe_k[l], cmp_pe_v[l], cmp_k_w1[l], cmp_k_w2[l],
                         cmp_v_w1[l], cmp_v_w2[l], swa_sinks[l], fox_b_f[l], rel_bias,
                         w_br_a[l], w_br_b[l], w_br_c[l], w_gate[l], b_gate[l], w_out[l])
        x = _layer_norm(DN_ALPHA * x + mix, ln2_g[l], ln2_b[l])
        x = _layer_norm(DN_ALPHA * x + 0.5 * _swiglu(x, ffn2_w1[l], ffn2_w2[l]), ln3_g[l], ln3_b[l])
    return x
```

```python
from contextlib import ExitStack
import numpy as np
import concourse.bass as bass
import concourse.mybir as mybir
from concourse.bass_utils import run_bass_kernel_spmd

F32 = mybir.dt.float32
BF16 = mybir.dt.bfloat16
AF = mybir.ActivationFunctionType
ALU = mybir.AluOpType
AX = mybir.AxisListType

ENGS = ["tensor", "vector", "scalar", "gpsimd", "sync"]
S_LEN = 4096
DM = 1024
FF = 2816
NFC = 22
DIN = 3616
NEG = -30000.0
ALPHA = 8.0 ** 0.25
LN_EPS = 1e-5
NLAYERS = 4
_STOP_AFTER = None
_MIX_STOP = 99
_MIX_OUT = True
_SUB = 99


class _Stop(Exception):
    pass


class Buf:
    __slots__ = ("w", "r", "excl")

    def __init__(self, excl=False):
        self.w = None
        self.r = {}
        self.excl = excl


class Sched:
    def __init__(self, nc, n_dma_sems=12, rot=20000):
        self.nc = nc
        self.prog = {e: [] for e in ENGS}
        self.seen = {e: {} for e in ENGS}
        self.cnt = {e: 0 for e in ENGS}
        self.epoch = {e: 0 for e in ENGS}
        self.rot = rot
        self.n_dma = n_dma_sems
        self.dma_uses = {}
        self.dma_rr = {e: 0 for e in ENGS}
        self.semkeys = []
        self.semset = set()
        self.stack = ExitStack()
        self.sb_off = 16512
        self.sb_id = 0

    def sb(self, shape, dtype):
        n = 1
        for s in shape[1:]:
            n *= s
        nbytes = n * (4 if dtype == F32 else 2)
        off = (self.sb_off + 63) // 64 * 64
        assert off + nbytes <= 229000, ("SBUF overflow", off, nbytes)
        self.sb_off = off + nbytes
        self.peak = max(getattr(self, 'peak', 0), self.sb_off)
        self.sb_id += 1
        return self.nc.alloc_sbuf_tensor_at("t%d" % self.sb_id, list(shape), dtype, offset=off)

    def mark(self):
        return self.sb_off

    def release(self, m):
        self.barrier()
        self.sb_off = m

    def _key(self, key):
        if key not in self.semset:
            self.semset.add(key)
            self.semkeys.append(key)
        return key

    def _collect(self, eng, reads, writes):
        deps = {}

        def add(tok):
            if tok is None:
                return
            k, v = tok
            if deps.get(k, 0) < v:
                deps[k] = v
        for b in reads:
            add(b.w)
            if b.excl:
                for k, v in b.r.items():
                    if k[0] != eng:
                        add((k, v))
        for b in writes:
            add(b.w)
            for k, v in b.r.items():
                add((k, v))
        waits = []
        seen = self.seen[eng]
        for k, v in deps.items():
            if eng == "tensor" and k[0] == "tensor":
                continue
            if seen.get(k, 0) >= v:
                continue
            seen[k] = v
            waits.append((k, v))
        return waits

    def _update(self, tok, reads, writes):
        k, v = tok
        for b in reads:
            if b.r.get(k, 0) < v:
                b.r[k] = v
        for b in writes:
            b.w = tok
            b.r = {}

    def op(self, eng, emit, reads=(), writes=()):
        waits = self._collect(eng, reads, writes)
        self.cnt[eng] += 1
        if self.cnt[eng] > self.rot:
            self.epoch[eng] += 1
            self.cnt[eng] = 1
        tok = (self._key((eng, self.epoch[eng])), self.cnt[eng])
        self.prog[eng].append((waits, emit, tok, 1))
        self._update(tok, reads, writes)
        return tok

    def dma(self, q, emit, reads=(), writes=()):
        i = self.dma_rr[q]
        self.dma_rr[q] = (i + 1) % self.n_dma
        key = self._key(("dma", q, i))
        k = self.dma_uses.get(key, 0) + 1
        self.dma_uses[key] = k
        waits = self._collect(q, reads, writes)
        if k > 1 and self.seen[q].get(key, 0) < 16 * (k - 1):
            self.seen[q][key] = 16 * (k - 1)
            waits.append((key, 16 * (k - 1)))
        tok = (key, 16 * k)
        self.prog[q].append((waits, emit, tok, 16))
        self._update(tok, reads, writes)
        return tok

    def _all_tokens(self):
        toks = []
        for key, k in self.dma_uses.items():
            toks.append((key, 16 * k))
        for e in ENGS:
            if self.cnt[e] > 0:
                toks.append(((e, self.epoch[e]), self.cnt[e]))
        return toks

    def barrier(self):
        toks = self._all_tokens()
        for e in ENGS:
            waits = []
            for k, v in toks:
                if self.seen[e].get(k, 0) < v:
                    self.seen[e][k] = v
                    waits.append((k, v))
            if waits:
                self.prog[e].append((waits, None, None, 0))

    def finish(self):
        nc = self.nc
        self.barrier()
        sems = {}
        for key in self.semkeys:
            nm = "s_" + "_".join(str(x) for x in key)
            sems[key] = self.stack.enter_context(nc.semaphore(nm))
        prog = self.prog
        with nc.Block() as block:
            def mk(ename):
                def body(eng):
                    for waits, emit, tok, inc in prog[ename]:
                        for k, v in waits:
                            eng.wait_ge(sems[k], v)
                        if emit is not None:
                            emit(eng).then_inc(sems[tok[0]], inc)
                return body
            block.tensor(mk("tensor"))
            block.vector(mk("vector"))
            block.scalar(mk("scalar"))
            block.gpsimd(mk("gpsimd"))
            block.sync(mk("sync"))
        self.stack.close()


class Ring:
    def __init__(self, items):
        self.items = items
        self.i = 0

    def next(self):
        it = self.items[self.i]
        self.i = (self.i + 1) % len(self.items)
        return it


def _win_passes():
    P = {}
    off = 0
    perm = []

    def add(name, cols):
        nonlocal off
        P[name] = (off, len(cols))
        perm.extend(cols)
        off += len(cols)
    r = lambda a, n: list(range(a, a + n))
    for h in range(8):
        add("fox_qk%d" % h, r(2072 + h * 64, 64) + r(2584 + h * 64, 64))
    add("fox_f", r(3608, 8))
    for h in range(8):
        add("fox_v%d" % h, r(3096 + h * 64, 64))
    for i in range(4):
        add("swa_q%d" % i, r(1304 + i * 128, 128))
    add("swa_k", r(1816, 128))
    add("swa_v", r(1944, 128))
    for i in range(4):
        add("nsa_q%d" % i, r(i * 128, 128))
    for g in range(2):
        add("nsa_cmp%d" % g, r(512 + g * 64, 64) + r(640 + g * 64, 64))
        add("nsa_kk%d" % g, r(768 + g * 64, 64) + r(1024 + g * 64, 64))
        add("nsa_v%d" % g, r(896 + g * 64, 64) + r(1152 + g * 64, 64))
    add("nsa_gate", r(1280, 24))
    assert off == DIN and sorted(perm) == list(range(DIN))
    return P, np.array(perm)


WIN_P, WIN_PERM = _win_passes()


def _t5_bucket(n):
    n = np.maximum(n, 0)
    lr = np.log(np.maximum(n, 1).astype(np.float32) / np.float32(16)) / np.float32(np.log(128 / 16))
    large = 16 + (lr.astype(np.float32) * np.float32(16)).astype(np.int32)
    return np.where(n < 16, n, np.minimum(large, 31))


def build_program(NL, dbg=None):
    nc = bass.Bass("TRN2", target_bir_lowering=False)
    S = Sched(nc)

    def din(name, shape, dt=F32):
        return nc.dram_tensor(name, list(shape), dt, kind="ExternalInput").ap()

    def dscr(name, shape, dt):
        return nc.dram_tensor(name, list(shape), dt, kind="Internal").ap()

    x_in = din("x", [S_LEN, DM])
    y_out = nc.dram_tensor("y", [S_LEN, DM], F32, kind="ExternalOutput").ap()
    w1_d = [din("w1_%d" % i, [NL, NFC, 128, 2048]) for i in range(2)]
    w2_d = [din("w2_%d" % i, [NL, 128, NFC * DM]) for i in range(2)]
    win_d = din("win", [NL, 128, 8 * DIN])
    wgate_d = din("wgate", [NL, 128, 8 * 3072])
    bgate_d = din("bgate", [NL, 128, 24])
    wbr_d = din("wbr", [NL, 3, 128, 4 * DM])
    wout_d = din("wout", [NL, 128, 8 * DM])
    cw1_d = din("cw1", [NL, 2, 64, 32 * 256])
    cw2_d = din("cw2", [NL, 2, 128, 128])
    pet_d = din("pet", [NL, 2, 64, 32])
    lng_d = din("lng", [NL, 3, 128, DM])
    lnb_d = din("lnb", [NL, 3, 128, DM])
    sinks_d = din("sinks", [NL, 128, 8])
    bf_d = din("bf", [NL, 8, 1])
    ident_d = din("ident", [128, 128])
    onehot_d = din("onehot", [64, S_LEN])
    biasT_d = din("biasT", [16, 2, 128, 128])
    cfar_d = din("cfar", [128, 16])
    tc_d = din("tc", [8, 128, 512])
    ft_d = din("ft", [128, 128])
    cm_d = din("cm", [128, 128])
    lt_d = din("lt", [128, 128])
    sel24_d = din("sel24", [24, 1536])

    xs = [dscr("xs0", [S_LEN, DM], F32), dscr("xs1", [S_LEN, DM], F32)]
    w1s = [dscr("w1s%d" % i, [NFC, 128, 2048], BF16) for i in range(2)]
    w2s = [dscr("w2s%d" % i, [128, NFC * DM], BF16) for i in range(2)]
    wins = dscr("wins", [128, 8 * DIN], BF16)
    wgs = dscr("wgs", [128, 8 * 3072], BF16)
    wbrs = dscr("wbrs", [3, 128, 4 * DM], BF16)
    wouts = dscr("wouts", [128, 8 * DM], BF16)
    cw1s = dscr("cw1s", [2, 64, 32 * 256], BF16)
    ohs = dscr("ohs", [64, S_LEN], BF16)
    if dbg and "oT" in dbg:
        oT = nc.dram_tensor("oT", [3, 512, S_LEN], BF16, kind="ExternalOutput").ap()
    else:
        oT = dscr("oT", [3, 512, S_LEN], BF16)
    if dbg and "oT" in dbg:
        ocmp = nc.dram_tensor("ocmp", [8, 64, S_LEN], F32, kind="ExternalOutput").ap()
        dsel = nc.dram_tensor("dsel", [8, 64, S_LEN], F32, kind="ExternalOutput").ap()
        dwin = nc.dram_tensor("dwin", [8, 64, S_LEN], F32, kind="ExternalOutput").ap()
    else:
        ocmp = dscr("ocmp", [8, 64, S_LEN], F32)
        dsel = dwin = None
    caug = dscr("caug", [8, 6, S_LEN], BF16)
    B_xs = [[Buf() for _ in range(8)] for _ in range(2)]
    B_w1s = [Buf(), Buf()]
    B_w2s = [Buf(), Buf()]
    B_wins, B_wgs, B_wbrs, B_wouts, B_cw1s, B_ohs = Buf(), Buf(), Buf(), Buf(), Buf(), Buf()
    B_oT = [[Buf() for _ in range(8)] for _ in range(3)]
    B_ocmp = [[Buf() for _ in range(8)] for _ in range(8)]
    B_caug = Buf()
    B_y = Buf()
    dbg_d = {}
    if dbg:
        for nm in dbg:
            if nm == "oT":
                continue
            dbg_d[nm] = nc.dram_tensor("dbg_" + nm, [S_LEN, DM], F32, kind="ExternalOutput").ap()

    banks = [S.stack.enter_context(nc.psum_tensor("bank%d" % i, [128, 512], F32)) for i in range(8)]
    B_bank = [Buf(excl=True) for _ in range(8)]
    ring5 = Ring([0, 1, 2, 3, 4])

    def bank():
        i = ring5.next()
        return banks[i], B_bank[i]

    cpy_rr = [0]

    def mm(out, lhsT, rhs, start, stop, reads, writes):
        S.op("tensor", lambda e: e.matmul(out, lhsT=lhsT, rhs=rhs, start=start, stop=stop),
             reads=reads, writes=writes)

    def tr(out, in_, ident, reads, writes):
        S.op("tensor", lambda e: e.transpose(out, in_, ident), reads=reads, writes=writes)

    def act(out, in_, func, reads, writes, bias=None, scale=None, accum=None):
        kw = {}
        if bias is not None:
            kw["bias"] = bias
        if scale is not None:
            kw["scale"] = scale
        if accum is not None:
            kw["accum_out"] = accum
        S.op("scalar", lambda e: e.activation(out=out, in_=in_, func=func, **kw), reads=reads, writes=writes)

    def vcopy(eng, out, in_, reads, writes):
        S.op(eng, lambda e: e.tensor_copy(out=out, in_=in_), reads=reads, writes=writes)

    def copy_any(out, in_, reads, writes, psum=True):
        cpy_rr[0] ^= 1
        if cpy_rr[0]:
            vcopy("vector", out, in_, reads, writes)
        else:
            act(out, in_, AF.Copy, reads, writes)

    def tt(eng, out, in0, in1, op, reads, writes):
        S.op(eng, lambda e: e.tensor_tensor(out=out, in0=in0, in1=in1, op=op), reads=reads, writes=writes)

    def ts(eng, out, in0, s1, s2, op0, op1, reads, writes):
        if op1 is None:
            S.op(eng, lambda e: e.tensor_scalar(out=out, in0=in0, scalar1=s1, scalar2=None, op0=op0),
                 reads=reads, writes=writes)
        else:
            S.op(eng, lambda e: e.tensor_scalar(out=out, in0=in0, scalar1=s1, scalar2=s2, op0=op0, op1=op1),
                 reads=reads, writes=writes)

    def stt(eng, out, in0, scalar, in1, op0, op1, reads, writes):
        S.op(eng, lambda e: e.scalar_tensor_tensor(out=out, in0=in0, scalar=scalar, in1=in1, op0=op0, op1=op1),
             reads=reads, writes=writes)

    def memset(eng, ap, val, writes):
        S.op(eng, lambda e: e.memset(ap, val), writes=writes)

    def ld(out, in_, reads, writes):
        S.dma("sync", lambda e: e.dma_start(out=out, in_=in_), reads=reads, writes=writes)

    ident_f = S.sb([128, 128], F32)
    ident_b = S.sb([128, 128], BF16)
    nsaT = S.sb([128, 8, 2, 128], BF16)
    swaT = S.sb([128, 8, 2, 128], BF16)
    cm_b = S.sb([128, 128], BF16)
    lt_b = S.sb([128, 128], BF16)
    tc_b = S.sb([128, 8, 512], BF16)
    ft_f = S.sb([128, 128], F32)
    sel24 = S.sb([24, 1536], BF16)
    cfar = S.sb([128, 16], F32)
    B_const = Buf()
    CH = 1024
    stg = Ring([(S.sb([128, CH], F32), S.sb([128, CH], BF16), Buf(), Buf()) for _ in range(3)])
    stage_f, stage_b, B_stage_f, B_stage_b = stg.items[0]
    cwstage = S.sb([128, 160], F32)
    B_cwstage = Buf()

    def stD(out, in_, reads, writes):
        S.dma("gpsimd", lambda e: e.dma_start(out=out, in_=in_), reads=reads, writes=writes)

    epst = S.sb([128, 2], F32)
    memset("vector", epst[:, 0:1], LN_EPS, [B_const])
    memset("vector", epst[:, 1:2], 1.0, [B_const])
    ld(ident_f[:], ident_d, [], [B_const])
    vcopy("vector", ident_b[:], ident_f[:], [B_const], [B_const])
    ld(cfar[:], cfar_d, [], [B_const])
    ld(ft_f[:], ft_d, [], [B_const])
    for (dst, src) in ((cm_b, cm_d), (lt_b, lt_d)):
        ld(stage_f[:, 0:128], src, [], [B_stage_f])
        vcopy("vector", dst[:], stage_f[:, 0:128], [B_stage_f], [B_const])
    for c in range(2):
        ld(stage_f[0:24, 0:768], sel24_d[:, c * 768:(c + 1) * 768], [], [B_stage_f])
        vcopy("vector", sel24[:, c * 768:(c + 1) * 768], stage_f[0:24, 0:768], [B_stage_f], [B_const])
    for h in range(8):
        ld(stage_f[:, 0:512], tc_d[h], [], [B_stage_f])
        vcopy("vector", tc_b[:, h, :], stage_f[:, 0:512], [B_stage_f], [B_const])
    for h in range(16):
        for k in range(2):
            ld(stage_f[:, 0:128], biasT_d[h, k], [], [B_stage_f])
            if h < 8:
                ts("vector", nsaT[:, h, k, :], stage_f[:, 0:128], cfar[:, h:h + 1], None, ALU.subtract, None,
                   [B_stage_f, B_const], [B_const])
            else:
                vcopy("vector", swaT[:, h - 8, k, :], stage_f[:, 0:128], [B_stage_f], [B_const])
    for c in range(4):
        ld(stage_f[0:64, :], onehot_d[:, c * 1024:(c + 1) * 1024], [], [B_stage_f])
        vcopy("vector", stage_b[0:64, :], stage_f[0:64, :], [B_stage_f], [B_stage_b])
        ld(ohs[:, c * 1024:(c + 1) * 1024], stage_b[0:64, :], [B_stage_b], [B_ohs])

    PQ = []
    inflight = []

    def prep(dst, src, n, rows, Bdst, scale=None):
        for c0 in range(0, n, CH):
            PQ.append((dst, src, c0, min(CH, n - c0), rows, Bdst, scale))

    def pump(k=1):
        for _ in range(k):
            if inflight and (len(inflight) >= 2 or not PQ):
                (dst, src, c0, w, rows, Bdst, scale), (sf, sbt, Bsf, Bsb) = inflight.pop(0)
                if scale is None:
                    vcopy("gpsimd", sbt[0:rows, 0:w], sf[0:rows, 0:w], [Bsf], [Bsb])
                else:
                    ts("gpsimd", sbt[0:rows, 0:w], sf[0:rows, 0:w], scale, None, ALU.mult, None, [Bsf], [Bsb])
                stD(dst[:, c0:c0 + w], sbt[0:rows, 0:w], [Bsb], [Bdst])
            if PQ:
                task = PQ.pop(0)
                bufs = stg.next()
                (dst, src, c0, w, rows, Bdst, scale) = task
                ld(bufs[0][0:rows, 0:w], src[:, c0:c0 + w], [], [bufs[2]])
                inflight.append((task, bufs))

    def prep_flush():
        while PQ or inflight:
            pump()

    def prep_ffn(l, which):
        prep(w2s[which], w2_d[which][l], NFC * DM, 128, B_w2s[which], scale=0.5)
        for fc in range(NFC):
            prep(w1s[which][fc], w1_d[which][l, fc], 2048, 128, B_w1s[which])

    def prep_mixer(l):
        prep(wins, win_d[l], 8 * DIN, 128, B_wins)
        prep(wgs, wgate_d[l], 8 * 3072, 128, B_wgs)
        for x in range(3):
            prep(wbrs[x], wbr_d[l, x], 4 * DM, 128, B_wbrs)
        prep(wouts, wout_d[l], 8 * DM, 128, B_wouts)
        for kv in range(2):
            prep(cw1s[kv], cw1_d[l, kv], 32 * 256, 64, B_cw1s)

    base_mark = S.mark()

    def ln_epilogue(xrow, b0, b1, Bb0, Bb1, Bx, lng, lnb, B_ln, tmp, dst_ap, Bdst, dbg_ap=None):
        z, junk, xo, st, B_z, B_junk, B_xo, B_st = tmp
        stt("vector", z[:, 0:512], xrow[:, 0:512], ALPHA, b0[:, :], ALU.mult, ALU.add, [Bx, Bb0], [B_z])
        stt("vector", z[:, 512:1024], xrow[:, 512:1024], ALPHA, b1[:, :], ALU.mult, ALU.add, [Bx, Bb1], [B_z])
        act(junk[:], z[:], AF.Copy, [B_z], [B_junk, B_st], accum=st[:, 0:1])
        act(junk[:], z[:], AF.Square, [B_z], [B_junk, B_st], accum=st[:, 1:2])
        ts("vector", st[:, 2:4], st[:, 0:2], 1.0 / DM, None, ALU.mult, None, [B_st], [B_st])
        stt("vector", st[:, 4:5], st[:, 2:3], -1.0, st[:, 2:3], ALU.mult, ALU.mult, [B_st], [B_st])
        tt("vector", st[:, 5:6], st[:, 4:5], st[:, 3:4], ALU.add, [B_st], [B_st])
        act(st[:, 8:9], st[:, 5:6], AF.Sqrt, [B_st, B_const], [B_st], bias=epst[:, 0:1])
        S.op("vector", lambda e: e.reciprocal(out=st[:, 6:7], in_=st[:, 8:9]), reads=[B_st], writes=[B_st])
        stt("vector", st[:, 7:8], st[:, 2:3], -1.0, st[:, 6:7], ALU.mult, ALU.mult, [B_st], [B_st])
        act(xo[:], z[:], AF.Identity, [B_z, B_st], [B_xo], bias=st[:, 7:8], scale=st[:, 6:7])
        tt("gpsimd", xo[:], xo[:], lng[:], ALU.mult, [B_xo, B_ln], [B_xo])
        tt("gpsimd", xo[:], xo[:], lnb[:], ALU.add, [B_xo, B_ln], [B_xo])
        stD(dst_ap, xo[:], [B_xo], [Bdst])
        if dbg_ap is not None:
            ld(dbg_ap, xo[:], [B_xo], [Buf()])

    def ln_tmp():
        z = S.sb([128, DM], F32)
        junk = S.sb([128, DM], F32)
        xo = S.sb([128, DM], F32)
        st = S.sb([128, 10], F32)
        return (z, junk, xo, st, Buf(), Buf(), Buf(), Buf())

    def load_xT(xin, Bxin, xT, BxT):
        for kc in range(8):
            bk, Bbk = bank()
            for n in range(4):
                tr(bk[:, n * 128:(n + 1) * 128], xin[:, n, kc * 128:(kc + 1) * 128], ident_f[:], [Bxin, B_const], [Bbk])
            copy_any(xT[:, kc, :], bk[:, :], [Bbk], [BxT])

    def ffn_stage(l, which, src, Bsrc, dst, Bdst, lnidx, dbg_ap=None):
        m0 = S.mark()
        prep_flush()
        if which == 0:
            prep_mixer(l)
        elif l + 1 < NL:
            prep_ffn(l + 1, 0)
        lng = S.sb([128, DM], F32)
        lnb = S.sb([128, DM], F32)
        B_ln = Buf()
        ld(lng[:], lng_d[l, lnidx], [], [B_ln])
        ld(lnb[:], lnb_d[l, lnidx], [], [B_ln])
        xin = S.sb([128, 4, DM], F32)
        Bxin = Buf()
        xT = S.sb([128, 8, 512], BF16)
        BxT = Buf()
        w1p = Ring([(S.sb([128, 8, 256], BF16), Buf()) for _ in range(3)])
        hT = S.sb([128, NFC, 512], BF16)
        BhT = [Buf() for _ in range(NFC)]
        w2t = S.sb([128, NFC, DM], BF16)
        Bw2t = Buf()
        sgr = Ring([(S.sb([128, 512], F32), Buf()) for _ in range(2)])
        tmp = ln_tmp()
        ld(w2t[:], w2s[which].rearrange("p (f d) -> p f d", d=DM), [B_w2s[which]], [Bw2t])
        for tg in range(8):
            ld(xin[:], src[tg * 512:(tg + 1) * 512, :].rearrange("(n p) d -> p n d", p=128), [Bsrc[tg]], [Bxin])
            load_xT(xin, Bxin, xT, BxT)
            for fc in range(NFC):
                pump()
                wp, Bwp = w1p.next()
                ld(wp[:], w1s[which][fc].rearrange("p (k c) -> p k c", c=256), [B_w1s[which]], [Bwp])
                bg, Bbg = bank()
                for kc in range(8):
                    mm(bg[:, :], wp[:, kc, 0:128], xT[:, kc, :], kc == 0, kc == 7, [Bwp, BxT], [Bbg])
                bu, Bbu = bank()
                for kc in range(8):
                    mm(bu[:, :], wp[:, kc, 128:256], xT[:, kc, :], kc == 0, kc == 7, [Bwp, BxT], [Bbu])
                sg, Bsg = sgr.next()
                act(sg[:], bg[:, :], AF.Silu, [Bbg], [Bsg])
                tt("vector", hT[:, fc, :], sg[:], bu[:, :], ALU.mult, [Bsg, Bbu], [BhT[fc]])
            for n in range(4):
                b0, b1 = banks[5], banks[6]
                for fc in range(NFC):
                    mm(b0[:, :], hT[:, fc, n * 128:(n + 1) * 128], w2t[:, fc, 0:512], fc == 0, fc == NFC - 1,
                       [BhT[fc], Bw2t], [B_bank[5]])
                    mm(b1[:, :], hT[:, fc, n * 128:(n + 1) * 128], w2t[:, fc, 512:1024], fc == 0, fc == NFC - 1,
                       [BhT[fc], Bw2t], [B_bank[6]])
                r0 = tg * 512 + n * 128
                ln_epilogue(xin[:, n, :], b0, b1, B_bank[5], B_bank[6], Bxin, lng, lnb, B_ln, tmp,
                            dst[r0:r0 + 128, :], Bdst[tg] if isinstance(Bdst, list) else Bdst,
                            None if dbg_ap is None else dbg_ap[r0:r0 + 128, :])
        S.release(m0)

    def mixer_stage(l, src, Bsrc, dst, Bdst, dbg_ap=None):
        m00 = S.mark()
        try:
            mixer_stage_(l, src, Bsrc, dst, Bdst, dbg_ap)
        except _Stop:
            S.release(m00)

    def chk(n):
        if _SUB <= n:
            raise _Stop()

    def mixer_stage_(l, src, Bsrc, dst, Bdst, dbg_ap=None):
        m0 = S.mark()
        prep_flush()
        prep_ffn(l, 1)
        wins3 = wins.rearrange("p (k c) -> p k c", c=DIN)

        hT = S.sb([128, 8, S_LEN], BF16)
        BhT = [Buf() for _ in range(8)]
        m_h = S.mark()
        xin = S.sb([128, 4, DM], F32)
        Bxin = Buf()
        xTt = S.sb([128, 8, 512], BF16)
        for tg in range(8):
            ld(xin[:], src[tg * 512:(tg + 1) * 512, :].rearrange("(n p) d -> p n d", p=128), [Bsrc[tg]], [Bxin])
            for kc in range(8):
                bk, Bbk = bank()
                for n in range(4):
                    tr(bk[:, n * 128:(n + 1) * 128], xin[:, n, kc * 128:(kc + 1) * 128], ident_f[:], [Bxin, B_const], [Bbk])
                copy_any(hT[:, kc, tg * 512:(tg + 1) * 512], bk[:, :], [Bbk], [BhT[tg]])
        S.release(m_h)

        Qa = [S.sb([128, S_LEN], BF16) for _ in range(2)]
        BQ = [[Buf() for _ in range(8)] for _ in range(2)]
        BQaug = [Buf(), Buf()]
        Ka = [S.sb([128, S_LEN], BF16) for _ in range(2)]
        BK = [[Buf() for _ in range(8)] for _ in range(2)]
        BKaug = [Buf(), Buf()]
        V1 = [S.sb([128, 32, 128], BF16) for _ in range(2)]
        BV = [[Buf() for _ in range(8)] for _ in range(2)]
        wpr = Ring([(S.sb([128, 1024], BF16), Buf()) for _ in range(2)])
        ptr = Ring([(S.sb([128, 512], BF16), Buf()) for _ in range(3)])
        den = S.sb([64, 512], F32)
        rec = S.sb([64, 512], F32)
        coef = S.sb([64, 512], F32)
        obf = S.sb([64, 512], BF16)
        B_den, B_rec, B_coef, B_obf = Buf(), Buf(), Buf(), Buf()
        gatesT = S.sb([24, S_LEN], BF16)
        B_gates = [Buf() for _ in range(8)]
        expsink = S.sb([128, 8], F32)
        B_sink = Buf()
        nbf = S.sb([8, 1], F32)
        B_nbf = Buf()
        if _MIX_STOP <= -1:
            S.release(m0)
            return
        for b in range(2):
            for tg in range(8):
                memset("gpsimd", V1[b][:, tg * 4:(tg + 1) * 4, 64:128], 1.0, [BV[b][tg]])
        ld(expsink[:], sinks_d[l], [], [B_sink])
        act(expsink[:], expsink[:], AF.Exp, [B_sink], [B_sink])
        ld(nbf[:], bf_d[l], [], [B_nbf])
        ts("vector", nbf[:], nbf[:], -1.0, None, ALU.mult, None, [B_nbf], [B_nbf])
        chk(1)

        def load_w(name, c0=0, wd=None):
            off, w = WIN_P[name]
            if wd is None:
                wd = w
            wpf, Bwp = wpr.next()
            wp = wpf[:, 0:8 * wd].rearrange("p (k c) -> p k c", c=wd)
            ld(wpf[:, 0:8 * wd], wins[:, 8 * off:8 * off + 8 * wd], [B_wins], [Bwp])
            return wp, Bwp, wd

        def proj_fm(name, evac, c0=0, wd=None, tgs=range(8)):
            wp, Bwp, wd = load_w(name, c0, wd)
            for tg in tgs:
                bk, Bbk = bank()
                for kc in range(8):
                    mm(bk[0:wd, :], wp[:, kc, 0:wd], hT[:, kc, tg * 512:(tg + 1) * 512], kc == 0, kc == 7,
                       [Bwp, BhT[tg]], [Bbk])
                evac(tg, bk, Bbk)

        def proj_tm(name, evac):
            wp, Bwp, wd = load_w(name)
            for t in range(32):
                bk, Bbk = bank()
                for kc in range(8):
                    mm(bk[:, 0:wd], hT[:, kc, t * 128:(t + 1) * 128], wp[:, kc, 0:wd], kc == 0, kc == 7,
                       [Bwp, BhT[t // 4]], [Bbk])
                evac(t, bk, Bbk)

        def ev_scaled(dst, r0, Bd, scale):
            def f(tg, bk, Bbk):
                act(dst[0:64, tg * 512:(tg + 1) * 512], bk[r0:r0 + 64, :], AF.Copy, [Bbk], [Bd[tg]], scale=scale)
            return f

        def ev_plain(dst, r0, Bd):
            def f(tg, bk, Bbk):
                if r0 == 0:
                    vcopy("vector", dst[0:64, tg * 512:(tg + 1) * 512], bk[r0:r0 + 64, :], [Bbk], [Bd[tg]])
                else:
                    act(dst[0:64, tg * 512:(tg + 1) * 512], bk[r0:r0 + 64, :], AF.Copy, [Bbk], [Bd[tg]])
            return f

        def ev_multi(*fs):
            def f(tg, bk, Bbk):
                for g in fs:
                    g(tg, bk, Bbk)
            return f

        def attn_group(g, Q, Qreads, kdim, K, Kreads, Vt, BVt, plan, fin):
            ob, Bob = banks[5 + (g % 2)], B_bank[5 + (g % 2)]
            ntouch = [0] * 4
            total = [0] * 4
            npv = [0]
            for (kb, n_lo, n_hi, bias) in plan:
                for n in range(n_lo, n_hi):
                    total[n] += 1
            def emit_qk(item):
                (kb, n_lo, n_hi, bias) = item
                bk, Bbk = bank()
                c0, c1 = n_lo * 128, n_hi * 128
                nb = len(bias)
                mm(bk[:, c0:c1], K[0:kdim, kb * 128:(kb + 1) * 128], Q[0:kdim, g * 512 + c0:g * 512 + c1],
                   True, nb == 0, Kreads(kb) + Qreads(g), [Bbk])
                for bi, (n, bap) in enumerate(sorted(bias.items())):
                    mm(bk[:, n * 128:(n + 1) * 128], ident_b[:], bap, False, bi == nb - 1, [B_const], [Bbk])
                return (item, bk, Bbk)

            def emit_pv(rec_):
                (kb, n_lo, n_hi, bias), bk, Bbk = rec_
                c0, c1 = n_lo * 128, n_hi * 128
                pt, Bpt = ptr.next()
                act(pt[:, c0:c1], bk[:, c0:c1], AF.Exp, [Bbk], [Bpt])
                n = n_lo
                while n < n_hi:
                    last = ntouch[n] == total[n] - 1
                    n2 = n + 1
                    while n2 < n_hi and (ntouch[n2] == total[n2] - 1) == last:
                        n2 += 1
                    S.op("tensor", lambda e, o_=ob[:, n * 128:n2 * 128], l_=Vt[:, kb, :], r_=pt[:, n * 128:n2 * 128],
                         st_=(npv[0] == 0), sp_=last: e.matmul(o_, lhsT=l_, rhs=r_, start=st_, stop=sp_, skip_group_check=True),
                         reads=[BVt[kb // 4], Bpt], writes=[Bob])
                    npv[0] += 1
                    for k in range(n, n2):
                        ntouch[k] += 1
                    n = n2

            pump()
            pend = []
            for item in plan:
                pend.append(emit_qk(item))
                if len(pend) > 2:
                    emit_pv(pend.pop(0))
            while pend:
                emit_pv(pend.pop(0))
            fin(ob, Bob)

        def finalize(ob, Bob, ncols, gate_row=None, gcols=None, sink_h=None, clampden=False, out=None, Bout=None):
            act(den[:, 0:ncols], ob[64:128, 0:ncols], AF.Copy, [Bob], [B_den])
            if sink_h is not None:
                ts("vector", den[:, 0:ncols], den[:, 0:ncols], expsink[0:64, sink_h:sink_h + 1], None, ALU.add, None,
                   [B_den, B_sink], [B_den])
            if clampden:
                ts("vector", den[:, 0:ncols], den[:, 0:ncols], 1e-30, None, ALU.max, None, [B_den], [B_den])
            S.op("vector", lambda e: e.reciprocal(out=rec[:, 0:ncols], in_=den[:, 0:ncols]), reads=[B_den], writes=[B_rec])
            src_coef = rec
            Bsc = B_rec
            if gate_row is not None:
                gb, Bgb = banks[7], B_bank[7]
                mm(gb[0:64, 0:ncols], sel24[0:24, gate_row * 64:(gate_row + 1) * 64],
                   gatesT[0:24, gcols:gcols + ncols], True, True, [B_const, B_gates[gcols // 512]], [Bgb])
                tt("vector", coef[:, 0:ncols], rec[:, 0:ncols], gb[0:64, 0:ncols], ALU.mult, [B_rec, Bgb], [B_coef])
                src_coef = coef
                Bsc = B_coef
            tt("vector", out, ob[0:64, 0:ncols], src_coef[:, 0:ncols], ALU.mult, [Bob, Bsc], [Bout])

        def causal_plan(g, diag_bias, sub_bias=None):
            plan = []
            for kb in range(4 * g + 4):
                m = kb - 4 * g
                if m < 0:
                    b = {}
                    if sub_bias is not None and m == -1:
                        b[0] = sub_bias
                    plan.append((kb, 0, 4, b))
                else:
                    b = {m: diag_bias}
                    if sub_bias is not None and m + 1 < 4:
                        b[m + 1] = sub_bias
                    plan.append((kb, m, 4, b))
            return plan

        m1 = S.mark()
        spc = S.sb([8, 512], F32)
        Cc = S.sb([8, 512], F32)
        r1 = S.sb([8, 512], F32)
        r2 = S.sb([8, 512], F32)
        cb = [S.sb([8, 512], BF16) for _ in range(6)]
        carry = S.sb([8, 1], F32)
        B_f = Buf()
        memset("vector", carry[:], 0.0, [B_f])

        def f_evac(tg, bk, Bbk):
            act(spc[:], bk[0:8, :], AF.Exp, [Bbk, B_nbf], [B_f], bias=nbf[:, 0:1], scale=-1.0)
            act(spc[:], spc[:], AF.Ln, [B_f, B_const], [B_f], bias=epst[0:8, 1:2])
            S.op("vector", lambda e: e.tensor_tensor_scan(out=Cc[:], data0=spc[:], data1=spc[:], initial=carry[:, 0:1],
                                                           op0=ALU.add, op1=ALU.max), reads=[B_f], writes=[B_f])
            vcopy("vector", carry[:], Cc[:, 511:512], [B_f], [B_f])
            vcopy("vector", cb[3][:], Cc[:], [B_f], [B_f])
            tt("vector", r1[:], Cc[:], cb[3][:], ALU.subtract, [B_f], [B_f])
            vcopy("vector", cb[4][:], r1[:], [B_f], [B_f])
            tt("vector", r2[:], r1[:], cb[4][:], ALU.subtract, [B_f], [B_f])
            vcopy("vector", cb[5][:], r2[:], [B_f], [B_f])
            for j in range(3):
                ts("vector", cb[j][:], cb[3 + j][:], -1.0, None, ALU.mult, None, [B_f], [B_f])
            for j in range(6):
                ld(caug[:, j, tg * 512:(tg + 1) * 512], cb[j][:], [B_f], [B_caug])
        if _MIX_STOP >= 2:
            proj_fm("fox_f", f_evac)
        for b in range(2):
            memset("vector", Qa[b][64:70, :], 1.0, [BQaug[b]])
            memset("vector", Ka[b][64:70, :], 1.0, [BKaug[b]])
        chk(2)
        for h in range(8 if _MIX_STOP >= 3 else 0):
            b = h % 2
            proj_fm("fox_qk%d" % h, ev_multi(ev_scaled(Qa[b], 0, BQ[b], 0.125), ev_plain(Ka[b], 64, BK[b])))
            ld(Qa[b][64:67, :], caug[h, 0:3, :], [B_caug], [BQaug[b]])
            ld(Ka[b][67:70, :], caug[h, 3:6, :], [B_caug], [BKaug[b]])

            def v_evac(t, bk, Bbk, b=b):
                copy_any(V1[b][:, t, 0:64], bk[:, 0:64], [Bbk], [BV[b][t // 4]])
            proj_tm("fox_v%d" % h, v_evac)
            for g in range(8):
                def fin(ob, Bob, g=g, h=h):
                    finalize(ob, Bob, 512, out=obf[:, :], Bout=B_obf)
                    stD(oT[2, h * 64:(h + 1) * 64, g * 512:(g + 1) * 512], obf[:, :], [B_obf], [B_oT[2][g]])
                attn_group(g, Qa[b], lambda gg, b=b: [BQ[b][gg], BQaug[b]], 70,
                           Ka[b], lambda kb, b=b: [BK[b][kb // 4], BKaug[b]], V1[b], BV[b],
                           causal_plan(g, cm_b[:]), fin)
        S.release(m1)

        m1 = S.mark()
        proj_fm("swa_k", ev_multi(ev_plain(Ka[0], 0, BK[0]), ev_plain(Ka[1], 64, BK[1])))

        def sv_evac(t, bk, Bbk):
            vcopy("vector", V1[0][:, t, 0:64], bk[:, 0:64], [Bbk], [BV[0][t // 4]])
            act(V1[1][:, t, 0:64], bk[:, 64:128], AF.Copy, [Bbk], [BV[1][t // 4]])
        proj_tm("swa_v", sv_evac)
        chk(3)
        for i in range(4 if _MIX_STOP >= 4 else 0):
            proj_fm("swa_q%d" % i, ev_multi(ev_scaled(Qa[0], 0, BQ[0], 0.125), ev_scaled(Qa[1], 64, BQ[1], 0.125)))
            for b in range(2):
                h = 2 * i + b
                gk = h // 4
                for g in range(8):
                    plan = []
                    for m in range(-1, 4):
                        kb = 4 * g + m
                        if kb < 0:
                            continue
                        n_lo, n_hi = max(0, m), min(4, m + 2)
                        bias = {}
                        for n in range(n_lo, n_hi):
                            bias[n] = swaT[:, h, n - m, :]
                        plan.append((kb, n_lo, n_hi, bias))

                    def fin(ob, Bob, g=g, h=h):
                        finalize(ob, Bob, 512, sink_h=h, out=obf[:, :], Bout=B_obf)
                        stD(oT[1, h * 64:(h + 1) * 64, g * 512:(g + 1) * 512], obf[:, :], [B_obf], [B_oT[1][g]])
                    attn_group(g, Qa[b], lambda gg, b=b: [BQ[b][gg]], 64,
                               Ka[gk], lambda kb, gk=gk: [BK[gk][kb // 4]], V1[gk], BV[gk], plan, fin)
        S.release(m1)

        m1 = S.mark()
        Ksel = Ka[0]
        BKsel = BK[0]
        B_oh = BKaug[0]
        Kwin = Ka[1]
        BKwin = BK[1]
        kcT = S.sb([64, 256], BF16)
        B_kcT = Buf()
        vc1 = S.sb([128, 2, 128], BF16)
        B_vc1 = Buf()
        cwc = Ring([(S.sb([64, 8, 256], BF16), Buf()) for _ in range(2)])
        cw2t = S.sb([128, 2, 64], BF16)
        pet = S.sb([64, 32], BF16)
        B_cw2 = Buf()
        bvec = S.sb([128, 2], F32)
        B_bvec = Buf()
        gx = [S.sb([128, 256], F32) for _ in range(4)]
        B_gx = Buf()
        hid = [S.sb([128, 256], BF16) for _ in range(2)]
        B_hid = Buf()
        qt = [S.sb([64, 128], BF16) for _ in range(4)]
        B_qt = [Buf() for _ in range(4)]
        s_t = S.sb([128, 256], F32)
        e_t = S.sb([128, 256], F32)
        rs = S.sb([128, 4], F32)
        pacc = S.sb([128, 260], F32)
        impt = S.sb([128, 64], F32)
        score = S.sb([128, 64], F32)
        top8 = S.sb([128, 8], F32)
        Mt = S.sb([128, 128], F32)
        eT = S.sb([128, 2, 128], BF16)
        ocst4 = [S.sb([64, 2, 128], F32) for _ in range(4)]
        B_ocst4 = [Buf() for _ in range(4)]
        B_s, B_e, B_rs, B_pacc, B_imp, B_Mt, B_eT, B_ocst = [Buf() for _ in range(8)]
        osel = S.sb([64, 512], F32)
        B_osel = Buf()
        ocl = S.sb([64, 512], F32)
        B_ocl = Buf()
        wq2 = [S.sb([128, 8, 128], BF16) for _ in range(2)]
        B_wq2 = Buf()

        ld(Ksel[64:128, :], ohs, [B_ohs], [B_oh])
        memset("vector", vc1[:, :, 64:128], 1.0, [B_vc1])
        memset("vector", pacc[:], 0.0, [B_pacc])
        memset("vector", Mt[:, 0:64], 0.0, [B_Mt])

        def gate_evac(tg, bk, Bbk):
            act(gatesT[0:24, tg * 512:(tg + 1) * 512], bk[0:24, :], AF.Sigmoid, [Bbk], [B_gates[tg]])
        chk(4)
        proj_fm("nsa_gate", gate_evac)
        chk(5)

        for g in range(2 if _MIX_STOP >= 5 else 0):
            proj_fm("nsa_cmp%d" % g, ev_multi(ev_plain(Ka[0], 0, BK[0]), ev_plain(Ka[1], 64, BK[1])))
            for kv in range(2):
                ld(cwstage[:, 0:128], cw2_d[l, kv], [], [B_cwstage])
                vcopy("vector", cw2t[:, :, :], cwstage[:, 0:128].rearrange("p (a b) -> p a b", b=64), [B_cwstage], [B_cw2])
                ld(cwstage[0:64, 128:160], pet_d[l, kv], [], [B_cwstage])
                vcopy("vector", pet[:], cwstage[0:64, 128:160], [B_cwstage], [B_cw2])
                bh = [bank(), bank()]
                bb = [bank(), bank()]
                src_t = Ka[kv]
                for c in range(4):
                    cw, Bcw = cwc.next()
                    ld(cw[:], cw1s[kv].rearrange("d (p c) -> d p c", c=256)[:, c * 8:(c + 1) * 8, :], [B_cw1s], [Bcw])
                    for pp in range(8):
                        p = c * 8 + pp
                        for hc in range(2):
                            mm(bh[hc][0][:, 0:255], cw[:, pp, hc * 128:(hc + 1) * 128],
                               src_t[0:64, p:p + 16 * 254 + 1:16], p == 0, p == 31,
                               [Bcw] + BK[kv], [bh[hc][1]])
                            mm(bb[hc][0][:, 0:1], cw[:, pp, hc * 128:(hc + 1) * 128], pet[:, p:p + 1], p == 0, p == 31,
                               [Bcw, B_cw2], [bb[hc][1]])
                for hc in range(2):
                    vcopy("vector", bvec[:, hc:hc + 1], bb[hc][0][:, 0:1], [bb[hc][1]], [B_bvec])
                    x1, sq, u, sg = gx
                    act(x1[:, 0:255], bh[hc][0][:, 0:255], AF.Identity, [bh[hc][1], B_bvec], [B_gx], bias=bvec[:, hc:hc + 1])
                    tt("vector", sq[:, 0:255], x1[:, 0:255], x1[:, 0:255], ALU.mult, [B_gx], [B_gx])
                    ts("vector", sq[:, 0:255], sq[:, 0:255], 0.044715, 1.0, ALU.mult, ALU.add, [B_gx], [B_gx])
                    tt("vector", u[:, 0:255], sq[:, 0:255], x1[:, 0:255], ALU.mult, [B_gx], [B_gx])
                    act(sg[:, 0:255], u[:, 0:255], AF.Sigmoid, [B_gx], [B_gx], scale=1.5957691216057308)
                    tt("vector", hid[hc][:, 0:255], x1[:, 0:255], sg[:, 0:255], ALU.mult, [B_gx], [B_hid])
                if kv == 0:
                    bk, Bbk = bank()
                    for hc in range(2):
                        mm(bk[0:64, 0:255], cw2t[:, hc, :], hid[hc][:, 0:255], hc == 0, hc == 1, [B_cw2, B_hid], [Bbk])
                    vcopy("vector", kcT[:, 0:255], bk[0:64, 0:255], [Bbk], [B_kcT])
                else:
                    for c in range(2):
                        nn = 128 if c == 0 else 127
                        bk, Bbk = bank()
                        for hc in range(2):
                            mm(bk[0:nn, 0:64], hid[hc][:, c * 128:c * 128 + nn], cw2t[:, hc, :], hc == 0, hc == 1,
                               [B_cw2, B_hid], [Bbk])
                        vcopy("vector", vc1[0:nn, c, 0:64], bk[0:nn, 0:64], [Bbk], [B_vc1])
            proj_fm("nsa_kk%d" % g, ev_multi(ev_plain(Ksel, 0, BKsel), ev_plain(Kwin, 64, BKwin)))

            def nv_evac(t, bk, Bbk):
                vcopy("vector", V1[0][:, t, 0:64], bk[:, 0:64], [Bbk], [BV[0][t // 4]])
                act(V1[1][:, t, 0:64], bk[:, 64:128], AF.Copy, [Bbk], [BV[1][t // 4]])
            proj_tm("nsa_v%d" % g, nv_evac)

            for i in range(2):
                off, w = WIN_P["nsa_q%d" % (2 * g + i)]
                ld(wq2[i][:], wins[:, 8 * off:8 * off + 1024].rearrange("p (k c) -> p k c", c=128), [B_wins], [B_wq2])
            for qb in range(32):
                tg = qb // 4
                ncols = min(255, 8 * qb + 7)
                memset("gpsimd", pacc[:], 0.0, [B_pacc])
                for i in range(2):
                    bk, Bbk = bank()
                    for kc in range(8):
                        mm(bk[:, 0:128], wq2[i][:, kc, :], hT[:, kc, qb * 128:(qb + 1) * 128], kc == 0, kc == 7,
                           [B_wq2, BhT[tg]], [Bbk])
                    act(qt[2 * i][:], bk[0:64, 0:128], AF.Copy, [Bbk], [B_qt[2 * i]], scale=0.125)
                    act(qt[2 * i + 1][:], bk[64:128, 0:128], AF.Copy, [Bbk], [B_qt[2 * i + 1]], scale=0.125)
                for hp in range(4):
                    h = 4 * g + hp
                    bk, Bbk = bank()
                    mm(bk[:, 0:ncols], qt[hp][:], kcT[:, 0:ncols], True, True, [B_qt[hp], B_kcT], [Bbk])
                    tt("vector", s_t[:, 0:ncols], bk[:, 0:ncols], tc_b[:, h, 256 - 8 * qb:256 - 8 * qb + ncols], ALU.add,
                       [Bbk, B_const], [B_s])
                    act(e_t[:, 0:ncols], s_t[:, 0:ncols], AF.Exp, [B_s], [B_e, B_rs], accum=rs[:, 0:1])
                    ts("vector", rs[:, 1:2], rs[:, 0:1], 1e-30, None, ALU.max, None, [B_rs], [B_rs])
                    S.op("vector", lambda e: e.reciprocal(out=rs[:, 2:3], in_=rs[:, 1:2]), reads=[B_rs], writes=[B_rs])
                    if hp == 0:
                        ts("vector", pacc[:, 1:1 + ncols], e_t[:, 0:ncols], rs[:, 2:3], None, ALU.mult, None,
                           [B_e, B_rs], [B_pacc])
                    else:
                        stt("vector", pacc[:, 1:1 + ncols], e_t[:, 0:ncols], rs[:, 2:3], pacc[:, 1:1 + ncols],
                            ALU.mult, ALU.add, [B_e, B_rs, B_pacc], [B_pacc])
                    ob, Bob = banks[5 + (hp % 2)], B_bank[5 + (hp % 2)]
                    nt = 1 if ncols <= 128 else 2
                    for c in range(nt):
                        nn = min(128, ncols - c * 128)
                        bk2, Bbk2 = bank()
                        tr(bk2[0:nn, 0:128], e_t[:, c * 128:c * 128 + nn], ident_f[:], [B_e, B_const], [Bbk2])
                        copy_any(eT[0:nn, c, :], bk2[0:nn, 0:128], [Bbk2], [B_eT])
                        mm(ob[:, 0:128], vc1[0:nn, c, :], eT[0:nn, c, :], c == 0, c == nt - 1, [B_vc1, B_eT], [Bob])
                    ocst = ocst4[hp]
                    finalize(ob, Bob, 128, gate_row=h * 3 + 0, gcols=qb * 128, clampden=True,
                             out=ocst[:, qb % 2, :], Bout=B_ocst4[hp])
                    if qb % 2 == 1:
                        stD(ocmp[h, :, (qb - 1) * 128:(qb + 1) * 128], ocst[:, :, :].rearrange("p a b -> p (a b)"),
                            [B_ocst4[hp]], [B_ocmp[h][tg]])
                S.op("vector", lambda e: e.tensor_reduce(out=impt[:], in_=pacc[:, 0:256].rearrange("p (j m) -> p j m", m=4),
                                                         axis=AX.X, op=ALU.add), reads=[B_pacc], writes=[B_imp])
                tt("vector", impt[:], impt[:], pacc[:, 4:260:4], ALU.add, [B_imp, B_pacc], [B_imp])
                tt("vector", score[:], impt[:], ft_f[:, 64 - 2 * qb:128 - 2 * qb], ALU.add, [B_imp, B_const], [B_imp])
                ts("vector", score[:, 0:1], score[:, 0:1], 100.0, None, ALU.add, None, [B_imp], [B_imp])
                S.op("vector", lambda e: e.max(out=top8[:], in_=score[:]), reads=[B_imp], writes=[B_imp])
                ts("vector", Mt[:, 64:128], score[:], top8[:, 7:8], None, ALU.is_ge, None, [B_imp, B_Mt], [B_Mt])
                ts("vector", Mt[:, 64:128], Mt[:, 64:128], -NEG, NEG, ALU.mult, ALU.add, [B_Mt], [B_Mt])
                bk, Bbk = bank()
                tr(bk[:, 0:128], Mt[:], ident_f[:], [B_Mt, B_const], [Bbk])
                vcopy("vector", Qa[0][64:128, qb * 128:(qb + 1) * 128], bk[64:128, 0:128], [Bbk], [BQaug[0]])
                act(Qa[1][64:128, qb * 128:(qb + 1) * 128], bk[64:128, 0:128], AF.Copy, [Bbk], [BQaug[1]])

            for i in range(2):
                proj_fm("nsa_q%d" % (2 * g + i),
                        ev_multi(ev_scaled(Qa[0], 0, BQ[0], 0.125), ev_scaled(Qa[1], 64, BQ[1], 0.125)))
                for b in range(2):
                    h = 4 * g + 2 * i + b
                    for gq in range(8):
                        ld(ocl[:], ocmp[h, :, gq * 512:(gq + 1) * 512], [B_ocmp[h][gq]], [B_ocl])

                        def fin_sel(ob, Bob, gq=gq, h=h):
                            finalize(ob, Bob, 512, gate_row=h * 3 + 1, gcols=gq * 512, out=osel[:, :], Bout=B_osel)
                            if dsel is not None:
                                ld(dsel[h, :, gq * 512:(gq + 1) * 512], osel[:, :], [B_osel], [Buf()])
                            tt("gpsimd", ocl[:], ocl[:], osel[:], ALU.add, [B_ocl, B_osel], [B_ocl])
                        attn_group(gq, Qa[b], lambda gg, b=b: [BQ[b][gg], BQaug[b]], 128,
                                   Ksel, lambda kb: [BKsel[kb // 4], B_oh], V1[0], BV[0],
                                   causal_plan(gq, nsaT[:, h, 0, :], nsaT[:, h, 1, :]), fin_sel)
                        plan = []
                        for m in range(-4, 4):
                            kb = 4 * gq + m
                            if kb < 0:
                                continue
                            n_lo, n_hi = max(0, m), min(4, m + 5)
                            bias = {}
                            for n in range(n_lo, n_hi):
                                if n - m == 0:
                                    bias[n] = nsaT[:, h, 0, :]
                                elif n - m == 1:
                                    bias[n] = nsaT[:, h, 1, :]
                                elif n - m == 4:
                                    bias[n] = lt_b[:]
                            plan.append((kb, n_lo, n_hi, bias))

                        def fin_win(ob, Bob, gq=gq, h=h):
                            finalize(ob, Bob, 512, gate_row=h * 3 + 2, gcols=gq * 512, out=osel[:, :], Bout=B_osel)
                            if dwin is not None:
                                ld(dwin[h, :, gq * 512:(gq + 1) * 512], osel[:, :], [B_osel], [Buf()])
                            tt("gpsimd", obf[:, :], ocl[:], osel[:], ALU.add, [B_ocl, B_osel], [B_obf])
                            stD(oT[0, h * 64:(h + 1) * 64, gq * 512:(gq + 1) * 512], obf[:, :], [B_obf], [B_oT[0][gq]])
                        attn_group(gq, Qa[b], lambda gg, b=b: [BQ[b][gg]], 64,
                                   Kwin, lambda kb: [BKwin[kb // 4]], V1[1], BV[1], plan, fin_win)
        S.release(m1)
        S.release(m_h)
        if not _MIX_OUT:
            S.release(m0)
            return

        lng = S.sb([128, DM], F32)
        lnb = S.sb([128, DM], F32)
        B_ln = Buf()
        ld(lng[:], lng_d[l, 1], [], [B_ln])
        ld(lnb[:], lnb_d[l, 1], [], [B_ln])
        bg_t = S.sb([128, 24], F32)
        ld(bg_t[:], bgate_d[l], [], [B_ln])
        woutt = S.sb([128, 8, DM], BF16)
        B_wo = Buf()
        ld(woutt[:], wouts.rearrange("p (k c) -> p k c", c=DM), [B_wouts], [B_wo])
        xin2 = S.sb([128, 4, DM], F32)
        Bxin2 = Buf()
        ot = [S.sb([128, 4, 512], BF16) for _ in range(3)]
        B_ot = [Buf() for _ in range(3)]
        wgr = Ring([(S.sb([128, 8, 128], BF16), Buf()) for _ in range(3)])
        wbrr = Ring([(S.sb([128, 4, 128], BF16), Buf()) for _ in range(3)])
        gsr = Ring([(S.sb([128, 512], F32), Buf()) for _ in range(2)])
        macc = S.sb([128, 512], F32)
        B_macc = Buf()
        mT = S.sb([128, 8, 512], BF16)
        BmT = [Buf() for _ in range(8)]
        tmp = ln_tmp()
        wgs3 = wgs.rearrange("p (k c) -> p k c", c=3072)
        for tg in range(8):
            ld(xin2[:], src[tg * 512:(tg + 1) * 512, :].rearrange("(n p) d -> p n d", p=128), [Bsrc[tg]], [Bxin2])
            for x in range(3):
                ld(ot[x][:], oT[x, :, tg * 512:(tg + 1) * 512].rearrange("(c p) t -> p c t", p=128), [B_oT[x][tg]], [B_ot[x]])
            for dc in range(8):
                for x in range(3):
                    pump()
                    wg, Bwg = wgr.next()
                    ld(wg[:], wgs3[:, :, x * DM + dc * 128:x * DM + (dc + 1) * 128], [B_wgs], [Bwg])
                    wb, Bwb = wbrr.next()
                    ld(wb[:], wbrs[x].rearrange("p (c d) -> p c d", d=DM)[:, :, dc * 128:(dc + 1) * 128], [B_wbrs], [Bwb])
                    bg, Bbg = bank()
                    for kc in range(8):
                        mm(bg[:, :], wg[:, kc, :], hT[:, kc, tg * 512:(tg + 1) * 512], kc == 0, kc == 7, [Bwg, BhT[tg]], [Bbg])
                    bb, Bbb = bank()
                    for c in range(4):
                        mm(bb[:, :], wb[:, c, :], ot[x][:, c, :], c == 0, c == 3, [Bwb, B_ot[x]], [Bbb])
                    gs, Bgs = gsr.next()
                    act(gs[:], bg[:, :], AF.Sigmoid, [Bbg, B_ln], [Bgs], bias=bg_t[:, x * 8 + dc:x * 8 + dc + 1])
                    if x == 0:
                        tt("vector", macc[:], gs[:], bb[:, :], ALU.mult, [Bgs, Bbb], [B_macc])
                    elif x == 1:
                        tt("vector", gs[:], gs[:], bb[:, :], ALU.mult, [Bgs, Bbb], [Bgs])
                        tt("gpsimd", macc[:], macc[:], gs[:], ALU.add, [Bgs, B_macc], [B_macc])
                    else:
                        tt("vector", gs[:], gs[:], bb[:, :], ALU.mult, [Bgs, Bbb], [Bgs])
                        tt("gpsimd", mT[:, dc, :], macc[:], gs[:], ALU.add, [Bgs, B_macc], [BmT[dc]])
            for n in range(4):
                b0, b1 = banks[5], banks[6]
                for dc in range(8):
                    mm(b0[:, :], mT[:, dc, n * 128:(n + 1) * 128], woutt[:, dc, 0:512], dc == 0, dc == 7,
                       [BmT[dc], B_wo], [B_bank[5]])
                    mm(b1[:, :], mT[:, dc, n * 128:(n + 1) * 128], woutt[:, dc, 512:1024], dc == 0, dc == 7,
                       [BmT[dc], B_wo], [B_bank[6]])
                r0 = tg * 512 + n * 128
                ln_epilogue(xin2[:, n, :], b0, b1, B_bank[5], B_bank[6], Bxin2, lng, lnb, B_ln, tmp,
                            dst[r0:r0 + 128, :], Bdst[tg] if isinstance(Bdst, list) else Bdst,
                            None if dbg_ap is None else dbg_ap[r0:r0 + 128, :])
        S.release(m0)

    cur, Bcur = x_in, [Buf() for _ in range(8)]
    pp = 0
    prep_ffn(0, 0)
    for l in range(NL):
        for st in range(3):
            lastst = (l == NL - 1 and st == 2) or (_STOP_AFTER is not None and l * 3 + st == _STOP_AFTER - 1)
            if _STOP_AFTER is not None and l * 3 + st >= _STOP_AFTER:
                continue
            if lastst:
                dst, Bdst = y_out, B_y
            else:
                dst, Bdst = xs[pp], B_xs[pp]
            dbg_ap = dbg_d.get("l%ds%d" % (l, st))
            if st == 0:
                ffn_stage(l, 0, cur, Bcur, dst, Bdst, 0, dbg_ap)
            elif st == 1:
                mixer_stage(l, cur, Bcur, dst, Bdst, dbg_ap)
            else:
                ffn_stage(l, 1, cur, Bcur, dst, Bdst, 2, dbg_ap)
            cur, Bcur = dst, Bdst
            pp ^= 1
    S.finish()
    return nc


def _consts(rel_bias):
    c = {}
    c["ident"] = np.eye(128, dtype=np.float32)
    s = np.arange(S_LEN)
    c["onehot"] = (s[None, :] // 64 == np.arange(64)[:, None]).astype(np.float32)
    j = np.arange(128)[:, None]
    i = np.arange(128)[None, :]
    d0 = i - j
    d1 = 128 + i - j
    bk0 = _t5_bucket(d0)
    bk1 = _t5_bucket(d1)
    relT = np.ascontiguousarray(rel_bias.T)
    biasT = np.empty((16, 2, 128, 128), np.float32)
    for h in range(16):
        t0 = relT[h][bk0]
        t1 = relT[h][bk1]
        biasT[h, 0] = np.where(d0 >= 0, t0, np.float32(NEG))
        if h < 8:
            biasT[h, 1] = t1
        else:
            biasT[h, 1] = np.where(d1 < 128, t1, np.float32(NEG))
    c["biasT"] = biasT
    c["cfar"] = np.ascontiguousarray(np.broadcast_to(rel_bias[31][None, :], (128, 16))).astype(np.float32)
    ii = np.arange(128)[:, None]
    u = np.arange(512)[None, :]
    dist = ii - 16 * (u - 256) - 31
    bkc = _t5_bucket(dist)
    tc = np.empty((8, 128, 512), np.float32)
    for h in range(8):
        tc[h] = np.where(dist >= 0, relT[h][bkc], np.float32(NEG))
    c["tc"] = tc
    uu = np.arange(128)[None, :] - 64
    curb = (ii >= 64).astype(np.int64)
    ft = np.zeros((128, 128), np.float32)
    ft[np.broadcast_to(uu > curb, (128, 128))] = -100.0
    ft[np.broadcast_to((uu == curb) | (uu == curb - 1), (128, 128))] = 100.0
    c["ft"] = ft
    c["cm"] = np.where(d0 >= 0, 0.0, NEG).astype(np.float32)
    c["lt"] = np.where(i < j, 0.0, NEG).astype(np.float32)
    sel = np.zeros((24, 1536), np.float32)
    for r in range(24):
        sel[r, r * 64:(r + 1) * 64] = 1.0
    c["sel24"] = sel
    return c


def _layer_weights(inp, ls):
    L = len(ls)
    w = {}

    def stack(f):
        return np.ascontiguousarray(np.stack([f(l) for l in ls]))
    for i, (k1, k2) in enumerate((("ffn1_w1", "ffn1_w2"), ("ffn2_w1", "ffn2_w2"))):
        w["w1_%d" % i] = stack(lambda l: inp[k1][l].reshape(8, 128, 2, NFC, 128).transpose(3, 1, 0, 2, 4).reshape(NFC, 128, 2048))
        w["w2_%d" % i] = stack(lambda l: inp[k2][l].reshape(NFC, 128, DM).transpose(1, 0, 2).reshape(128, NFC * DM))
    def _win_layout(l):
        wp = inp["w_in"][l][:, WIN_PERM]
        blocks = []
        for name, (off, wd) in WIN_P.items():
            blocks.append(wp[:, off:off + wd].reshape(8, 128, wd).transpose(1, 0, 2).reshape(128, 8 * wd))
        return np.concatenate(blocks, axis=1)
    w["win"] = stack(_win_layout)
    w["wgate"] = stack(lambda l: inp["w_gate"][l].reshape(8, 128, 3072).transpose(1, 0, 2).reshape(128, 8 * 3072))
    w["bgate"] = stack(lambda l: inp["b_gate"][l].reshape(24, 128).T)
    w["wbr"] = stack(lambda l: np.stack([inp[k][l].reshape(4, 128, DM).transpose(1, 0, 2).reshape(128, 4 * DM)
                                         for k in ("w_br_a", "w_br_b", "w_br_c")]))
    w["wout"] = stack(lambda l: inp["w_out"][l].reshape(8, 128, DM).transpose(1, 0, 2).reshape(128, 8 * DM))
    w["cw1"] = stack(lambda l: np.stack([inp[k][l].reshape(32, 64, 256).transpose(1, 0, 2).reshape(64, 32 * 256)
                                         for k in ("cmp_k_w1", "cmp_v_w1")]))
    w["cw2"] = stack(lambda l: np.stack([inp[k][l].reshape(2, 128, 64).transpose(1, 0, 2).reshape(128, 128)
                                         for k in ("cmp_k_w2", "cmp_v_w2")]))
    w["pet"] = stack(lambda l: np.stack([inp[k][l].T for k in ("cmp_pe_k", "cmp_pe_v")]))
    w["lng"] = stack(lambda l: np.stack([np.broadcast_to(inp[k][l][None, :], (128, DM)) for k in ("ln1_g", "ln2_g", "ln3_g")]))
    w["lnb"] = stack(lambda l: np.stack([np.broadcast_to(inp[k][l][None, :], (128, DM)) for k in ("ln1_b", "ln2_b", "ln3_b")]))
    w["sinks"] = stack(lambda l: np.broadcast_to(inp["swa_sinks"][l][None, :], (128, 8)))
    w["bf"] = stack(lambda l: inp["fox_b_f"][l].reshape(8, 1))
    return {k: np.ascontiguousarray(v, dtype=np.float32) for k, v in w.items()}


_PROG = {}


def _get_prog(NL):
    if NL not in _PROG:
        _PROG[NL] = build_program(NL)
    return _PROG[NL]


def kernel(**inputs):
    inp = {k: np.asarray(v, dtype=np.float32) for k, v in inputs.items()}
    consts = _consts(inp["rel_bias"])
    x = inp["x"]
    nb = x.shape[0]
    NL = NLAYERS
    nc = _get_prog(NL)
    wl = _layer_weights(inp, list(range(NL)))
    in_maps = []
    for b in range(nb):
        m = {"x": np.ascontiguousarray(x[b])}
        m.update(wl)
        m.update(consts)
        in_maps.append(m)
    res = run_bass_kernel_spmd(nc, in_maps, core_ids=list(range(nb)))
    return np.stack([np.asarray(r["y"], dtype=np.float32) for r in res.results], axis=0)
```

```python
from contextlib import ExitStack
import numpy as np
import concourse.bass as bass
import concourse.mybir as mybir
from concourse.bass_utils import run_bass_kernel_spmd

F32 = mybir.dt.float32
BF16 = mybir.dt.bfloat16
AF = mybir.ActivationFunctionType
ALU = mybir.AluOpType
AX = mybir.AxisListType

ENGS = ["tensor", "vector", "scalar", "gpsimd", "sync"]
S_LEN = 4096
DM = 1024
FF = 2816
NFC = 22
DIN = 3616
NEG = -30000.0
ALPHA = 8.0 ** 0.25
LN_EPS = 1e-5
NLAYERS = 4
_STOP_AFTER = None
_MIX_STOP = 99
_MIX_OUT = True
_SUB = 99


class _Stop(Exception):
    pass


class Buf:
    __slots__ = ("w", "r", "excl")

    def __init__(self, excl=False):
        self.w = None
        self.r = {}
        self.excl = excl


class Sched:
    def __init__(self, nc, n_dma_sems=12, rot=20000):
        self.nc = nc
        self.prog = {e: [] for e in ENGS}
        self.seen = {e: {} for e in ENGS}
        self.cnt = {e: 0 for e in ENGS}
        self.epoch = {e: 0 for e in ENGS}
        self.rot = rot
        self.n_dma = n_dma_sems
        self.dma_uses = {}
        self.dma_rr = {e: 0 for e in ENGS}
        self.semkeys = []
        self.semset = set()
        self.stack = ExitStack()
        self.sb_off = 16512
        self.sb_id = 0

    def sb(self, shape, dtype):
        n = 1
        for s in shape[1:]:
            n *= s
        nbytes = n * (4 if dtype == F32 else 2)
        off = (self.sb_off + 63) // 64 * 64
        assert off + nbytes <= 229000, ("SBUF overflow", off, nbytes)
        self.sb_off = off + nbytes
        self.peak = max(getattr(self, 'peak', 0), self.sb_off)
        self.sb_id += 1
        return self.nc.alloc_sbuf_tensor_at("t%d" % self.sb_id, list(shape), dtype, offset=off)

    def mark(self):
        return self.sb_off

    def release(self, m):
        self.barrier()
        self.sb_off = m

    def _key(self, key):
        if key not in self.semset:
            self.semset.add(key)
            self.semkeys.append(key)
        return key

    def _collect(self, eng, reads, writes):
        deps = {}

        def add(tok):
            if tok is None:
                return
            k, v = tok
            if deps.get(k, 0) < v:
                deps[k] = v
        for b in reads:
            add(b.w)
            if b.excl:
                for k, v in b.r.items():
                    if k[0] != eng:
                        add((k, v))
        for b in writes:
            add(b.w)
            for k, v in b.r.items():
                add((k, v))
        waits = []
        seen = self.seen[eng]
        for k, v in deps.items():
            if eng == "tensor" and k[0] == "tensor":
                continue
            if seen.get(k, 0) >= v:
                continue
            seen[k] = v
            waits.append((k, v))
        return waits

    def _update(self, tok, reads, writes):
        k, v = tok
        for b in reads:
            if b.r.get(k, 0) < v:
                b.r[k] = v
        for b in writes:
            b.w = tok
            b.r = {}

    def op(self, eng, emit, reads=(), writes=()):
        waits = self._collect(eng, reads, writes)
        self.cnt[eng] += 1
        if self.cnt[eng] > self.rot:
            self.epoch[eng] += 1
            self.cnt[eng] = 1
        tok = (self._key((eng, self.epoch[eng])), self.cnt[eng])
        self.prog[eng].append((waits, emit, tok, 1))
        self._update(tok, reads, writes)
        return tok

    def dma(self, q, emit, reads=(), writes=()):
        i = self.dma_rr[q]
        self.dma_rr[q] = (i + 1) % self.n_dma
        key = self._key(("dma", q, i))
        k = self.dma_uses.get(key, 0) + 1
        self.dma_uses[key] = k
        waits = self._collect(q, reads, writes)
        if k > 1 and self.seen[q].get(key, 0) < 16 * (k - 1):
            self.seen[q][key] = 16 * (k - 1)
            waits.append((key, 16 * (k - 1)))
        tok = (key, 16 * k)
        self.prog[q].append((waits, emit, tok, 16))
        self._update(tok, reads, writes)
        return tok

    def _all_tokens(self):
        toks = []
        for key, k in self.dma_uses.items():
            toks.append((key, 16 * k))
        for e in ENGS:
            if self.cnt[e] > 0:
                toks.append(((e, self.epoch[e]), self.cnt[e]))
        return toks

    def barrier(self):
        toks = self._all_tokens()
        for e in ENGS:
            waits = []
            for k, v in toks:
                if self.seen[e].get(k, 0) < v:
                    self.seen[e][k] = v
                    waits.append((k, v))
            if waits:
                self.prog[e].append((waits, None, None, 0))

    def finish(self):
        nc = self.nc
        self.barrier()
        sems = {}
        for key in self.semkeys:
            nm = "s_" + "_".join(str(x) for x in key)
            sems[key] = self.stack.enter_context(nc.semaphore(nm))
        prog = self.prog
        with nc.Block() as block:
            def mk(ename):
                def body(eng):
                    for waits, emit, tok, inc in prog[ename]:
                        for k, v in waits:
                            eng.wait_ge(sems[k], v)
                        if emit is not None:
                            emit(eng).then_inc(sems[tok[0]], inc)
                return body
            block.tensor(mk("tensor"))
            block.vector(mk("vector"))
            block.scalar(mk("scalar"))
            block.gpsimd(mk("gpsimd"))
            block.sync(mk("sync"))
        self.stack.close()


class Ring:
    def __init__(self, items):
        self.items = items
        self.i = 0

    def next(self):
        it = self.items[self.i]
        self.i = (self.i + 1) % len(self.items)
        return it


def _win_passes():
    P = {}
    off = 0
    perm = []

    def add(name, cols):
        nonlocal off
        P[name] = (off, len(cols))
        perm.extend(cols)
        off += len(cols)
    r = lambda a, n: list(range(a, a + n))
    for h in range(8):
        add("fox_qk%d" % h, r(2072 + h * 64, 64) + r(2584 + h * 64, 64))
    add("fox_f", r(3608, 8))
    for h in range(8):
        add("fox_v%d" % h, r(3096 + h * 64, 64))
    for i in range(4):
        add("swa_q%d" % i, r(1304 + i * 128, 128))
    add("swa_k", r(1816, 128))
    add("swa_v", r(1944, 128))
    for i in range(4):
        add("nsa_q%d" % i, r(i * 128, 128))
    for g in range(2):
        add("nsa_cmp%d" % g, r(512 + g * 64, 64) + r(640 + g * 64, 64))
        add("nsa_kk%d" % g, r(768 + g * 64, 64) + r(1024 + g * 64, 64))
        add("nsa_v%d" % g, r(896 + g * 64, 64) + r(1152 + g * 64, 64))
    add("nsa_gate", r(1280, 24))
    assert off == DIN and sorted(perm) == list(range(DIN))
    return P, np.array(perm)


WIN_P, WIN_PERM = _win_passes()


def _t5_bucket(n):
    n = np.maximum(n, 0)
    lr = np.log(np.maximum(n, 1).astype(np.float32) / np.float32(16)) / np.float32(np.log(128 / 16))
    large = 16 + (lr.astype(np.float32) * np.float32(16)).astype(np.int32)
    return np.where(n < 16, n, np.minimum(large, 31))


def build_program(NL, dbg=None):
    nc = bass.Bass("TRN2", target_bir_lowering=False)
    S = Sched(nc)

    def din(name, shape, dt=F32):
        return nc.dram_tensor(name, list(shape), dt, kind="ExternalInput").ap()

    def dscr(name, shape, dt):
        return nc.dram_tensor(name, list(shape), dt, kind="Internal").ap()

    x_in = din("x", [S_LEN, DM])
    y_out = nc.dram_tensor("y", [S_LEN, DM], F32, kind="ExternalOutput").ap()
    w1_d = [din("w1_%d" % i, [NL, NFC, 128, 2048]) for i in range(2)]
    w2_d = [din("w2_%d" % i, [NL, 128, NFC * DM]) for i in range(2)]
    win_d = din("win", [NL, 128, 8 * DIN])
    wgate_d = din("wgate", [NL, 128, 8 * 3072])
    bgate_d = din("bgate", [NL, 128, 24])
    wbr_d = din("wbr", [NL, 3, 128, 4 * DM])
    wout_d = din("wout", [NL, 128, 8 * DM])
    cw1_d = din("cw1", [NL, 2, 64, 32 * 256])
    cw2_d = din("cw2", [NL, 2, 128, 128])
    pet_d = din("pet", [NL, 2, 64, 32])
    lng_d = din("lng", [NL, 3, 128, DM])
    lnb_d = din("lnb", [NL, 3, 128, DM])
    sinks_d = din("sinks", [NL, 128, 8])
    bf_d = din("bf", [NL, 8, 1])
    ident_d = din("ident", [128, 128])
    onehot_d = din("onehot", [64, S_LEN])
    biasT_d = din("biasT", [16, 2, 128, 128])
    cfar_d = din("cfar", [128, 16])
    tc_d = din("tc", [8, 128, 512])
    ft_d = din("ft", [128, 128])
    cm_d = din("cm", [128, 128])
    lt_d = din("lt", [128, 128])
    sel24_d = din("sel24", [24, 1536])

    xs = [dscr("xs0", [S_LEN, DM], F32), dscr("xs1", [S_LEN, DM], F32)]
    w1s = [dscr("w1s%d" % i, [NFC, 128, 2048], BF16) for i in range(2)]
    w2s = [dscr("w2s%d" % i, [128, NFC * DM], BF16) for i in range(2)]
    wins = dscr("wins", [128, 8 * DIN], BF16)
    wgs = dscr("wgs", [128, 8 * 3072], BF16)
    wbrs = dscr("wbrs", [3, 128, 4 * DM], BF16)
    wouts = dscr("wouts", [128, 8 * DM], BF16)
    cw1s = dscr("cw1s", [2, 64, 32 * 256], BF16)
    ohs = dscr("ohs", [64, S_LEN], BF16)
    if dbg and "oT" in dbg:
        oT = nc.dram_tensor("oT", [3, 512, S_LEN], BF16, kind="ExternalOutput").ap()
    else:
        oT = dscr("oT", [3, 512, S_LEN], BF16)
    if dbg and "oT" in dbg:
        ocmp = nc.dram_tensor("ocmp", [8, 64, S_LEN], F32, kind="ExternalOutput").ap()
        dsel = nc.dram_tensor("dsel", [8, 64, S_LEN], F32, kind="ExternalOutput").ap()
        dwin = nc.dram_tensor("dwin", [8, 64, S_LEN], F32, kind="ExternalOutput").ap()
    else:
        ocmp = dscr("ocmp", [8, 64, S_LEN], F32)
        dsel = dwin = None
    caug = dscr("caug", [8, 6, S_LEN], BF16)
    B_xs = [[Buf() for _ in range(8)] for _ in range(2)]
    B_w1s = [Buf(), Buf()]
    B_w2s = [Buf(), Buf()]
    B_wins, B_wgs, B_wbrs, B_wouts, B_cw1s, B_ohs = Buf(), Buf(), Buf(), Buf(), Buf(), Buf()
    B_oT = [[Buf() for _ in range(8)] for _ in range(3)]
    B_ocmp = [[Buf() for _ in range(8)] for _ in range(8)]
    B_caug = Buf()
    B_y = Buf()
    dbg_d = {}
    if dbg:
        for nm in dbg:
            if nm == "oT":
                continue
            dbg_d[nm] = nc.dram_tensor("dbg_" + nm, [S_LEN, DM], F32, kind="ExternalOutput").ap()

    banks = [S.stack.enter_context(nc.psum_tensor("bank%d" % i, [128, 512], F32)) for i in range(8)]
    B_bank = [Buf(excl=True) for _ in range(8)]
    ring5 = Ring([0, 1, 2, 3, 4])

    def bank():
        i = ring5.next()
        return banks[i], B_bank[i]

    cpy_rr = [0]

    def mm(out, lhsT, rhs, start, stop, reads, writes):
        S.op("tensor", lambda e: e.matmul(out, lhsT=lhsT, rhs=rhs, start=start, stop=stop),
             reads=reads, writes=writes)

    def tr(out, in_, ident, reads, writes):
        S.op("tensor", lambda e: e.transpose(out, in_, ident), reads=reads, writes=writes)

    def act(out, in_, func, reads, writes, bias=None, scale=None, accum=None):
        kw = {}
        if bias is not None:
            kw["bias"] = bias
        if scale is not None:
            kw["scale"] = scale
        if accum is not None:
            kw["accum_out"] = accum
        S.op("scalar", lambda e: e.activation(out=out, in_=in_, func=func, **kw), reads=reads, writes=writes)

    def vcopy(eng, out, in_, reads, writes):
        S.op(eng, lambda e: e.tensor_copy(out=out, in_=in_), reads=reads, writes=writes)

    def copy_any(out, in_, reads, writes, psum=True):
        cpy_rr[0] ^= 1
        if cpy_rr[0]:
            vcopy("vector", out, in_, reads, writes)
        else:
            act(out, in_, AF.Copy, reads, writes)

    def tt(eng, out, in0, in1, op, reads, writes):
        S.op(eng, lambda e: e.tensor_tensor(out=out, in0=in0, in1=in1, op=op), reads=reads, writes=writes)

    def ts(eng, out, in0, s1, s2, op0, op1, reads, writes):
        if op1 is None:
            S.op(eng, lambda e: e.tensor_scalar(out=out, in0=in0, scalar1=s1, scalar2=None, op0=op0),
                 reads=reads, writes=writes)
        else:
            S.op(eng, lambda e: e.tensor_scalar(out=out, in0=in0, scalar1=s1, scalar2=s2, op0=op0, op1=op1),
                 reads=reads, writes=writes)

    def stt(eng, out, in0, scalar, in1, op0, op1, reads, writes):
        S.op(eng, lambda e: e.scalar_tensor_tensor(out=out, in0=in0, scalar=scalar, in1=in1, op0=op0, op1=op1),
             reads=reads, writes=writes)

    def memset(eng, ap, val, writes):
        S.op(eng, lambda e: e.memset(ap, val), writes=writes)

    def ld(out, in_, reads, writes):
        S.dma("sync", lambda e: e.dma_start(out=out, in_=in_), reads=reads, writes=writes)

    ident_f = S.sb([128, 128], F32)
    ident_b = S.sb([128, 128], BF16)
    nsaT = S.sb([128, 8, 2, 128], BF16)
    swaT = S.sb([128, 8, 2, 128], BF16)
    cm_b = S.sb([128, 128], BF16)
    lt_b = S.sb([128, 128], BF16)
    tc_b = S.sb([128, 8, 512], BF16)
    ft_f = S.sb([128, 128], F32)
    sel24 = S.sb([24, 1536], BF16)
    cfar = S.sb([128, 16], F32)
    B_const = Buf()
    CH = 1024
    stg = Ring([(S.sb([128, CH], F32), S.sb([128, CH], BF16), Buf(), Buf()) for _ in range(3)])
    stage_f, stage_b, B_stage_f, B_stage_b = stg.items[0]
    cwstage = S.sb([128, 160], F32)
    B_cwstage = Buf()

    def stD(out, in_, reads, writes):
        S.dma("gpsimd", lambda e: e.dma_start(out=out, in_=in_), reads=reads, writes=writes)

    epst = S.sb([128, 2], F32)
    memset("vector", epst[:, 0:1], LN_EPS, [B_const])
    memset("vector", epst[:, 1:2], 1.0, [B_const])
    ld(ident_f[:], ident_d, [], [B_const])
    vcopy("vector", ident_b[:], ident_f[:], [B_const], [B_const])
    ld(cfar[:], cfar_d, [], [B_const])
    ld(ft_f[:], ft_d, [], [B_const])
    for (dst, src) in ((cm_b, cm_d), (lt_b, lt_d)):
        ld(stage_f[:, 0:128], src, [], [B_stage_f])
        vcopy("vector", dst[:], stage_f[:, 0:128], [B_stage_f], [B_const])
    for c in range(2):
        ld(stage_f[0:24, 0:768], sel24_d[:, c * 768:(c + 1) * 768], [], [B_stage_f])
        vcopy("vector", sel24[:, c * 768:(c + 1) * 768], stage_f[0:24, 0:768], [B_stage_f], [B_const])
    for h in range(8):
        ld(stage_f[:, 0:512], tc_d[h], [], [B_stage_f])
        vcopy("vector", tc_b[:, h, :], stage_f[:, 0:512], [B_stage_f], [B_const])
    for h in range(16):
        for k in range(2):
            ld(stage_f[:, 0:128], biasT_d[h, k], [], [B_stage_f])
            if h < 8:
                ts("vector", nsaT[:, h, k, :], stage_f[:, 0:128], cfar[:, h:h + 1], None, ALU.subtract, None,
                   [B_stage_f, B_const], [B_const])
            else:
                vcopy("vector", swaT[:, h - 8, k, :], stage_f[:, 0:128], [B_stage_f], [B_const])
    for c in range(4):
        ld(stage_f[0:64, :], onehot_d[:, c * 1024:(c + 1) * 1024], [], [B_stage_f])
        vcopy("vector", stage_b[0:64, :], stage_f[0:64, :], [B_stage_f], [B_stage_b])
        ld(ohs[:, c * 1024:(c + 1) * 1024], stage_b[0:64, :], [B_stage_b], [B_ohs])

    PQ = []
    inflight = []

    def prep(dst, src, n, rows, Bdst, scale=None):
        for c0 in range(0, n, CH):
            PQ.append((dst, src, c0, min(CH, n - c0), rows, Bdst, scale))

    def pump(k=1):
        for _ in range(k):
            if inflight and (len(inflight) >= 2 or not PQ):
                (dst, src, c0, w, rows, Bdst, scale), (sf, sbt, Bsf, Bsb) = inflight.pop(0)
                if scale is None:
                    vcopy("gpsimd", sbt[0:rows, 0:w], sf[0:rows, 0:w], [Bsf], [Bsb])
                else:
                    ts("gpsimd", sbt[0:rows, 0:w], sf[0:rows, 0:w], scale, None, ALU.mult, None, [Bsf], [Bsb])
                stD(dst[:, c0:c0 + w], sbt[0:rows, 0:w], [Bsb], [Bdst])
            if PQ:
                task = PQ.pop(0)
                bufs = stg.next()
                (dst, src, c0, w, rows, Bdst, scale) = task
                ld(bufs[0][0:rows, 0:w], src[:, c0:c0 + w], [], [bufs[2]])
                inflight.append((task, bufs))

    def prep_flush():
        while PQ or inflight:
            pump()

    def prep_ffn(l, which):
        prep(w2s[which], w2_d[which][l], NFC * DM, 128, B_w2s[which], scale=0.5)
        for fc in range(NFC):
            prep(w1s[which][fc], w1_d[which][l, fc], 2048, 128, B_w1s[which])

    def prep_mixer(l):
        prep(wins, win_d[l], 8 * DIN, 128, B_wins)
        prep(wgs, wgate_d[l], 8 * 3072, 128, B_wgs)
        for x in range(3):
            prep(wbrs[x], wbr_d[l, x], 4 * DM, 128, B_wbrs)
        prep(wouts, wout_d[l], 8 * DM, 128, B_wouts)
        for kv in range(2):
            prep(cw1s[kv], cw1_d[l, kv], 32 * 256, 64, B_cw1s)

    base_mark = S.mark()

    def ln_epilogue(xrow, b0, b1, Bb0, Bb1, Bx, lng, lnb, B_ln, tmp, dst_ap, Bdst, dbg_ap=None):
        z, junk, xo, st, B_z, B_junk, B_xo, B_st = tmp
        stt("vector", z[:, 0:512], xrow[:, 0:512], ALPHA, b0[:, :], ALU.mult, ALU.add, [Bx, Bb0], [B_z])
        stt("vector", z[:, 512:1024], xrow[:, 512:1024], ALPHA, b1[:, :], ALU.mult, ALU.add, [Bx, Bb1], [B_z])
        act(junk[:], z[:], AF.Copy, [B_z], [B_junk, B_st], accum=st[:, 0:1])
        act(junk[:], z[:], AF.Square, [B_z], [B_junk, B_st], accum=st[:, 1:2])
        ts("vector", st[:, 2:4], st[:, 0:2], 1.0 / DM, None, ALU.mult, None, [B_st], [B_st])
        stt("vector", st[:, 4:5], st[:, 2:3], -1.0, st[:, 2:3], ALU.mult, ALU.mult, [B_st], [B_st])
        tt("vector", st[:, 5:6], st[:, 4:5], st[:, 3:4], ALU.add, [B_st], [B_st])
        act(st[:, 8:9], st[:, 5:6], AF.Sqrt, [B_st, B_const], [B_st], bias=epst[:, 0:1])
        S.op("vector", lambda e: e.reciprocal(out=st[:, 6:7], in_=st[:, 8:9]), reads=[B_st], writes=[B_st])
        stt("vector", st[:, 7:8], st[:, 2:3], -1.0, st[:, 6:7], ALU.mult, ALU.mult, [B_st], [B_st])
        act(xo[:], z[:], AF.Identity, [B_z, B_st], [B_xo], bias=st[:, 7:8], scale=st[:, 6:7])
        tt("gpsimd", xo[:], xo[:], lng[:], ALU.mult, [B_xo, B_ln], [B_xo])
        tt("gpsimd", xo[:], xo[:], lnb[:], ALU.add, [B_xo, B_ln], [B_xo])
        stD(dst_ap, xo[:], [B_xo], [Bdst])
        if dbg_ap is not None:
            ld(dbg_ap, xo[:], [B_xo], [Buf()])

    def ln_tmp():
        z = S.sb([128, DM], F32)
        junk = S.sb([128, DM], F32)
        xo = S.sb([128, DM], F32)
        st = S.sb([128, 10], F32)
        return (z, junk, xo, st, Buf(), Buf(), Buf(), Buf())

    def load_xT(xin, Bxin, xT, BxT):
        for kc in range(8):
            bk, Bbk = bank()
            for n in range(4):
                tr(bk[:, n * 128:(n + 1) * 128], xin[:, n, kc * 128:(kc + 1) * 128], ident_f[:], [Bxin, B_const], [Bbk])
            copy_any(xT[:, kc, :], bk[:, :], [Bbk], [BxT])

    def ffn_stage(l, which, src, Bsrc, dst, Bdst, lnidx, dbg_ap=None):
        m0 = S.mark()
        prep_flush()
        if which == 0:
            prep_mixer(l)
        elif l + 1 < NL:
            prep_ffn(l + 1, 0)
        lng = S.sb([128, DM], F32)
        lnb = S.sb([128, DM], F32)
        B_ln = Buf()
        ld(lng[:], lng_d[l, lnidx], [], [B_ln])
        ld(lnb[:], lnb_d[l, lnidx], [], [B_ln])
        xin = S.sb([128, 4, DM], F32)
        Bxin = Buf()
        xT = S.sb([128, 8, 512], BF16)
        BxT = Buf()
        w1p = Ring([(S.sb([128, 8, 256], BF16), Buf()) for _ in range(3)])
        hT = S.sb([128, NFC, 512], BF16)
        BhT = [Buf() for _ in range(NFC)]
        w2t = S.sb([128, NFC, DM], BF16)
        Bw2t = Buf()
        sgr = Ring([(S.sb([128, 512], F32), Buf()) for _ in range(2)])
        tmp = ln_tmp()
        ld(w2t[:], w2s[which].rearrange("p (f d) -> p f d", d=DM), [B_w2s[which]], [Bw2t])
        for tg in range(8):
            ld(xin[:], src[tg * 512:(tg + 1) * 512, :].rearrange("(n p) d -> p n d", p=128), [Bsrc[tg]], [Bxin])
            load_xT(xin, Bxin, xT, BxT)
            for fc in range(NFC):
                pump()
                wp, Bwp = w1p.next()
                ld(wp[:], w1s[which][fc].rearrange("p (k c) -> p k c", c=256), [B_w1s[which]], [Bwp])
                bg, Bbg = bank()
                for kc in range(8):
                    mm(bg[:, :], wp[:, kc, 0:128], xT[:, kc, :], kc == 0, kc == 7, [Bwp, BxT], [Bbg])
                bu, Bbu = bank()
                for kc in range(8):
                    mm(bu[:, :], wp[:, kc, 128:256], xT[:, kc, :], kc == 0, kc == 7, [Bwp, BxT], [Bbu])
                sg, Bsg = sgr.next()
                act(sg[:], bg[:, :], AF.Silu, [Bbg], [Bsg])
                tt("vector", hT[:, fc, :], sg[:], bu[:, :], ALU.mult, [Bsg, Bbu], [BhT[fc]])
            for n in range(4):
                b0, b1 = banks[5], banks[6]
                for fc in range(NFC):
                    mm(b0[:, :], hT[:, fc, n * 128:(n + 1) * 128], w2t[:, fc, 0:512], fc == 0, fc == NFC - 1,
                       [BhT[fc], Bw2t], [B_bank[5]])
                    mm(b1[:, :], hT[:, fc, n * 128:(n + 1) * 128], w2t[:, fc, 512:1024], fc == 0, fc == NFC - 1,
                       [BhT[fc], Bw2t], [B_bank[6]])
                r0 = tg * 512 + n * 128
                ln_epilogue(xin[:, n, :], b0, b1, B_bank[5], B_bank[6], Bxin, lng, lnb, B_ln, tmp,
                            dst[r0:r0 + 128, :], Bdst[tg] if isinstance(Bdst, list) else Bdst,
                            None if dbg_ap is None else dbg_ap[r0:r0 + 128, :])
        S.release(m0)

    def mixer_stage(l, src, Bsrc, dst, Bdst, dbg_ap=None):
        m00 = S.mark()
        try:
            mixer_stage_(l, src, Bsrc, dst, Bdst, dbg_ap)
        except _Stop:
            S.release(m00)

    def chk(n):
        if _SUB <= n:
            raise _Stop()

    def mixer_stage_(l, src, Bsrc, dst, Bdst, dbg_ap=None):
        m0 = S.mark()
        prep_flush()
        prep_ffn(l, 1)
        wins3 = wins.rearrange("p (k c) -> p k c", c=DIN)

        hT = S.sb([128, 8, S_LEN], BF16)
        BhT = [Buf() for _ in range(8)]
        m_h = S.mark()
        xin = S.sb([128, 4, DM], F32)
        Bxin = Buf()
        xTt = S.sb([128, 8, 512], BF16)
        for tg in range(8):
            ld(xin[:], src[tg * 512:(tg + 1) * 512, :].rearrange("(n p) d -> p n d", p=128), [Bsrc[tg]], [Bxin])
            for kc in range(8):
                bk, Bbk = bank()
                for n in range(4):
                    tr(bk[:, n * 128:(n + 1) * 128], xin[:, n, kc * 128:(kc + 1) * 128], ident_f[:], [Bxin, B_const], [Bbk])
                copy_any(hT[:, kc, tg * 512:(tg + 1) * 512], bk[:, :], [Bbk], [BhT[tg]])
        S.release(m_h)

        Qa = [S.sb([128, S_LEN], BF16) for _ in range(2)]
        BQ = [[Buf() for _ in range(8)] for _ in range(2)]
        BQaug = [Buf(), Buf()]
        Ka = [S.sb([128, S_LEN], BF16) for _ in range(2)]
        BK = [[Buf() for _ in range(8)] for _ in range(2)]
        BKaug = [Buf(), Buf()]
        V1 = [S.sb([128, 32, 128], BF16) for _ in range(2)]
        BV = [[Buf() for _ in range(8)] for _ in range(2)]
        wpr = Ring([(S.sb([128, 1024], BF16), Buf()) for _ in range(2)])
        ptr = Ring([(S.sb([128, 512], BF16), Buf()) for _ in range(3)])
        den_g = S.sb([64, 512], F32)
        rec_g = S.sb([64, 512], F32)
        coef_g = S.sb([64, 512], F32)
        obf = S.sb([64, 512], BF16)
        B_den_g, B_rec_g, B_coef_g, B_obf = Buf(), Buf(), Buf(), Buf()
        gatesT = S.sb([24, S_LEN], BF16)
        B_gates = [Buf() for _ in range(8)]
        expsink = S.sb([128, 8], F32)
        B_sink = Buf()
        nbf = S.sb([8, 1], F32)
        B_nbf = Buf()
        if _MIX_STOP <= -1:
            S.release(m0)
            return
        for b in range(2):
            for tg in range(8):
                memset("gpsimd", V1[b][:, tg * 4:(tg + 1) * 4, 64:128], 1.0, [BV[b][tg]])
        ld(expsink[:], sinks_d[l], [], [B_sink])
        act(expsink[:], expsink[:], AF.Exp, [B_sink], [B_sink])
        ld(nbf[:], bf_d[l], [], [B_nbf])
        ts("vector", nbf[:], nbf[:], -1.0, None, ALU.mult, None, [B_nbf], [B_nbf])
        chk(1)

        def load_w(name, c0=0, wd=None):
            off, w = WIN_P[name]
            if wd is None:
                wd = w
            wpf, Bwp = wpr.next()
            wp = wpf[:, 0:8 * wd].rearrange("p (k c) -> p k c", c=wd)
            ld(wpf[:, 0:8 * wd], wins[:, 8 * off:8 * off + 8 * wd], [B_wins], [Bwp])
            return wp, Bwp, wd

        def proj_fm(name, evac, c0=0, wd=None, tgs=range(8)):
            wp, Bwp, wd = load_w(name, c0, wd)
            for tg in tgs:
                bk, Bbk = bank()
                for kc in range(8):
                    mm(bk[0:wd, :], wp[:, kc, 0:wd], hT[:, kc, tg * 512:(tg + 1) * 512], kc == 0, kc == 7,
                       [Bwp, BhT[tg]], [Bbk])
                evac(tg, bk, Bbk)

        def proj_tm(name, evac):
            wp, Bwp, wd = load_w(name)
            for t in range(32):
                bk, Bbk = bank()
                for kc in range(8):
                    mm(bk[:, 0:wd], hT[:, kc, t * 128:(t + 1) * 128], wp[:, kc, 0:wd], kc == 0, kc == 7,
                       [Bwp, BhT[t // 4]], [Bbk])
                evac(t, bk, Bbk)

        def ev_scaled(dst, r0, Bd, scale):
            def f(tg, bk, Bbk):
                if r0 == 0:
                    ts("vector", dst[0:64, tg * 512:(tg + 1) * 512], bk[0:64, :], scale, None, ALU.mult, None,
                       [Bbk], [Bd[tg]])
                else:
                    act(dst[0:64, tg * 512:(tg + 1) * 512], bk[r0:r0 + 64, :], AF.Copy, [Bbk], [Bd[tg]], scale=scale)
            return f

        def ev_plain(dst, r0, Bd):
            def f(tg, bk, Bbk):
                if r0 == 0:
                    vcopy("vector", dst[0:64, tg * 512:(tg + 1) * 512], bk[r0:r0 + 64, :], [Bbk], [Bd[tg]])
                else:
                    act(dst[0:64, tg * 512:(tg + 1) * 512], bk[r0:r0 + 64, :], AF.Copy, [Bbk], [Bd[tg]])
            return f

        def ev_multi(*fs):
            def f(tg, bk, Bbk):
                for g in fs:
                    g(tg, bk, Bbk)
            return f

        def attn_group(g, Q, Qreads, kdim, K, Kreads, Vt, BVt, plan, fin):
            ob, Bob = banks[5 + (g % 2)], B_bank[5 + (g % 2)]
            ntouch = [0] * 4
            total = [0] * 4
            npv = [0]
            for (kb, n_lo, n_hi, bias) in plan:
                for n in range(n_lo, n_hi):
                    total[n] += 1
            def emit_qk(item):
                (kb, n_lo, n_hi, bias) = item
                bk, Bbk = bank()
                c0, c1 = n_lo * 128, n_hi * 128
                nb = len(bias)
                mm(bk[:, c0:c1], K[0:kdim, kb * 128:(kb + 1) * 128], Q[0:kdim, g * 512 + c0:g * 512 + c1],
                   True, nb == 0, Kreads(kb) + Qreads(g), [Bbk])
                for bi, (n, bap) in enumerate(sorted(bias.items())):
                    mm(bk[:, n * 128:(n + 1) * 128], ident_b[:], bap, False, bi == nb - 1, [B_const], [Bbk])
                return (item, bk, Bbk)

            def emit_pv(rec_):
                (kb, n_lo, n_hi, bias), bk, Bbk = rec_
                c0, c1 = n_lo * 128, n_hi * 128
                pt, Bpt = ptr.next()
                act(pt[:, c0:c1], bk[:, c0:c1], AF.Exp, [Bbk], [Bpt])
                n = n_lo
                while n < n_hi:
                    last = ntouch[n] == total[n] - 1
                    n2 = n + 1
                    while n2 < n_hi and (ntouch[n2] == total[n2] - 1) == last:
                        n2 += 1
                    S.op("tensor", lambda e, o_=ob[:, n * 128:n2 * 128], l_=Vt[:, kb, :], r_=pt[:, n * 128:n2 * 128],
                         st_=(npv[0] == 0), sp_=last: e.matmul(o_, lhsT=l_, rhs=r_, start=st_, stop=sp_, skip_group_check=True),
                         reads=[BVt[kb // 4], Bpt], writes=[Bob])
                    npv[0] += 1
                    for k in range(n, n2):
                        ntouch[k] += 1
                    n = n2

            pump()
            pend = []
            for item in plan:
                pend.append(emit_qk(item))
                if len(pend) > 2:
                    emit_pv(pend.pop(0))
            while pend:
                emit_pv(pend.pop(0))
            fin(ob, Bob)

        def finalize(ob, Bob, ncols, gate_row=None, gcols=None, sink_h=None, clampden=False, out=None, Bout=None,
                     tmp=None):
            if tmp is None:
                tmp = (den_g, rec_g, coef_g, B_den_g, B_rec_g, B_coef_g)
            den, rec, coef, B_den, B_rec, B_coef = tmp
            act(den[:, 0:ncols], ob[64:128, 0:ncols], AF.Copy, [Bob], [B_den])
            if sink_h is not None:
                ts("vector", den[:, 0:ncols], den[:, 0:ncols], expsink[0:64, sink_h:sink_h + 1], None, ALU.add, None,
                   [B_den, B_sink], [B_den])
            if clampden:
                ts("vector", den[:, 0:ncols], den[:, 0:ncols], 1e-30, None, ALU.max, None, [B_den], [B_den])
            S.op("vector", lambda e: e.reciprocal(out=rec[:, 0:ncols], in_=den[:, 0:ncols]), reads=[B_den], writes=[B_rec])
            src_coef = rec
            Bsc = B_rec
            if gate_row is not None:
                gb, Bgb = banks[7], B_bank[7]
                mm(gb[0:64, 0:ncols], sel24[0:24, gate_row * 64:(gate_row + 1) * 64],
                   gatesT[0:24, gcols:gcols + ncols], True, True, [B_const, B_gates[gcols // 512]], [Bgb])
                tt("vector", coef[:, 0:ncols], rec[:, 0:ncols], gb[0:64, 0:ncols], ALU.mult, [B_rec, Bgb], [B_coef])
                src_coef = coef
                Bsc = B_coef
            tt("vector", out, ob[0:64, 0:ncols], src_coef[:, 0:ncols], ALU.mult, [Bob, Bsc], [Bout])

        def causal_plan(g, diag_bias, sub_bias=None):
            plan = []
            for kb in range(4 * g + 4):
                m = kb - 4 * g
                if m < 0:
                    b = {}
                    if sub_bias is not None and m == -1:
                        b[0] = sub_bias
                    plan.append((kb, 0, 4, b))
                else:
                    b = {m: diag_bias}
                    if sub_bias is not None and m + 1 < 4:
                        b[m + 1] = sub_bias
                    plan.append((kb, m, 4, b))
            return plan

        m1 = S.mark()
        spc = S.sb([8, 512], F32)
        Cc = S.sb([8, 512], F32)
        r1 = S.sb([8, 512], F32)
        r2 = S.sb([8, 512], F32)
        cb = [S.sb([8, 512], BF16) for _ in range(6)]
        carry = S.sb([8, 1], F32)
        B_f = Buf()
        memset("vector", carry[:], 0.0, [B_f])

        def f_evac(tg, bk, Bbk):
            act(spc[:], bk[0:8, :], AF.Exp, [Bbk, B_nbf], [B_f], bias=nbf[:, 0:1], scale=-1.0)
            act(spc[:], spc[:], AF.Ln, [B_f, B_const], [B_f], bias=epst[0:8, 1:2])
            S.op("vector", lambda e: e.tensor_tensor_scan(out=Cc[:], data0=spc[:], data1=spc[:], initial=carry[:, 0:1],
                                                           op0=ALU.add, op1=ALU.max), reads=[B_f], writes=[B_f])
            vcopy("vector", carry[:], Cc[:, 511:512], [B_f], [B_f])
            vcopy("vector", cb[3][:], Cc[:], [B_f], [B_f])
            tt("vector", r1[:], Cc[:], cb[3][:], ALU.subtract, [B_f], [B_f])
            vcopy("vector", cb[4][:], r1[:], [B_f], [B_f])
            tt("vector", r2[:], r1[:], cb[4][:], ALU.subtract, [B_f], [B_f])
            vcopy("vector", cb[5][:], r2[:], [B_f], [B_f])
            for j in range(3):
                ts("vector", cb[j][:], cb[3 + j][:], -1.0, None, ALU.mult, None, [B_f], [B_f])
            for j in range(6):
                ld(caug[:, j, tg * 512:(tg + 1) * 512], cb[j][:], [B_f], [B_caug])
        if _MIX_STOP >= 2:
            proj_fm("fox_f", f_evac)
        for b in range(2):
            memset("vector", Qa[b][64:70, :], 1.0, [BQaug[b]])
            memset("vector", Ka[b][64:70, :], 1.0, [BKaug[b]])
        chk(2)
        for h in range(8 if _MIX_STOP >= 3 else 0):
            b = h % 2
            proj_fm("fox_qk%d" % h, ev_multi(ev_scaled(Qa[b], 0, BQ[b], 0.125), ev_plain(Ka[b], 64, BK[b])))
            ld(Qa[b][64:67, :], caug[h, 0:3, :], [B_caug], [BQaug[b]])
            ld(Ka[b][67:70, :], caug[h, 3:6, :], [B_caug], [BKaug[b]])

            def v_evac(t, bk, Bbk, b=b):
                vcopy("vector", V1[b][:, t, 0:64], bk[:, 0:64], [Bbk], [BV[b][t // 4]])
            proj_tm("fox_v%d" % h, v_evac)
            for g in range(8):
                def fin(ob, Bob, g=g, h=h):
                    finalize(ob, Bob, 512, out=obf[:, :], Bout=B_obf)
                    stD(oT[2, h * 64:(h + 1) * 64, g * 512:(g + 1) * 512], obf[:, :], [B_obf], [B_oT[2][g]])
                attn_group(g, Qa[b], lambda gg, b=b: [BQ[b][gg], BQaug[b]], 70,
                           Ka[b], lambda kb, b=b: [BK[b][kb // 4], BKaug[b]], V1[b], BV[b],
                           causal_plan(g, cm_b[:]), fin)
        S.release(m1)

        m1 = S.mark()
        proj_fm("swa_k", ev_multi(ev_plain(Ka[0], 0, BK[0]), ev_plain(Ka[1], 64, BK[1])))

        def sv_evac(t, bk, Bbk):
            vcopy("vector", V1[0][:, t, 0:64], bk[:, 0:64], [Bbk], [BV[0][t // 4]])
            act(V1[1][:, t, 0:64], bk[:, 64:128], AF.Copy, [Bbk], [BV[1][t // 4]])
        proj_tm("swa_v", sv_evac)
        chk(3)
        for i in range(4 if _MIX_STOP >= 4 else 0):
            proj_fm("swa_q%d" % i, ev_multi(ev_scaled(Qa[0], 0, BQ[0], 0.125), ev_scaled(Qa[1], 64, BQ[1], 0.125)))
            for b in range(2):
                h = 2 * i + b
                gk = h // 4
                for g in range(8):
                    plan = []
                    for m in range(-1, 4):
                        kb = 4 * g + m
                        if kb < 0:
                            continue
                        n_lo, n_hi = max(0, m), min(4, m + 2)
                        bias = {}
                        for n in range(n_lo, n_hi):
                            bias[n] = swaT[:, h, n - m, :]
                        plan.append((kb, n_lo, n_hi, bias))

                    def fin(ob, Bob, g=g, h=h):
                        finalize(ob, Bob, 512, sink_h=h, out=obf[:, :], Bout=B_obf)
                        stD(oT[1, h * 64:(h + 1) * 64, g * 512:(g + 1) * 512], obf[:, :], [B_obf], [B_oT[1][g]])
                    attn_group(g, Qa[b], lambda gg, b=b: [BQ[b][gg]], 64,
                               Ka[gk], lambda kb, gk=gk: [BK[gk][kb // 4]], V1[gk], BV[gk], plan, fin)
        S.release(m1)

        m1 = S.mark()
        Ksel = Ka[0]
        BKsel = BK[0]
        B_oh = BKaug[0]
        Kwin = Ka[1]
        BKwin = BK[1]
        kcT = S.sb([64, 256], BF16)
        B_kcT = Buf()
        vc1 = S.sb([128, 2, 128], BF16)
        B_vc1 = Buf()
        m_cmp = S.mark()
        cwc = Ring([(S.sb([64, 8, 256], BF16), Buf()) for _ in range(2)])
        cw2t = S.sb([128, 2, 64], BF16)
        pet = S.sb([64, 32], BF16)
        B_cw2 = Buf()
        bvec = S.sb([128, 2], F32)
        B_bvec = Buf()
        gx = [S.sb([128, 256], F32) for _ in range(4)]
        B_gx = Buf()
        hid = [S.sb([128, 256], BF16) for _ in range(2)]
        B_hid = Buf()
        cmp_end = S.sb_off
        S.sb_off = m_cmp
        qt2 = [[S.sb([64, 128], BF16) for _ in range(4)] for _ in range(2)]
        B_qt2 = [[Buf() for _ in range(4)] for _ in range(2)]
        s4 = [S.sb([128, 256], F32) for _ in range(4)]
        e4 = [S.sb([128, 256], F32) for _ in range(4)]
        rs4 = [S.sb([128, 4], F32) for _ in range(4)]
        eT4 = [S.sb([128, 2, 128], BF16) for _ in range(4)]
        B_s4 = [Buf() for _ in range(4)]
        B_e4 = [Buf() for _ in range(4)]
        B_rs4 = [Buf() for _ in range(4)]
        B_eT4 = [Buf() for _ in range(4)]
        pacc2 = [S.sb([128, 260], F32) for _ in range(2)]
        B_pacc2 = [Buf(), Buf()]
        impt = S.sb([128, 64], F32)
        score = S.sb([128, 64], F32)
        top8 = S.sb([128, 8], F32)
        Mt = S.sb([128, 128], F32)
        ocst4 = [S.sb([64, 2, 128], F32) for _ in range(4)]
        B_ocst4 = [Buf() for _ in range(4)]
        fin_tmp = [(S.sb([64, 128], F32), S.sb([64, 128], F32), S.sb([64, 128], F32), Buf(), Buf(), Buf()) for _ in range(2)]
        B_imp, B_Mt = Buf(), Buf()
        S.sb_off = max(S.sb_off, cmp_end)
        osel = S.sb([64, 512], F32)
        B_osel = Buf()
        ocl = S.sb([64, 512], F32)
        B_ocl = Buf()
        wq2 = [S.sb([128, 8, 128], BF16) for _ in range(2)]
        B_wq2 = Buf()

        ld(Ksel[64:128, :], ohs, [B_ohs], [B_oh])
        memset("vector", vc1[:, :, 64:128], 1.0, [B_vc1])

        def gate_evac(tg, bk, Bbk):
            act(gatesT[0:24, tg * 512:(tg + 1) * 512], bk[0:24, :], AF.Sigmoid, [Bbk], [B_gates[tg]])
        chk(4)
        proj_fm("nsa_gate", gate_evac)
        chk(5)

        for g in range(2 if _MIX_STOP >= 5 else 0):
            S.barrier()
            proj_fm("nsa_cmp%d" % g, ev_multi(ev_plain(Ka[0], 0, BK[0]), ev_plain(Ka[1], 64, BK[1])))
            for kv in range(2):
                ld(cwstage[:, 0:128], cw2_d[l, kv], [], [B_cwstage])
                vcopy("vector", cw2t[:, :, :], cwstage[:, 0:128].rearrange("p (a b) -> p a b", b=64), [B_cwstage], [B_cw2])
                ld(cwstage[0:64, 128:160], pet_d[l, kv], [], [B_cwstage])
                vcopy("vector", pet[:], cwstage[0:64, 128:160], [B_cwstage], [B_cw2])
                bh = [bank(), bank()]
                bb = [bank(), bank()]
                src_t = Ka[kv]
                for c in range(4):
                    cw, Bcw = cwc.next()
                    ld(cw[:], cw1s[kv].rearrange("d (p c) -> d p c", c=256)[:, c * 8:(c + 1) * 8, :], [B_cw1s], [Bcw])
                    for pp in range(8):
                        p = c * 8 + pp
                        for hc in range(2):
                            mm(bh[hc][0][:, 0:255], cw[:, pp, hc * 128:(hc + 1) * 128],
                               src_t[0:64, p:p + 16 * 254 + 1:16], p == 0, p == 31,
                               [Bcw] + BK[kv], [bh[hc][1]])
                            mm(bb[hc][0][:, 0:1], cw[:, pp, hc * 128:(hc + 1) * 128], pet[:, p:p + 1], p == 0, p == 31,
                               [Bcw, B_cw2], [bb[hc][1]])
                for hc in range(2):
                    vcopy("vector", bvec[:, hc:hc + 1], bb[hc][0][:, 0:1], [bb[hc][1]], [B_bvec])
                    x1, sq, u, sg = gx
                    act(x1[:, 0:255], bh[hc][0][:, 0:255], AF.Identity, [bh[hc][1], B_bvec], [B_gx], bias=bvec[:, hc:hc + 1])
                    tt("vector", sq[:, 0:255], x1[:, 0:255], x1[:, 0:255], ALU.mult, [B_gx], [B_gx])
                    ts("vector", sq[:, 0:255], sq[:, 0:255], 0.044715, 1.0, ALU.mult, ALU.add, [B_gx], [B_gx])
                    tt("vector", u[:, 0:255], sq[:, 0:255], x1[:, 0:255], ALU.mult, [B_gx], [B_gx])
                    act(sg[:, 0:255], u[:, 0:255], AF.Sigmoid, [B_gx], [B_gx], scale=1.5957691216057308)
                    tt("vector", hid[hc][:, 0:255], x1[:, 0:255], sg[:, 0:255], ALU.mult, [B_gx], [B_hid])
                if kv == 0:
                    bk, Bbk = bank()
                    for hc in range(2):
                        mm(bk[0:64, 0:255], cw2t[:, hc, :], hid[hc][:, 0:255], hc == 0, hc == 1, [B_cw2, B_hid], [Bbk])
                    vcopy("vector", kcT[:, 0:255], bk[0:64, 0:255], [Bbk], [B_kcT])
                else:
                    for c in range(2):
                        nn = 128 if c == 0 else 127
                        bk, Bbk = bank()
                        for hc in range(2):
                            mm(bk[0:nn, 0:64], hid[hc][:, c * 128:c * 128 + nn], cw2t[:, hc, :], hc == 0, hc == 1,
                               [B_cw2, B_hid], [Bbk])
                        vcopy("vector", vc1[0:nn, c, 0:64], bk[0:nn, 0:64], [Bbk], [B_vc1])
            proj_fm("nsa_kk%d" % g, ev_multi(ev_plain(Ksel, 0, BKsel), ev_plain(Kwin, 64, BKwin)))

            def nv_evac(t, bk, Bbk):
                vcopy("vector", V1[0][:, t, 0:64], bk[:, 0:64], [Bbk], [BV[0][t // 4]])
                act(V1[1][:, t, 0:64], bk[:, 64:128], AF.Copy, [Bbk], [BV[1][t // 4]])
            proj_tm("nsa_v%d" % g, nv_evac)

            for i in range(2):
                off, w = WIN_P["nsa_q%d" % (2 * g + i)]
                ld(wq2[i][:], wins[:, 8 * off:8 * off + 1024].rearrange("p (k c) -> p k c", c=128), [B_wins], [B_wq2])
            S.barrier()
            memset("vector", Mt[:, 0:64], 0.0, [B_Mt])

            def q_stage(qb):
                par = qb % 2
                for i in range(2):
                    bk, Bbk = bank()
                    for kc in range(8):
                        mm(bk[:, 0:128], wq2[i][:, kc, :], hT[:, kc, qb * 128:(qb + 1) * 128], kc == 0, kc == 7,
                           [B_wq2, BhT[qb // 4]], [Bbk])
                    act(qt2[par][2 * i][:], bk[0:64, 0:128], AF.Copy, [Bbk], [B_qt2[par][2 * i]], scale=0.125)
                    act(qt2[par][2 * i + 1][:], bk[64:128, 0:128], AF.Copy, [Bbk], [B_qt2[par][2 * i + 1]], scale=0.125)

            q_stage(0)
            for qb in range(32):
                par = qb % 2
                tg = qb // 4
                ncols = min(255, 8 * qb + 7)
                nt = 1 if ncols <= 128 else 2
                pacc, B_pacc = pacc2[par], B_pacc2[par]
                memset("gpsimd", pacc[:], 0.0, [B_pacc])
                sbk = []
                for hp in range(4):
                    bk, Bbk = bank()
                    mm(bk[:, 0:ncols], qt2[par][hp][:], kcT[:, 0:ncols], True, True, [B_qt2[par][hp], B_kcT], [Bbk])
                    sbk.append((bk, Bbk))
                for hp in range(4):
                    h = 4 * g + hp
                    bk, Bbk = sbk[hp]
                    tt("vector", s4[hp][:, 0:ncols], bk[:, 0:ncols], tc_b[:, h, 256 - 8 * qb:256 - 8 * qb + ncols], ALU.add,
                       [Bbk, B_const], [B_s4[hp]])
                    act(e4[hp][:, 0:ncols], s4[hp][:, 0:ncols], AF.Exp, [B_s4[hp]], [B_e4[hp], B_rs4[hp]],
                        accum=rs4[hp][:, 0:1])
                if qb + 1 < 32:
                    q_stage(qb + 1)
                for hp in range(4):
                    rs = rs4[hp]
                    ts("vector", rs[:, 1:2], rs[:, 0:1], 1e-30, None, ALU.max, None, [B_rs4[hp]], [B_rs4[hp]])
                    S.op("vector", lambda e, rs=rs: e.reciprocal(out=rs[:, 2:3], in_=rs[:, 1:2]),
                         reads=[B_rs4[hp]], writes=[B_rs4[hp]])
                    if hp == 0:
                        ts("vector", pacc[:, 1:1 + ncols], e4[hp][:, 0:ncols], rs[:, 2:3], None, ALU.mult, None,
                           [B_e4[hp], B_rs4[hp]], [B_pacc])
                    else:
                        stt("vector", pacc[:, 1:1 + ncols], e4[hp][:, 0:ncols], rs[:, 2:3], pacc[:, 1:1 + ncols],
                            ALU.mult, ALU.add, [B_e4[hp], B_rs4[hp], B_pacc], [B_pacc])
                for hp in range(4):
                    h = 4 * g + hp
                    ob, Bob = banks[5 + (hp % 2)], B_bank[5 + (hp % 2)]
                    bk2, Bbk2 = bank()
                    for c in range(nt):
                        nn = min(128, ncols - c * 128)
                        tr(bk2[0:nn, c * 128:(c + 1) * 128], e4[hp][:, c * 128:c * 128 + nn], ident_f[:],
                           [B_e4[hp], B_const], [Bbk2])
                    for c in range(nt):
                        nn = min(128, ncols - c * 128)
                        act(eT4[hp][0:nn, c, :], bk2[0:nn, c * 128:(c + 1) * 128], AF.Copy, [Bbk2], [B_eT4[hp]])
                    for c in range(nt):
                        nn = min(128, ncols - c * 128)
                        mm(ob[:, 0:128], vc1[0:nn, c, :], eT4[hp][0:nn, c, :], c == 0, c == nt - 1,
                           [B_vc1, B_eT4[hp]], [Bob])
                    ocst = ocst4[hp]
                    finalize(ob, Bob, 128, gate_row=h * 3 + 0, gcols=qb * 128, clampden=True,
                             out=ocst[:, qb % 2, :], Bout=B_ocst4[hp], tmp=fin_tmp[hp % 2])
                    if qb % 2 == 1:
                        stD(ocmp[h, :, (qb - 1) * 128:(qb + 1) * 128], ocst[:, :, :].rearrange("p a b -> p (a b)"),
                            [B_ocst4[hp]], [B_ocmp[h][tg]])
                S.op("vector", lambda e, pacc=pacc: e.tensor_reduce(out=impt[:], in_=pacc[:, 0:256].rearrange("p (j m) -> p j m", m=4),
                                                                    axis=AX.X, op=ALU.add), reads=[B_pacc], writes=[B_imp])
                tt("vector", impt[:], impt[:], pacc[:, 4:260:4], ALU.add, [B_imp, B_pacc], [B_imp])
                tt("vector", score[:], impt[:], ft_f[:, 64 - 2 * qb:128 - 2 * qb], ALU.add, [B_imp, B_const], [B_imp])
                ts("vector", score[:, 0:1], score[:, 0:1], 100.0, None, ALU.add, None, [B_imp], [B_imp])
                S.op("vector", lambda e: e.max(out=top8[:], in_=score[:]), reads=[B_imp], writes=[B_imp])
                ts("vector", Mt[:, 64:128], score[:], top8[:, 7:8], None, ALU.is_ge, None, [B_imp, B_Mt], [B_Mt])
                ts("vector", Mt[:, 64:128], Mt[:, 64:128], -NEG, NEG, ALU.mult, ALU.add, [B_Mt], [B_Mt])
                bk, Bbk = bank()
                tr(bk[:, 0:128], Mt[:], ident_f[:], [B_Mt, B_const], [Bbk])
                vcopy("vector", Qa[0][64:128, qb * 128:(qb + 1) * 128], bk[64:128, 0:128], [Bbk], [BQaug[0]])
                act(Qa[1][64:128, qb * 128:(qb + 1) * 128], bk[64:128, 0:128], AF.Copy, [Bbk], [BQaug[1]])

            for i in range(2):
                proj_fm("nsa_q%d" % (2 * g + i),
                        ev_multi(ev_scaled(Qa[0], 0, BQ[0], 0.125), ev_scaled(Qa[1], 64, BQ[1], 0.125)))
                for b in range(2):
                    h = 4 * g + 2 * i + b
                    for gq in range(8):
                        ld(ocl[:], ocmp[h, :, gq * 512:(gq + 1) * 512], [B_ocmp[h][gq]], [B_ocl])

                        def fin_sel(ob, Bob, gq=gq, h=h):
                            finalize(ob, Bob, 512, gate_row=h * 3 + 1, gcols=gq * 512, out=osel[:, :], Bout=B_osel)
                            if dsel is not None:
                                ld(dsel[h, :, gq * 512:(gq + 1) * 512], osel[:, :], [B_osel], [Buf()])
                            tt("gpsimd", ocl[:], ocl[:], osel[:], ALU.add, [B_ocl, B_osel], [B_ocl])
                        attn_group(gq, Qa[b], lambda gg, b=b: [BQ[b][gg], BQaug[b]], 128,
                                   Ksel, lambda kb: [BKsel[kb // 4], B_oh], V1[0], BV[0],
                                   causal_plan(gq, nsaT[:, h, 0, :], nsaT[:, h, 1, :]), fin_sel)
                        plan = []
                        for m in range(-4, 4):
                            kb = 4 * gq + m
                            if kb < 0:
                                continue
                            n_lo, n_hi = max(0, m), min(4, m + 5)
                            bias = {}
                            for n in range(n_lo, n_hi):
                                if n - m == 0:
                                    bias[n] = nsaT[:, h, 0, :]
                                elif n - m == 1:
                                    bias[n] = nsaT[:, h, 1, :]
                                elif n - m == 4:
                                    bias[n] = lt_b[:]
                            plan.append((kb, n_lo, n_hi, bias))

                        def fin_win(ob, Bob, gq=gq, h=h):
                            finalize(ob, Bob, 512, gate_row=h * 3 + 2, gcols=gq * 512, out=osel[:, :], Bout=B_osel)
                            if dwin is not None:
                                ld(dwin[h, :, gq * 512:(gq + 1) * 512], osel[:, :], [B_osel], [Buf()])
                            tt("gpsimd", obf[:, :], ocl[:], osel[:], ALU.add, [B_ocl, B_osel], [B_obf])
                            stD(oT[0, h * 64:(h + 1) * 64, gq * 512:(gq + 1) * 512], obf[:, :], [B_obf], [B_oT[0][gq]])
                        attn_group(gq, Qa[b], lambda gg, b=b: [BQ[b][gg]], 64,
                                   Kwin, lambda kb: [BKwin[kb // 4]], V1[1], BV[1], plan, fin_win)
        S.release(m1)
        S.release(m_h)
        if not _MIX_OUT:
            S.release(m0)
            return

        lng = S.sb([128, DM], F32)
        lnb = S.sb([128, DM], F32)
        B_ln = Buf()
        ld(lng[:], lng_d[l, 1], [], [B_ln])
        ld(lnb[:], lnb_d[l, 1], [], [B_ln])
        bg_t = S.sb([128, 24], F32)
        ld(bg_t[:], bgate_d[l], [], [B_ln])
        woutt = S.sb([128, 8, DM], BF16)
        B_wo = Buf()
        ld(woutt[:], wouts.rearrange("p (k c) -> p k c", c=DM), [B_wouts], [B_wo])
        xin2 = S.sb([128, 4, DM], F32)
        Bxin2 = Buf()
        ot = [S.sb([128, 4, 512], BF16) for _ in range(3)]
        B_ot = [Buf() for _ in range(3)]
        wgr = Ring([(S.sb([128, 8, 128], BF16), Buf()) for _ in range(3)])
        wbrr = Ring([(S.sb([128, 4, 128], BF16), Buf()) for _ in range(3)])
        gsr = Ring([(S.sb([128, 512], F32), Buf()) for _ in range(2)])
        macc = S.sb([128, 512], F32)
        B_macc = Buf()
        mT = S.sb([128, 8, 512], BF16)
        BmT = [Buf() for _ in range(8)]
        tmp = ln_tmp()
        wgs3 = wgs.rearrange("p (k c) -> p k c", c=3072)
        for tg in range(8):
            ld(xin2[:], src[tg * 512:(tg + 1) * 512, :].rearrange("(n p) d -> p n d", p=128), [Bsrc[tg]], [Bxin2])
            for x in range(3):
                ld(ot[x][:], oT[x, :, tg * 512:(tg + 1) * 512].rearrange("(c p) t -> p c t", p=128), [B_oT[x][tg]], [B_ot[x]])
            for dc in range(8):
                for x in range(3):
                    pump()
                    wg, Bwg = wgr.next()
                    ld(wg[:], wgs3[:, :, x * DM + dc * 128:x * DM + (dc + 1) * 128], [B_wgs], [Bwg])
                    wb, Bwb = wbrr.next()
                    ld(wb[:], wbrs[x].rearrange("p (c d) -> p c d", d=DM)[:, :, dc * 128:(dc + 1) * 128], [B_wbrs], [Bwb])
                    bg, Bbg = bank()
                    for kc in range(8):
                        mm(bg[:, :], wg[:, kc, :], hT[:, kc, tg * 512:(tg + 1) * 512], kc == 0, kc == 7, [Bwg, BhT[tg]], [Bbg])
                    bb, Bbb = bank()
                    for c in range(4):
                        mm(bb[:, :], wb[:, c, :], ot[x][:, c, :], c == 0, c == 3, [Bwb, B_ot[x]], [Bbb])
                    gs, Bgs = gsr.next()
                    act(gs[:], bg[:, :], AF.Sigmoid, [Bbg, B_ln], [Bgs], bias=bg_t[:, x * 8 + dc:x * 8 + dc + 1])
                    if x == 0:
                        tt("vector", macc[:], gs[:], bb[:, :], ALU.mult, [Bgs, Bbb], [B_macc])
                    elif x == 1:
                        tt("vector", gs[:], gs[:], bb[:, :], ALU.mult, [Bgs, Bbb], [Bgs])
                        tt("gpsimd", macc[:], macc[:], gs[:], ALU.add, [Bgs, B_macc], [B_macc])
                    else:
                        tt("vector", gs[:], gs[:], bb[:, :], ALU.mult, [Bgs, Bbb], [Bgs])
                        tt("gpsimd", mT[:, dc, :], macc[:], gs[:], ALU.add, [Bgs, B_macc], [BmT[dc]])
            for n in range(4):
                b0, b1 = banks[5], banks[6]
                for dc in range(8):
                    mm(b0[:, :], mT[:, dc, n * 128:(n + 1) * 128], woutt[:, dc, 0:512], dc == 0, dc == 7,
                       [BmT[dc], B_wo], [B_bank[5]])
                    mm(b1[:, :], mT[:, dc, n * 128:(n + 1) * 128], woutt[:, dc, 512:1024], dc == 0, dc == 7,
                       [BmT[dc], B_wo], [B_bank[6]])
                r0 = tg * 512 + n * 128
                ln_epilogue(xin2[:, n, :], b0, b1, B_bank[5], B_bank[6], Bxin2, lng, lnb, B_ln, tmp,
                            dst[r0:r0 + 128, :], Bdst[tg] if isinstance(Bdst, list) else Bdst,
                            None if dbg_ap is None else dbg_ap[r0:r0 + 128, :])
        S.release(m0)

    cur, Bcur = x_in, [Buf() for _ in range(8)]
    pp = 0
    prep_ffn(0, 0)
    for l in range(NL):
        for st in range(3):
            lastst = (l == NL - 1 and st == 2) or (_STOP_AFTER is not None and l * 3 + st == _STOP_AFTER - 1)
            if _STOP_AFTER is not None and l * 3 + st >= _STOP_AFTER:
                continue
            if lastst:
                dst, Bdst = y_out, B_y
            else:
                dst, Bdst = xs[pp], B_xs[pp]
            dbg_ap = dbg_d.get("l%ds%d" % (l, st))
            if st == 0:
                ffn_stage(l, 0, cur, Bcur, dst, Bdst, 0, dbg_ap)
            elif st == 1:
                mixer_stage(l, cur, Bcur, dst, Bdst, dbg_ap)
            else:
                ffn_stage(l, 1, cur, Bcur, dst, Bdst, 2, dbg_ap)
            cur, Bcur = dst, Bdst
            pp ^= 1
    S.finish()
    return nc


def _consts(rel_bias):
    c = {}
    c["ident"] = np.eye(128, dtype=np.float32)
    s = np.arange(S_LEN)
    c["onehot"] = (s[None, :] // 64 == np.arange(64)[:, None]).astype(np.float32)
    j = np.arange(128)[:, None]
    i = np.arange(128)[None, :]
    d0 = i - j
    d1 = 128 + i - j
    bk0 = _t5_bucket(d0)
    bk1 = _t5_bucket(d1)
    relT = np.ascontiguousarray(rel_bias.T)
    biasT = np.empty((16, 2, 128, 128), np.float32)
    for h in range(16):
        t0 = relT[h][bk0]
        t1 = relT[h][bk1]
        biasT[h, 0] = np.where(d0 >= 0, t0, np.float32(NEG))
        if h < 8:
            biasT[h, 1] = t1
        else:
            biasT[h, 1] = np.where(d1 < 128, t1, np.float32(NEG))
    c["biasT"] = biasT
    c["cfar"] = np.ascontiguousarray(np.broadcast_to(rel_bias[31][None, :], (128, 16))).astype(np.float32)
    ii = np.arange(128)[:, None]
    u = np.arange(512)[None, :]
    dist = ii - 16 * (u - 256) - 31
    bkc = _t5_bucket(dist)
    tc = np.empty((8, 128, 512), np.float32)
    for h in range(8):
        tc[h] = np.where(dist >= 0, relT[h][bkc], np.float32(NEG))
    c["tc"] = tc
    uu = np.arange(128)[None, :] - 64
    curb = (ii >= 64).astype(np.int64)
    ft = np.zeros((128, 128), np.float32)
    ft[np.broadcast_to(uu > curb, (128, 128))] = -100.0
    ft[np.broadcast_to((uu == curb) | (uu == curb - 1), (128, 128))] = 100.0
    c["ft"] = ft
    c["cm"] = np.where(d0 >= 0, 0.0, NEG).astype(np.float32)
    c["lt"] = np.where(i < j, 0.0, NEG).astype(np.float32)
    sel = np.zeros((24, 1536), np.float32)
    for r in range(24):
        sel[r, r * 64:(r + 1) * 64] = 1.0
    c["sel24"] = sel
    return c


def _layer_weights(inp, ls):
    L = len(ls)
    w = {}

    def stack(f):
        return np.ascontiguousarray(np.stack([f(l) for l in ls]))
    for i, (k1, k2) in enumerate((("ffn1_w1", "ffn1_w2"), ("ffn2_w1", "ffn2_w2"))):
        w["w1_%d" % i] = stack(lambda l: inp[k1][l].reshape(8, 128, 2, NFC, 128).transpose(3, 1, 0, 2, 4).reshape(NFC, 128, 2048))
        w["w2_%d" % i] = stack(lambda l: inp[k2][l].reshape(NFC, 128, DM).transpose(1, 0, 2).reshape(128, NFC * DM))
    def _win_layout(l):
        wp = inp["w_in"][l][:, WIN_PERM]
        blocks = []
        for name, (off, wd) in WIN_P.items():
            blocks.append(wp[:, off:off + wd].reshape(8, 128, wd).transpose(1, 0, 2).reshape(128, 8 * wd))
        return np.concatenate(blocks, axis=1)
    w["win"] = stack(_win_layout)
    w["wgate"] = stack(lambda l: inp["w_gate"][l].reshape(8, 128, 3072).transpose(1, 0, 2).reshape(128, 8 * 3072))
    w["bgate"] = stack(lambda l: inp["b_gate"][l].reshape(24, 128).T)
    w["wbr"] = stack(lambda l: np.stack([inp[k][l].reshape(4, 128, DM).transpose(1, 0, 2).reshape(128, 4 * DM)
                                         for k in ("w_br_a", "w_br_b", "w_br_c")]))
    w["wout"] = stack(lambda l: inp["w_out"][l].reshape(8, 128, DM).transpose(1, 0, 2).reshape(128, 8 * DM))
    w["cw1"] = stack(lambda l: np.stack([inp[k][l].reshape(32, 64, 256).transpose(1, 0, 2).reshape(64, 32 * 256)
                                         for k in ("cmp_k_w1", "cmp_v_w1")]))
    w["cw2"] = stack(lambda l: np.stack([inp[k][l].reshape(2, 128, 64).transpose(1, 0, 2).reshape(128, 128)
                                         for k in ("cmp_k_w2", "cmp_v_w2")]))
    w["pet"] = stack(lambda l: np.stack([inp[k][l].T for k in ("cmp_pe_k", "cmp_pe_v")]))
    w["lng"] = stack(lambda l: np.stack([np.broadcast_to(inp[k][l][None, :], (128, DM)) for k in ("ln1_g", "ln2_g", "ln3_g")]))
    w["lnb"] = stack(lambda l: np.stack([np.broadcast_to(inp[k][l][None, :], (128, DM)) for k in ("ln1_b", "ln2_b", "ln3_b")]))
    w["sinks"] = stack(lambda l: np.broadcast_to(inp["swa_sinks"][l][None, :], (128, 8)))
    w["bf"] = stack(lambda l: inp["fox_b_f"][l].reshape(8, 1))
    return {k: np.ascontiguousarray(v, dtype=np.float32) for k, v in w.items()}


_PROG = {}


def _get_prog(NL):
    if NL not in _PROG:
        _PROG[NL] = build_program(NL)
    return _PROG[NL]


def kernel(**inputs):
    inp = {k: np.asarray(v, dtype=np.float32) for k, v in inputs.items()}
    consts = _consts(inp["rel_bias"])
    x = inp["x"]
    nb = x.shape[0]
    NL = NLAYERS
    nc = _get_prog(NL)
    wl = _layer_weights(inp, list(range(NL)))
    in_maps = []
    for b in range(nb):
        m = {"x": np.ascontiguousarray(x[b])}
        m.update(wl)
        m.update(consts)
        in_maps.append(m)
    res = run_bass_kernel_spmd(nc, in_maps, core_ids=list(range(nb)))
    return np.stack([np.asarray(r["y"], dtype=np.float32) for r in res.results], axis=0)
```

```python
from contextlib import ExitStack
import numpy as np
import concourse.bass as bass
import concourse.mybir as mybir
from concourse.bass_utils import run_bass_kernel_spmd

F32 = mybir.dt.float32
BF16 = mybir.dt.bfloat16
AF = mybir.ActivationFunctionType
ALU = mybir.AluOpType
AX = mybir.AxisListType

ENGS = ["tensor", "vector", "scalar", "gpsimd", "sync"]
S_LEN = 4096
DM = 1024
FF = 2816
NFC = 22
DIN = 3616
NEG = -30000.0
ALPHA = 8.0 ** 0.25
LN_EPS = 1e-5
NLAYERS = 4
_STOP_AFTER = None
_MIX_STOP = 99
_MIX_OUT = True
_SUB = 99


class _Stop(Exception):
    pass


class Buf:
    __slots__ = ("w", "r", "excl")

    def __init__(self, excl=False):
        self.w = None
        self.r = {}
        self.excl = excl


class Sched:
    def __init__(self, nc, n_dma_sems=12, rot=20000):
        self.nc = nc
        self.prog = {e: [] for e in ENGS}
        self.seen = {e: {} for e in ENGS}
        self.cnt = {e: 0 for e in ENGS}
        self.epoch = {e: 0 for e in ENGS}
        self.rot = rot
        self.n_dma = n_dma_sems
        self.dma_uses = {}
        self.dma_rr = {e: 0 for e in ENGS}
        self.semkeys = []
        self.semset = set()
        self.stack = ExitStack()
        self.sb_off = 16512
        self.sb_id = 0

    def sb(self, shape, dtype):
        n = 1
        for s in shape[1:]:
            n *= s
        nbytes = n * (4 if dtype == F32 else 2)
        off = (self.sb_off + 63) // 64 * 64
        assert off + nbytes <= 229000, ("SBUF overflow", off, nbytes)
        self.sb_off = off + nbytes
        self.peak = max(getattr(self, 'peak', 0), self.sb_off)
        self.sb_id += 1
        return self.nc.alloc_sbuf_tensor_at("t%d" % self.sb_id, list(shape), dtype, offset=off)

    def mark(self):
        return self.sb_off

    def release(self, m):
        self.barrier()
        self.sb_off = m

    def _key(self, key):
        if key not in self.semset:
            self.semset.add(key)
            self.semkeys.append(key)
        return key

    def _collect(self, eng, reads, writes):
        deps = {}

        def add(tok):
            if tok is None:
                return
            k, v = tok
            if deps.get(k, 0) < v:
                deps[k] = v
        for b in reads:
            add(b.w)
            if b.excl:
                for k, v in b.r.items():
                    if k[0] != eng:
                        add((k, v))
        for b in writes:
            add(b.w)
            for k, v in b.r.items():
                add((k, v))
        waits = []
        seen = self.seen[eng]
        for k, v in deps.items():
            if eng == "tensor" and k[0] == "tensor":
                continue
            if seen.get(k, 0) >= v:
                continue
            seen[k] = v
            waits.append((k, v))
        return waits

    def _update(self, tok, reads, writes):
        k, v = tok
        for b in reads:
            if b.r.get(k, 0) < v:
                b.r[k] = v
        for b in writes:
            b.w = tok
            b.r = {}

    def op(self, eng, emit, reads=(), writes=()):
        waits = self._collect(eng, reads, writes)
        self.cnt[eng] += 1
        if self.cnt[eng] > self.rot:
            self.epoch[eng] += 1
            self.cnt[eng] = 1
        tok = (self._key((eng, self.epoch[eng])), self.cnt[eng])
        self.prog[eng].append((waits, emit, tok, 1))
        self._update(tok, reads, writes)
        return tok

    def dma(self, q, emit, reads=(), writes=()):
        i = self.dma_rr[q]
        self.dma_rr[q] = (i + 1) % self.n_dma
        key = self._key(("dma", q, i))
        k = self.dma_uses.get(key, 0) + 1
        self.dma_uses[key] = k
        waits = self._collect(q, reads, writes)
        if k > 1 and self.seen[q].get(key, 0) < 16 * (k - 1):
            self.seen[q][key] = 16 * (k - 1)
            waits.append((key, 16 * (k - 1)))
        tok = (key, 16 * k)
        self.prog[q].append((waits, emit, tok, 16))
        self._update(tok, reads, writes)
        return tok

    def _all_tokens(self):
        toks = []
        for key, k in self.dma_uses.items():
            toks.append((key, 16 * k))
        for e in ENGS:
            if self.cnt[e] > 0:
                toks.append(((e, self.epoch[e]), self.cnt[e]))
        return toks

    def barrier(self):
        toks = self._all_tokens()
        for e in ENGS:
            waits = []
            for k, v in toks:
                if self.seen[e].get(k, 0) < v:
                    self.seen[e][k] = v
                    waits.append((k, v))
            if waits:
                self.prog[e].append((waits, None, None, 0))

    def finish(self):
        nc = self.nc
        self.barrier()
        sems = {}
        for key in self.semkeys:
            nm = "s_" + "_".join(str(x) for x in key)
            sems[key] = self.stack.enter_context(nc.semaphore(nm))
        prog = self.prog
        with nc.Block() as block:
            def mk(ename):
                def body(eng):
                    for waits, emit, tok, inc in prog[ename]:
                        for k, v in waits:
                            eng.wait_ge(sems[k], v)
                        if emit is not None:
                            emit(eng).then_inc(sems[tok[0]], inc)
                return body
            block.tensor(mk("tensor"))
            block.vector(mk("vector"))
            block.scalar(mk("scalar"))
            block.gpsimd(mk("gpsimd"))
            block.sync(mk("sync"))
        self.stack.close()


class Ring:
    def __init__(self, items):
        self.items = items
        self.i = 0

    def next(self):
        it = self.items[self.i]
        self.i = (self.i + 1) % len(self.items)
        return it


def _win_passes():
    P = {}
    off = 0
    perm = []

    def add(name, cols):
        nonlocal off
        P[name] = (off, len(cols))
        perm.extend(cols)
        off += len(cols)
    r = lambda a, n: list(range(a, a + n))
    for h in range(8):
        add("fox_qk%d" % h, r(2072 + h * 64, 64) + r(2584 + h * 64, 64))
    add("fox_f", r(3608, 8))
    for h in range(8):
        add("fox_v%d" % h, r(3096 + h * 64, 64))
    for i in range(4):
        add("swa_q%d" % i, r(1304 + i * 128, 128))
    add("swa_k", r(1816, 128))
    add("swa_v", r(1944, 128))
    for i in range(4):
        add("nsa_q%d" % i, r(i * 128, 128))
    for g in range(2):
        add("nsa_cmp%d" % g, r(512 + g * 64, 64) + r(640 + g * 64, 64))
        add("nsa_kk%d" % g, r(768 + g * 64, 64) + r(1024 + g * 64, 64))
        add("nsa_v%d" % g, r(896 + g * 64, 64) + r(1152 + g * 64, 64))
    add("nsa_gate", r(1280, 24))
    assert off == DIN and sorted(perm) == list(range(DIN))
    return P, np.array(perm)


WIN_P, WIN_PERM = _win_passes()


def _t5_bucket(n):
    n = np.maximum(n, 0)
    lr = np.log(np.maximum(n, 1).astype(np.float32) / np.float32(16)) / np.float32(np.log(128 / 16))
    large = 16 + (lr.astype(np.float32) * np.float32(16)).astype(np.int32)
    return np.where(n < 16, n, np.minimum(large, 31))


def build_program(NL, dbg=None):
    nc = bass.Bass("TRN2", target_bir_lowering=False)
    S = Sched(nc)

    def din(name, shape, dt=F32):
        return nc.dram_tensor(name, list(shape), dt, kind="ExternalInput").ap()

    def dscr(name, shape, dt):
        return nc.dram_tensor(name, list(shape), dt, kind="Internal").ap()

    x_in = din("x", [S_LEN, DM])
    y_out = nc.dram_tensor("y", [S_LEN, DM], F32, kind="ExternalOutput").ap()
    w1_d = [din("w1_%d" % i, [NL, NFC, 128, 2048]) for i in range(2)]
    w2_d = [din("w2_%d" % i, [NL, 128, NFC * DM]) for i in range(2)]
    win_d = din("win", [NL, 128, 8 * DIN])
    wgate_d = din("wgate", [NL, 128, 8 * 3072])
    bgate_d = din("bgate", [NL, 128, 24])
    wbr_d = din("wbr", [NL, 3, 128, 4 * DM])
    wout_d = din("wout", [NL, 128, 8 * DM])
    cw1_d = din("cw1", [NL, 2, 64, 32 * 256])
    cw2_d = din("cw2", [NL, 2, 128, 128])
    pet_d = din("pet", [NL, 2, 64, 32])
    lng_d = din("lng", [NL, 3, 128, DM])
    lnb_d = din("lnb", [NL, 3, 128, DM])
    sinks_d = din("sinks", [NL, 128, 8])
    bf_d = din("bf", [NL, 8, 1])
    ident_d = din("ident", [128, 128])
    onehot_d = din("onehot", [64, S_LEN])
    biasT_d = din("biasT", [16, 2, 128, 128])
    cfar_d = din("cfar", [128, 16])
    tc_d = din("tc", [8, 128, 512])
    ft_d = din("ft", [128, 128])
    cm_d = din("cm", [128, 128])
    lt_d = din("lt", [128, 128])
    sel24_d = din("sel24", [24, 1536])

    xs = [dscr("xs0", [S_LEN, DM], F32), dscr("xs1", [S_LEN, DM], F32)]
    w1s = [dscr("w1s%d" % i, [NFC, 128, 2048], BF16) for i in range(2)]
    w2s = [dscr("w2s%d" % i, [128, NFC * DM], BF16) for i in range(2)]
    wins = dscr("wins", [128, 8 * DIN], BF16)
    wgs = dscr("wgs", [128, 8 * 3072], BF16)
    wbrs = dscr("wbrs", [3, 128, 4 * DM], BF16)
    wouts = dscr("wouts", [128, 8 * DM], BF16)
    cw1s = dscr("cw1s", [2, 64, 32 * 256], BF16)
    ohs = dscr("ohs", [64, S_LEN], BF16)
    if dbg and "oT" in dbg:
        oT = nc.dram_tensor("oT", [3, 512, S_LEN], BF16, kind="ExternalOutput").ap()
    else:
        oT = dscr("oT", [3, 512, S_LEN], BF16)
    if dbg and "oT" in dbg:
        ocmp = nc.dram_tensor("ocmp", [8, 64, S_LEN], F32, kind="ExternalOutput").ap()
        dsel = nc.dram_tensor("dsel", [8, 64, S_LEN], F32, kind="ExternalOutput").ap()
        dwin = nc.dram_tensor("dwin", [8, 64, S_LEN], F32, kind="ExternalOutput").ap()
    else:
        ocmp = dscr("ocmp", [8, 64, S_LEN], F32)
        dsel = dwin = None
    caug = dscr("caug", [8, 6, S_LEN], BF16)
    B_xs = [[Buf() for _ in range(8)] for _ in range(2)]
    B_w1s = [Buf(), Buf()]
    B_w2s = [Buf(), Buf()]
    B_wins, B_wgs, B_wbrs, B_wouts, B_cw1s, B_ohs = Buf(), Buf(), Buf(), Buf(), Buf(), Buf()
    B_oT = [[Buf() for _ in range(8)] for _ in range(3)]
    B_ocmp = [[Buf() for _ in range(8)] for _ in range(8)]
    B_caug = Buf()
    B_y = Buf()
    dbg_d = {}
    if dbg:
        for nm in dbg:
            if nm == "oT":
                continue
            dbg_d[nm] = nc.dram_tensor("dbg_" + nm, [S_LEN, DM], F32, kind="ExternalOutput").ap()

    banks = [S.stack.enter_context(nc.psum_tensor("bank%d" % i, [128, 512], F32)) for i in range(8)]
    B_bank = [Buf(excl=True) for _ in range(8)]
    ring5 = Ring([0, 1, 2, 3, 4])

    def bank():
        i = ring5.next()
        return banks[i], B_bank[i]

    cpy_rr = [0]

    def mm(out, lhsT, rhs, start, stop, reads, writes):
        S.op("tensor", lambda e: e.matmul(out, lhsT=lhsT, rhs=rhs, start=start, stop=stop),
             reads=reads, writes=writes)

    def tr(out, in_, ident, reads, writes):
        S.op("tensor", lambda e: e.transpose(out, in_, ident), reads=reads, writes=writes)

    def act(out, in_, func, reads, writes, bias=None, scale=None, accum=None):
        kw = {}
        if bias is not None:
            kw["bias"] = bias
        if scale is not None:
            kw["scale"] = scale
        if accum is not None:
            kw["accum_out"] = accum
        S.op("scalar", lambda e: e.activation(out=out, in_=in_, func=func, **kw), reads=reads, writes=writes)

    def vcopy(eng, out, in_, reads, writes):
        S.op(eng, lambda e: e.tensor_copy(out=out, in_=in_), reads=reads, writes=writes)

    def copy_any(out, in_, reads, writes, psum=True):
        cpy_rr[0] ^= 1
        if cpy_rr[0]:
            vcopy("vector", out, in_, reads, writes)
        else:
            act(out, in_, AF.Copy, reads, writes)

    def tt(eng, out, in0, in1, op, reads, writes):
        S.op(eng, lambda e: e.tensor_tensor(out=out, in0=in0, in1=in1, op=op), reads=reads, writes=writes)

    def ts(eng, out, in0, s1, s2, op0, op1, reads, writes):
        if op1 is None:
            S.op(eng, lambda e: e.tensor_scalar(out=out, in0=in0, scalar1=s1, scalar2=None, op0=op0),
                 reads=reads, writes=writes)
        else:
            S.op(eng, lambda e: e.tensor_scalar(out=out, in0=in0, scalar1=s1, scalar2=s2, op0=op0, op1=op1),
                 reads=reads, writes=writes)

    def stt(eng, out, in0, scalar, in1, op0, op1, reads, writes):
        S.op(eng, lambda e: e.scalar_tensor_tensor(out=out, in0=in0, scalar=scalar, in1=in1, op0=op0, op1=op1),
             reads=reads, writes=writes)

    def memset(eng, ap, val, writes):
        S.op(eng, lambda e: e.memset(ap, val), writes=writes)

    def ld(out, in_, reads, writes):
        S.dma("sync", lambda e: e.dma_start(out=out, in_=in_), reads=reads, writes=writes)

    ident_f = S.sb([128, 128], F32)
    ident_b = S.sb([128, 128], BF16)
    nsaT = S.sb([128, 8, 2, 128], BF16)
    swaT = S.sb([128, 8, 2, 128], BF16)
    cm_b = S.sb([128, 128], BF16)
    lt_b = S.sb([128, 128], BF16)
    tc_b = S.sb([128, 8, 512], BF16)
    ft_f = S.sb([128, 128], F32)
    sel24 = S.sb([24, 1536], BF16)
    cfar = S.sb([128, 16], F32)
    B_const = Buf()
    CH = 1024
    stg = Ring([(S.sb([128, CH], F32), S.sb([128, CH], BF16), Buf(), Buf()) for _ in range(3)])
    stage_f, stage_b, B_stage_f, B_stage_b = stg.items[0]
    cwstage = S.sb([128, 160], F32)
    B_cwstage = Buf()

    def stD(out, in_, reads, writes):
        S.dma("gpsimd", lambda e: e.dma_start(out=out, in_=in_), reads=reads, writes=writes)

    epst = S.sb([128, 2], F32)
    memset("vector", epst[:, 0:1], LN_EPS, [B_const])
    memset("vector", epst[:, 1:2], 1.0, [B_const])
    ld(ident_f[:], ident_d, [], [B_const])
    vcopy("vector", ident_b[:], ident_f[:], [B_const], [B_const])
    ld(cfar[:], cfar_d, [], [B_const])
    ld(ft_f[:], ft_d, [], [B_const])
    for (dst, src) in ((cm_b, cm_d), (lt_b, lt_d)):
        ld(stage_f[:, 0:128], src, [], [B_stage_f])
        vcopy("vector", dst[:], stage_f[:, 0:128], [B_stage_f], [B_const])
    for c in range(2):
        ld(stage_f[0:24, 0:768], sel24_d[:, c * 768:(c + 1) * 768], [], [B_stage_f])
        vcopy("vector", sel24[:, c * 768:(c + 1) * 768], stage_f[0:24, 0:768], [B_stage_f], [B_const])
    for h in range(8):
        ld(stage_f[:, 0:512], tc_d[h], [], [B_stage_f])
        vcopy("vector", tc_b[:, h, :], stage_f[:, 0:512], [B_stage_f], [B_const])
    for h in range(16):
        for k in range(2):
            ld(stage_f[:, 0:128], biasT_d[h, k], [], [B_stage_f])
            if h < 8:
                ts("vector", nsaT[:, h, k, :], stage_f[:, 0:128], cfar[:, h:h + 1], None, ALU.subtract, None,
                   [B_stage_f, B_const], [B_const])
            else:
                vcopy("vector", swaT[:, h - 8, k, :], stage_f[:, 0:128], [B_stage_f], [B_const])
    for c in range(4):
        ld(stage_f[0:64, :], onehot_d[:, c * 1024:(c + 1) * 1024], [], [B_stage_f])
        vcopy("vector", stage_b[0:64, :], stage_f[0:64, :], [B_stage_f], [B_stage_b])
        ld(ohs[:, c * 1024:(c + 1) * 1024], stage_b[0:64, :], [B_stage_b], [B_ohs])

    PQ = []
    inflight = []
    prep_rr = [0]

    def prep(dst, src, n, rows, Bdst, scale=None):
        for c0 in range(0, n, CH):
            PQ.append((dst, src, c0, min(CH, n - c0), rows, Bdst, scale))

    def pump(k=1):
        for _ in range(k):
            if inflight and (len(inflight) >= 2 or not PQ):
                (dst, src, c0, w, rows, Bdst, scale), (sf, sbt, Bsf, Bsb) = inflight.pop(0)
                prep_rr[0] ^= 1
                ceng = "vector" if prep_rr[0] else "gpsimd"
                if scale is None:
                    vcopy(ceng, sbt[0:rows, 0:w], sf[0:rows, 0:w], [Bsf], [Bsb])
                else:
                    ts(ceng, sbt[0:rows, 0:w], sf[0:rows, 0:w], scale, None, ALU.mult, None, [Bsf], [Bsb])
                stD(dst[:, c0:c0 + w], sbt[0:rows, 0:w], [Bsb], [Bdst])
            if PQ:
                task = PQ.pop(0)
                bufs = stg.next()
                (dst, src, c0, w, rows, Bdst, scale) = task
                stD(bufs[0][0:rows, 0:w], src[:, c0:c0 + w], [], [bufs[2]])
                inflight.append((task, bufs))

    def prep_flush():
        while PQ or inflight:
            pump()

    def prep_ffn(l, which):
        prep(w2s[which], w2_d[which][l], NFC * DM, 128, B_w2s[which], scale=0.5)
        for fc in range(NFC):
            prep(w1s[which][fc], w1_d[which][l, fc], 2048, 128, B_w1s[which])

    def prep_mixer(l):
        prep(wins, win_d[l], 8 * DIN, 128, B_wins)
        prep(wgs, wgate_d[l], 8 * 3072, 128, B_wgs)
        for x in range(3):
            prep(wbrs[x], wbr_d[l, x], 4 * DM, 128, B_wbrs)
        prep(wouts, wout_d[l], 8 * DM, 128, B_wouts)
        for kv in range(2):
            prep(cw1s[kv], cw1_d[l, kv], 32 * 256, 64, B_cw1s)

    base_mark = S.mark()

    def ln_epilogue(xrow, b0, b1, Bb0, Bb1, Bx, lng, lnb, B_ln, tmp, dst_ap, Bdst, dbg_ap=None):
        z, junk, xo, st, B_z, B_junk, B_xo, B_st = tmp
        stt("vector", z[:, 0:512], xrow[:, 0:512], ALPHA, b0[:, :], ALU.mult, ALU.add, [Bx, Bb0], [B_z])
        stt("vector", z[:, 512:1024], xrow[:, 512:1024], ALPHA, b1[:, :], ALU.mult, ALU.add, [Bx, Bb1], [B_z])
        act(junk[:], z[:], AF.Copy, [B_z], [B_junk, B_st], accum=st[:, 0:1])
        act(junk[:], z[:], AF.Square, [B_z], [B_junk, B_st], accum=st[:, 1:2])
        ts("vector", st[:, 2:4], st[:, 0:2], 1.0 / DM, None, ALU.mult, None, [B_st], [B_st])
        stt("vector", st[:, 4:5], st[:, 2:3], -1.0, st[:, 2:3], ALU.mult, ALU.mult, [B_st], [B_st])
        tt("vector", st[:, 5:6], st[:, 4:5], st[:, 3:4], ALU.add, [B_st], [B_st])
        act(st[:, 8:9], st[:, 5:6], AF.Sqrt, [B_st, B_const], [B_st], bias=epst[:, 0:1])
        S.op("vector", lambda e: e.reciprocal(out=st[:, 6:7], in_=st[:, 8:9]), reads=[B_st], writes=[B_st])
        stt("vector", st[:, 7:8], st[:, 2:3], -1.0, st[:, 6:7], ALU.mult, ALU.mult, [B_st], [B_st])
        act(xo[:], z[:], AF.Identity, [B_z, B_st], [B_xo], bias=st[:, 7:8], scale=st[:, 6:7])
        tt("gpsimd", xo[:], xo[:], lng[:], ALU.mult, [B_xo, B_ln], [B_xo])
        tt("gpsimd", xo[:], xo[:], lnb[:], ALU.add, [B_xo, B_ln], [B_xo])
        stD(dst_ap, xo[:], [B_xo], [Bdst])
        if dbg_ap is not None:
            ld(dbg_ap, xo[:], [B_xo], [Buf()])

    def ln_tmp():
        z = S.sb([128, DM], F32)
        junk = S.sb([128, DM], F32)
        xo = S.sb([128, DM], F32)
        st = S.sb([128, 10], F32)
        return (z, junk, xo, st, Buf(), Buf(), Buf(), Buf())

    def load_xT(xin, Bxin, xT, BxT):
        for kc in range(8):
            bk, Bbk = bank()
            for n in range(4):
                tr(bk[:, n * 128:(n + 1) * 128], xin[:, n, kc * 128:(kc + 1) * 128], ident_f[:], [Bxin, B_const], [Bbk])
            copy_any(xT[:, kc, :], bk[:, :], [Bbk], [BxT])

    def ffn_stage(l, which, src, Bsrc, dst, Bdst, lnidx, dbg_ap=None):
        m0 = S.mark()
        prep_flush()
        if which == 0:
            prep_mixer(l)
        elif l + 1 < NL:
            prep_ffn(l + 1, 0)
        lng = S.sb([128, DM], F32)
        lnb = S.sb([128, DM], F32)
        B_ln = Buf()
        ld(lng[:], lng_d[l, lnidx], [], [B_ln])
        ld(lnb[:], lnb_d[l, lnidx], [], [B_ln])
        xin_r = Ring([(S.sb([128, 4, DM], F32), Buf()) for _ in range(2)])
        xT = S.sb([128, 8, 512], BF16)
        BxT = Buf()
        w1p = Ring([(S.sb([128, 8, 256], BF16), Buf()) for _ in range(3)])
        hT = S.sb([128, NFC, 512], BF16)
        BhT = [Buf() for _ in range(NFC)]
        w2t = S.sb([128, NFC, DM], BF16)
        Bw2t = Buf()
        sgr = Ring([(S.sb([128, 512], F32), Buf()) for _ in range(2)])
        tmp = ln_tmp()
        ld(w2t[:], w2s[which].rearrange("p (f d) -> p f d", d=DM), [B_w2s[which]], [Bw2t])
        for tg in range(8):
            xin, Bxin = xin_r.next()
            ld(xin[:], src[tg * 512:(tg + 1) * 512, :].rearrange("(n p) d -> p n d", p=128), [Bsrc[tg]], [Bxin])
            load_xT(xin, Bxin, xT, BxT)
            for fc in range(NFC):
                pump()
                wp, Bwp = w1p.next()
                ld(wp[:], w1s[which][fc].rearrange("p (k c) -> p k c", c=256), [B_w1s[which]], [Bwp])
                bg, Bbg = bank()
                for kc in range(8):
                    mm(bg[:, :], wp[:, kc, 0:128], xT[:, kc, :], kc == 0, kc == 7, [Bwp, BxT], [Bbg])
                bu, Bbu = bank()
                for kc in range(8):
                    mm(bu[:, :], wp[:, kc, 128:256], xT[:, kc, :], kc == 0, kc == 7, [Bwp, BxT], [Bbu])
                sg, Bsg = sgr.next()
                act(sg[:], bg[:, :], AF.Silu, [Bbg], [Bsg])
                tt("vector", hT[:, fc, :], sg[:], bu[:, :], ALU.mult, [Bsg, Bbu], [BhT[fc]])
            for n in range(4):
                b0, b1 = banks[5], banks[6]
                for fc in range(NFC):
                    mm(b0[:, :], hT[:, fc, n * 128:(n + 1) * 128], w2t[:, fc, 0:512], fc == 0, fc == NFC - 1,
                       [BhT[fc], Bw2t], [B_bank[5]])
                    mm(b1[:, :], hT[:, fc, n * 128:(n + 1) * 128], w2t[:, fc, 512:1024], fc == 0, fc == NFC - 1,
                       [BhT[fc], Bw2t], [B_bank[6]])
                r0 = tg * 512 + n * 128
                ln_epilogue(xin[:, n, :], b0, b1, B_bank[5], B_bank[6], Bxin, lng, lnb, B_ln, tmp,
                            dst[r0:r0 + 128, :], Bdst[tg] if isinstance(Bdst, list) else Bdst,
                            None if dbg_ap is None else dbg_ap[r0:r0 + 128, :])
        S.release(m0)

    def mixer_stage(l, src, Bsrc, dst, Bdst, dbg_ap=None):
        m00 = S.mark()
        try:
            mixer_stage_(l, src, Bsrc, dst, Bdst, dbg_ap)
        except _Stop:
            S.release(m00)

    def chk(n):
        if _SUB <= n:
            raise _Stop()

    def mixer_stage_(l, src, Bsrc, dst, Bdst, dbg_ap=None):
        m0 = S.mark()
        prep_flush()
        prep_ffn(l, 1)
        wins3 = wins.rearrange("p (k c) -> p k c", c=DIN)

        hT = S.sb([128, 8, S_LEN], BF16)
        BhT = [Buf() for _ in range(8)]
        m_h = S.mark()
        xin = S.sb([128, 4, DM], F32)
        Bxin = Buf()
        xTt = S.sb([128, 8, 512], BF16)
        for tg in range(8):
            ld(xin[:], src[tg * 512:(tg + 1) * 512, :].rearrange("(n p) d -> p n d", p=128), [Bsrc[tg]], [Bxin])
            for kc in range(8):
                bk, Bbk = bank()
                for n in range(4):
                    tr(bk[:, n * 128:(n + 1) * 128], xin[:, n, kc * 128:(kc + 1) * 128], ident_f[:], [Bxin, B_const], [Bbk])
                copy_any(hT[:, kc, tg * 512:(tg + 1) * 512], bk[:, :], [Bbk], [BhT[tg]])
        S.release(m_h)

        Qa = [S.sb([128, S_LEN], BF16) for _ in range(2)]
        BQ = [[Buf() for _ in range(8)] for _ in range(2)]
        BQaug = [Buf(), Buf()]
        Ka = [S.sb([128, S_LEN], BF16) for _ in range(2)]
        BK = [[Buf() for _ in range(8)] for _ in range(2)]
        BKaug = [Buf(), Buf()]
        V1 = [S.sb([128, 32, 128], BF16) for _ in range(2)]
        BV = [[Buf() for _ in range(8)] for _ in range(2)]
        wpr = Ring([(S.sb([128, 1024], BF16), Buf()) for _ in range(2)])
        ptr = Ring([(S.sb([128, 512], BF16), Buf()) for _ in range(4)])
        den_g = S.sb([64, 512], F32)
        rec_g = S.sb([64, 512], F32)
        coef_g = S.sb([64, 512], F32)
        obf = S.sb([64, 512], BF16)
        B_den_g, B_rec_g, B_coef_g, B_obf = Buf(), Buf(), Buf(), Buf()
        gatesT = S.sb([24, S_LEN], BF16)
        B_gates = [Buf() for _ in range(8)]
        expsink = S.sb([128, 8], F32)
        B_sink = Buf()
        nbf = S.sb([8, 1], F32)
        B_nbf = Buf()
        if _MIX_STOP <= -1:
            S.release(m0)
            return
        for b in range(2):
            for tg in range(8):
                memset("gpsimd", V1[b][:, tg * 4:(tg + 1) * 4, 64:128], 1.0, [BV[b][tg]])
        ld(expsink[:], sinks_d[l], [], [B_sink])
        act(expsink[:], expsink[:], AF.Exp, [B_sink], [B_sink])
        ld(nbf[:], bf_d[l], [], [B_nbf])
        ts("vector", nbf[:], nbf[:], -1.0, None, ALU.mult, None, [B_nbf], [B_nbf])
        chk(1)

        def load_w(name, c0=0, wd=None):
            off, w = WIN_P[name]
            if wd is None:
                wd = w
            wpf, Bwp = wpr.next()
            wp = wpf[:, 0:8 * wd].rearrange("p (k c) -> p k c", c=wd)
            ld(wpf[:, 0:8 * wd], wins[:, 8 * off:8 * off + 8 * wd], [B_wins], [Bwp])
            return wp, Bwp, wd

        def proj_fm(name, evac, c0=0, wd=None, tgs=range(8)):
            wp, Bwp, wd = load_w(name, c0, wd)
            for tg in tgs:
                bk, Bbk = bank()
                for kc in range(8):
                    mm(bk[0:wd, :], wp[:, kc, 0:wd], hT[:, kc, tg * 512:(tg + 1) * 512], kc == 0, kc == 7,
                       [Bwp, BhT[tg]], [Bbk])
                evac(tg, bk, Bbk)

        def proj_tm(name, evac):
            wp, Bwp, wd = load_w(name)
            for t in range(32):
                bk, Bbk = bank()
                for kc in range(8):
                    mm(bk[:, 0:wd], hT[:, kc, t * 128:(t + 1) * 128], wp[:, kc, 0:wd], kc == 0, kc == 7,
                       [Bwp, BhT[t // 4]], [Bbk])
                evac(t, bk, Bbk)

        def ev_scaled(dst, r0, Bd, scale):
            def f(tg, bk, Bbk):
                if r0 == 0:
                    ts("vector", dst[0:64, tg * 512:(tg + 1) * 512], bk[0:64, :], scale, None, ALU.mult, None,
                       [Bbk], [Bd[tg]])
                else:
                    act(dst[0:64, tg * 512:(tg + 1) * 512], bk[r0:r0 + 64, :], AF.Copy, [Bbk], [Bd[tg]], scale=scale)
            return f

        def ev_plain(dst, r0, Bd):
            def f(tg, bk, Bbk):
                if r0 == 0:
                    vcopy("vector", dst[0:64, tg * 512:(tg + 1) * 512], bk[r0:r0 + 64, :], [Bbk], [Bd[tg]])
                else:
                    act(dst[0:64, tg * 512:(tg + 1) * 512], bk[r0:r0 + 64, :], AF.Copy, [Bbk], [Bd[tg]])
            return f

        def ev_multi(*fs):
            def f(tg, bk, Bbk):
                for g in fs:
                    g(tg, bk, Bbk)
            return f

        AJ = []
        job_ctr = [0]
        LOOK = 3

        def attn_group(g, Q, Qreads, kdim, K, Kreads, Vt, BVt, plan, fin):
            AJ.append((g, Q, Qreads, kdim, K, Kreads, Vt, BVt, plan, fin))

        def attn_flush():
            pend = []

            def emit_qk(job, stt_, item):
                (g, Q, Qreads, kdim, K, Kreads, Vt, BVt, plan, fin) = job
                (kb, n_lo, n_hi, bias) = item
                bk, Bbk = bank()
                c0, c1 = n_lo * 128, n_hi * 128
                nb = len(bias)
                mm(bk[:, c0:c1], K[0:kdim, kb * 128:(kb + 1) * 128], Q[0:kdim, g * 512 + c0:g * 512 + c1],
                   True, nb == 0, Kreads(kb) + Qreads(g), [Bbk])
                for bi, (n, bap) in enumerate(sorted(bias.items())):
                    mm(bk[:, n * 128:(n + 1) * 128], ident_b[:], bap, False, bi == nb - 1, [B_const], [Bbk])
                return (job, stt_, item, bk, Bbk)

            def emit_pv(rec_):
                job, stt_, (kb, n_lo, n_hi, bias), bk, Bbk = rec_
                Vt, BVt = job[6], job[7]
                ob, Bob = stt_["ob"], stt_["Bob"]
                ntouch, total = stt_["ntouch"], stt_["total"]
                c0, c1 = n_lo * 128, n_hi * 128
                pt, Bpt = ptr.next()
                act(pt[:, c0:c1], bk[:, c0:c1], AF.Exp, [Bbk], [Bpt])
                n = n_lo
                while n < n_hi:
                    last = ntouch[n] == total[n] - 1
                    n2 = n + 1
                    while n2 < n_hi and (ntouch[n2] == total[n2] - 1) == last:
                        n2 += 1
                    S.op("tensor", lambda e, o_=ob[:, n * 128:n2 * 128], l_=Vt[:, kb, :], r_=pt[:, n * 128:n2 * 128],
                         st_=(stt_["npv"] == 0), sp_=last: e.matmul(o_, lhsT=l_, rhs=r_, start=st_, stop=sp_, skip_group_check=True),
                         reads=[BVt[kb // 4], Bpt], writes=[Bob])
                    stt_["npv"] += 1
                    for k in range(n, n2):
                        ntouch[k] += 1
                    n = n2
                stt_["left"] -= 1
                if stt_["left"] == 0:
                    stt_["fin"](ob, Bob)

            for job in AJ:
                plan, fin = job[8], job[9]
                jb = job_ctr[0]
                job_ctr[0] += 1
                stt_ = dict(ob=banks[5 + jb % 2], Bob=B_bank[5 + jb % 2], ntouch=[0] * 4, total=[0] * 4, npv=0,
                            left=len(plan), fin=fin)
                for (kb, n_lo, n_hi, bias) in plan:
                    for n in range(n_lo, n_hi):
                        stt_["total"][n] += 1
                pump()
                for item in plan:
                    pend.append(emit_qk(job, stt_, item))
                    if len(pend) > LOOK:
                        emit_pv(pend.pop(0))
            while pend:
                emit_pv(pend.pop(0))
            del AJ[:]

        def finalize(ob, Bob, ncols, gate_row=None, gcols=None, sink_h=None, clampden=False, out=None, Bout=None,
                     tmp=None):
            if tmp is None:
                tmp = (den_g, rec_g, coef_g, B_den_g, B_rec_g, B_coef_g)
            den, rec, coef, B_den, B_rec, B_coef = tmp
            act(den[:, 0:ncols], ob[64:128, 0:ncols], AF.Copy, [Bob], [B_den])
            if sink_h is not None:
                ts("vector", den[:, 0:ncols], den[:, 0:ncols], expsink[0:64, sink_h:sink_h + 1], None, ALU.add, None,
                   [B_den, B_sink], [B_den])
            if clampden:
                ts("vector", den[:, 0:ncols], den[:, 0:ncols], 1e-30, None, ALU.max, None, [B_den], [B_den])
            S.op("vector", lambda e: e.reciprocal(out=rec[:, 0:ncols], in_=den[:, 0:ncols]), reads=[B_den], writes=[B_rec])
            src_coef = rec
            Bsc = B_rec
            if gate_row is not None:
                gb, Bgb = banks[7], B_bank[7]
                mm(gb[0:64, 0:ncols], sel24[0:24, gate_row * 64:(gate_row + 1) * 64],
                   gatesT[0:24, gcols:gcols + ncols], True, True, [B_const, B_gates[gcols // 512]], [Bgb])
                tt("vector", coef[:, 0:ncols], rec[:, 0:ncols], gb[0:64, 0:ncols], ALU.mult, [B_rec, Bgb], [B_coef])
                src_coef = coef
                Bsc = B_coef
            tt("vector", out, ob[0:64, 0:ncols], src_coef[:, 0:ncols], ALU.mult, [Bob, Bsc], [Bout])

        def causal_plan(g, diag_bias, sub_bias=None):
            plan = []
            for kb in range(4 * g + 4):
                m = kb - 4 * g
                if m < 0:
                    b = {}
                    if sub_bias is not None and m == -1:
                        b[0] = sub_bias
                    plan.append((kb, 0, 4, b))
                else:
                    b = {m: diag_bias}
                    if sub_bias is not None and m + 1 < 4:
                        b[m + 1] = sub_bias
                    plan.append((kb, m, 4, b))
            return plan

        m1 = S.mark()
        spc = S.sb([8, 512], F32)
        Cc = S.sb([8, 512], F32)
        r1 = S.sb([8, 512], F32)
        r2 = S.sb([8, 512], F32)
        cb = [S.sb([8, 512], BF16) for _ in range(6)]
        carry = S.sb([8, 1], F32)
        B_f = Buf()
        memset("vector", carry[:], 0.0, [B_f])

        def f_evac(tg, bk, Bbk):
            act(spc[:], bk[0:8, :], AF.Exp, [Bbk, B_nbf], [B_f], bias=nbf[:, 0:1], scale=-1.0)
            act(spc[:], spc[:], AF.Ln, [B_f, B_const], [B_f], bias=epst[0:8, 1:2])
            S.op("vector", lambda e: e.tensor_tensor_scan(out=Cc[:], data0=spc[:], data1=spc[:], initial=carry[:, 0:1],
                                                           op0=ALU.add, op1=ALU.max), reads=[B_f], writes=[B_f])
            vcopy("vector", carry[:], Cc[:, 511:512], [B_f], [B_f])
            vcopy("vector", cb[3][:], Cc[:], [B_f], [B_f])
            tt("vector", r1[:], Cc[:], cb[3][:], ALU.subtract, [B_f], [B_f])
            vcopy("vector", cb[4][:], r1[:], [B_f], [B_f])
            tt("vector", r2[:], r1[:], cb[4][:], ALU.subtract, [B_f], [B_f])
            vcopy("vector", cb[5][:], r2[:], [B_f], [B_f])
            for j in range(3):
                ts("vector", cb[j][:], cb[3 + j][:], -1.0, None, ALU.mult, None, [B_f], [B_f])
            for j in range(6):
                ld(caug[:, j, tg * 512:(tg + 1) * 512], cb[j][:], [B_f], [B_caug])
        if _MIX_STOP >= 2:
            proj_fm("fox_f", f_evac)
        for b in range(2):
            memset("vector", Qa[b][64:70, :], 1.0, [BQaug[b]])
            memset("vector", Ka[b][64:70, :], 1.0, [BKaug[b]])
        chk(2)
        for h in range(8 if _MIX_STOP >= 3 else 0):
            b = h % 2
            proj_fm("fox_qk%d" % h, ev_multi(ev_scaled(Qa[b], 0, BQ[b], 0.125), ev_plain(Ka[b], 64, BK[b])))
            ld(Qa[b][64:67, :], caug[h, 0:3, :], [B_caug], [BQaug[b]])
            ld(Ka[b][67:70, :], caug[h, 3:6, :], [B_caug], [BKaug[b]])

            def v_evac(t, bk, Bbk, b=b):
                vcopy("vector", V1[b][:, t, 0:64], bk[:, 0:64], [Bbk], [BV[b][t // 4]])
            proj_tm("fox_v%d" % h, v_evac)
            for g in range(8):
                def fin(ob, Bob, g=g, h=h):
                    finalize(ob, Bob, 512, out=obf[:, :], Bout=B_obf)
                    stD(oT[2, h * 64:(h + 1) * 64, g * 512:(g + 1) * 512], obf[:, :], [B_obf], [B_oT[2][g]])
                attn_group(g, Qa[b], lambda gg, b=b: [BQ[b][gg], BQaug[b]], 70,
                           Ka[b], lambda kb, b=b: [BK[b][kb // 4], BKaug[b]], V1[b], BV[b],
                           causal_plan(g, cm_b[:]), fin)
            attn_flush()
        S.release(m1)

        m1 = S.mark()
        proj_fm("swa_k", ev_multi(ev_plain(Ka[0], 0, BK[0]), ev_plain(Ka[1], 64, BK[1])))

        def sv_evac(t, bk, Bbk):
            vcopy("vector", V1[0][:, t, 0:64], bk[:, 0:64], [Bbk], [BV[0][t // 4]])
            act(V1[1][:, t, 0:64], bk[:, 64:128], AF.Copy, [Bbk], [BV[1][t // 4]])
        proj_tm("swa_v", sv_evac)
        chk(3)
        for i in range(4 if _MIX_STOP >= 4 else 0):
            proj_fm("swa_q%d" % i, ev_multi(ev_scaled(Qa[0], 0, BQ[0], 0.125), ev_scaled(Qa[1], 64, BQ[1], 0.125)))
            for b in range(2):
                h = 2 * i + b
                gk = h // 4
                for g in range(8):
                    plan = []
                    for m in range(-1, 4):
                        kb = 4 * g + m
                        if kb < 0:
                            continue
                        n_lo, n_hi = max(0, m), min(4, m + 2)
                        bias = {}
                        for n in range(n_lo, n_hi):
                            bias[n] = swaT[:, h, n - m, :]
                        plan.append((kb, n_lo, n_hi, bias))

                    def fin(ob, Bob, g=g, h=h):
                        finalize(ob, Bob, 512, sink_h=h, out=obf[:, :], Bout=B_obf)
                        stD(oT[1, h * 64:(h + 1) * 64, g * 512:(g + 1) * 512], obf[:, :], [B_obf], [B_oT[1][g]])
                    attn_group(g, Qa[b], lambda gg, b=b: [BQ[b][gg]], 64,
                               Ka[gk], lambda kb, gk=gk: [BK[gk][kb // 4]], V1[gk], BV[gk], plan, fin)
            attn_flush()
        S.release(m1)

        m1 = S.mark()
        Ksel = Ka[0]
        BKsel = BK[0]
        B_oh = BKaug[0]
        Kwin = Ka[1]
        BKwin = BK[1]
        kcT = S.sb([64, 256], BF16)
        B_kcT = Buf()
        vc1 = S.sb([128, 2, 128], BF16)
        B_vc1 = Buf()
        m_cmp = S.mark()
        cwc = Ring([(S.sb([64, 8, 256], BF16), Buf()) for _ in range(2)])
        cw2t = S.sb([128, 2, 64], BF16)
        pet = S.sb([64, 32], BF16)
        B_cw2 = Buf()
        bvec = S.sb([128, 2], F32)
        B_bvec = Buf()
        gx = [S.sb([128, 256], F32) for _ in range(4)]
        B_gx = Buf()
        hid = [S.sb([128, 256], BF16) for _ in range(2)]
        B_hid = Buf()
        cmp_end = S.sb_off
        S.sb_off = m_cmp
        qt2 = [[S.sb([64, 128], BF16) for _ in range(4)] for _ in range(2)]
        B_qt2 = [[Buf() for _ in range(4)] for _ in range(2)]
        s4 = [S.sb([128, 256], F32) for _ in range(4)]
        e4 = [S.sb([128, 256], F32) for _ in range(4)]
        rs4 = [S.sb([128, 4], F32) for _ in range(4)]
        eT4 = [S.sb([128, 2, 128], BF16) for _ in range(4)]
        B_s4 = [Buf() for _ in range(4)]
        B_e4 = [Buf() for _ in range(4)]
        B_rs4 = [Buf() for _ in range(4)]
        B_eT4 = [Buf() for _ in range(4)]
        pacc2 = [S.sb([128, 260], F32) for _ in range(2)]
        B_pacc2 = [Buf(), Buf()]
        impt = S.sb([128, 64], F32)
        score = S.sb([128, 64], F32)
        top8 = S.sb([128, 8], F32)
        Mt = S.sb([128, 128], F32)
        ocst4 = [S.sb([64, 2, 128], F32) for _ in range(4)]
        B_ocst4 = [Buf() for _ in range(4)]
        fin_tmp = [(S.sb([64, 128], F32), S.sb([64, 128], F32), S.sb([64, 128], F32), Buf(), Buf(), Buf()) for _ in range(2)]
        B_imp, B_Mt = Buf(), Buf()
        S.sb_off = max(S.sb_off, cmp_end)
        osel = S.sb([64, 512], F32)
        B_osel = Buf()
        ocl = S.sb([64, 512], F32)
        B_ocl = Buf()
        wq2 = [S.sb([128, 8, 128], BF16) for _ in range(2)]
        B_wq2 = Buf()

        ld(Ksel[64:128, :], ohs, [B_ohs], [B_oh])
        memset("vector", vc1[:, :, 64:128], 1.0, [B_vc1])

        def gate_evac(tg, bk, Bbk):
            act(gatesT[0:24, tg * 512:(tg + 1) * 512], bk[0:24, :], AF.Sigmoid, [Bbk], [B_gates[tg]])
        chk(4)
        proj_fm("nsa_gate", gate_evac)
        chk(5)

        for g in range(2 if _MIX_STOP >= 5 else 0):
            S.barrier()
            proj_fm("nsa_cmp%d" % g, ev_multi(ev_plain(Ka[0], 0, BK[0]), ev_plain(Ka[1], 64, BK[1])))
            for kv in range(2):
                ld(cwstage[:, 0:128], cw2_d[l, kv], [], [B_cwstage])
                vcopy("vector", cw2t[:, :, :], cwstage[:, 0:128].rearrange("p (a b) -> p a b", b=64), [B_cwstage], [B_cw2])
                ld(cwstage[0:64, 128:160], pet_d[l, kv], [], [B_cwstage])
                vcopy("vector", pet[:], cwstage[0:64, 128:160], [B_cwstage], [B_cw2])
                bh = [bank(), bank()]
                bb = [bank(), bank()]
                src_t = Ka[kv]
                for c in range(4):
                    cw, Bcw = cwc.next()
                    ld(cw[:], cw1s[kv].rearrange("d (p c) -> d p c", c=256)[:, c * 8:(c + 1) * 8, :], [B_cw1s], [Bcw])
                    for pp in range(8):
                        p = c * 8 + pp
                        for hc in range(2):
                            mm(bh[hc][0][:, 0:255], cw[:, pp, hc * 128:(hc + 1) * 128],
                               src_t[0:64, p:p + 16 * 254 + 1:16], p == 0, p == 31,
                               [Bcw] + BK[kv], [bh[hc][1]])
                            mm(bb[hc][0][:, 0:1], cw[:, pp, hc * 128:(hc + 1) * 128], pet[:, p:p + 1], p == 0, p == 31,
                               [Bcw, B_cw2], [bb[hc][1]])
                for hc in range(2):
                    vcopy("vector", bvec[:, hc:hc + 1], bb[hc][0][:, 0:1], [bb[hc][1]], [B_bvec])
                    x1, sq, u, sg = gx
                    act(x1[:, 0:255], bh[hc][0][:, 0:255], AF.Identity, [bh[hc][1], B_bvec], [B_gx], bias=bvec[:, hc:hc + 1])
                    tt("vector", sq[:, 0:255], x1[:, 0:255], x1[:, 0:255], ALU.mult, [B_gx], [B_gx])
                    ts("vector", sq[:, 0:255], sq[:, 0:255], 0.044715, 1.0, ALU.mult, ALU.add, [B_gx], [B_gx])
                    tt("vector", u[:, 0:255], sq[:, 0:255], x1[:, 0:255], ALU.mult, [B_gx], [B_gx])
                    act(sg[:, 0:255], u[:, 0:255], AF.Sigmoid, [B_gx], [B_gx], scale=1.5957691216057308)
                    tt("vector", hid[hc][:, 0:255], x1[:, 0:255], sg[:, 0:255], ALU.mult, [B_gx], [B_hid])
                if kv == 0:
                    bk, Bbk = bank()
                    for hc in range(2):
                        mm(bk[0:64, 0:255], cw2t[:, hc, :], hid[hc][:, 0:255], hc == 0, hc == 1, [B_cw2, B_hid], [Bbk])
                    vcopy("vector", kcT[:, 0:255], bk[0:64, 0:255], [Bbk], [B_kcT])
                else:
                    for c in range(2):
                        nn = 128 if c == 0 else 127
                        bk, Bbk = bank()
                        for hc in range(2):
                            mm(bk[0:nn, 0:64], hid[hc][:, c * 128:c * 128 + nn], cw2t[:, hc, :], hc == 0, hc == 1,
                               [B_cw2, B_hid], [Bbk])
                        vcopy("vector", vc1[0:nn, c, 0:64], bk[0:nn, 0:64], [Bbk], [B_vc1])
            proj_fm("nsa_kk%d" % g, ev_multi(ev_plain(Ksel, 0, BKsel), ev_plain(Kwin, 64, BKwin)))

            def nv_evac(t, bk, Bbk):
                vcopy("vector", V1[0][:, t, 0:64], bk[:, 0:64], [Bbk], [BV[0][t // 4]])
                act(V1[1][:, t, 0:64], bk[:, 64:128], AF.Copy, [Bbk], [BV[1][t // 4]])
            proj_tm("nsa_v%d" % g, nv_evac)

            for i in range(2):
                off, w = WIN_P["nsa_q%d" % (2 * g + i)]
                ld(wq2[i][:], wins[:, 8 * off:8 * off + 1024].rearrange("p (k c) -> p k c", c=128), [B_wins], [B_wq2])
            S.barrier()
            memset("vector", Mt[:, 0:64], 0.0, [B_Mt])

            def q_stage(qb):
                par = qb % 2
                for i in range(2):
                    bk, Bbk = bank()
                    for kc in range(8):
                        mm(bk[:, 0:128], wq2[i][:, kc, :], hT[:, kc, qb * 128:(qb + 1) * 128], kc == 0, kc == 7,
                           [B_wq2, BhT[qb // 4]], [Bbk])
                    act(qt2[par][2 * i][:], bk[0:64, 0:128], AF.Copy, [Bbk], [B_qt2[par][2 * i]], scale=0.125)
                    act(qt2[par][2 * i + 1][:], bk[64:128, 0:128], AF.Copy, [Bbk], [B_qt2[par][2 * i + 1]], scale=0.125)

            q_stage(0)
            for qb in range(32):
                par = qb % 2
                tg = qb // 4
                ncols = min(255, 8 * qb + 7)
                nt = 1 if ncols <= 128 else 2
                pacc, B_pacc = pacc2[par], B_pacc2[par]
                memset("gpsimd", pacc[:], 0.0, [B_pacc])
                sbk = []
                for hp in range(4):
                    bk, Bbk = bank()
                    mm(bk[:, 0:ncols], qt2[par][hp][:], kcT[:, 0:ncols], True, True, [B_qt2[par][hp], B_kcT], [Bbk])
                    sbk.append((bk, Bbk))
                for hp in range(4):
                    h = 4 * g + hp
                    bk, Bbk = sbk[hp]
                    tt("vector", s4[hp][:, 0:ncols], bk[:, 0:ncols], tc_b[:, h, 256 - 8 * qb:256 - 8 * qb + ncols], ALU.add,
                       [Bbk, B_const], [B_s4[hp]])
                    act(e4[hp][:, 0:ncols], s4[hp][:, 0:ncols], AF.Exp, [B_s4[hp]], [B_e4[hp], B_rs4[hp]],
                        accum=rs4[hp][:, 0:1])
                if qb + 1 < 32:
                    q_stage(qb + 1)
                for hp in range(4):
                    rs = rs4[hp]
                    ts("vector", rs[:, 1:2], rs[:, 0:1], 1e-30, None, ALU.max, None, [B_rs4[hp]], [B_rs4[hp]])
                    S.op("vector", lambda e, rs=rs: e.reciprocal(out=rs[:, 2:3], in_=rs[:, 1:2]),
                         reads=[B_rs4[hp]], writes=[B_rs4[hp]])
                    if hp == 0:
                        ts("vector", pacc[:, 1:1 + ncols], e4[hp][:, 0:ncols], rs[:, 2:3], None, ALU.mult, None,
                           [B_e4[hp], B_rs4[hp]], [B_pacc])
                    else:
                        stt("vector", pacc[:, 1:1 + ncols], e4[hp][:, 0:ncols], rs[:, 2:3], pacc[:, 1:1 + ncols],
                            ALU.mult, ALU.add, [B_e4[hp], B_rs4[hp], B_pacc], [B_pacc])
                for hp in range(4):
                    h = 4 * g + hp
                    ob, Bob = banks[5 + (hp % 2)], B_bank[5 + (hp % 2)]
                    bk2, Bbk2 = bank()
                    for c in range(nt):
                        nn = min(128, ncols - c * 128)
                        tr(bk2[0:nn, c * 128:(c + 1) * 128], e4[hp][:, c * 128:c * 128 + nn], ident_f[:],
                           [B_e4[hp], B_const], [Bbk2])
                    for c in range(nt):
                        nn = min(128, ncols - c * 128)
                        act(eT4[hp][0:nn, c, :], bk2[0:nn, c * 128:(c + 1) * 128], AF.Copy, [Bbk2], [B_eT4[hp]])
                    for c in range(nt):
                        nn = min(128, ncols - c * 128)
                        mm(ob[:, 0:128], vc1[0:nn, c, :], eT4[hp][0:nn, c, :], c == 0, c == nt - 1,
                           [B_vc1, B_eT4[hp]], [Bob])
                    ocst = ocst4[hp]
                    finalize(ob, Bob, 128, gate_row=h * 3 + 0, gcols=qb * 128, clampden=True,
                             out=ocst[:, qb % 2, :], Bout=B_ocst4[hp], tmp=fin_tmp[hp % 2])
                    if qb % 2 == 1:
                        stD(ocmp[h, :, (qb - 1) * 128:(qb + 1) * 128], ocst[:, :, :].rearrange("p a b -> p (a b)"),
                            [B_ocst4[hp]], [B_ocmp[h][tg]])
                S.op("vector", lambda e, pacc=pacc: e.tensor_reduce(out=impt[:], in_=pacc[:, 0:256].rearrange("p (j m) -> p j m", m=4),
                                                                    axis=AX.X, op=ALU.add), reads=[B_pacc], writes=[B_imp])
                tt("vector", impt[:], impt[:], pacc[:, 4:260:4], ALU.add, [B_imp, B_pacc], [B_imp])
                tt("vector", score[:], impt[:], ft_f[:, 64 - 2 * qb:128 - 2 * qb], ALU.add, [B_imp, B_const], [B_imp])
                ts("vector", score[:, 0:1], score[:, 0:1], 100.0, None, ALU.add, None, [B_imp], [B_imp])
                S.op("vector", lambda e: e.max(out=top8[:], in_=score[:]), reads=[B_imp], writes=[B_imp])
                ts("vector", Mt[:, 64:128], score[:], top8[:, 7:8], None, ALU.is_ge, None, [B_imp, B_Mt], [B_Mt])
                ts("vector", Mt[:, 64:128], Mt[:, 64:128], -NEG, NEG, ALU.mult, ALU.add, [B_Mt], [B_Mt])
                bk, Bbk = bank()
                tr(bk[:, 0:128], Mt[:], ident_f[:], [B_Mt, B_const], [Bbk])
                vcopy("vector", Qa[0][64:128, qb * 128:(qb + 1) * 128], bk[64:128, 0:128], [Bbk], [BQaug[0]])
                act(Qa[1][64:128, qb * 128:(qb + 1) * 128], bk[64:128, 0:128], AF.Copy, [Bbk], [BQaug[1]])

            for i in range(2):
                proj_fm("nsa_q%d" % (2 * g + i),
                        ev_multi(ev_scaled(Qa[0], 0, BQ[0], 0.125), ev_scaled(Qa[1], 64, BQ[1], 0.125)))
                for b in range(2):
                    h = 4 * g + 2 * i + b
                    for gq in range(8):
                        def fin_sel(ob, Bob, gq=gq, h=h):
                            ld(ocl[:], ocmp[h, :, gq * 512:(gq + 1) * 512], [B_ocmp[h][gq]], [B_ocl])
                            finalize(ob, Bob, 512, gate_row=h * 3 + 1, gcols=gq * 512, out=osel[:, :], Bout=B_osel)
                            if dsel is not None:
                                ld(dsel[h, :, gq * 512:(gq + 1) * 512], osel[:, :], [B_osel], [Buf()])
                            tt("gpsimd", ocl[:], ocl[:], osel[:], ALU.add, [B_ocl, B_osel], [B_ocl])
                        attn_group(gq, Qa[b], lambda gg, b=b: [BQ[b][gg], BQaug[b]], 128,
                                   Ksel, lambda kb: [BKsel[kb // 4], B_oh], V1[0], BV[0],
                                   causal_plan(gq, nsaT[:, h, 0, :], nsaT[:, h, 1, :]), fin_sel)
                        plan = []
                        for m in range(-4, 4):
                            kb = 4 * gq + m
                            if kb < 0:
                                continue
                            n_lo, n_hi = max(0, m), min(4, m + 5)
                            bias = {}
                            for n in range(n_lo, n_hi):
                                if n - m == 0:
                                    bias[n] = nsaT[:, h, 0, :]
                                elif n - m == 1:
                                    bias[n] = nsaT[:, h, 1, :]
                                elif n - m == 4:
                                    bias[n] = lt_b[:]
                            plan.append((kb, n_lo, n_hi, bias))

                        def fin_win(ob, Bob, gq=gq, h=h):
                            finalize(ob, Bob, 512, gate_row=h * 3 + 2, gcols=gq * 512, out=osel[:, :], Bout=B_osel)
                            if dwin is not None:
                                ld(dwin[h, :, gq * 512:(gq + 1) * 512], osel[:, :], [B_osel], [Buf()])
                            tt("gpsimd", obf[:, :], ocl[:], osel[:], ALU.add, [B_ocl, B_osel], [B_obf])
                            stD(oT[0, h * 64:(h + 1) * 64, gq * 512:(gq + 1) * 512], obf[:, :], [B_obf], [B_oT[0][gq]])
                        attn_group(gq, Qa[b], lambda gg, b=b: [BQ[b][gg]], 64,
                                   Kwin, lambda kb: [BKwin[kb // 4]], V1[1], BV[1], plan, fin_win)
                attn_flush()
        S.release(m1)
        S.release(m_h)
        if not _MIX_OUT:
            S.release(m0)
            return

        lng = S.sb([128, DM], F32)
        lnb = S.sb([128, DM], F32)
        B_ln = Buf()
        ld(lng[:], lng_d[l, 1], [], [B_ln])
        ld(lnb[:], lnb_d[l, 1], [], [B_ln])
        bg_t = S.sb([128, 24], F32)
        ld(bg_t[:], bgate_d[l], [], [B_ln])
        woutt = S.sb([128, 8, DM], BF16)
        B_wo = Buf()
        ld(woutt[:], wouts.rearrange("p (k c) -> p k c", c=DM), [B_wouts], [B_wo])
        xin2 = S.sb([128, 4, DM], F32)
        Bxin2 = Buf()
        ot = [S.sb([128, 4, 512], BF16) for _ in range(3)]
        B_ot = [Buf() for _ in range(3)]
        wgr = Ring([(S.sb([128, 8, 128], BF16), Buf()) for _ in range(3)])
        wbrr = Ring([(S.sb([128, 4, 128], BF16), Buf()) for _ in range(3)])
        gsr = Ring([(S.sb([128, 512], F32), Buf()) for _ in range(2)])
        macc = S.sb([128, 512], F32)
        B_macc = Buf()
        mT = S.sb([128, 8, 512], BF16)
        BmT = [Buf() for _ in range(8)]
        tmp = ln_tmp()
        wgs3 = wgs.rearrange("p (k c) -> p k c", c=3072)
        for tg in range(8):
            ld(xin2[:], src[tg * 512:(tg + 1) * 512, :].rearrange("(n p) d -> p n d", p=128), [Bsrc[tg]], [Bxin2])
            for x in range(3):
                ld(ot[x][:], oT[x, :, tg * 512:(tg + 1) * 512].rearrange("(c p) t -> p c t", p=128), [B_oT[x][tg]], [B_ot[x]])
            for dc in range(8):
                for x in range(3):
                    pump()
                    wg, Bwg = wgr.next()
                    ld(wg[:], wgs3[:, :, x * DM + dc * 128:x * DM + (dc + 1) * 128], [B_wgs], [Bwg])
                    wb, Bwb = wbrr.next()
                    ld(wb[:], wbrs[x].rearrange("p (c d) -> p c d", d=DM)[:, :, dc * 128:(dc + 1) * 128], [B_wbrs], [Bwb])
                    bg, Bbg = bank()
                    for kc in range(8):
                        mm(bg[:, :], wg[:, kc, :], hT[:, kc, tg * 512:(tg + 1) * 512], kc == 0, kc == 7, [Bwg, BhT[tg]], [Bbg])
                    bb, Bbb = bank()
                    for c in range(4):
                        mm(bb[:, :], wb[:, c, :], ot[x][:, c, :], c == 0, c == 3, [Bwb, B_ot[x]], [Bbb])
                    gs, Bgs = gsr.next()
                    act(gs[:], bg[:, :], AF.Sigmoid, [Bbg, B_ln], [Bgs], bias=bg_t[:, x * 8 + dc:x * 8 + dc + 1])
                    if x == 0:
                        tt("vector", macc[:], gs[:], bb[:, :], ALU.mult, [Bgs, Bbb], [B_macc])
                    elif x == 1:
                        tt("vector", gs[:], gs[:], bb[:, :], ALU.mult, [Bgs, Bbb], [Bgs])
                        tt("gpsimd", macc[:], macc[:], gs[:], ALU.add, [Bgs, B_macc], [B_macc])
                    else:
                        tt("vector", gs[:], gs[:], bb[:, :], ALU.mult, [Bgs, Bbb], [Bgs])
                        tt("gpsimd", mT[:, dc, :], macc[:], gs[:], ALU.add, [Bgs, B_macc], [BmT[dc]])
            for n in range(4):
                b0, b1 = banks[5], banks[6]
                for dc in range(8):
                    mm(b0[:, :], mT[:, dc, n * 128:(n + 1) * 128], woutt[:, dc, 0:512], dc == 0, dc == 7,
                       [BmT[dc], B_wo], [B_bank[5]])
                    mm(b1[:, :], mT[:, dc, n * 128:(n + 1) * 128], woutt[:, dc, 512:1024], dc == 0, dc == 7,
                       [BmT[dc], B_wo], [B_bank[6]])
                r0 = tg * 512 + n * 128
                ln_epilogue(xin2[:, n, :], b0, b1, B_bank[5], B_bank[6], Bxin2, lng, lnb, B_ln, tmp,
                            dst[r0:r0 + 128, :], Bdst[tg] if isinstance(Bdst, list) else Bdst,
                            None if dbg_ap is None else dbg_ap[r0:r0 + 128, :])
        S.release(m0)

    cur, Bcur = x_in, [Buf() for _ in range(8)]
    pp = 0
    prep_ffn(0, 0)
    for l in range(NL):
        for st in range(3):
            lastst = (l == NL - 1 and st == 2) or (_STOP_AFTER is not None and l * 3 + st == _STOP_AFTER - 1)
            if _STOP_AFTER is not None and l * 3 + st >= _STOP_AFTER:
                continue
            if lastst:
                dst, Bdst = y_out, B_y
            else:
                dst, Bdst = xs[pp], B_xs[pp]
            dbg_ap = dbg_d.get("l%ds%d" % (l, st))
            if st == 0:
                ffn_stage(l, 0, cur, Bcur, dst, Bdst, 0, dbg_ap)
            elif st == 1:
                mixer_stage(l, cur, Bcur, dst, Bdst, dbg_ap)
            else:
                ffn_stage(l, 1, cur, Bcur, dst, Bdst, 2, dbg_ap)
            cur, Bcur = dst, Bdst
            pp ^= 1
    S.finish()
    return nc


def _consts(rel_bias):
    c = {}
    c["ident"] = np.eye(128, dtype=np.float32)
    s = np.arange(S_LEN)
    c["onehot"] = (s[None, :] // 64 == np.arange(64)[:, None]).astype(np.float32)
    j = np.arange(128)[:, None]
    i = np.arange(128)[None, :]
    d0 = i - j
    d1 = 128 + i - j
    bk0 = _t5_bucket(d0)
    bk1 = _t5_bucket(d1)
    relT = np.ascontiguousarray(rel_bias.T)
    biasT = np.empty((16, 2, 128, 128), np.float32)
    for h in range(16):
        t0 = relT[h][bk0]
        t1 = relT[h][bk1]
        biasT[h, 0] = np.where(d0 >= 0, t0, np.float32(NEG))
        if h < 8:
            biasT[h, 1] = t1
        else:
            biasT[h, 1] = np.where(d1 < 128, t1, np.float32(NEG))
    c["biasT"] = biasT
    c["cfar"] = np.ascontiguousarray(np.broadcast_to(rel_bias[31][None, :], (128, 16))).astype(np.float32)
    ii = np.arange(128)[:, None]
    u = np.arange(512)[None, :]
    dist = ii - 16 * (u - 256) - 31
    bkc = _t5_bucket(dist)
    tc = np.empty((8, 128, 512), np.float32)
    for h in range(8):
        tc[h] = np.where(dist >= 0, relT[h][bkc], np.float32(NEG))
    c["tc"] = tc
    uu = np.arange(128)[None, :] - 64
    curb = (ii >= 64).astype(np.int64)
    ft = np.zeros((128, 128), np.float32)
    ft[np.broadcast_to(uu > curb, (128, 128))] = -100.0
    ft[np.broadcast_to((uu == curb) | (uu == curb - 1), (128, 128))] = 100.0
    c["ft"] = ft
    c["cm"] = np.where(d0 >= 0, 0.0, NEG).astype(np.float32)
    c["lt"] = np.where(i < j, 0.0, NEG).astype(np.float32)
    sel = np.zeros((24, 1536), np.float32)
    for r in range(24):
        sel[r, r * 64:(r + 1) * 64] = 1.0
    c["sel24"] = sel
    return c


def _layer_weights(inp, ls):
    L = len(ls)
    w = {}

    def stack(f):
        return np.ascontiguousarray(np.stack([f(l) for l in ls]))
    for i, (k1, k2) in enumerate((("ffn1_w1", "ffn1_w2"), ("ffn2_w1", "ffn2_w2"))):
        w["w1_%d" % i] = stack(lambda l: inp[k1][l].reshape(8, 128, 2, NFC, 128).transpose(3, 1, 0, 2, 4).reshape(NFC, 128, 2048))
        w["w2_%d" % i] = stack(lambda l: inp[k2][l].reshape(NFC, 128, DM).transpose(1, 0, 2).reshape(128, NFC * DM))
    def _win_layout(l):
        wp = inp["w_in"][l][:, WIN_PERM]
        blocks = []
        for name, (off, wd) in WIN_P.items():
            blocks.append(wp[:, off:off + wd].reshape(8, 128, wd).transpose(1, 0, 2).reshape(128, 8 * wd))
        return np.concatenate(blocks, axis=1)
    w["win"] = stack(_win_layout)
    w["wgate"] = stack(lambda l: inp["w_gate"][l].reshape(8, 128, 3072).transpose(1, 0, 2).reshape(128, 8 * 3072))
    w["bgate"] = stack(lambda l: inp["b_gate"][l].reshape(24, 128).T)
    w["wbr"] = stack(lambda l: np.stack([inp[k][l].reshape(4, 128, DM).transpose(1, 0, 2).reshape(128, 4 * DM)
                                         for k in ("w_br_a", "w_br_b", "w_br_c")]))
    w["wout"] = stack(lambda l: inp["w_out"][l].reshape(8, 128, DM).transpose(1, 0, 2).reshape(128, 8 * DM))
    w["cw1"] = stack(lambda l: np.stack([inp[k][l].reshape(32, 64, 256).transpose(1, 0, 2).reshape(64, 32 * 256)
                                         for k in ("cmp_k_w1", "cmp_v_w1")]))
    w["cw2"] = stack(lambda l: np.stack([inp[k][l].reshape(2, 128, 64).transpose(1, 0, 2).reshape(128, 128)
                                         for k in ("cmp_k_w2", "cmp_v_w2")]))
    w["pet"] = stack(lambda l: np.stack([inp[k][l].T for k in ("cmp_pe_k", "cmp_pe_v")]))
    w["lng"] = stack(lambda l: np.stack([np.broadcast_to(inp[k][l][None, :], (128, DM)) for k in ("ln1_g", "ln2_g", "ln3_g")]))
    w["lnb"] = stack(lambda l: np.stack([np.broadcast_to(inp[k][l][None, :], (128, DM)) for k in ("ln1_b", "ln2_b", "ln3_b")]))
    w["sinks"] = stack(lambda l: np.broadcast_to(inp["swa_sinks"][l][None, :], (128, 8)))
    w["bf"] = stack(lambda l: inp["fox_b_f"][l].reshape(8, 1))
    return {k: np.ascontiguousarray(v, dtype=np.float32) for k, v in w.items()}


_PROG = {}


def _get_prog(NL):
    if NL not in _PROG:
        _PROG[NL] = build_program(NL)
    return _PROG[NL]


def kernel(**inputs):
    inp = {k: np.asarray(v, dtype=np.float32) for k, v in inputs.items()}
    consts = _consts(inp["rel_bias"])
    x = inp["x"]
    nb = x.shape[0]
    NL = NLAYERS
    nc = _get_prog(NL)
    wl = _layer_weights(inp, list(range(NL)))
    in_maps = []
    for b in range(nb):
        m = {"x": np.ascontiguousarray(x[b])}
        m.update(wl)
        m.update(consts)
        in_maps.append(m)
    res = run_bass_kernel_spmd(nc, in_maps, core_ids=list(range(nb)))
    return np.stack([np.asarray(r["y"], dtype=np.float32) for r in res.results], axis=0)
```

```python
from contextlib import ExitStack
import numpy as np
import concourse.bass as bass
import concourse.mybir as mybir
from concourse.bass_utils import run_bass_kernel_spmd

F32 = mybir.dt.float32
BF16 = mybir.dt.bfloat16
AF = mybir.ActivationFunctionType
ALU = mybir.AluOpType
AX = mybir.AxisListType

ENGS = ["tensor", "vector", "scalar", "gpsimd", "sync"]
S_LEN = 4096
DM = 1024
FF = 2816
NFC = 22
DIN = 3616
NEG = -30000.0
ALPHA = 8.0 ** 0.25
LN_EPS = 1e-5
NLAYERS = 4
_STOP_AFTER = None
_MIX_STOP = 99
_MIX_OUT = True
_SUB = 99


class _Stop(Exception):
    pass


class Buf:
    __slots__ = ("w", "r", "excl")

    def __init__(self, excl=False):
        self.w = None
        self.r = {}
        self.excl = excl


class Sched:
    def __init__(self, nc, n_dma_sems=12, rot=20000):
        self.nc = nc
        self.prog = {e: [] for e in ENGS}
        self.seen = {e: {} for e in ENGS}
        self.cnt = {e: 0 for e in ENGS}
        self.epoch = {e: 0 for e in ENGS}
        self.rot = rot
        self.n_dma = n_dma_sems
        self.dma_uses = {}
        self.dma_rr = {e: 0 for e in ENGS}
        self.semkeys = []
        self.semset = set()
        self.stack = ExitStack()
        self.sb_off = 16512
        self.sb_id = 0

    def sb(self, shape, dtype):
        n = 1
        for s in shape[1:]:
            n *= s
        nbytes = n * (4 if dtype == F32 else 2)
        off = (self.sb_off + 63) // 64 * 64
        assert off + nbytes <= 229000, ("SBUF overflow", off, nbytes)
        self.sb_off = off + nbytes
        self.peak = max(getattr(self, 'peak', 0), self.sb_off)
        self.sb_id += 1
        return self.nc.alloc_sbuf_tensor_at("t%d" % self.sb_id, list(shape), dtype, offset=off)

    def mark(self):
        return self.sb_off

    def release(self, m):
        self.barrier()
        self.sb_off = m

    def _key(self, key):
        if key not in self.semset:
            self.semset.add(key)
            self.semkeys.append(key)
        return key

    def _collect(self, eng, reads, writes):
        deps = {}

        def add(tok):
            if tok is None:
                return
            k, v = tok
            if deps.get(k, 0) < v:
                deps[k] = v
        for b in reads:
            add(b.w)
            if b.excl:
                for k, v in b.r.items():
                    if k[0] != eng:
                        add((k, v))
        for b in writes:
            add(b.w)
            for k, v in b.r.items():
                add((k, v))
        waits = []
        seen = self.seen[eng]
        for k, v in deps.items():
            if eng == "tensor" and k[0] == "tensor":
                continue
            if seen.get(k, 0) >= v:
                continue
            seen[k] = v
            waits.append((k, v))
        return waits

    def _update(self, tok, reads, writes):
        k, v = tok
        for b in reads:
            if b.r.get(k, 0) < v:
                b.r[k] = v
        for b in writes:
            b.w = tok
            b.r = {}

    def op(self, eng, emit, reads=(), writes=()):
        waits = self._collect(eng, reads, writes)
        self.cnt[eng] += 1
        if self.cnt[eng] > self.rot:
            self.epoch[eng] += 1
            self.cnt[eng] = 1
        tok = (self._key((eng, self.epoch[eng])), self.cnt[eng])
        self.prog[eng].append((waits, emit, tok, 1))
        self._update(tok, reads, writes)
        return tok

    def dma(self, q, emit, reads=(), writes=()):
        i = self.dma_rr[q]
        self.dma_rr[q] = (i + 1) % self.n_dma
        key = self._key(("dma", q, i))
        k = self.dma_uses.get(key, 0) + 1
        self.dma_uses[key] = k
        waits = self._collect(q, reads, writes)
        if k > 1 and self.seen[q].get(key, 0) < 16 * (k - 1):
            self.seen[q][key] = 16 * (k - 1)
            waits.append((key, 16 * (k - 1)))
        tok = (key, 16 * k)
        self.prog[q].append((waits, emit, tok, 16))
        self._update(tok, reads, writes)
        return tok

    def _all_tokens(self):
        toks = []
        for key, k in self.dma_uses.items():
            toks.append((key, 16 * k))
        for e in ENGS:
            if self.cnt[e] > 0:
                toks.append(((e, self.epoch[e]), self.cnt[e]))
        return toks

    def barrier(self):
        toks = self._all_tokens()
        for e in ENGS:
            waits = []
            for k, v in toks:
                if self.seen[e].get(k, 0) < v:
                    self.seen[e][k] = v
                    waits.append((k, v))
            if waits:
                self.prog[e].append((waits, None, None, 0))

    def finish(self):
        nc = self.nc
        self.barrier()
        sems = {}
        for key in self.semkeys:
            nm = "s_" + "_".join(str(x) for x in key)
            sems[key] = self.stack.enter_context(nc.semaphore(nm))
        prog = self.prog
        with nc.Block() as block:
            def mk(ename):
                def body(eng):
                    for waits, emit, tok, inc in prog[ename]:
                        for k, v in waits:
                            eng.wait_ge(sems[k], v)
                        if emit is not None:
                            emit(eng).then_inc(sems[tok[0]], inc)
                return body
            block.tensor(mk("tensor"))
            block.vector(mk("vector"))
            block.scalar(mk("scalar"))
            block.gpsimd(mk("gpsimd"))
            block.sync(mk("sync"))
        self.stack.close()


class Ring:
    def __init__(self, items):
        self.items = items
        self.i = 0

    def next(self):
        it = self.items[self.i]
        self.i = (self.i + 1) % len(self.items)
        return it


def _win_passes():
    P = {}
    off = 0
    perm = []

    def add(name, cols):
        nonlocal off
        P[name] = (off, len(cols))
        perm.extend(cols)
        off += len(cols)
    r = lambda a, n: list(range(a, a + n))
    for h in range(8):
        add("fox_qk%d" % h, r(2072 + h * 64, 64) + r(2584 + h * 64, 64))
    add("fox_f", r(3608, 8))
    for h in range(8):
        add("fox_v%d" % h, r(3096 + h * 64, 64))
    for i in range(4):
        add("swa_q%d" % i, r(1304 + i * 128, 128))
    add("swa_k", r(1816, 128))
    add("swa_v", r(1944, 128))
    for i in range(4):
        add("nsa_q%d" % i, r(i * 128, 128))
    for g in range(2):
        add("nsa_cmp%d" % g, r(512 + g * 64, 64) + r(640 + g * 64, 64))
        add("nsa_kk%d" % g, r(768 + g * 64, 64) + r(1024 + g * 64, 64))
        add("nsa_v%d" % g, r(896 + g * 64, 64) + r(1152 + g * 64, 64))
    add("nsa_gate", r(1280, 24))
    assert off == DIN and sorted(perm) == list(range(DIN))
    return P, np.array(perm)


WIN_P, WIN_PERM = _win_passes()


def _t5_bucket(n):
    n = np.maximum(n, 0)
    lr = np.log(np.maximum(n, 1).astype(np.float32) / np.float32(16)) / np.float32(np.log(128 / 16))
    large = 16 + (lr.astype(np.float32) * np.float32(16)).astype(np.int32)
    return np.where(n < 16, n, np.minimum(large, 31))


def build_program(NL, dbg=None):
    nc = bass.Bass("TRN2", target_bir_lowering=False)
    S = Sched(nc)

    def din(name, shape, dt=F32):
        return nc.dram_tensor(name, list(shape), dt, kind="ExternalInput").ap()

    def dscr(name, shape, dt):
        return nc.dram_tensor(name, list(shape), dt, kind="Internal").ap()

    x_in = din("x", [S_LEN, DM])
    y_out = nc.dram_tensor("y", [S_LEN, DM], F32, kind="ExternalOutput").ap()
    w1_d = [din("w1_%d" % i, [NL, NFC, 128, 2048]) for i in range(2)]
    w2_d = [din("w2_%d" % i, [NL, 128, NFC * DM]) for i in range(2)]
    win_d = din("win", [NL, 128, 8 * DIN])
    wgate_d = din("wgate", [NL, 128, 8 * 3072])
    bgate_d = din("bgate", [NL, 128, 24])
    wbr_d = din("wbr", [NL, 3, 128, 4 * DM])
    wout_d = din("wout", [NL, 128, 8 * DM])
    cw1_d = din("cw1", [NL, 2, 64, 32 * 256])
    cw2_d = din("cw2", [NL, 2, 128, 128])
    pet_d = din("pet", [NL, 2, 64, 32])
    lng_d = din("lng", [NL, 3, 128, DM])
    lnb_d = din("lnb", [NL, 3, 128, DM])
    sinks_d = din("sinks", [NL, 128, 8])
    bf_d = din("bf", [NL, 8, 1])
    ident_d = din("ident", [128, 128])
    onehot_d = din("onehot", [64, S_LEN])
    biasT_d = din("biasT", [16, 2, 128, 128])
    cfar_d = din("cfar", [128, 16])
    tc_d = din("tc", [8, 128, 512])
    ft_d = din("ft", [128, 128])
    cm_d = din("cm", [128, 128])
    lt_d = din("lt", [128, 128])
    sel24_d = din("sel24", [24, 1536])

    xs = [dscr("xs0", [S_LEN, DM], F32), dscr("xs1", [S_LEN, DM], F32)]
    w1s = [dscr("w1s%d" % i, [NFC, 128, 2048], BF16) for i in range(2)]
    w2s = [dscr("w2s%d" % i, [128, NFC * DM], BF16) for i in range(2)]
    wins = dscr("wins", [128, 8 * DIN], BF16)
    wgs = dscr("wgs", [128, 8 * 3072], BF16)
    wbrs = dscr("wbrs", [3, 128, 4 * DM], BF16)
    wouts = dscr("wouts", [128, 8 * DM], BF16)
    cw1s = dscr("cw1s", [2, 64, 32 * 256], BF16)
    ohs = dscr("ohs", [64, S_LEN], BF16)
    if dbg and "oT" in dbg:
        oT = nc.dram_tensor("oT", [3, 512, S_LEN], BF16, kind="ExternalOutput").ap()
    else:
        oT = dscr("oT", [3, 512, S_LEN], BF16)
    if dbg and "oT" in dbg:
        ocmp = nc.dram_tensor("ocmp", [8, 64, S_LEN], F32, kind="ExternalOutput").ap()
        dsel = nc.dram_tensor("dsel", [8, 64, S_LEN], F32, kind="ExternalOutput").ap()
        dwin = nc.dram_tensor("dwin", [8, 64, S_LEN], F32, kind="ExternalOutput").ap()
    else:
        ocmp = dscr("ocmp", [8, 64, S_LEN], F32)
        dsel = dwin = None
    caug = dscr("caug", [8, 6, S_LEN], BF16)
    B_xs = [[Buf() for _ in range(8)] for _ in range(2)]
    B_w1s = [Buf(), Buf()]
    B_w2s = [Buf(), Buf()]
    B_wins, B_wgs, B_wbrs, B_wouts, B_cw1s, B_ohs = Buf(), Buf(), Buf(), Buf(), Buf(), Buf()
    B_oT = [[Buf() for _ in range(8)] for _ in range(3)]
    B_ocmp = [[Buf() for _ in range(8)] for _ in range(8)]
    B_caug = Buf()
    B_y = Buf()
    dbg_d = {}
    if dbg:
        for nm in dbg:
            if nm == "oT":
                continue
            dbg_d[nm] = nc.dram_tensor("dbg_" + nm, [S_LEN, DM], F32, kind="ExternalOutput").ap()

    banks = [S.stack.enter_context(nc.psum_tensor("bank%d" % i, [128, 512], F32)) for i in range(8)]
    B_bank = [Buf(excl=True) for _ in range(8)]
    ring5 = Ring([0, 1, 2, 3, 4])

    def bank():
        i = ring5.next()
        return banks[i], B_bank[i]

    cpy_rr = [0]

    def mm(out, lhsT, rhs, start, stop, reads, writes):
        S.op("tensor", lambda e: e.matmul(out, lhsT=lhsT, rhs=rhs, start=start, stop=stop),
             reads=reads, writes=writes)

    def tr(out, in_, ident, reads, writes):
        S.op("tensor", lambda e: e.transpose(out, in_, ident), reads=reads, writes=writes)

    def act(out, in_, func, reads, writes, bias=None, scale=None, accum=None):
        kw = {}
        if bias is not None:
            kw["bias"] = bias
        if scale is not None:
            kw["scale"] = scale
        if accum is not None:
            kw["accum_out"] = accum
        S.op("scalar", lambda e: e.activation(out=out, in_=in_, func=func, **kw), reads=reads, writes=writes)

    def vcopy(eng, out, in_, reads, writes):
        S.op(eng, lambda e: e.tensor_copy(out=out, in_=in_), reads=reads, writes=writes)

    def copy_any(out, in_, reads, writes, psum=True):
        cpy_rr[0] ^= 1
        if cpy_rr[0]:
            vcopy("vector", out, in_, reads, writes)
        else:
            act(out, in_, AF.Copy, reads, writes)

    def tt(eng, out, in0, in1, op, reads, writes):
        S.op(eng, lambda e: e.tensor_tensor(out=out, in0=in0, in1=in1, op=op), reads=reads, writes=writes)

    def ts(eng, out, in0, s1, s2, op0, op1, reads, writes):
        if op1 is None:
            S.op(eng, lambda e: e.tensor_scalar(out=out, in0=in0, scalar1=s1, scalar2=None, op0=op0),
                 reads=reads, writes=writes)
        else:
            S.op(eng, lambda e: e.tensor_scalar(out=out, in0=in0, scalar1=s1, scalar2=s2, op0=op0, op1=op1),
                 reads=reads, writes=writes)

    def stt(eng, out, in0, scalar, in1, op0, op1, reads, writes):
        S.op(eng, lambda e: e.scalar_tensor_tensor(out=out, in0=in0, scalar=scalar, in1=in1, op0=op0, op1=op1),
             reads=reads, writes=writes)

    def memset(eng, ap, val, writes):
        S.op(eng, lambda e: e.memset(ap, val), writes=writes)

    def ld(out, in_, reads, writes):
        S.dma("sync", lambda e: e.dma_start(out=out, in_=in_), reads=reads, writes=writes)

    ident_f = S.sb([128, 128], F32)
    ident_b = S.sb([128, 128], BF16)
    nsaT = S.sb([128, 8, 2, 128], BF16)
    swaT = S.sb([128, 8, 2, 128], BF16)
    cm_b = S.sb([128, 128], BF16)
    lt_b = S.sb([128, 128], BF16)
    tc_b = S.sb([128, 8, 512], BF16)
    ft_f = S.sb([128, 128], F32)
    sel24 = S.sb([24, 1536], BF16)
    cfar = S.sb([128, 16], F32)
    B_const = Buf()
    CH = 1024
    stg = Ring([(S.sb([128, CH], F32), S.sb([128, CH], BF16), Buf(), Buf()) for _ in range(3)])
    stage_f, stage_b, B_stage_f, B_stage_b = stg.items[0]
    cwstage = S.sb([128, 160], F32)
    B_cwstage = Buf()

    def stD(out, in_, reads, writes):
        S.dma("gpsimd", lambda e: e.dma_start(out=out, in_=in_), reads=reads, writes=writes)

    epst = S.sb([128, 2], F32)
    memset("vector", epst[:, 0:1], LN_EPS, [B_const])
    memset("vector", epst[:, 1:2], 1.0, [B_const])
    ld(ident_f[:], ident_d, [], [B_const])
    vcopy("vector", ident_b[:], ident_f[:], [B_const], [B_const])
    ld(cfar[:], cfar_d, [], [B_const])
    ld(ft_f[:], ft_d, [], [B_const])
    for (dst, src) in ((cm_b, cm_d), (lt_b, lt_d)):
        ld(stage_f[:, 0:128], src, [], [B_stage_f])
        vcopy("vector", dst[:], stage_f[:, 0:128], [B_stage_f], [B_const])
    for c in range(2):
        ld(stage_f[0:24, 0:768], sel24_d[:, c * 768:(c + 1) * 768], [], [B_stage_f])
        vcopy("vector", sel24[:, c * 768:(c + 1) * 768], stage_f[0:24, 0:768], [B_stage_f], [B_const])
    for h in range(8):
        ld(stage_f[:, 0:512], tc_d[h], [], [B_stage_f])
        vcopy("vector", tc_b[:, h, :], stage_f[:, 0:512], [B_stage_f], [B_const])
    for h in range(16):
        for k in range(2):
            ld(stage_f[:, 0:128], biasT_d[h, k], [], [B_stage_f])
            if h < 8:
                ts("vector", nsaT[:, h, k, :], stage_f[:, 0:128], cfar[:, h:h + 1], None, ALU.subtract, None,
                   [B_stage_f, B_const], [B_const])
            else:
                vcopy("vector", swaT[:, h - 8, k, :], stage_f[:, 0:128], [B_stage_f], [B_const])
    for c in range(4):
        ld(stage_f[0:64, :], onehot_d[:, c * 1024:(c + 1) * 1024], [], [B_stage_f])
        vcopy("vector", stage_b[0:64, :], stage_f[0:64, :], [B_stage_f], [B_stage_b])
        ld(ohs[:, c * 1024:(c + 1) * 1024], stage_b[0:64, :], [B_stage_b], [B_ohs])

    PQ = []
    inflight = []
    prep_rr = [0]

    def prep(dst, src, n, rows, Bdst, scale=None):
        for c0 in range(0, n, CH):
            PQ.append((dst, src, c0, min(CH, n - c0), rows, Bdst, scale))

    def pump(k=1):
        for _ in range(k):
            if inflight and (len(inflight) >= 2 or not PQ):
                (dst, src, c0, w, rows, Bdst, scale), (sf, sbt, Bsf, Bsb) = inflight.pop(0)
                prep_rr[0] ^= 1
                ceng = "vector" if prep_rr[0] else "gpsimd"
                if scale is None:
                    vcopy(ceng, sbt[0:rows, 0:w], sf[0:rows, 0:w], [Bsf], [Bsb])
                else:
                    ts(ceng, sbt[0:rows, 0:w], sf[0:rows, 0:w], scale, None, ALU.mult, None, [Bsf], [Bsb])
                stD(dst[:, c0:c0 + w], sbt[0:rows, 0:w], [Bsb], [Bdst])
            if PQ:
                task = PQ.pop(0)
                bufs = stg.next()
                (dst, src, c0, w, rows, Bdst, scale) = task
                stD(bufs[0][0:rows, 0:w], src[:, c0:c0 + w], [], [bufs[2]])
                inflight.append((task, bufs))

    def prep_flush():
        while PQ or inflight:
            pump()

    def prep_ffn(l, which):
        prep(w2s[which], w2_d[which][l], NFC * DM, 128, B_w2s[which], scale=0.5)
        for fc in range(NFC):
            prep(w1s[which][fc], w1_d[which][l, fc], 2048, 128, B_w1s[which])

    def prep_mixer(l):
        prep(wins, win_d[l], 8 * DIN, 128, B_wins)
        prep(wgs, wgate_d[l], 8 * 3072, 128, B_wgs)
        for x in range(3):
            prep(wbrs[x], wbr_d[l, x], 4 * DM, 128, B_wbrs)
        prep(wouts, wout_d[l], 8 * DM, 128, B_wouts)
        for kv in range(2):
            prep(cw1s[kv], cw1_d[l, kv], 32 * 256, 64, B_cw1s)

    base_mark = S.mark()

    def ln_epilogue(xrow, b0, b1, Bb0, Bb1, Bx, lng, lnb, B_ln, tmp, dst_ap, Bdst, dbg_ap=None):
        z, junk, xo, st, B_z, B_junk, B_xo, B_st = tmp
        stt("vector", z[:, 0:512], xrow[:, 0:512], ALPHA, b0[:, :], ALU.mult, ALU.add, [Bx, Bb0], [B_z])
        stt("vector", z[:, 512:1024], xrow[:, 512:1024], ALPHA, b1[:, :], ALU.mult, ALU.add, [Bx, Bb1], [B_z])
        act(junk[:], z[:], AF.Copy, [B_z], [B_junk, B_st], accum=st[:, 0:1])
        act(junk[:], z[:], AF.Square, [B_z], [B_junk, B_st], accum=st[:, 1:2])
        ts("vector", st[:, 2:4], st[:, 0:2], 1.0 / DM, None, ALU.mult, None, [B_st], [B_st])
        stt("vector", st[:, 4:5], st[:, 2:3], -1.0, st[:, 2:3], ALU.mult, ALU.mult, [B_st], [B_st])
        tt("vector", st[:, 5:6], st[:, 4:5], st[:, 3:4], ALU.add, [B_st], [B_st])
        act(st[:, 8:9], st[:, 5:6], AF.Sqrt, [B_st, B_const], [B_st], bias=epst[:, 0:1])
        S.op("vector", lambda e: e.reciprocal(out=st[:, 6:7], in_=st[:, 8:9]), reads=[B_st], writes=[B_st])
        stt("vector", st[:, 7:8], st[:, 2:3], -1.0, st[:, 6:7], ALU.mult, ALU.mult, [B_st], [B_st])
        act(xo[:], z[:], AF.Identity, [B_z, B_st], [B_xo], bias=st[:, 7:8], scale=st[:, 6:7])
        tt("gpsimd", xo[:], xo[:], lng[:], ALU.mult, [B_xo, B_ln], [B_xo])
        tt("gpsimd", xo[:], xo[:], lnb[:], ALU.add, [B_xo, B_ln], [B_xo])
        stD(dst_ap, xo[:], [B_xo], [Bdst])
        if dbg_ap is not None:
            ld(dbg_ap, xo[:], [B_xo], [Buf()])

    def ln_tmp():
        z = S.sb([128, DM], F32)
        junk = S.sb([128, DM], F32)
        xo = S.sb([128, DM], F32)
        st = S.sb([128, 10], F32)
        return (z, junk, xo, st, Buf(), Buf(), Buf(), Buf())

    def load_xT(xin, Bxin, xT, BxT):
        for kc in range(8):
            bk, Bbk = bank()
            for n in range(4):
                tr(bk[:, n * 128:(n + 1) * 128], xin[:, n, kc * 128:(kc + 1) * 128], ident_f[:], [Bxin, B_const], [Bbk])
            copy_any(xT[:, kc, :], bk[:, :], [Bbk], [BxT])

    def ffn_stage(l, which, src, Bsrc, dst, Bdst, lnidx, dbg_ap=None):
        m0 = S.mark()
        prep_flush()
        if which == 0:
            prep_mixer(l)
        elif l + 1 < NL:
            prep_ffn(l + 1, 0)
        lng = S.sb([128, DM], F32)
        lnb = S.sb([128, DM], F32)
        B_ln = Buf()
        ld(lng[:], lng_d[l, lnidx], [], [B_ln])
        ld(lnb[:], lnb_d[l, lnidx], [], [B_ln])
        xin_r = Ring([(S.sb([128, 4, DM], F32), Buf()) for _ in range(2)])
        xT = S.sb([128, 8, 512], BF16)
        BxT = Buf()
        w1p = Ring([(S.sb([128, 8, 256], BF16), Buf()) for _ in range(3)])
        hT = S.sb([128, NFC, 512], BF16)
        BhT = [Buf() for _ in range(NFC)]
        w2t = S.sb([128, NFC, DM], BF16)
        Bw2t = Buf()
        sgr = Ring([(S.sb([128, 512], F32), Buf()) for _ in range(2)])
        tmp = ln_tmp()
        ld(w2t[:], w2s[which].rearrange("p (f d) -> p f d", d=DM), [B_w2s[which]], [Bw2t])
        for tg in range(8):
            xin, Bxin = xin_r.next()
            ld(xin[:], src[tg * 512:(tg + 1) * 512, :].rearrange("(n p) d -> p n d", p=128), [Bsrc[tg]], [Bxin])
            load_xT(xin, Bxin, xT, BxT)
            for fc in range(NFC):
                pump()
                wp, Bwp = w1p.next()
                ld(wp[:], w1s[which][fc].rearrange("p (k c) -> p k c", c=256), [B_w1s[which]], [Bwp])
                bg, Bbg = bank()
                for kc in range(8):
                    mm(bg[:, :], wp[:, kc, 0:128], xT[:, kc, :], kc == 0, kc == 7, [Bwp, BxT], [Bbg])
                bu, Bbu = bank()
                for kc in range(8):
                    mm(bu[:, :], wp[:, kc, 128:256], xT[:, kc, :], kc == 0, kc == 7, [Bwp, BxT], [Bbu])
                sg, Bsg = sgr.next()
                act(sg[:], bg[:, :], AF.Silu, [Bbg], [Bsg])
                tt("vector", hT[:, fc, :], sg[:], bu[:, :], ALU.mult, [Bsg, Bbu], [BhT[fc]])
            for n in range(4):
                b0, b1 = banks[5], banks[6]
                for fc in range(NFC):
                    mm(b0[:, :], hT[:, fc, n * 128:(n + 1) * 128], w2t[:, fc, 0:512], fc == 0, fc == NFC - 1,
                       [BhT[fc], Bw2t], [B_bank[5]])
                    mm(b1[:, :], hT[:, fc, n * 128:(n + 1) * 128], w2t[:, fc, 512:1024], fc == 0, fc == NFC - 1,
                       [BhT[fc], Bw2t], [B_bank[6]])
                r0 = tg * 512 + n * 128
                ln_epilogue(xin[:, n, :], b0, b1, B_bank[5], B_bank[6], Bxin, lng, lnb, B_ln, tmp,
                            dst[r0:r0 + 128, :], Bdst[tg] if isinstance(Bdst, list) else Bdst,
                            None if dbg_ap is None else dbg_ap[r0:r0 + 128, :])
        S.release(m0)

    def mixer_stage(l, src, Bsrc, dst, Bdst, dbg_ap=None):
        m00 = S.mark()
        try:
            mixer_stage_(l, src, Bsrc, dst, Bdst, dbg_ap)
        except _Stop:
            S.release(m00)

    def chk(n):
        if _SUB <= n:
            raise _Stop()

    def mixer_stage_(l, src, Bsrc, dst, Bdst, dbg_ap=None):
        m0 = S.mark()
        prep_flush()
        prep_ffn(l, 1)
        wins3 = wins.rearrange("p (k c) -> p k c", c=DIN)

        hT = S.sb([128, 8, S_LEN], BF16)
        BhT = [Buf() for _ in range(8)]
        m_h = S.mark()
        xin = S.sb([128, 4, DM], F32)
        Bxin = Buf()
        xTt = S.sb([128, 8, 512], BF16)
        for tg in range(8):
            ld(xin[:], src[tg * 512:(tg + 1) * 512, :].rearrange("(n p) d -> p n d", p=128), [Bsrc[tg]], [Bxin])
            for kc in range(8):
                bk, Bbk = bank()
                for n in range(4):
                    tr(bk[:, n * 128:(n + 1) * 128], xin[:, n, kc * 128:(kc + 1) * 128], ident_f[:], [Bxin, B_const], [Bbk])
                copy_any(hT[:, kc, tg * 512:(tg + 1) * 512], bk[:, :], [Bbk], [BhT[tg]])
        S.release(m_h)

        Qa = [S.sb([128, S_LEN], BF16) for _ in range(2)]
        BQ = [[Buf() for _ in range(8)] for _ in range(2)]
        BQaug = [Buf(), Buf()]
        Ka = [S.sb([128, S_LEN], BF16) for _ in range(2)]
        BK = [[Buf() for _ in range(8)] for _ in range(2)]
        BKaug = [Buf(), Buf()]
        V1 = [S.sb([128, 32, 128], BF16) for _ in range(2)]
        BV = [[Buf() for _ in range(8)] for _ in range(2)]
        wpr = Ring([(S.sb([128, 1024], BF16), Buf()) for _ in range(2)])
        ptr = Ring([(S.sb([128, 512], BF16), Buf()) for _ in range(4)])
        den_g = S.sb([64, 512], F32)
        rec_g = S.sb([64, 512], F32)
        coef_g = S.sb([64, 512], F32)
        obf = S.sb([64, 512], BF16)
        B_den_g, B_rec_g, B_coef_g, B_obf = Buf(), Buf(), Buf(), Buf()
        gatesT = S.sb([24, S_LEN], BF16)
        B_gates = [Buf() for _ in range(8)]
        expsink = S.sb([128, 8], F32)
        B_sink = Buf()
        nbf = S.sb([8, 1], F32)
        B_nbf = Buf()
        if _MIX_STOP <= -1:
            S.release(m0)
            return
        for b in range(2):
            for tg in range(8):
                memset("gpsimd", V1[b][:, tg * 4:(tg + 1) * 4, 64:128], 1.0, [BV[b][tg]])
        ld(expsink[:], sinks_d[l], [], [B_sink])
        act(expsink[:], expsink[:], AF.Exp, [B_sink], [B_sink])
        ld(nbf[:], bf_d[l], [], [B_nbf])
        ts("vector", nbf[:], nbf[:], -1.0, None, ALU.mult, None, [B_nbf], [B_nbf])
        chk(1)

        def load_w(name, c0=0, wd=None):
            off, w = WIN_P[name]
            if wd is None:
                wd = w
            wpf, Bwp = wpr.next()
            wp = wpf[:, 0:8 * wd].rearrange("p (k c) -> p k c", c=wd)
            ld(wpf[:, 0:8 * wd], wins[:, 8 * off:8 * off + 8 * wd], [B_wins], [Bwp])
            return wp, Bwp, wd

        def proj_fm(name, evac, c0=0, wd=None, tgs=range(8)):
            wp, Bwp, wd = load_w(name, c0, wd)
            for tg in tgs:
                bk, Bbk = bank()
                for kc in range(8):
                    mm(bk[0:wd, :], wp[:, kc, 0:wd], hT[:, kc, tg * 512:(tg + 1) * 512], kc == 0, kc == 7,
                       [Bwp, BhT[tg]], [Bbk])
                evac(tg, bk, Bbk)

        def proj_tm(name, evac):
            wp, Bwp, wd = load_w(name)
            for t in range(32):
                bk, Bbk = bank()
                for kc in range(8):
                    mm(bk[:, 0:wd], hT[:, kc, t * 128:(t + 1) * 128], wp[:, kc, 0:wd], kc == 0, kc == 7,
                       [Bwp, BhT[t // 4]], [Bbk])
                evac(t, bk, Bbk)

        def ev_scaled(dst, r0, Bd, scale):
            def f(tg, bk, Bbk):
                if r0 == 0:
                    ts("vector", dst[0:64, tg * 512:(tg + 1) * 512], bk[0:64, :], scale, None, ALU.mult, None,
                       [Bbk], [Bd[tg]])
                else:
                    act(dst[0:64, tg * 512:(tg + 1) * 512], bk[r0:r0 + 64, :], AF.Copy, [Bbk], [Bd[tg]], scale=scale)
            return f

        def ev_plain(dst, r0, Bd):
            def f(tg, bk, Bbk):
                vcopy("vector", dst[0:64, tg * 512:(tg + 1) * 512], bk[r0:r0 + 64, :], [Bbk], [Bd[tg]])
            return f

        def ev_multi(*fs):
            def f(tg, bk, Bbk):
                for g in fs:
                    g(tg, bk, Bbk)
            return f

        AJ = []
        job_ctr = [0]
        LOOK = 3

        def attn_group(g, Q, Qreads, kdim, K, Kreads, Vt, BVt, plan, fin):
            AJ.append((g, Q, Qreads, kdim, K, Kreads, Vt, BVt, plan, fin))

        def attn_flush():
            pend = []

            def emit_qk(job, stt_, item):
                (g, Q, Qreads, kdim, K, Kreads, Vt, BVt, plan, fin) = job
                (kb, n_lo, n_hi, bias) = item
                bk, Bbk = bank()
                c0, c1 = n_lo * 128, n_hi * 128
                nb = len(bias)
                mm(bk[:, c0:c1], K[0:kdim, kb * 128:(kb + 1) * 128], Q[0:kdim, g * 512 + c0:g * 512 + c1],
                   True, nb == 0, Kreads(kb) + Qreads(g), [Bbk])
                for bi, (n, bap) in enumerate(sorted(bias.items())):
                    mm(bk[:, n * 128:(n + 1) * 128], ident_b[:], bap, False, bi == nb - 1, [B_const], [Bbk])
                return (job, stt_, item, bk, Bbk)

            def emit_pv(rec_):
                job, stt_, (kb, n_lo, n_hi, bias), bk, Bbk = rec_
                Vt, BVt = job[6], job[7]
                ob, Bob = stt_["ob"], stt_["Bob"]
                ntouch, total = stt_["ntouch"], stt_["total"]
                c0, c1 = n_lo * 128, n_hi * 128
                pt, Bpt = ptr.next()
                act(pt[:, c0:c1], bk[:, c0:c1], AF.Exp, [Bbk], [Bpt])
                n = n_lo
                while n < n_hi:
                    last = ntouch[n] == total[n] - 1
                    n2 = n + 1
                    while n2 < n_hi and (ntouch[n2] == total[n2] - 1) == last:
                        n2 += 1
                    S.op("tensor", lambda e, o_=ob[:, n * 128:n2 * 128], l_=Vt[:, kb, :], r_=pt[:, n * 128:n2 * 128],
                         st_=(stt_["npv"] == 0), sp_=last: e.matmul(o_, lhsT=l_, rhs=r_, start=st_, stop=sp_, skip_group_check=True),
                         reads=[BVt[kb // 4], Bpt], writes=[Bob])
                    stt_["npv"] += 1
                    for k in range(n, n2):
                        ntouch[k] += 1
                    n = n2
                stt_["left"] -= 1
                if stt_["left"] == 0:
                    stt_["fin"](ob, Bob)

            for job in AJ:
                plan, fin = job[8], job[9]
                jb = job_ctr[0]
                job_ctr[0] += 1
                stt_ = dict(ob=banks[5 + jb % 2], Bob=B_bank[5 + jb % 2], ntouch=[0] * 4, total=[0] * 4, npv=0,
                            left=len(plan), fin=fin)
                for (kb, n_lo, n_hi, bias) in plan:
                    for n in range(n_lo, n_hi):
                        stt_["total"][n] += 1
                pump()
                for item in plan:
                    pend.append(emit_qk(job, stt_, item))
                    if len(pend) > LOOK:
                        emit_pv(pend.pop(0))
            while pend:
                emit_pv(pend.pop(0))
            del AJ[:]

        def finalize(ob, Bob, ncols, gate_row=None, gcols=None, sink_h=None, clampden=False, out=None, Bout=None,
                     tmp=None):
            if tmp is None:
                tmp = (den_g, rec_g, coef_g, B_den_g, B_rec_g, B_coef_g)
            den, rec, coef, B_den, B_rec, B_coef = tmp
            if ncols >= 512:
                vcopy("vector", den[:, 0:ncols], ob[64:128, 0:ncols], [Bob], [B_den])
            else:
                act(den[:, 0:ncols], ob[64:128, 0:ncols], AF.Copy, [Bob], [B_den])
            if sink_h is not None:
                ts("vector", den[:, 0:ncols], den[:, 0:ncols], expsink[0:64, sink_h:sink_h + 1], None, ALU.add, None,
                   [B_den, B_sink], [B_den])
            if clampden:
                ts("vector", den[:, 0:ncols], den[:, 0:ncols], 1e-30, None, ALU.max, None, [B_den], [B_den])
            S.op("vector", lambda e: e.reciprocal(out=rec[:, 0:ncols], in_=den[:, 0:ncols]), reads=[B_den], writes=[B_rec])
            src_coef = rec
            Bsc = B_rec
            if gate_row is not None:
                gb, Bgb = banks[7], B_bank[7]
                mm(gb[0:64, 0:ncols], sel24[0:24, gate_row * 64:(gate_row + 1) * 64],
                   gatesT[0:24, gcols:gcols + ncols], True, True, [B_const, B_gates[gcols // 512]], [Bgb])
                tt("vector", coef[:, 0:ncols], rec[:, 0:ncols], gb[0:64, 0:ncols], ALU.mult, [B_rec, Bgb], [B_coef])
                src_coef = coef
                Bsc = B_coef
            tt("vector", out, ob[0:64, 0:ncols], src_coef[:, 0:ncols], ALU.mult, [Bob, Bsc], [Bout])

        def causal_plan(g, diag_bias, sub_bias=None):
            plan = []
            for kb in range(4 * g + 4):
                m = kb - 4 * g
                if m < 0:
                    b = {}
                    if sub_bias is not None and m == -1:
                        b[0] = sub_bias
                    plan.append((kb, 0, 4, b))
                else:
                    b = {m: diag_bias}
                    if sub_bias is not None and m + 1 < 4:
                        b[m + 1] = sub_bias
                    plan.append((kb, m, 4, b))
            return plan

        m1 = S.mark()
        spc = S.sb([8, 512], F32)
        Cc = S.sb([8, 512], F32)
        r1 = S.sb([8, 512], F32)
        r2 = S.sb([8, 512], F32)
        cb = [S.sb([8, 512], BF16) for _ in range(6)]
        carry = S.sb([8, 1], F32)
        B_f = Buf()
        memset("vector", carry[:], 0.0, [B_f])

        def f_evac(tg, bk, Bbk):
            act(spc[:], bk[0:8, :], AF.Exp, [Bbk, B_nbf], [B_f], bias=nbf[:, 0:1], scale=-1.0)
            act(spc[:], spc[:], AF.Ln, [B_f, B_const], [B_f], bias=epst[0:8, 1:2])
            S.op("vector", lambda e: e.tensor_tensor_scan(out=Cc[:], data0=spc[:], data1=spc[:], initial=carry[:, 0:1],
                                                           op0=ALU.add, op1=ALU.max), reads=[B_f], writes=[B_f])
            vcopy("vector", carry[:], Cc[:, 511:512], [B_f], [B_f])
            vcopy("vector", cb[3][:], Cc[:], [B_f], [B_f])
            tt("vector", r1[:], Cc[:], cb[3][:], ALU.subtract, [B_f], [B_f])
            vcopy("vector", cb[4][:], r1[:], [B_f], [B_f])
            tt("vector", r2[:], r1[:], cb[4][:], ALU.subtract, [B_f], [B_f])
            vcopy("vector", cb[5][:], r2[:], [B_f], [B_f])
            for j in range(3):
                ts("vector", cb[j][:], cb[3 + j][:], -1.0, None, ALU.mult, None, [B_f], [B_f])
            for j in range(6):
                ld(caug[:, j, tg * 512:(tg + 1) * 512], cb[j][:], [B_f], [B_caug])
        if _MIX_STOP >= 2:
            proj_fm("fox_f", f_evac)
        for b in range(2):
            memset("vector", Qa[b][64:70, :], 1.0, [BQaug[b]])
            memset("vector", Ka[b][64:70, :], 1.0, [BKaug[b]])
        chk(2)
        for h in range(8 if _MIX_STOP >= 3 else 0):
            b = h % 2
            proj_fm("fox_qk%d" % h, ev_multi(ev_scaled(Qa[b], 0, BQ[b], 0.125), ev_plain(Ka[b], 64, BK[b])))
            ld(Qa[b][64:67, :], caug[h, 0:3, :], [B_caug], [BQaug[b]])
            ld(Ka[b][67:70, :], caug[h, 3:6, :], [B_caug], [BKaug[b]])

            def v_evac(t, bk, Bbk, b=b):
                vcopy("vector", V1[b][:, t, 0:64], bk[:, 0:64], [Bbk], [BV[b][t // 4]])
            proj_tm("fox_v%d" % h, v_evac)
            for g in range(8):
                def fin(ob, Bob, g=g, h=h):
                    finalize(ob, Bob, 512, out=obf[:, :], Bout=B_obf)
                    stD(oT[2, h * 64:(h + 1) * 64, g * 512:(g + 1) * 512], obf[:, :], [B_obf], [B_oT[2][g]])
                attn_group(g, Qa[b], lambda gg, b=b: [BQ[b][gg], BQaug[b]], 70,
                           Ka[b], lambda kb, b=b: [BK[b][kb // 4], BKaug[b]], V1[b], BV[b],
                           causal_plan(g, cm_b[:]), fin)
            attn_flush()
        S.release(m1)

        m1 = S.mark()
        proj_fm("swa_k", ev_multi(ev_plain(Ka[0], 0, BK[0]), ev_plain(Ka[1], 64, BK[1])))

        def sv_evac(t, bk, Bbk):
            vcopy("vector", V1[0][:, t, 0:64], bk[:, 0:64], [Bbk], [BV[0][t // 4]])
            act(V1[1][:, t, 0:64], bk[:, 64:128], AF.Copy, [Bbk], [BV[1][t // 4]])
        proj_tm("swa_v", sv_evac)
        chk(3)
        for i in range(4 if _MIX_STOP >= 4 else 0):
            proj_fm("swa_q%d" % i, ev_multi(ev_scaled(Qa[0], 0, BQ[0], 0.125), ev_scaled(Qa[1], 64, BQ[1], 0.125)))
            for b in range(2):
                h = 2 * i + b
                gk = h // 4
                for g in range(8):
                    plan = []
                    for m in range(-1, 4):
                        kb = 4 * g + m
                        if kb < 0:
                            continue
                        n_lo, n_hi = max(0, m), min(4, m + 2)
                        bias = {}
                        for n in range(n_lo, n_hi):
                            bias[n] = swaT[:, h, n - m, :]
                        plan.append((kb, n_lo, n_hi, bias))

                    def fin(ob, Bob, g=g, h=h):
                        finalize(ob, Bob, 512, sink_h=h, out=obf[:, :], Bout=B_obf)
                        stD(oT[1, h * 64:(h + 1) * 64, g * 512:(g + 1) * 512], obf[:, :], [B_obf], [B_oT[1][g]])
                    attn_group(g, Qa[b], lambda gg, b=b: [BQ[b][gg]], 64,
                               Ka[gk], lambda kb, gk=gk: [BK[gk][kb // 4]], V1[gk], BV[gk], plan, fin)
            attn_flush()
        S.release(m1)

        m1 = S.mark()
        Ksel = Ka[0]
        BKsel = BK[0]
        B_oh = BKaug[0]
        Kwin = Ka[1]
        BKwin = BK[1]
        kcT = S.sb([64, 256], BF16)
        B_kcT = Buf()
        vc1 = S.sb([128, 2, 128], BF16)
        B_vc1 = Buf()
        m_cmp = S.mark()
        cwc = Ring([(S.sb([64, 8, 256], BF16), Buf()) for _ in range(2)])
        cw2t = S.sb([128, 2, 64], BF16)
        pet = S.sb([64, 32], BF16)
        B_cw2 = Buf()
        bvec = S.sb([128, 2], F32)
        B_bvec = Buf()
        gx = [S.sb([128, 256], F32) for _ in range(4)]
        B_gx = Buf()
        hid = [S.sb([128, 256], BF16) for _ in range(2)]
        B_hid = Buf()
        cmp_end = S.sb_off
        S.sb_off = m_cmp
        qt2 = [[S.sb([64, 128], BF16) for _ in range(4)] for _ in range(2)]
        B_qt2 = [[Buf() for _ in range(4)] for _ in range(2)]
        s4 = [S.sb([128, 256], F32) for _ in range(4)]
        e4 = [S.sb([128, 256], F32) for _ in range(4)]
        rs4 = [S.sb([128, 4], F32) for _ in range(4)]
        eT4 = [S.sb([128, 2, 128], BF16) for _ in range(4)]
        B_s4 = [Buf() for _ in range(4)]
        B_e4 = [Buf() for _ in range(4)]
        B_rs4 = [Buf() for _ in range(4)]
        B_eT4 = [Buf() for _ in range(4)]
        pacc2 = [S.sb([128, 260], F32) for _ in range(2)]
        B_pacc2 = [Buf(), Buf()]
        impt = S.sb([128, 64], F32)
        score = S.sb([128, 64], F32)
        top8 = S.sb([128, 8], F32)
        Mt = S.sb([128, 128], F32)
        ocst4 = [S.sb([64, 2, 128], F32) for _ in range(4)]
        B_ocst4 = [Buf() for _ in range(4)]
        fin_tmp = [(S.sb([64, 128], F32), S.sb([64, 128], F32), S.sb([64, 128], F32), Buf(), Buf(), Buf()) for _ in range(2)]
        B_imp, B_Mt = Buf(), Buf()
        S.sb_off = max(S.sb_off, cmp_end)
        osel = S.sb([64, 512], F32)
        B_osel = Buf()
        ocl = S.sb([64, 512], F32)
        B_ocl = Buf()
        wq2 = [S.sb([128, 8, 128], BF16) for _ in range(2)]
        B_wq2 = Buf()

        ld(Ksel[64:128, :], ohs, [B_ohs], [B_oh])
        memset("vector", vc1[:, :, 64:128], 1.0, [B_vc1])

        def gate_evac(tg, bk, Bbk):
            act(gatesT[0:24, tg * 512:(tg + 1) * 512], bk[0:24, :], AF.Sigmoid, [Bbk], [B_gates[tg]])
        chk(4)
        proj_fm("nsa_gate", gate_evac)
        chk(5)

        for g in range(2 if _MIX_STOP >= 5 else 0):
            S.barrier()
            proj_fm("nsa_cmp%d" % g, ev_multi(ev_plain(Ka[0], 0, BK[0]), ev_plain(Ka[1], 64, BK[1])))
            for kv in range(2):
                ld(cwstage[:, 0:128], cw2_d[l, kv], [], [B_cwstage])
                vcopy("vector", cw2t[:, :, :], cwstage[:, 0:128].rearrange("p (a b) -> p a b", b=64), [B_cwstage], [B_cw2])
                ld(cwstage[0:64, 128:160], pet_d[l, kv], [], [B_cwstage])
                vcopy("vector", pet[:], cwstage[0:64, 128:160], [B_cwstage], [B_cw2])
                bh = [bank(), bank()]
                bb = [bank(), bank()]
                src_t = Ka[kv]
                for c in range(4):
                    cw, Bcw = cwc.next()
                    ld(cw[:], cw1s[kv].rearrange("d (p c) -> d p c", c=256)[:, c * 8:(c + 1) * 8, :], [B_cw1s], [Bcw])
                    for pp in range(8):
                        p = c * 8 + pp
                        for hc in range(2):
                            mm(bh[hc][0][:, 0:255], cw[:, pp, hc * 128:(hc + 1) * 128],
                               src_t[0:64, p:p + 16 * 254 + 1:16], p == 0, p == 31,
                               [Bcw] + BK[kv], [bh[hc][1]])
                            mm(bb[hc][0][:, 0:1], cw[:, pp, hc * 128:(hc + 1) * 128], pet[:, p:p + 1], p == 0, p == 31,
                               [Bcw, B_cw2], [bb[hc][1]])
                for hc in range(2):
                    vcopy("vector", bvec[:, hc:hc + 1], bb[hc][0][:, 0:1], [bb[hc][1]], [B_bvec])
                    x1, sq, u, sg = gx
                    act(x1[:, 0:255], bh[hc][0][:, 0:255], AF.Identity, [bh[hc][1], B_bvec], [B_gx], bias=bvec[:, hc:hc + 1])
                    tt("vector", sq[:, 0:255], x1[:, 0:255], x1[:, 0:255], ALU.mult, [B_gx], [B_gx])
                    ts("vector", sq[:, 0:255], sq[:, 0:255], 0.044715, 1.0, ALU.mult, ALU.add, [B_gx], [B_gx])
                    tt("vector", u[:, 0:255], sq[:, 0:255], x1[:, 0:255], ALU.mult, [B_gx], [B_gx])
                    act(sg[:, 0:255], u[:, 0:255], AF.Sigmoid, [B_gx], [B_gx], scale=1.5957691216057308)
                    tt("vector", hid[hc][:, 0:255], x1[:, 0:255], sg[:, 0:255], ALU.mult, [B_gx], [B_hid])
                if kv == 0:
                    bk, Bbk = bank()
                    for hc in range(2):
                        mm(bk[0:64, 0:255], cw2t[:, hc, :], hid[hc][:, 0:255], hc == 0, hc == 1, [B_cw2, B_hid], [Bbk])
                    vcopy("vector", kcT[:, 0:255], bk[0:64, 0:255], [Bbk], [B_kcT])
                else:
                    for c in range(2):
                        nn = 128 if c == 0 else 127
                        bk, Bbk = bank()
                        for hc in range(2):
                            mm(bk[0:nn, 0:64], hid[hc][:, c * 128:c * 128 + nn], cw2t[:, hc, :], hc == 0, hc == 1,
                               [B_cw2, B_hid], [Bbk])
                        vcopy("vector", vc1[0:nn, c, 0:64], bk[0:nn, 0:64], [Bbk], [B_vc1])
            proj_fm("nsa_kk%d" % g, ev_multi(ev_plain(Ksel, 0, BKsel), ev_plain(Kwin, 64, BKwin)))

            def nv_evac(t, bk, Bbk):
                vcopy("vector", V1[0][:, t, 0:64], bk[:, 0:64], [Bbk], [BV[0][t // 4]])
                act(V1[1][:, t, 0:64], bk[:, 64:128], AF.Copy, [Bbk], [BV[1][t // 4]])
            proj_tm("nsa_v%d" % g, nv_evac)

            for i in range(2):
                off, w = WIN_P["nsa_q%d" % (2 * g + i)]
                ld(wq2[i][:], wins[:, 8 * off:8 * off + 1024].rearrange("p (k c) -> p k c", c=128), [B_wins], [B_wq2])
            S.barrier()
            memset("vector", Mt[:, 0:64], 0.0, [B_Mt])

            def q_stage(qb):
                par = qb % 2
                for i in range(2):
                    bk, Bbk = bank()
                    for kc in range(8):
                        mm(bk[:, 0:128], wq2[i][:, kc, :], hT[:, kc, qb * 128:(qb + 1) * 128], kc == 0, kc == 7,
                           [B_wq2, BhT[qb // 4]], [Bbk])
                    act(qt2[par][2 * i][:], bk[0:64, 0:128], AF.Copy, [Bbk], [B_qt2[par][2 * i]], scale=0.125)
                    act(qt2[par][2 * i + 1][:], bk[64:128, 0:128], AF.Copy, [Bbk], [B_qt2[par][2 * i + 1]], scale=0.125)

            q_stage(0)
            for qb in range(32):
                par = qb % 2
                tg = qb // 4
                ncols = min(255, 8 * qb + 7)
                nt = 1 if ncols <= 128 else 2
                pacc, B_pacc = pacc2[par], B_pacc2[par]
                memset("gpsimd", pacc[:], 0.0, [B_pacc])
                sbk = []
                for hp in range(4):
                    bk, Bbk = bank()
                    mm(bk[:, 0:ncols], qt2[par][hp][:], kcT[:, 0:ncols], True, True, [B_qt2[par][hp], B_kcT], [Bbk])
                    sbk.append((bk, Bbk))
                for hp in range(4):
                    h = 4 * g + hp
                    bk, Bbk = sbk[hp]
                    tt("vector", s4[hp][:, 0:ncols], bk[:, 0:ncols], tc_b[:, h, 256 - 8 * qb:256 - 8 * qb + ncols], ALU.add,
                       [Bbk, B_const], [B_s4[hp]])
                    act(e4[hp][:, 0:ncols], s4[hp][:, 0:ncols], AF.Exp, [B_s4[hp]], [B_e4[hp], B_rs4[hp]],
                        accum=rs4[hp][:, 0:1])
                if qb + 1 < 32:
                    q_stage(qb + 1)
                for hp in range(4):
                    rs = rs4[hp]
                    ts("vector", rs[:, 1:2], rs[:, 0:1], 1e-30, None, ALU.max, None, [B_rs4[hp]], [B_rs4[hp]])
                    S.op("vector", lambda e, rs=rs: e.reciprocal(out=rs[:, 2:3], in_=rs[:, 1:2]),
                         reads=[B_rs4[hp]], writes=[B_rs4[hp]])
                    if hp == 0:
                        ts("vector", pacc[:, 1:1 + ncols], e4[hp][:, 0:ncols], rs[:, 2:3], None, ALU.mult, None,
                           [B_e4[hp], B_rs4[hp]], [B_pacc])
                    else:
                        stt("vector", pacc[:, 1:1 + ncols], e4[hp][:, 0:ncols], rs[:, 2:3], pacc[:, 1:1 + ncols],
                            ALU.mult, ALU.add, [B_e4[hp], B_rs4[hp], B_pacc], [B_pacc])
                for hp in range(4):
                    h = 4 * g + hp
                    ob, Bob = banks[5 + (hp % 2)], B_bank[5 + (hp % 2)]
                    bk2, Bbk2 = bank()
                    for c in range(nt):
                        nn = min(128, ncols - c * 128)
                        tr(bk2[0:nn, c * 128:(c + 1) * 128], e4[hp][:, c * 128:c * 128 + nn], ident_f[:],
                           [B_e4[hp], B_const], [Bbk2])
                    for c in range(nt):
                        nn = min(128, ncols - c * 128)
                        act(eT4[hp][0:nn, c, :], bk2[0:nn, c * 128:(c + 1) * 128], AF.Copy, [Bbk2], [B_eT4[hp]])
                    for c in range(nt):
                        nn = min(128, ncols - c * 128)
                        mm(ob[:, 0:128], vc1[0:nn, c, :], eT4[hp][0:nn, c, :], c == 0, c == nt - 1,
                           [B_vc1, B_eT4[hp]], [Bob])
                    ocst = ocst4[hp]
                    finalize(ob, Bob, 128, gate_row=h * 3 + 0, gcols=qb * 128, clampden=True,
                             out=ocst[:, qb % 2, :], Bout=B_ocst4[hp], tmp=fin_tmp[hp % 2])
                    if qb % 2 == 1:
                        stD(ocmp[h, :, (qb - 1) * 128:(qb + 1) * 128], ocst[:, :, :].rearrange("p a b -> p (a b)"),
                            [B_ocst4[hp]], [B_ocmp[h][tg]])
                S.op("vector", lambda e, pacc=pacc: e.tensor_reduce(out=impt[:], in_=pacc[:, 0:256].rearrange("p (j m) -> p j m", m=4),
                                                                    axis=AX.X, op=ALU.add), reads=[B_pacc], writes=[B_imp])
                tt("vector", impt[:], impt[:], pacc[:, 4:260:4], ALU.add, [B_imp, B_pacc], [B_imp])
                tt("vector", score[:], impt[:], ft_f[:, 64 - 2 * qb:128 - 2 * qb], ALU.add, [B_imp, B_const], [B_imp])
                ts("vector", score[:, 0:1], score[:, 0:1], 100.0, None, ALU.add, None, [B_imp], [B_imp])
                S.op("vector", lambda e: e.max(out=top8[:], in_=score[:]), reads=[B_imp], writes=[B_imp])
                ts("vector", Mt[:, 64:128], score[:], top8[:, 7:8], None, ALU.is_ge, None, [B_imp, B_Mt], [B_Mt])
                ts("vector", Mt[:, 64:128], Mt[:, 64:128], -NEG, NEG, ALU.mult, ALU.add, [B_Mt], [B_Mt])
                bk, Bbk = bank()
                tr(bk[:, 0:128], Mt[:], ident_f[:], [B_Mt, B_const], [Bbk])
                vcopy("vector", Qa[0][64:128, qb * 128:(qb + 1) * 128], bk[64:128, 0:128], [Bbk], [BQaug[0]])
                act(Qa[1][64:128, qb * 128:(qb + 1) * 128], bk[64:128, 0:128], AF.Copy, [Bbk], [BQaug[1]])

            for i in range(2):
                proj_fm("nsa_q%d" % (2 * g + i),
                        ev_multi(ev_scaled(Qa[0], 0, BQ[0], 0.125), ev_scaled(Qa[1], 64, BQ[1], 0.125)))
                for b in range(2):
                    h = 4 * g + 2 * i + b
                    for gq in range(8):
                        def fin_sel(ob, Bob, gq=gq, h=h):
                            ld(ocl[:], ocmp[h, :, gq * 512:(gq + 1) * 512], [B_ocmp[h][gq]], [B_ocl])
                            finalize(ob, Bob, 512, gate_row=h * 3 + 1, gcols=gq * 512, out=osel[:, :], Bout=B_osel)
                            if dsel is not None:
                                ld(dsel[h, :, gq * 512:(gq + 1) * 512], osel[:, :], [B_osel], [Buf()])
                            tt("gpsimd", ocl[:], ocl[:], osel[:], ALU.add, [B_ocl, B_osel], [B_ocl])
                        attn_group(gq, Qa[b], lambda gg, b=b: [BQ[b][gg], BQaug[b]], 128,
                                   Ksel, lambda kb: [BKsel[kb // 4], B_oh], V1[0], BV[0],
                                   causal_plan(gq, nsaT[:, h, 0, :], nsaT[:, h, 1, :]), fin_sel)
                        plan = []
                        for m in range(-4, 4):
                            kb = 4 * gq + m
                            if kb < 0:
                                continue
                            n_lo, n_hi = max(0, m), min(4, m + 5)
                            bias = {}
                            for n in range(n_lo, n_hi):
                                if n - m == 0:
                                    bias[n] = nsaT[:, h, 0, :]
                                elif n - m == 1:
                                    bias[n] = nsaT[:, h, 1, :]
                                elif n - m == 4:
                                    bias[n] = lt_b[:]
                            plan.append((kb, n_lo, n_hi, bias))

                        def fin_win(ob, Bob, gq=gq, h=h):
                            finalize(ob, Bob, 512, gate_row=h * 3 + 2, gcols=gq * 512, out=osel[:, :], Bout=B_osel)
                            if dwin is not None:
                                ld(dwin[h, :, gq * 512:(gq + 1) * 512], osel[:, :], [B_osel], [Buf()])
                            tt("gpsimd", obf[:, :], ocl[:], osel[:], ALU.add, [B_ocl, B_osel], [B_obf])
                            stD(oT[0, h * 64:(h + 1) * 64, gq * 512:(gq + 1) * 512], obf[:, :], [B_obf], [B_oT[0][gq]])
                        attn_group(gq, Qa[b], lambda gg, b=b: [BQ[b][gg]], 64,
                                   Kwin, lambda kb: [BKwin[kb // 4]], V1[1], BV[1], plan, fin_win)
                attn_flush()
        S.release(m1)
        S.release(m_h)
        if not _MIX_OUT:
            S.release(m0)
            return

        lng = S.sb([128, DM], F32)
        lnb = S.sb([128, DM], F32)
        B_ln = Buf()
        ld(lng[:], lng_d[l, 1], [], [B_ln])
        ld(lnb[:], lnb_d[l, 1], [], [B_ln])
        bg_t = S.sb([128, 24], F32)
        ld(bg_t[:], bgate_d[l], [], [B_ln])
        woutt = S.sb([128, 8, DM], BF16)
        B_wo = Buf()
        ld(woutt[:], wouts.rearrange("p (k c) -> p k c", c=DM), [B_wouts], [B_wo])
        xin2 = S.sb([128, 4, DM], F32)
        Bxin2 = Buf()
        ot = [S.sb([128, 4, 512], BF16) for _ in range(3)]
        B_ot = [Buf() for _ in range(3)]
        wgr = Ring([(S.sb([128, 8, 128], BF16), Buf()) for _ in range(3)])
        wbrr = Ring([(S.sb([128, 4, 128], BF16), Buf()) for _ in range(3)])
        gsr = Ring([(S.sb([128, 512], F32), Buf()) for _ in range(2)])
        macc = S.sb([128, 512], F32)
        B_macc = Buf()
        mT = S.sb([128, 8, 512], BF16)
        BmT = [Buf() for _ in range(8)]
        tmp = ln_tmp()
        wgs3 = wgs.rearrange("p (k c) -> p k c", c=3072)
        for tg in range(8):
            ld(xin2[:], src[tg * 512:(tg + 1) * 512, :].rearrange("(n p) d -> p n d", p=128), [Bsrc[tg]], [Bxin2])
            for x in range(3):
                ld(ot[x][:], oT[x, :, tg * 512:(tg + 1) * 512].rearrange("(c p) t -> p c t", p=128), [B_oT[x][tg]], [B_ot[x]])
            for dc in range(8):
                for x in range(3):
                    pump()
                    wg, Bwg = wgr.next()
                    ld(wg[:], wgs3[:, :, x * DM + dc * 128:x * DM + (dc + 1) * 128], [B_wgs], [Bwg])
                    wb, Bwb = wbrr.next()
                    ld(wb[:], wbrs[x].rearrange("p (c d) -> p c d", d=DM)[:, :, dc * 128:(dc + 1) * 128], [B_wbrs], [Bwb])
                    bg, Bbg = bank()
                    for kc in range(8):
                        mm(bg[:, :], wg[:, kc, :], hT[:, kc, tg * 512:(tg + 1) * 512], kc == 0, kc == 7, [Bwg, BhT[tg]], [Bbg])
                    bb, Bbb = bank()
                    for c in range(4):
                        mm(bb[:, :], wb[:, c, :], ot[x][:, c, :], c == 0, c == 3, [Bwb, B_ot[x]], [Bbb])
                    gs, Bgs = gsr.next()
                    act(gs[:], bg[:, :], AF.Sigmoid, [Bbg, B_ln], [Bgs], bias=bg_t[:, x * 8 + dc:x * 8 + dc + 1])
                    if x == 0:
                        tt("vector", macc[:], gs[:], bb[:, :], ALU.mult, [Bgs, Bbb], [B_macc])
                    elif x == 1:
                        tt("vector", gs[:], gs[:], bb[:, :], ALU.mult, [Bgs, Bbb], [Bgs])
                        tt("gpsimd", macc[:], macc[:], gs[:], ALU.add, [Bgs, B_macc], [B_macc])
                    else:
                        tt("vector", gs[:], gs[:], bb[:, :], ALU.mult, [Bgs, Bbb], [Bgs])
                        tt("gpsimd", mT[:, dc, :], macc[:], gs[:], ALU.add, [Bgs, B_macc], [BmT[dc]])
            for n in range(4):
                b0, b1 = banks[5], banks[6]
                for dc in range(8):
                    mm(b0[:, :], mT[:, dc, n * 128:(n + 1) * 128], woutt[:, dc, 0:512], dc == 0, dc == 7,
                       [BmT[dc], B_wo], [B_bank[5]])
                    mm(b1[:, :], mT[:, dc, n * 128:(n + 1) * 128], woutt[:, dc, 512:1024], dc == 0, dc == 7,
                       [BmT[dc], B_wo], [B_bank[6]])
                r0 = tg * 512 + n * 128
                ln_epilogue(xin2[:, n, :], b0, b1, B_bank[5], B_bank[6], Bxin2, lng, lnb, B_ln, tmp,
                            dst[r0:r0 + 128, :], Bdst[tg] if isinstance(Bdst, list) else Bdst,
                            None if dbg_ap is None else dbg_ap[r0:r0 + 128, :])
        S.release(m0)

    cur, Bcur = x_in, [Buf() for _ in range(8)]
    pp = 0
    prep_ffn(0, 0)
    for l in range(NL):
        for st in range(3):
            lastst = (l == NL - 1 and st == 2) or (_STOP_AFTER is not None and l * 3 + st == _STOP_AFTER - 1)
            if _STOP_AFTER is not None and l * 3 + st >= _STOP_AFTER:
                continue
            if lastst:
                dst, Bdst = y_out, B_y
            else:
                dst, Bdst = xs[pp], B_xs[pp]
            dbg_ap = dbg_d.get("l%ds%d" % (l, st))
            if st == 0:
                ffn_stage(l, 0, cur, Bcur, dst, Bdst, 0, dbg_ap)
            elif st == 1:
                mixer_stage(l, cur, Bcur, dst, Bdst, dbg_ap)
            else:
                ffn_stage(l, 1, cur, Bcur, dst, Bdst, 2, dbg_ap)
            cur, Bcur = dst, Bdst
            pp ^= 1
    S.finish()
    return nc


def _consts(rel_bias):
    c = {}
    c["ident"] = np.eye(128, dtype=np.float32)
    s = np.arange(S_LEN)
    c["onehot"] = (s[None, :] // 64 == np.arange(64)[:, None]).astype(np.float32)
    j = np.arange(128)[:, None]
    i = np.arange(128)[None, :]
    d0 = i - j
    d1 = 128 + i - j
    bk0 = _t5_bucket(d0)
    bk1 = _t5_bucket(d1)
    relT = np.ascontiguousarray(rel_bias.T)
    biasT = np.empty((16, 2, 128, 128), np.float32)
    for h in range(16):
        t0 = relT[h][bk0]
        t1 = relT[h][bk1]
        biasT[h, 0] = np.where(d0 >= 0, t0, np.float32(NEG))
        if h < 8:
            biasT[h, 1] = t1
        else:
            biasT[h, 1] = np.where(d1 < 128, t1, np.float32(NEG))
    c["biasT"] = biasT
    c["cfar"] = np.ascontiguousarray(np.broadcast_to(rel_bias[31][None, :], (128, 16))).astype(np.float32)
    ii = np.arange(128)[:, None]
    u = np.arange(512)[None, :]
    dist = ii - 16 * (u - 256) - 31
    bkc = _t5_bucket(dist)
    tc = np.empty((8, 128, 512), np.float32)
    for h in range(8):
        tc[h] = np.where(dist >= 0, relT[h][bkc], np.float32(NEG))
    c["tc"] = tc
    uu = np.arange(128)[None, :] - 64
    curb = (ii >= 64).astype(np.int64)
    ft = np.zeros((128, 128), np.float32)
    ft[np.broadcast_to(uu > curb, (128, 128))] = -100.0
    ft[np.broadcast_to((uu == curb) | (uu == curb - 1), (128, 128))] = 100.0
    c["ft"] = ft
    c["cm"] = np.where(d0 >= 0, 0.0, NEG).astype(np.float32)
    c["lt"] = np.where(i < j, 0.0, NEG).astype(np.float32)
    sel = np.zeros((24, 1536), np.float32)
    for r in range(24):
        sel[r, r * 64:(r + 1) * 64] = 1.0
    c["sel24"] = sel
    return c


def _layer_weights(inp, ls):
    L = len(ls)
    w = {}

    def stack(f):
        return np.ascontiguousarray(np.stack([f(l) for l in ls]))
    for i, (k1, k2) in enumerate((("ffn1_w1", "ffn1_w2"), ("ffn2_w1", "ffn2_w2"))):
        w["w1_%d" % i] = stack(lambda l: inp[k1][l].reshape(8, 128, 2, NFC, 128).transpose(3, 1, 0, 2, 4).reshape(NFC, 128, 2048))
        w["w2_%d" % i] = stack(lambda l: inp[k2][l].reshape(NFC, 128, DM).transpose(1, 0, 2).reshape(128, NFC * DM))
    def _win_layout(l):
        wp = inp["w_in"][l][:, WIN_PERM]
        blocks = []
        for name, (off, wd) in WIN_P.items():
            blocks.append(wp[:, off:off + wd].reshape(8, 128, wd).transpose(1, 0, 2).reshape(128, 8 * wd))
        return np.concatenate(blocks, axis=1)
    w["win"] = stack(_win_layout)
    w["wgate"] = stack(lambda l: inp["w_gate"][l].reshape(8, 128, 3072).transpose(1, 0, 2).reshape(128, 8 * 3072))
    w["bgate"] = stack(lambda l: inp["b_gate"][l].reshape(24, 128).T)
    w["wbr"] = stack(lambda l: np.stack([inp[k][l].reshape(4, 128, DM).transpose(1, 0, 2).reshape(128, 4 * DM)
                                         for k in ("w_br_a", "w_br_b", "w_br_c")]))
    w["wout"] = stack(lambda l: inp["w_out"][l].reshape(8, 128, DM).transpose(1, 0, 2).reshape(128, 8 * DM))
    w["cw1"] = stack(lambda l: np.stack([inp[k][l].reshape(32, 64, 256).transpose(1, 0, 2).reshape(64, 32 * 256)
                                         for k in ("cmp_k_w1", "cmp_v_w1")]))
    w["cw2"] = stack(lambda l: np.stack([inp[k][l].reshape(2, 128, 64).transpose(1, 0, 2).reshape(128, 128)
                                         for k in ("cmp_k_w2", "cmp_v_w2")]))
    w["pet"] = stack(lambda l: np.stack([inp[k][l].T for k in ("cmp_pe_k", "cmp_pe_v")]))
    w["lng"] = stack(lambda l: np.stack([np.broadcast_to(inp[k][l][None, :], (128, DM)) for k in ("ln1_g", "ln2_g", "ln3_g")]))
    w["lnb"] = stack(lambda l: np.stack([np.broadcast_to(inp[k][l][None, :], (128, DM)) for k in ("ln1_b", "ln2_b", "ln3_b")]))
    w["sinks"] = stack(lambda l: np.broadcast_to(inp["swa_sinks"][l][None, :], (128, 8)))
    w["bf"] = stack(lambda l: inp["fox_b_f"][l].reshape(8, 1))
    return {k: np.ascontiguousarray(v, dtype=np.float32) for k, v in w.items()}


_PROG = {}


def _get_prog(NL):
    if NL not in _PROG:
        _PROG[NL] = build_program(NL)
    return _PROG[NL]


def kernel(**inputs):
    inp = {k: np.asarray(v, dtype=np.float32) for k, v in inputs.items()}
    consts = _consts(inp["rel_bias"])
    x = inp["x"]
    nb = x.shape[0]
    NL = NLAYERS
    nc = _get_prog(NL)
    wl = _layer_weights(inp, list(range(NL)))
    in_maps = []
    for b in range(nb):
        m = {"x": np.ascontiguousarray(x[b])}
        m.update(wl)
        m.update(consts)
        in_maps.append(m)
    res = run_bass_kernel_spmd(nc, in_maps, core_ids=list(range(nb)))
    return np.stack([np.asarray(r["y"], dtype=np.float32) for r in res.results], axis=0)
```

```python
from contextlib import ExitStack
import numpy as np
import concourse.bass as bass
import concourse.mybir as mybir
from concourse.bass_utils import run_bass_kernel_spmd

F32 = mybir.dt.float32
BF16 = mybir.dt.bfloat16
AF = mybir.ActivationFunctionType
ALU = mybir.AluOpType
AX = mybir.AxisListType

ENGS = ["tensor", "vector", "scalar", "gpsimd", "sync"]
S_LEN = 4096
DM = 1024
FF = 2816
NFC = 22
DIN = 3616
NEG = -30000.0
ALPHA = 8.0 ** 0.25
LN_EPS = 1e-5
NLAYERS = 4
_STOP_AFTER = None
_MIX_STOP = 99
_MIX_OUT = True
_SUB = 99


class _Stop(Exception):
    pass


class Buf:
    __slots__ = ("w", "r", "excl")

    def __init__(self, excl=False):
        self.w = None
        self.r = {}
        self.excl = excl


class Sched:
    def __init__(self, nc, n_dma_sems=12, rot=20000):
        self.nc = nc
        self.prog = {e: [] for e in ENGS}
        self.seen = {e: {} for e in ENGS}
        self.cnt = {e: 0 for e in ENGS}
        self.epoch = {e: 0 for e in ENGS}
        self.rot = rot
        self.n_dma = n_dma_sems
        self.dma_uses = {}
        self.dma_rr = {e: 0 for e in ENGS}
        self.semkeys = []
        self.semset = set()
        self.stack = ExitStack()
        self.sb_off = 16512
        self.sb_id = 0

    def sb(self, shape, dtype):
        n = 1
        for s in shape[1:]:
            n *= s
        nbytes = n * (4 if dtype == F32 else 2)
        off = (self.sb_off + 63) // 64 * 64
        assert off + nbytes <= 229000, ("SBUF overflow", off, nbytes)
        self.sb_off = off + nbytes
        self.peak = max(getattr(self, 'peak', 0), self.sb_off)
        self.sb_id += 1
        return self.nc.alloc_sbuf_tensor_at("t%d" % self.sb_id, list(shape), dtype, offset=off)

    def mark(self):
        return self.sb_off

    def release(self, m):
        self.barrier()
        self.sb_off = m

    def _key(self, key):
        if key not in self.semset:
            self.semset.add(key)
            self.semkeys.append(key)
        return key

    def _collect(self, eng, reads, writes):
        deps = {}

        def add(tok):
            if tok is None:
                return
            k, v = tok
            if deps.get(k, 0) < v:
                deps[k] = v
        for b in reads:
            add(b.w)
            if b.excl:
                for k, v in b.r.items():
                    if k[0] != eng:
                        add((k, v))
        for b in writes:
            add(b.w)
            for k, v in b.r.items():
                add((k, v))
        waits = []
        seen = self.seen[eng]
        for k, v in deps.items():
            if eng == "tensor" and k[0] == "tensor":
                continue
            if seen.get(k, 0) >= v:
                continue
            seen[k] = v
            waits.append((k, v))
        return waits

    def _update(self, tok, reads, writes):
        k, v = tok
        for b in reads:
            if b.r.get(k, 0) < v:
                b.r[k] = v
        for b in writes:
            b.w = tok
            b.r = {}

    def op(self, eng, emit, reads=(), writes=()):
        waits = self._collect(eng, reads, writes)
        self.cnt[eng] += 1
        if self.cnt[eng] > self.rot:
            self.epoch[eng] += 1
            self.cnt[eng] = 1
        tok = (self._key((eng, self.epoch[eng])), self.cnt[eng])
        self.prog[eng].append((waits, emit, tok, 1))
        self._update(tok, reads, writes)
        return tok

    def dma(self, q, emit, reads=(), writes=()):
        i = self.dma_rr[q]
        self.dma_rr[q] = (i + 1) % self.n_dma
        key = self._key(("dma", q, i))
        k = self.dma_uses.get(key, 0) + 1
        self.dma_uses[key] = k
        waits = self._collect(q, reads, writes)
        if k > 1 and self.seen[q].get(key, 0) < 16 * (k - 1):
            self.seen[q][key] = 16 * (k - 1)
            waits.append((key, 16 * (k - 1)))
        tok = (key, 16 * k)
        self.prog[q].append((waits, emit, tok, 16))
        self._update(tok, reads, writes)
        return tok

    def _all_tokens(self):
        toks = []
        for key, k in self.dma_uses.items():
            toks.append((key, 16 * k))
        for e in ENGS:
            if self.cnt[e] > 0:
                toks.append(((e, self.epoch[e]), self.cnt[e]))
        return toks

    def barrier(self):
        toks = self._all_tokens()
        for e in ENGS:
            waits = []
            for k, v in toks:
                if self.seen[e].get(k, 0) < v:
                    self.seen[e][k] = v
                    waits.append((k, v))
            if waits:
                self.prog[e].append((waits, None, None, 0))

    def finish(self):
        nc = self.nc
        self.barrier()
        sems = {}
        for key in self.semkeys:
            nm = "s_" + "_".join(str(x) for x in key)
            sems[key] = self.stack.enter_context(nc.semaphore(nm))
        prog = self.prog
        with nc.Block() as block:
            def mk(ename):
                def body(eng):
                    for waits, emit, tok, inc in prog[ename]:
                        for k, v in waits:
                            eng.wait_ge(sems[k], v)
                        if emit is not None:
                            emit(eng).then_inc(sems[tok[0]], inc)
                return body
            block.tensor(mk("tensor"))
            block.vector(mk("vector"))
            block.scalar(mk("scalar"))
            block.gpsimd(mk("gpsimd"))
            block.sync(mk("sync"))
        self.stack.close()


class Ring:
    def __init__(self, items):
        self.items = items
        self.i = 0

    def next(self):
        it = self.items[self.i]
        self.i = (self.i + 1) % len(self.items)
        return it


def _win_passes():
    P = {}
    off = 0
    perm = []

    def add(name, cols):
        nonlocal off
        P[name] = (off, len(cols))
        perm.extend(cols)
        off += len(cols)
    r = lambda a, n: list(range(a, a + n))
    for h in range(8):
        add("fox_qk%d" % h, r(2072 + h * 64, 64) + r(2584 + h * 64, 64))
    add("fox_f", r(3608, 8))
    for h in range(8):
        add("fox_v%d" % h, r(3096 + h * 64, 64))
    for i in range(4):
        add("swa_q%d" % i, r(1304 + i * 128, 128))
    add("swa_k", r(1816, 128))
    add("swa_v", r(1944, 128))
    for i in range(4):
        add("nsa_q%d" % i, r(i * 128, 128))
    for g in range(2):
        add("nsa_cmp%d" % g, r(512 + g * 64, 64) + r(640 + g * 64, 64))
        add("nsa_kk%d" % g, r(768 + g * 64, 64) + r(1024 + g * 64, 64))
        add("nsa_v%d" % g, r(896 + g * 64, 64) + r(1152 + g * 64, 64))
    add("nsa_gate", r(1280, 24))
    assert off == DIN and sorted(perm) == list(range(DIN))
    return P, np.array(perm)


WIN_P, WIN_PERM = _win_passes()


def _t5_bucket(n):
    n = np.maximum(n, 0)
    lr = np.log(np.maximum(n, 1).astype(np.float32) / np.float32(16)) / np.float32(np.log(128 / 16))
    large = 16 + (lr.astype(np.float32) * np.float32(16)).astype(np.int32)
    return np.where(n < 16, n, np.minimum(large, 31))


def build_program(NL, dbg=None):
    nc = bass.Bass("TRN2", target_bir_lowering=False)
    S = Sched(nc)

    def din(name, shape, dt=F32):
        return nc.dram_tensor(name, list(shape), dt, kind="ExternalInput").ap()

    def dscr(name, shape, dt):
        return nc.dram_tensor(name, list(shape), dt, kind="Internal").ap()

    x_in = din("x", [S_LEN, DM])
    y_out = nc.dram_tensor("y", [S_LEN, DM], F32, kind="ExternalOutput").ap()
    w1_d = [din("w1_%d" % i, [NL, NFC, 128, 2048]) for i in range(2)]
    w2_d = [din("w2_%d" % i, [NL, 128, NFC * DM]) for i in range(2)]
    win_d = din("win", [NL, 128, 8 * DIN])
    wgate_d = din("wgate", [NL, 128, 8 * 3072])
    bgate_d = din("bgate", [NL, 128, 24])
    wbr_d = din("wbr", [NL, 3, 128, 4 * DM])
    wout_d = din("wout", [NL, 128, 8 * DM])
    cw1_d = din("cw1", [NL, 2, 64, 32 * 256])
    cw2_d = din("cw2", [NL, 2, 128, 128])
    pet_d = din("pet", [NL, 2, 64, 32])
    lng_d = din("lng", [NL, 3, 128, DM])
    lnb_d = din("lnb", [NL, 3, 128, DM])
    sinks_d = din("sinks", [NL, 128, 8])
    bf_d = din("bf", [NL, 8, 1])
    ident_d = din("ident", [128, 128])
    onehot_d = din("onehot", [64, S_LEN])
    biasT_d = din("biasT", [16, 2, 128, 128])
    cfar_d = din("cfar", [128, 16])
    tc_d = din("tc", [8, 128, 512])
    ft_d = din("ft", [128, 128])
    cm_d = din("cm", [128, 128])
    lt_d = din("lt", [128, 128])
    sel24_d = din("sel24", [24, 1536])

    xs = [dscr("xs0", [S_LEN, DM], F32), dscr("xs1", [S_LEN, DM], F32)]
    w1s = [dscr("w1s%d" % i, [NFC, 128, 2048], BF16) for i in range(2)]
    w2s = [dscr("w2s%d" % i, [128, NFC * DM], BF16) for i in range(2)]
    wins = dscr("wins", [128, 8 * DIN], BF16)
    wgs = dscr("wgs", [128, 8 * 3072], BF16)
    wbrs = dscr("wbrs", [3, 128, 4 * DM], BF16)
    wouts = dscr("wouts", [128, 8 * DM], BF16)
    cw1s = dscr("cw1s", [2, 64, 32 * 256], BF16)
    ohs = dscr("ohs", [64, S_LEN], BF16)
    if dbg and "oT" in dbg:
        oT = nc.dram_tensor("oT", [3, 512, S_LEN], BF16, kind="ExternalOutput").ap()
    else:
        oT = dscr("oT", [3, 512, S_LEN], BF16)
    if dbg and "oT" in dbg:
        ocmp = nc.dram_tensor("ocmp", [8, 64, S_LEN], F32, kind="ExternalOutput").ap()
        dsel = nc.dram_tensor("dsel", [8, 64, S_LEN], F32, kind="ExternalOutput").ap()
        dwin = nc.dram_tensor("dwin", [8, 64, S_LEN], F32, kind="ExternalOutput").ap()
    else:
        ocmp = dscr("ocmp", [8, 64, S_LEN], F32)
        dsel = dwin = None
    caug = dscr("caug", [8, 6, S_LEN], BF16)
    B_xs = [[Buf() for _ in range(8)] for _ in range(2)]
    B_w1s = [Buf(), Buf()]
    B_w2s = [Buf(), Buf()]
    B_wins, B_wgs, B_wbrs, B_wouts, B_cw1s, B_ohs = Buf(), Buf(), Buf(), Buf(), Buf(), Buf()
    B_oT = [[Buf() for _ in range(8)] for _ in range(3)]
    B_ocmp = [[Buf() for _ in range(8)] for _ in range(8)]
    B_caug = Buf()
    B_y = Buf()
    dbg_d = {}
    if dbg:
        for nm in dbg:
            if nm == "oT":
                continue
            dbg_d[nm] = nc.dram_tensor("dbg_" + nm, [S_LEN, DM], F32, kind="ExternalOutput").ap()

    banks = [S.stack.enter_context(nc.psum_tensor("bank%d" % i, [128, 512], F32)) for i in range(8)]
    B_bank = [Buf(excl=True) for _ in range(8)]
    ring5 = Ring([0, 1, 2, 3, 4])

    def bank():
        i = ring5.next()
        return banks[i], B_bank[i]

    cpy_rr = [0]

    def mm(out, lhsT, rhs, start, stop, reads, writes):
        S.op("tensor", lambda e: e.matmul(out, lhsT=lhsT, rhs=rhs, start=start, stop=stop),
             reads=reads, writes=writes)

    def tr(out, in_, ident, reads, writes):
        S.op("tensor", lambda e: e.transpose(out, in_, ident), reads=reads, writes=writes)

    def act(out, in_, func, reads, writes, bias=None, scale=None, accum=None):
        kw = {}
        if bias is not None:
            kw["bias"] = bias
        if scale is not None:
            kw["scale"] = scale
        if accum is not None:
            kw["accum_out"] = accum
        S.op("scalar", lambda e: e.activation(out=out, in_=in_, func=func, **kw), reads=reads, writes=writes)

    def vcopy(eng, out, in_, reads, writes):
        S.op(eng, lambda e: e.tensor_copy(out=out, in_=in_), reads=reads, writes=writes)

    def copy_any(out, in_, reads, writes, psum=True):
        cpy_rr[0] ^= 1
        if cpy_rr[0]:
            vcopy("vector", out, in_, reads, writes)
        else:
            act(out, in_, AF.Copy, reads, writes)

    def tt(eng, out, in0, in1, op, reads, writes):
        S.op(eng, lambda e: e.tensor_tensor(out=out, in0=in0, in1=in1, op=op), reads=reads, writes=writes)

    def ts(eng, out, in0, s1, s2, op0, op1, reads, writes):
        if op1 is None:
            S.op(eng, lambda e: e.tensor_scalar(out=out, in0=in0, scalar1=s1, scalar2=None, op0=op0),
                 reads=reads, writes=writes)
        else:
            S.op(eng, lambda e: e.tensor_scalar(out=out, in0=in0, scalar1=s1, scalar2=s2, op0=op0, op1=op1),
                 reads=reads, writes=writes)

    def stt(eng, out, in0, scalar, in1, op0, op1, reads, writes):
        S.op(eng, lambda e: e.scalar_tensor_tensor(out=out, in0=in0, scalar=scalar, in1=in1, op0=op0, op1=op1),
             reads=reads, writes=writes)

    def memset(eng, ap, val, writes):
        S.op(eng, lambda e: e.memset(ap, val), writes=writes)

    def ld(out, in_, reads, writes):
        S.dma("sync", lambda e: e.dma_start(out=out, in_=in_), reads=reads, writes=writes)

    ident_f = S.sb([128, 128], F32)
    ident_b = S.sb([128, 128], BF16)
    nsaT = S.sb([128, 8, 2, 128], BF16)
    swaT = S.sb([128, 8, 2, 128], BF16)
    cm_b = S.sb([128, 128], BF16)
    lt_b = S.sb([128, 128], BF16)
    tc_b = S.sb([128, 8, 512], BF16)
    ft_f = S.sb([128, 128], F32)
    sel24 = S.sb([24, 1536], BF16)
    cfar = S.sb([128, 16], F32)
    B_const = Buf()
    CH = 1024
    stg = Ring([(S.sb([128, CH], F32), S.sb([128, CH], BF16), Buf(), Buf()) for _ in range(3)])
    stage_f, stage_b, B_stage_f, B_stage_b = stg.items[0]
    cwstage = S.sb([128, 160], F32)
    B_cwstage = Buf()

    def stD(out, in_, reads, writes):
        S.dma("gpsimd", lambda e: e.dma_start(out=out, in_=in_), reads=reads, writes=writes)

    epst = S.sb([128, 2], F32)
    memset("vector", epst[:, 0:1], LN_EPS, [B_const])
    memset("vector", epst[:, 1:2], 1.0, [B_const])
    ld(ident_f[:], ident_d, [], [B_const])
    vcopy("vector", ident_b[:], ident_f[:], [B_const], [B_const])
    ld(cfar[:], cfar_d, [], [B_const])
    ld(ft_f[:], ft_d, [], [B_const])
    for (dst, src) in ((cm_b, cm_d), (lt_b, lt_d)):
        ld(stage_f[:, 0:128], src, [], [B_stage_f])
        vcopy("vector", dst[:], stage_f[:, 0:128], [B_stage_f], [B_const])
    for c in range(2):
        ld(stage_f[0:24, 0:768], sel24_d[:, c * 768:(c + 1) * 768], [], [B_stage_f])
        vcopy("vector", sel24[:, c * 768:(c + 1) * 768], stage_f[0:24, 0:768], [B_stage_f], [B_const])
    for h in range(8):
        ld(stage_f[:, 0:512], tc_d[h], [], [B_stage_f])
        vcopy("vector", tc_b[:, h, :], stage_f[:, 0:512], [B_stage_f], [B_const])
    for h in range(16):
        for k in range(2):
            ld(stage_f[:, 0:128], biasT_d[h, k], [], [B_stage_f])
            if h < 8:
                ts("vector", nsaT[:, h, k, :], stage_f[:, 0:128], cfar[:, h:h + 1], None, ALU.subtract, None,
                   [B_stage_f, B_const], [B_const])
            else:
                vcopy("vector", swaT[:, h - 8, k, :], stage_f[:, 0:128], [B_stage_f], [B_const])
    for c in range(4):
        ld(stage_f[0:64, :], onehot_d[:, c * 1024:(c + 1) * 1024], [], [B_stage_f])
        vcopy("vector", stage_b[0:64, :], stage_f[0:64, :], [B_stage_f], [B_stage_b])
        ld(ohs[:, c * 1024:(c + 1) * 1024], stage_b[0:64, :], [B_stage_b], [B_ohs])

    PQ = []
    inflight = []
    prep_rr = [0]

    def prep(dst, src, n, rows, Bdst, scale=None):
        for c0 in range(0, n, CH):
            PQ.append((dst, src, c0, min(CH, n - c0), rows, Bdst, scale))

    def pump(k=1):
        for _ in range(k):
            if inflight and (len(inflight) >= 2 or not PQ):
                (dst, src, c0, w, rows, Bdst, scale), (sf, sbt, Bsf, Bsb) = inflight.pop(0)
                prep_rr[0] ^= 1
                ceng = "vector" if prep_rr[0] else "gpsimd"
                if scale is None:
                    vcopy(ceng, sbt[0:rows, 0:w], sf[0:rows, 0:w], [Bsf], [Bsb])
                else:
                    ts(ceng, sbt[0:rows, 0:w], sf[0:rows, 0:w], scale, None, ALU.mult, None, [Bsf], [Bsb])
                stD(dst[:, c0:c0 + w], sbt[0:rows, 0:w], [Bsb], [Bdst])
            if PQ:
                task = PQ.pop(0)
                bufs = stg.next()
                (dst, src, c0, w, rows, Bdst, scale) = task
                stD(bufs[0][0:rows, 0:w], src[:, c0:c0 + w], [], [bufs[2]])
                inflight.append((task, bufs))

    def prep_flush():
        while PQ or inflight:
            pump()

    def prep_ffn(l, which):
        prep(w2s[which], w2_d[which][l], NFC * DM, 128, B_w2s[which], scale=0.5)
        for fc in range(NFC):
            prep(w1s[which][fc], w1_d[which][l, fc], 2048, 128, B_w1s[which])

    def prep_mixer(l):
        prep(wins, win_d[l], 8 * DIN, 128, B_wins)
        prep(wgs, wgate_d[l], 8 * 3072, 128, B_wgs)
        for x in range(3):
            prep(wbrs[x], wbr_d[l, x], 4 * DM, 128, B_wbrs)
        prep(wouts, wout_d[l], 8 * DM, 128, B_wouts)
        for kv in range(2):
            prep(cw1s[kv], cw1_d[l, kv], 32 * 256, 64, B_cw1s)

    base_mark = S.mark()

    def ln_epilogue(xrow, b0, b1, Bb0, Bb1, Bx, lng, lnb, B_ln, tmp, dst_ap, Bdst, dbg_ap=None):
        z, junk, xo, st, B_z, B_junk, B_xo, B_st = tmp
        stt("vector", z[:, 0:512], xrow[:, 0:512], ALPHA, b0[:, :], ALU.mult, ALU.add, [Bx, Bb0], [B_z])
        stt("vector", z[:, 512:1024], xrow[:, 512:1024], ALPHA, b1[:, :], ALU.mult, ALU.add, [Bx, Bb1], [B_z])
        act(junk[:], z[:], AF.Copy, [B_z], [B_junk, B_st], accum=st[:, 0:1])
        act(junk[:], z[:], AF.Square, [B_z], [B_junk, B_st], accum=st[:, 1:2])
        ts("vector", st[:, 2:4], st[:, 0:2], 1.0 / DM, None, ALU.mult, None, [B_st], [B_st])
        stt("vector", st[:, 4:5], st[:, 2:3], -1.0, st[:, 2:3], ALU.mult, ALU.mult, [B_st], [B_st])
        tt("vector", st[:, 5:6], st[:, 4:5], st[:, 3:4], ALU.add, [B_st], [B_st])
        act(st[:, 8:9], st[:, 5:6], AF.Sqrt, [B_st, B_const], [B_st], bias=epst[:, 0:1])
        S.op("vector", lambda e: e.reciprocal(out=st[:, 6:7], in_=st[:, 8:9]), reads=[B_st], writes=[B_st])
        stt("vector", st[:, 7:8], st[:, 2:3], -1.0, st[:, 6:7], ALU.mult, ALU.mult, [B_st], [B_st])
        act(xo[:], z[:], AF.Identity, [B_z, B_st], [B_xo], bias=st[:, 7:8], scale=st[:, 6:7])
        tt("vector", xo[:], xo[:], lng[:], ALU.mult, [B_xo, B_ln], [B_xo])
        tt("gpsimd", xo[:], xo[:], lnb[:], ALU.add, [B_xo, B_ln], [B_xo])
        stD(dst_ap, xo[:], [B_xo], [Bdst])
        if dbg_ap is not None:
            ld(dbg_ap, xo[:], [B_xo], [Buf()])

    def ln_tmp():
        z = S.sb([128, DM], F32)
        junk = S.sb([128, DM], F32)
        xo = S.sb([128, DM], F32)
        st = S.sb([128, 10], F32)
        return (z, junk, xo, st, Buf(), Buf(), Buf(), Buf())

    def load_xT(xin, Bxin, xT, BxT, kcs=range(8)):
        for kc in kcs:
            bk, Bbk = bank()
            for n in range(4):
                tr(bk[:, n * 128:(n + 1) * 128], xin[:, n, kc * 128:(kc + 1) * 128], ident_f[:], [Bxin, B_const], [Bbk])
            copy_any(xT[:, kc, :], bk[:, :], [Bbk], [BxT])

    def ffn_stage(l, which, src, Bsrc, dst, Bdst, lnidx, dbg_ap=None):
        m0 = S.mark()
        prep_flush()
        if which == 0:
            prep_mixer(l)
        elif l + 1 < NL:
            prep_ffn(l + 1, 0)
        lng = S.sb([128, DM], F32)
        lnb = S.sb([128, DM], F32)
        B_ln = Buf()
        ld(lng[:], lng_d[l, lnidx], [], [B_ln])
        ld(lnb[:], lnb_d[l, lnidx], [], [B_ln])
        xin_r = Ring([(S.sb([128, 4, DM], F32), Buf()) for _ in range(2)])
        xT_r = Ring([(S.sb([128, 8, 512], BF16), Buf()) for _ in range(2)])
        w1p = Ring([(S.sb([128, 8, 256], BF16), Buf()) for _ in range(3)])
        hT = S.sb([128, NFC, 512], BF16)
        BhT = [Buf() for _ in range(NFC)]
        w2t = S.sb([128, NFC, DM], BF16)
        Bw2t = Buf()
        sgr = Ring([(S.sb([128, 512], F32), Buf()) for _ in range(2)])
        tmps = Ring([ln_tmp(), ln_tmp()])
        ring5.items = [0, 1, 2, 3]
        ring5.i = 0
        ypairs = [(5, 6), (7, 4)]
        ld(w2t[:], w2s[which].rearrange("p (f d) -> p f d", d=DM), [B_w2s[which]], [Bw2t])
        xin, Bxin = xin_r.next()
        ld(xin[:], src[0:512, :].rearrange("(n p) d -> p n d", p=128), [Bsrc[0]], [Bxin])
        xT, BxT = xT_r.next()
        load_xT(xin, Bxin, xT, BxT)
        for tg in range(8):
            for fc in range(NFC):
                pump()
                wp, Bwp = w1p.next()
                ld(wp[:], w1s[which][fc].rearrange("p (k c) -> p k c", c=256), [B_w1s[which]], [Bwp])
                bg, Bbg = bank()
                for kc in range(8):
                    mm(bg[:, :], wp[:, kc, 0:128], xT[:, kc, :], kc == 0, kc == 7, [Bwp, BxT], [Bbg])
                bu, Bbu = bank()
                for kc in range(8):
                    mm(bu[:, :], wp[:, kc, 128:256], xT[:, kc, :], kc == 0, kc == 7, [Bwp, BxT], [Bbu])
                sg, Bsg = sgr.next()
                act(sg[:], bg[:, :], AF.Silu, [Bbg], [Bsg])
                tt("vector", hT[:, fc, :], sg[:], bu[:, :], ALU.mult, [Bsg, Bbu], [BhT[fc]])
            if tg + 1 < 8:
                xin_n, Bxin_n = xin_r.next()
                ld(xin_n[:], src[(tg + 1) * 512:(tg + 2) * 512, :].rearrange("(n p) d -> p n d", p=128),
                   [Bsrc[tg + 1]], [Bxin_n])
                xT_n, BxT_n = xT_r.next()
            for n in range(4):
                i0, i1 = ypairs[n % 2]
                b0, b1 = banks[i0], banks[i1]
                for fc in range(NFC):
                    mm(b0[:, :], hT[:, fc, n * 128:(n + 1) * 128], w2t[:, fc, 0:512], fc == 0, fc == NFC - 1,
                       [BhT[fc], Bw2t], [B_bank[i0]])
                    mm(b1[:, :], hT[:, fc, n * 128:(n + 1) * 128], w2t[:, fc, 512:1024], fc == 0, fc == NFC - 1,
                       [BhT[fc], Bw2t], [B_bank[i1]])
                if tg + 1 < 8:
                    load_xT(xin_n, Bxin_n, xT_n, BxT_n, kcs=range(2 * n, 2 * n + 2))
                r0 = tg * 512 + n * 128
                ln_epilogue(xin[:, n, :], b0, b1, B_bank[i0], B_bank[i1], Bxin, lng, lnb, B_ln, tmps.next(),
                            dst[r0:r0 + 128, :], Bdst[tg] if isinstance(Bdst, list) else Bdst,
                            None if dbg_ap is None else dbg_ap[r0:r0 + 128, :])
            if tg + 1 < 8:
                xin, Bxin, xT, BxT = xin_n, Bxin_n, xT_n, BxT_n
        ring5.items = [0, 1, 2, 3, 4]
        ring5.i = 0
        S.release(m0)

    def mixer_stage(l, src, Bsrc, dst, Bdst, dbg_ap=None):
        m00 = S.mark()
        try:
            mixer_stage_(l, src, Bsrc, dst, Bdst, dbg_ap)
        except _Stop:
            S.release(m00)

    def chk(n):
        if _SUB <= n:
            raise _Stop()

    def mixer_stage_(l, src, Bsrc, dst, Bdst, dbg_ap=None):
        m0 = S.mark()
        prep_flush()
        prep_ffn(l, 1)
        wins3 = wins.rearrange("p (k c) -> p k c", c=DIN)

        hT = S.sb([128, 8, S_LEN], BF16)
        BhT = [Buf() for _ in range(8)]
        m_h = S.mark()
        xin = S.sb([128, 4, DM], F32)
        Bxin = Buf()
        xTt = S.sb([128, 8, 512], BF16)
        for tg in range(8):
            ld(xin[:], src[tg * 512:(tg + 1) * 512, :].rearrange("(n p) d -> p n d", p=128), [Bsrc[tg]], [Bxin])
            for kc in range(8):
                bk, Bbk = bank()
                for n in range(4):
                    tr(bk[:, n * 128:(n + 1) * 128], xin[:, n, kc * 128:(kc + 1) * 128], ident_f[:], [Bxin, B_const], [Bbk])
                copy_any(hT[:, kc, tg * 512:(tg + 1) * 512], bk[:, :], [Bbk], [BhT[tg]])
        S.release(m_h)

        Qa = [S.sb([128, S_LEN], BF16) for _ in range(2)]
        BQ = [[Buf() for _ in range(8)] for _ in range(2)]
        BQaug = [Buf(), Buf()]
        Ka = [S.sb([128, S_LEN], BF16) for _ in range(2)]
        BK = [[Buf() for _ in range(8)] for _ in range(2)]
        BKaug = [Buf(), Buf()]
        V1 = [S.sb([128, 32, 128], BF16) for _ in range(2)]
        BV = [[Buf() for _ in range(8)] for _ in range(2)]
        wpr = Ring([(S.sb([128, 1024], BF16), Buf()) for _ in range(2)])
        ptr = Ring([(S.sb([128, 512], BF16), Buf()) for _ in range(4)])
        den_g = S.sb([64, 512], F32)
        rec_g = S.sb([64, 512], F32)
        coef_g = S.sb([64, 512], F32)
        obf = S.sb([64, 512], BF16)
        B_den_g, B_rec_g, B_coef_g, B_obf = Buf(), Buf(), Buf(), Buf()
        gatesT = S.sb([24, S_LEN], BF16)
        B_gates = [Buf() for _ in range(8)]
        expsink = S.sb([128, 8], F32)
        B_sink = Buf()
        nbf = S.sb([8, 1], F32)
        B_nbf = Buf()
        if _MIX_STOP <= -1:
            S.release(m0)
            return
        for b in range(2):
            for tg in range(8):
                memset("gpsimd", V1[b][:, tg * 4:(tg + 1) * 4, 64:128], 1.0, [BV[b][tg]])
        ld(expsink[:], sinks_d[l], [], [B_sink])
        act(expsink[:], expsink[:], AF.Exp, [B_sink], [B_sink])
        ld(nbf[:], bf_d[l], [], [B_nbf])
        ts("vector", nbf[:], nbf[:], -1.0, None, ALU.mult, None, [B_nbf], [B_nbf])
        chk(1)

        def load_w(name, c0=0, wd=None):
            off, w = WIN_P[name]
            if wd is None:
                wd = w
            wpf, Bwp = wpr.next()
            wp = wpf[:, 0:8 * wd].rearrange("p (k c) -> p k c", c=wd)
            ld(wpf[:, 0:8 * wd], wins[:, 8 * off:8 * off + 8 * wd], [B_wins], [Bwp])
            return wp, Bwp, wd

        def proj_fm(name, evac, c0=0, wd=None, tgs=range(8)):
            wp, Bwp, wd = load_w(name, c0, wd)
            for tg in tgs:
                bk, Bbk = bank()
                for kc in range(8):
                    mm(bk[0:wd, :], wp[:, kc, 0:wd], hT[:, kc, tg * 512:(tg + 1) * 512], kc == 0, kc == 7,
                       [Bwp, BhT[tg]], [Bbk])
                evac(tg, bk, Bbk)

        def proj_tm(name, evac):
            wp, Bwp, wd = load_w(name)
            for t in range(32):
                bk, Bbk = bank()
                for kc in range(8):
                    mm(bk[:, 0:wd], hT[:, kc, t * 128:(t + 1) * 128], wp[:, kc, 0:wd], kc == 0, kc == 7,
                       [Bwp, BhT[t // 4]], [Bbk])
                evac(t, bk, Bbk)

        def ev_scaled(dst, r0, Bd, scale):
            def f(tg, bk, Bbk):
                if r0 == 0:
                    ts("vector", dst[0:64, tg * 512:(tg + 1) * 512], bk[0:64, :], scale, None, ALU.mult, None,
                       [Bbk], [Bd[tg]])
                else:
                    act(dst[0:64, tg * 512:(tg + 1) * 512], bk[r0:r0 + 64, :], AF.Copy, [Bbk], [Bd[tg]], scale=scale)
            return f

        def ev_plain(dst, r0, Bd):
            def f(tg, bk, Bbk):
                if r0 == 0:
                    vcopy("vector", dst[0:64, tg * 512:(tg + 1) * 512], bk[r0:r0 + 64, :], [Bbk], [Bd[tg]])
                else:
                    act(dst[0:64, tg * 512:(tg + 1) * 512], bk[r0:r0 + 64, :], AF.Copy, [Bbk], [Bd[tg]])
            return f

        def ev_multi(*fs):
            def f(tg, bk, Bbk):
                for g in fs:
                    g(tg, bk, Bbk)
            return f

        AJ = []
        job_ctr = [0]
        LOOK = 3

        def attn_group(g, Q, Qreads, kdim, K, Kreads, Vt, BVt, plan, fin):
            AJ.append((g, Q, Qreads, kdim, K, Kreads, Vt, BVt, plan, fin))

        def attn_flush():
            pend = []

            def emit_qk(job, stt_, item):
                (g, Q, Qreads, kdim, K, Kreads, Vt, BVt, plan, fin) = job
                (kb, n_lo, n_hi, bias) = item
                bk, Bbk = bank()
                c0, c1 = n_lo * 128, n_hi * 128
                nb = len(bias)
                mm(bk[:, c0:c1], K[0:kdim, kb * 128:(kb + 1) * 128], Q[0:kdim, g * 512 + c0:g * 512 + c1],
                   True, nb == 0, Kreads(kb) + Qreads(g), [Bbk])
                for bi, (n, bap) in enumerate(sorted(bias.items())):
                    mm(bk[:, n * 128:(n + 1) * 128], ident_b[:], bap, False, bi == nb - 1, [B_const], [Bbk])
                return (job, stt_, item, bk, Bbk)

            def emit_pv(rec_):
                job, stt_, (kb, n_lo, n_hi, bias), bk, Bbk = rec_
                Vt, BVt = job[6], job[7]
                ob, Bob = stt_["ob"], stt_["Bob"]
                ntouch, total = stt_["ntouch"], stt_["total"]
                c0, c1 = n_lo * 128, n_hi * 128
                pt, Bpt = ptr.next()
                act(pt[:, c0:c1], bk[:, c0:c1], AF.Exp, [Bbk], [Bpt])
                n = n_lo
                while n < n_hi:
                    last = ntouch[n] == total[n] - 1
                    n2 = n + 1
                    while n2 < n_hi and (ntouch[n2] == total[n2] - 1) == last:
                        n2 += 1
                    S.op("tensor", lambda e, o_=ob[:, n * 128:n2 * 128], l_=Vt[:, kb, :], r_=pt[:, n * 128:n2 * 128],
                         st_=(stt_["npv"] == 0), sp_=last: e.matmul(o_, lhsT=l_, rhs=r_, start=st_, stop=sp_, skip_group_check=True),
                         reads=[BVt[kb // 4], Bpt], writes=[Bob])
                    stt_["npv"] += 1
                    for k in range(n, n2):
                        ntouch[k] += 1
                    n = n2
                stt_["left"] -= 1
                if stt_["left"] == 0:
                    stt_["fin"](ob, Bob)

            for job in AJ:
                plan, fin = job[8], job[9]
                jb = job_ctr[0]
                job_ctr[0] += 1
                stt_ = dict(ob=banks[5 + jb % 2], Bob=B_bank[5 + jb % 2], ntouch=[0] * 4, total=[0] * 4, npv=0,
                            left=len(plan), fin=fin)
                for (kb, n_lo, n_hi, bias) in plan:
                    for n in range(n_lo, n_hi):
                        stt_["total"][n] += 1
                pump()
                for item in plan:
                    pend.append(emit_qk(job, stt_, item))
                    if len(pend) > LOOK:
                        emit_pv(pend.pop(0))
            while pend:
                emit_pv(pend.pop(0))
            del AJ[:]

        def finalize(ob, Bob, ncols, gate_row=None, gcols=None, sink_h=None, clampden=False, out=None, Bout=None,
                     tmp=None):
            if tmp is None:
                tmp = (den_g, rec_g, coef_g, B_den_g, B_rec_g, B_coef_g)
            den, rec, coef, B_den, B_rec, B_coef = tmp
            act(den[:, 0:ncols], ob[64:128, 0:ncols], AF.Copy, [Bob], [B_den])
            if sink_h is not None:
                ts("vector", den[:, 0:ncols], den[:, 0:ncols], expsink[0:64, sink_h:sink_h + 1], None, ALU.add, None,
                   [B_den, B_sink], [B_den])
            if clampden:
                ts("vector", den[:, 0:ncols], den[:, 0:ncols], 1e-30, None, ALU.max, None, [B_den], [B_den])
            S.op("vector", lambda e: e.reciprocal(out=rec[:, 0:ncols], in_=den[:, 0:ncols]), reads=[B_den], writes=[B_rec])
            src_coef = rec
            Bsc = B_rec
            if gate_row is not None:
                gb, Bgb = banks[7], B_bank[7]
                mm(gb[0:64, 0:ncols], sel24[0:24, gate_row * 64:(gate_row + 1) * 64],
                   gatesT[0:24, gcols:gcols + ncols], True, True, [B_const, B_gates[gcols // 512]], [Bgb])
                tt("vector", coef[:, 0:ncols], rec[:, 0:ncols], gb[0:64, 0:ncols], ALU.mult, [B_rec, Bgb], [B_coef])
                src_coef = coef
                Bsc = B_coef
            tt("vector", out, ob[0:64, 0:ncols], src_coef[:, 0:ncols], ALU.mult, [Bob, Bsc], [Bout])

        def causal_plan(g, diag_bias, sub_bias=None):
            plan = []
            for kb in range(4 * g + 4):
                m = kb - 4 * g
                if m < 0:
                    b = {}
                    if sub_bias is not None and m == -1:
                        b[0] = sub_bias
                    plan.append((kb, 0, 4, b))
                else:
                    b = {m: diag_bias}
                    if sub_bias is not None and m + 1 < 4:
                        b[m + 1] = sub_bias
                    plan.append((kb, m, 4, b))
            return plan

        m1 = S.mark()
        spc = S.sb([8, 512], F32)
        Cc = S.sb([8, 512], F32)
        r1 = S.sb([8, 512], F32)
        r2 = S.sb([8, 512], F32)
        cb = [S.sb([8, 512], BF16) for _ in range(6)]
        carry = S.sb([8, 1], F32)
        B_f = Buf()
        memset("vector", carry[:], 0.0, [B_f])

        def f_evac(tg, bk, Bbk):
            act(spc[:], bk[0:8, :], AF.Exp, [Bbk, B_nbf], [B_f], bias=nbf[:, 0:1], scale=-1.0)
            act(spc[:], spc[:], AF.Ln, [B_f, B_const], [B_f], bias=epst[0:8, 1:2])
            S.op("vector", lambda e: e.tensor_tensor_scan(out=Cc[:], data0=spc[:], data1=spc[:], initial=carry[:, 0:1],
                                                           op0=ALU.add, op1=ALU.max), reads=[B_f], writes=[B_f])
            vcopy("vector", carry[:], Cc[:, 511:512], [B_f], [B_f])
            vcopy("vector", cb[3][:], Cc[:], [B_f], [B_f])
            tt("vector", r1[:], Cc[:], cb[3][:], ALU.subtract, [B_f], [B_f])
            vcopy("vector", cb[4][:], r1[:], [B_f], [B_f])
            tt("vector", r2[:], r1[:], cb[4][:], ALU.subtract, [B_f], [B_f])
            vcopy("vector", cb[5][:], r2[:], [B_f], [B_f])
            for j in range(3):
                ts("vector", cb[j][:], cb[3 + j][:], -1.0, None, ALU.mult, None, [B_f], [B_f])
            for j in range(6):
                ld(caug[:, j, tg * 512:(tg + 1) * 512], cb[j][:], [B_f], [B_caug])
        if _MIX_STOP >= 2:
            proj_fm("fox_f", f_evac)
        for b in range(2):
            memset("vector", Qa[b][64:70, :], 1.0, [BQaug[b]])
            memset("vector", Ka[b][64:70, :], 1.0, [BKaug[b]])
        chk(2)
        for h in range(8 if _MIX_STOP >= 3 else 0):
            b = h % 2
            proj_fm("fox_qk%d" % h, ev_multi(ev_scaled(Qa[b], 0, BQ[b], 0.125), ev_plain(Ka[b], 64, BK[b])))
            ld(Qa[b][64:67, :], caug[h, 0:3, :], [B_caug], [BQaug[b]])
            ld(Ka[b][67:70, :], caug[h, 3:6, :], [B_caug], [BKaug[b]])

            def v_evac(t, bk, Bbk, b=b):
                vcopy("vector", V1[b][:, t, 0:64], bk[:, 0:64], [Bbk], [BV[b][t // 4]])
            proj_tm("fox_v%d" % h, v_evac)
            for g in range(8):
                def fin(ob, Bob, g=g, h=h):
                    finalize(ob, Bob, 512, out=obf[:, :], Bout=B_obf)
                    stD(oT[2, h * 64:(h + 1) * 64, g * 512:(g + 1) * 512], obf[:, :], [B_obf], [B_oT[2][g]])
                attn_group(g, Qa[b], lambda gg, b=b: [BQ[b][gg], BQaug[b]], 70,
                           Ka[b], lambda kb, b=b: [BK[b][kb // 4], BKaug[b]], V1[b], BV[b],
                           causal_plan(g, cm_b[:]), fin)
            attn_flush()
        S.release(m1)

        m1 = S.mark()
        proj_fm("swa_k", ev_multi(ev_plain(Ka[0], 0, BK[0]), ev_plain(Ka[1], 64, BK[1])))

        def sv_evac(t, bk, Bbk):
            vcopy("vector", V1[0][:, t, 0:64], bk[:, 0:64], [Bbk], [BV[0][t // 4]])
            act(V1[1][:, t, 0:64], bk[:, 64:128], AF.Copy, [Bbk], [BV[1][t // 4]])
        proj_tm("swa_v", sv_evac)
        chk(3)
        for i in range(4 if _MIX_STOP >= 4 else 0):
            proj_fm("swa_q%d" % i, ev_multi(ev_scaled(Qa[0], 0, BQ[0], 0.125), ev_scaled(Qa[1], 64, BQ[1], 0.125)))
            for b in range(2):
                h = 2 * i + b
                gk = h // 4
                for g in range(8):
                    plan = []
                    for m in range(-1, 4):
                        kb = 4 * g + m
                        if kb < 0:
                            continue
                        n_lo, n_hi = max(0, m), min(4, m + 2)
                        bias = {}
                        for n in range(n_lo, n_hi):
                            bias[n] = swaT[:, h, n - m, :]
                        plan.append((kb, n_lo, n_hi, bias))

                    def fin(ob, Bob, g=g, h=h):
                        finalize(ob, Bob, 512, sink_h=h, out=obf[:, :], Bout=B_obf)
                        stD(oT[1, h * 64:(h + 1) * 64, g * 512:(g + 1) * 512], obf[:, :], [B_obf], [B_oT[1][g]])
                    attn_group(g, Qa[b], lambda gg, b=b: [BQ[b][gg]], 64,
                               Ka[gk], lambda kb, gk=gk: [BK[gk][kb // 4]], V1[gk], BV[gk], plan, fin)
            attn_flush()
        S.release(m1)

        m1 = S.mark()
        Ksel = Ka[0]
        BKsel = BK[0]
        B_oh = BKaug[0]
        Kwin = Ka[1]
        BKwin = BK[1]
        kcT = S.sb([64, 256], BF16)
        B_kcT = Buf()
        vc1 = S.sb([128, 2, 128], BF16)
        B_vc1 = Buf()
        m_cmp = S.mark()
        cwc = Ring([(S.sb([64, 8, 256], BF16), Buf()) for _ in range(2)])
        cw2t = S.sb([128, 2, 64], BF16)
        pet = S.sb([64, 32], BF16)
        B_cw2 = Buf()
        bvec = S.sb([128, 2], F32)
        B_bvec = Buf()
        gx = [S.sb([128, 256], F32) for _ in range(4)]
        B_gx = Buf()
        hid = [S.sb([128, 256], BF16) for _ in range(2)]
        B_hid = Buf()
        cmp_end = S.sb_off
        S.sb_off = m_cmp
        qt2 = [[S.sb([64, 128], BF16) for _ in range(4)] for _ in range(2)]
        B_qt2 = [[Buf() for _ in range(4)] for _ in range(2)]
        s4 = [S.sb([128, 256], F32) for _ in range(4)]
        e4 = [S.sb([128, 256], F32) for _ in range(4)]
        rs4 = [S.sb([128, 4], F32) for _ in range(4)]
        eT4 = [S.sb([128, 2, 128], BF16) for _ in range(4)]
        B_s4 = [Buf() for _ in range(4)]
        B_e4 = [Buf() for _ in range(4)]
        B_rs4 = [Buf() for _ in range(4)]
        B_eT4 = [Buf() for _ in range(4)]
        pacc2 = [S.sb([128, 260], F32) for _ in range(2)]
        B_pacc2 = [Buf(), Buf()]
        impt = S.sb([128, 64], F32)
        score = S.sb([128, 64], F32)
        top8 = S.sb([128, 8], F32)
        Mt = S.sb([128, 128], F32)
        ocst4 = [S.sb([64, 2, 128], F32) for _ in range(4)]
        B_ocst4 = [Buf() for _ in range(4)]
        fin_tmp = [(S.sb([64, 128], F32), S.sb([64, 128], F32), S.sb([64, 128], F32), Buf(), Buf(), Buf()) for _ in range(2)]
        B_imp, B_Mt = Buf(), Buf()
        S.sb_off = max(S.sb_off, cmp_end)
        osel = S.sb([64, 512], F32)
        B_osel = Buf()
        ocl = S.sb([64, 512], F32)
        B_ocl = Buf()
        wq2 = [S.sb([128, 8, 128], BF16) for _ in range(2)]
        B_wq2 = Buf()

        ld(Ksel[64:128, :], ohs, [B_ohs], [B_oh])
        memset("vector", vc1[:, :, 64:128], 1.0, [B_vc1])

        def gate_evac(tg, bk, Bbk):
            act(gatesT[0:24, tg * 512:(tg + 1) * 512], bk[0:24, :], AF.Sigmoid, [Bbk], [B_gates[tg]])
        chk(4)
        proj_fm("nsa_gate", gate_evac)
        chk(5)

        for g in range(2 if _MIX_STOP >= 5 else 0):
            S.barrier()
            proj_fm("nsa_cmp%d" % g, ev_multi(ev_plain(Ka[0], 0, BK[0]), ev_plain(Ka[1], 64, BK[1])))
            for kv in range(2):
                ld(cwstage[:, 0:128], cw2_d[l, kv], [], [B_cwstage])
                vcopy("vector", cw2t[:, :, :], cwstage[:, 0:128].rearrange("p (a b) -> p a b", b=64), [B_cwstage], [B_cw2])
                ld(cwstage[0:64, 128:160], pet_d[l, kv], [], [B_cwstage])
                vcopy("vector", pet[:], cwstage[0:64, 128:160], [B_cwstage], [B_cw2])
                bh = [bank(), bank()]
                bb = [bank(), bank()]
                src_t = Ka[kv]
                for c in range(4):
                    cw, Bcw = cwc.next()
                    ld(cw[:], cw1s[kv].rearrange("d (p c) -> d p c", c=256)[:, c * 8:(c + 1) * 8, :], [B_cw1s], [Bcw])
                    for pp in range(8):
                        p = c * 8 + pp
                        for hc in range(2):
                            mm(bh[hc][0][:, 0:255], cw[:, pp, hc * 128:(hc + 1) * 128],
                               src_t[0:64, p:p + 16 * 254 + 1:16], p == 0, p == 31,
                               [Bcw] + BK[kv], [bh[hc][1]])
                            mm(bb[hc][0][:, 0:1], cw[:, pp, hc * 128:(hc + 1) * 128], pet[:, p:p + 1], p == 0, p == 31,
                               [Bcw, B_cw2], [bb[hc][1]])
                for hc in range(2):
                    vcopy("vector", bvec[:, hc:hc + 1], bb[hc][0][:, 0:1], [bb[hc][1]], [B_bvec])
                    x1, sq, u, sg = gx
                    act(x1[:, 0:255], bh[hc][0][:, 0:255], AF.Identity, [bh[hc][1], B_bvec], [B_gx], bias=bvec[:, hc:hc + 1])
                    tt("vector", sq[:, 0:255], x1[:, 0:255], x1[:, 0:255], ALU.mult, [B_gx], [B_gx])
                    ts("vector", sq[:, 0:255], sq[:, 0:255], 0.044715, 1.0, ALU.mult, ALU.add, [B_gx], [B_gx])
                    tt("vector", u[:, 0:255], sq[:, 0:255], x1[:, 0:255], ALU.mult, [B_gx], [B_gx])
                    act(sg[:, 0:255], u[:, 0:255], AF.Sigmoid, [B_gx], [B_gx], scale=1.5957691216057308)
                    tt("vector", hid[hc][:, 0:255], x1[:, 0:255], sg[:, 0:255], ALU.mult, [B_gx], [B_hid])
                if kv == 0:
                    bk, Bbk = bank()
                    for hc in range(2):
                        mm(bk[0:64, 0:255], cw2t[:, hc, :], hid[hc][:, 0:255], hc == 0, hc == 1, [B_cw2, B_hid], [Bbk])
                    vcopy("vector", kcT[:, 0:255], bk[0:64, 0:255], [Bbk], [B_kcT])
                else:
                    for c in range(2):
                        nn = 128 if c == 0 else 127
                        bk, Bbk = bank()
                        for hc in range(2):
                            mm(bk[0:nn, 0:64], hid[hc][:, c * 128:c * 128 + nn], cw2t[:, hc, :], hc == 0, hc == 1,
                               [B_cw2, B_hid], [Bbk])
                        vcopy("vector", vc1[0:nn, c, 0:64], bk[0:nn, 0:64], [Bbk], [B_vc1])
            proj_fm("nsa_kk%d" % g, ev_multi(ev_plain(Ksel, 0, BKsel), ev_plain(Kwin, 64, BKwin)))

            def nv_evac(t, bk, Bbk):
                vcopy("vector", V1[0][:, t, 0:64], bk[:, 0:64], [Bbk], [BV[0][t // 4]])
                act(V1[1][:, t, 0:64], bk[:, 64:128], AF.Copy, [Bbk], [BV[1][t // 4]])
            proj_tm("nsa_v%d" % g, nv_evac)

            for i in range(2):
                off, w = WIN_P["nsa_q%d" % (2 * g + i)]
                ld(wq2[i][:], wins[:, 8 * off:8 * off + 1024].rearrange("p (k c) -> p k c", c=128), [B_wins], [B_wq2])
            S.barrier()
            memset("vector", Mt[:, 0:64], 0.0, [B_Mt])

            def q_stage(qb):
                par = qb % 2
                for i in range(2):
                    bk, Bbk = bank()
                    for kc in range(8):
                        mm(bk[:, 0:128], wq2[i][:, kc, :], hT[:, kc, qb * 128:(qb + 1) * 128], kc == 0, kc == 7,
                           [B_wq2, BhT[qb // 4]], [Bbk])
                    act(qt2[par][2 * i][:], bk[0:64, 0:128], AF.Copy, [Bbk], [B_qt2[par][2 * i]], scale=0.125)
                    act(qt2[par][2 * i + 1][:], bk[64:128, 0:128], AF.Copy, [Bbk], [B_qt2[par][2 * i + 1]], scale=0.125)

            q_stage(0)
            for qb in range(32):
                par = qb % 2
                tg = qb // 4
                ncols = min(255, 8 * qb + 7)
                nt = 1 if ncols <= 128 else 2
                pacc, B_pacc = pacc2[par], B_pacc2[par]
                memset("gpsimd", pacc[:], 0.0, [B_pacc])
                sbk = []
                for hp in range(4):
                    bk, Bbk = bank()
                    mm(bk[:, 0:ncols], qt2[par][hp][:], kcT[:, 0:ncols], True, True, [B_qt2[par][hp], B_kcT], [Bbk])
                    sbk.append((bk, Bbk))
                for hp in range(4):
                    h = 4 * g + hp
                    bk, Bbk = sbk[hp]
                    tt("vector", s4[hp][:, 0:ncols], bk[:, 0:ncols], tc_b[:, h, 256 - 8 * qb:256 - 8 * qb + ncols], ALU.add,
                       [Bbk, B_const], [B_s4[hp]])
                    act(e4[hp][:, 0:ncols], s4[hp][:, 0:ncols], AF.Exp, [B_s4[hp]], [B_e4[hp], B_rs4[hp]],
                        accum=rs4[hp][:, 0:1])
                if qb + 1 < 32:
                    q_stage(qb + 1)
                for hp in range(4):
                    rs = rs4[hp]
                    ts("vector", rs[:, 1:2], rs[:, 0:1], 1e-30, None, ALU.max, None, [B_rs4[hp]], [B_rs4[hp]])
                    S.op("vector", lambda e, rs=rs: e.reciprocal(out=rs[:, 2:3], in_=rs[:, 1:2]),
                         reads=[B_rs4[hp]], writes=[B_rs4[hp]])
                    if hp == 0:
                        ts("vector", pacc[:, 1:1 + ncols], e4[hp][:, 0:ncols], rs[:, 2:3], None, ALU.mult, None,
                           [B_e4[hp], B_rs4[hp]], [B_pacc])
                    else:
                        stt("vector", pacc[:, 1:1 + ncols], e4[hp][:, 0:ncols], rs[:, 2:3], pacc[:, 1:1 + ncols],
                            ALU.mult, ALU.add, [B_e4[hp], B_rs4[hp], B_pacc], [B_pacc])
                for hp in range(4):
                    h = 4 * g + hp
                    ob, Bob = banks[5 + (hp % 2)], B_bank[5 + (hp % 2)]
                    bk2, Bbk2 = bank()
                    for c in range(nt):
                        nn = min(128, ncols - c * 128)
                        tr(bk2[0:nn, c * 128:(c + 1) * 128], e4[hp][:, c * 128:c * 128 + nn], ident_f[:],
                           [B_e4[hp], B_const], [Bbk2])
                    for c in range(nt):
                        nn = min(128, ncols - c * 128)
                        act(eT4[hp][0:nn, c, :], bk2[0:nn, c * 128:(c + 1) * 128], AF.Copy, [Bbk2], [B_eT4[hp]])
                    for c in range(nt):
                        nn = min(128, ncols - c * 128)
                        mm(ob[:, 0:128], vc1[0:nn, c, :], eT4[hp][0:nn, c, :], c == 0, c == nt - 1,
                           [B_vc1, B_eT4[hp]], [Bob])
                    ocst = ocst4[hp]
                    finalize(ob, Bob, 128, gate_row=h * 3 + 0, gcols=qb * 128, clampden=True,
                             out=ocst[:, qb % 2, :], Bout=B_ocst4[hp], tmp=fin_tmp[hp % 2])
                    if qb % 2 == 1:
                        stD(ocmp[h, :, (qb - 1) * 128:(qb + 1) * 128], ocst[:, :, :].rearrange("p a b -> p (a b)"),
                            [B_ocst4[hp]], [B_ocmp[h][tg]])
                S.op("vector", lambda e, pacc=pacc: e.tensor_reduce(out=impt[:], in_=pacc[:, 0:256].rearrange("p (j m) -> p j m", m=4),
                                                                    axis=AX.X, op=ALU.add), reads=[B_pacc], writes=[B_imp])
                tt("vector", impt[:], impt[:], pacc[:, 4:260:4], ALU.add, [B_imp, B_pacc], [B_imp])
                tt("vector", score[:], impt[:], ft_f[:, 64 - 2 * qb:128 - 2 * qb], ALU.add, [B_imp, B_const], [B_imp])
                ts("vector", score[:, 0:1], score[:, 0:1], 100.0, None, ALU.add, None, [B_imp], [B_imp])
                S.op("vector", lambda e: e.max(out=top8[:], in_=score[:]), reads=[B_imp], writes=[B_imp])
                ts("vector", Mt[:, 64:128], score[:], top8[:, 7:8], None, ALU.is_ge, None, [B_imp, B_Mt], [B_Mt])
                ts("vector", Mt[:, 64:128], Mt[:, 64:128], -NEG, NEG, ALU.mult, ALU.add, [B_Mt], [B_Mt])
                bk, Bbk = bank()
                tr(bk[:, 0:128], Mt[:], ident_f[:], [B_Mt, B_const], [Bbk])
                vcopy("vector", Qa[0][64:128, qb * 128:(qb + 1) * 128], bk[64:128, 0:128], [Bbk], [BQaug[0]])
                act(Qa[1][64:128, qb * 128:(qb + 1) * 128], bk[64:128, 0:128], AF.Copy, [Bbk], [BQaug[1]])

            for i in range(2):
                proj_fm("nsa_q%d" % (2 * g + i),
                        ev_multi(ev_scaled(Qa[0], 0, BQ[0], 0.125), ev_scaled(Qa[1], 64, BQ[1], 0.125)))
                for b in range(2):
                    h = 4 * g + 2 * i + b
                    for gq in range(8):
                        def fin_sel(ob, Bob, gq=gq, h=h):
                            ld(ocl[:], ocmp[h, :, gq * 512:(gq + 1) * 512], [B_ocmp[h][gq]], [B_ocl])
                            finalize(ob, Bob, 512, gate_row=h * 3 + 1, gcols=gq * 512, out=osel[:, :], Bout=B_osel)
                            if dsel is not None:
                                ld(dsel[h, :, gq * 512:(gq + 1) * 512], osel[:, :], [B_osel], [Buf()])
                            tt("gpsimd", ocl[:], ocl[:], osel[:], ALU.add, [B_ocl, B_osel], [B_ocl])
                        attn_group(gq, Qa[b], lambda gg, b=b: [BQ[b][gg], BQaug[b]], 128,
                                   Ksel, lambda kb: [BKsel[kb // 4], B_oh], V1[0], BV[0],
                                   causal_plan(gq, nsaT[:, h, 0, :], nsaT[:, h, 1, :]), fin_sel)
                        plan = []
                        for m in range(-4, 4):
                            kb = 4 * gq + m
                            if kb < 0:
                                continue
                            n_lo, n_hi = max(0, m), min(4, m + 5)
                            bias = {}
                            for n in range(n_lo, n_hi):
                                if n - m == 0:
                                    bias[n] = nsaT[:, h, 0, :]
                                elif n - m == 1:
                                    bias[n] = nsaT[:, h, 1, :]
                                elif n - m == 4:
                                    bias[n] = lt_b[:]
                            plan.append((kb, n_lo, n_hi, bias))

                        def fin_win(ob, Bob, gq=gq, h=h):
                            finalize(ob, Bob, 512, gate_row=h * 3 + 2, gcols=gq * 512, out=osel[:, :], Bout=B_osel)
                            if dwin is not None:
                                ld(dwin[h, :, gq * 512:(gq + 1) * 512], osel[:, :], [B_osel], [Buf()])
                            tt("gpsimd", obf[:, :], ocl[:], osel[:], ALU.add, [B_ocl, B_osel], [B_obf])
                            stD(oT[0, h * 64:(h + 1) * 64, gq * 512:(gq + 1) * 512], obf[:, :], [B_obf], [B_oT[0][gq]])
                        attn_group(gq, Qa[b], lambda gg, b=b: [BQ[b][gg]], 64,
                                   Kwin, lambda kb: [BKwin[kb // 4]], V1[1], BV[1], plan, fin_win)
                attn_flush()
        S.release(m1)
        S.release(m_h)
        if not _MIX_OUT:
            S.release(m0)
            return

        lng = S.sb([128, DM], F32)
        lnb = S.sb([128, DM], F32)
        B_ln = Buf()
        ld(lng[:], lng_d[l, 1], [], [B_ln])
        ld(lnb[:], lnb_d[l, 1], [], [B_ln])
        bg_t = S.sb([128, 24], F32)
        ld(bg_t[:], bgate_d[l], [], [B_ln])
        woutt = S.sb([128, 8, DM], BF16)
        B_wo = Buf()
        ld(woutt[:], wouts.rearrange("p (k c) -> p k c", c=DM), [B_wouts], [B_wo])
        xin2 = S.sb([128, 4, DM], F32)
        Bxin2 = Buf()
        ot = [S.sb([128, 4, 512], BF16) for _ in range(3)]
        B_ot = [Buf() for _ in range(3)]
        wgr = Ring([(S.sb([128, 8, 128], BF16), Buf()) for _ in range(3)])
        wbrr = Ring([(S.sb([128, 4, 128], BF16), Buf()) for _ in range(3)])
        gsr = Ring([(S.sb([128, 512], F32), Buf()) for _ in range(2)])
        macc = S.sb([128, 512], F32)
        B_macc = Buf()
        mT = S.sb([128, 8, 512], BF16)
        BmT = [Buf() for _ in range(8)]
        tmps = Ring([ln_tmp(), ln_tmp()])
        ring5.items = [0, 1, 2, 3]
        ring5.i = 0
        ypairs = [(5, 6), (7, 4)]
        wgs3 = wgs.rearrange("p (k c) -> p k c", c=3072)
        for tg in range(8):
            ld(xin2[:], src[tg * 512:(tg + 1) * 512, :].rearrange("(n p) d -> p n d", p=128), [Bsrc[tg]], [Bxin2])
            for x in range(3):
                ld(ot[x][:], oT[x, :, tg * 512:(tg + 1) * 512].rearrange("(c p) t -> p c t", p=128), [B_oT[x][tg]], [B_ot[x]])
            for dc in range(8):
                for x in range(3):
                    pump()
                    wg, Bwg = wgr.next()
                    ld(wg[:], wgs3[:, :, x * DM + dc * 128:x * DM + (dc + 1) * 128], [B_wgs], [Bwg])
                    wb, Bwb = wbrr.next()
                    ld(wb[:], wbrs[x].rearrange("p (c d) -> p c d", d=DM)[:, :, dc * 128:(dc + 1) * 128], [B_wbrs], [Bwb])
                    bg, Bbg = bank()
                    for kc in range(8):
                        mm(bg[:, :], wg[:, kc, :], hT[:, kc, tg * 512:(tg + 1) * 512], kc == 0, kc == 7, [Bwg, BhT[tg]], [Bbg])
                    bb, Bbb = bank()
                    for c in range(4):
                        mm(bb[:, :], wb[:, c, :], ot[x][:, c, :], c == 0, c == 3, [Bwb, B_ot[x]], [Bbb])
                    gs, Bgs = gsr.next()
                    act(gs[:], bg[:, :], AF.Sigmoid, [Bbg, B_ln], [Bgs], bias=bg_t[:, x * 8 + dc:x * 8 + dc + 1])
                    if x == 0:
                        tt("vector", macc[:], gs[:], bb[:, :], ALU.mult, [Bgs, Bbb], [B_macc])
                    elif x == 1:
                        tt("vector", gs[:], gs[:], bb[:, :], ALU.mult, [Bgs, Bbb], [Bgs])
                        tt("gpsimd", macc[:], macc[:], gs[:], ALU.add, [Bgs, B_macc], [B_macc])
                    else:
                        tt("vector", gs[:], gs[:], bb[:, :], ALU.mult, [Bgs, Bbb], [Bgs])
                        tt("gpsimd", mT[:, dc, :], macc[:], gs[:], ALU.add, [Bgs, B_macc], [BmT[dc]])
            for n in range(4):
                i0, i1 = ypairs[n % 2]
                b0, b1 = banks[i0], banks[i1]
                for dc in range(8):
                    mm(b0[:, :], mT[:, dc, n * 128:(n + 1) * 128], woutt[:, dc, 0:512], dc == 0, dc == 7,
                       [BmT[dc], B_wo], [B_bank[i0]])
                    mm(b1[:, :], mT[:, dc, n * 128:(n + 1) * 128], woutt[:, dc, 512:1024], dc == 0, dc == 7,
                       [BmT[dc], B_wo], [B_bank[i1]])
                r0 = tg * 512 + n * 128
                ln_epilogue(xin2[:, n, :], b0, b1, B_bank[i0], B_bank[i1], Bxin2, lng, lnb, B_ln, tmps.next(),
                            dst[r0:r0 + 128, :], Bdst[tg] if isinstance(Bdst, list) else Bdst,
                            None if dbg_ap is None else dbg_ap[r0:r0 + 128, :])
        ring5.items = [0, 1, 2, 3, 4]
        ring5.i = 0
        S.release(m0)

    cur, Bcur = x_in, [Buf() for _ in range(8)]
    pp = 0
    prep_ffn(0, 0)
    for l in range(NL):
        for st in range(3):
            lastst = (l == NL - 1 and st == 2) or (_STOP_AFTER is not None and l * 3 + st == _STOP_AFTER - 1)
            if _STOP_AFTER is not None and l * 3 + st >= _STOP_AFTER:
                continue
            if lastst:
                dst, Bdst = y_out, B_y
            else:
                dst, Bdst = xs[pp], B_xs[pp]
            dbg_ap = dbg_d.get("l%ds%d" % (l, st))
            if st == 0:
                ffn_stage(l, 0, cur, Bcur, dst, Bdst, 0, dbg_ap)
            elif st == 1:
                mixer_stage(l, cur, Bcur, dst, Bdst, dbg_ap)
            else:
                ffn_stage(l, 1, cur, Bcur, dst, Bdst, 2, dbg_ap)
            cur, Bcur = dst, Bdst
            pp ^= 1
    S.finish()
    return nc


def _consts(rel_bias):
    c = {}
    c["ident"] = np.eye(128, dtype=np.float32)
    s = np.arange(S_LEN)
    c["onehot"] = (s[None, :] // 64 == np.arange(64)[:, None]).astype(np.float32)
    j = np.arange(128)[:, None]
    i = np.arange(128)[None, :]
    d0 = i - j
    d1 = 128 + i - j
    bk0 = _t5_bucket(d0)
    bk1 = _t5_bucket(d1)
    relT = np.ascontiguousarray(rel_bias.T)
    biasT = np.empty((16, 2, 128, 128), np.float32)
    for h in range(16):
        t0 = relT[h][bk0]
        t1 = relT[h][bk1]
        biasT[h, 0] = np.where(d0 >= 0, t0, np.float32(NEG))
        if h < 8:
            biasT[h, 1] = t1
        else:
            biasT[h, 1] = np.where(d1 < 128, t1, np.float32(NEG))
    c["biasT"] = biasT
    c["cfar"] = np.ascontiguousarray(np.broadcast_to(rel_bias[31][None, :], (128, 16))).astype(np.float32)
    ii = np.arange(128)[:, None]
    u = np.arange(512)[None, :]
    dist = ii - 16 * (u - 256) - 31
    bkc = _t5_bucket(dist)
    tc = np.empty((8, 128, 512), np.float32)
    for h in range(8):
        tc[h] = np.where(dist >= 0, relT[h][bkc], np.float32(NEG))
    c["tc"] = tc
    uu = np.arange(128)[None, :] - 64
    curb = (ii >= 64).astype(np.int64)
    ft = np.zeros((128, 128), np.float32)
    ft[np.broadcast_to(uu > curb, (128, 128))] = -100.0
    ft[np.broadcast_to((uu == curb) | (uu == curb - 1), (128, 128))] = 100.0
    c["ft"] = ft
    c["cm"] = np.where(d0 >= 0, 0.0, NEG).astype(np.float32)
    c["lt"] = np.where(i < j, 0.0, NEG).astype(np.float32)
    sel = np.zeros((24, 1536), np.float32)
    for r in range(24):
        sel[r, r * 64:(r + 1) * 64] = 1.0
    c["sel24"] = sel
    return c


def _layer_weights(inp, ls):
    L = len(ls)
    w = {}

    def stack(f):
        return np.ascontiguousarray(np.stack([f(l) for l in ls]))
    for i, (k1, k2) in enumerate((("ffn1_w1", "ffn1_w2"), ("ffn2_w1", "ffn2_w2"))):
        w["w1_%d" % i] = stack(lambda l: inp[k1][l].reshape(8, 128, 2, NFC, 128).transpose(3, 1, 0, 2, 4).reshape(NFC, 128, 2048))
        w["w2_%d" % i] = stack(lambda l: inp[k2][l].reshape(NFC, 128, DM).transpose(1, 0, 2).reshape(128, NFC * DM))
    def _win_layout(l):
        wp = inp["w_in"][l][:, WIN_PERM]
        blocks = []
        for name, (off, wd) in WIN_P.items():
            blocks.append(wp[:, off:off + wd].reshape(8, 128, wd).transpose(1, 0, 2).reshape(128, 8 * wd))
        return np.concatenate(blocks, axis=1)
    w["win"] = stack(_win_layout)
    w["wgate"] = stack(lambda l: inp["w_gate"][l].reshape(8, 128, 3072).transpose(1, 0, 2).reshape(128, 8 * 3072))
    w["bgate"] = stack(lambda l: inp["b_gate"][l].reshape(24, 128).T)
    w["wbr"] = stack(lambda l: np.stack([inp[k][l].reshape(4, 128, DM).transpose(1, 0, 2).reshape(128, 4 * DM)
                                         for k in ("w_br_a", "w_br_b", "w_br_c")]))
    w["wout"] = stack(lambda l: inp["w_out"][l].reshape(8, 128, DM).transpose(1, 0, 2).reshape(128, 8 * DM))
    w["cw1"] = stack(lambda l: np.stack([inp[k][l].reshape(32, 64, 256).transpose(1, 0, 2).reshape(64, 32 * 256)
                                         for k in ("cmp_k_w1", "cmp_v_w1")]))
    w["cw2"] = stack(lambda l: np.stack([inp[k][l].reshape(2, 128, 64).transpose(1, 0, 2).reshape(128, 128)
                                         for k in ("cmp_k_w2", "cmp_v_w2")]))
    w["pet"] = stack(lambda l: np.stack([inp[k][l].T for k in ("cmp_pe_k", "cmp_pe_v")]))
    w["lng"] = stack(lambda l: np.stack([np.broadcast_to(inp[k][l][None, :], (128, DM)) for k in ("ln1_g", "ln2_g", "ln3_g")]))
    w["lnb"] = stack(lambda l: np.stack([np.broadcast_to(inp[k][l][None, :], (128, DM)) for k in ("ln1_b", "ln2_b", "ln3_b")]))
    w["sinks"] = stack(lambda l: np.broadcast_to(inp["swa_sinks"][l][None, :], (128, 8)))
    w["bf"] = stack(lambda l: inp["fox_b_f"][l].reshape(8, 1))
    return {k: np.ascontiguousarray(v, dtype=np.float32) for k, v in w.items()}


_PROG = {}


def _get_prog(NL):
    if NL not in _PROG:
        _PROG[NL] = build_program(NL)
    return _PROG[NL]


def kernel(**inputs):
    inp = {k: np.asarray(v, dtype=np.float32) for k, v in inputs.items()}
    consts = _consts(inp["rel_bias"])
    x = inp["x"]
    nb = x.shape[0]
    NL = NLAYERS
    nc = _get_prog(NL)
    wl = _layer_weights(inp, list(range(NL)))
    in_maps = []
    for b in range(nb):
        m = {"x": np.ascontiguousarray(x[b])}
        m.update(wl)
        m.update(consts)
        in_maps.append(m)
    res = run_bass_kernel_spmd(nc, in_maps, core_ids=list(range(nb)))
    return np.stack([np.asarray(r["y"], dtype=np.float32) for r in res.results], axis=0)
```

```python
from contextlib import ExitStack
import numpy as np
import concourse.bass as bass
import concourse.mybir as mybir
from concourse.bass_utils import run_bass_kernel_spmd

F32 = mybir.dt.float32
BF16 = mybir.dt.bfloat16
AF = mybir.ActivationFunctionType
ALU = mybir.AluOpType
AX = mybir.AxisListType

ENGS = ["tensor", "vector", "scalar", "gpsimd", "sync"]
S_LEN = 4096
DM = 1024
FF = 2816
NFC = 22
DIN = 3616
NEG = -30000.0
ALPHA = 8.0 ** 0.25
LN_EPS = 1e-5
NLAYERS = 4
_STOP_AFTER = None
_MIX_STOP = 99
_MIX_OUT = True
_SUB = 99


class _Stop(Exception):
    pass


class Buf:
    __slots__ = ("w", "r", "excl")

    def __init__(self, excl=False):
        self.w = None
        self.r = {}
        self.excl = excl


class Sched:
    def __init__(self, nc, n_dma_sems=12, rot=20000):
        self.nc = nc
        self.prog = {e: [] for e in ENGS}
        self.seen = {e: {} for e in ENGS}
        self.cnt = {e: 0 for e in ENGS}
        self.epoch = {e: 0 for e in ENGS}
        self.rot = rot
        self.n_dma = n_dma_sems
        self.dma_uses = {}
        self.dma_rr = {e: 0 for e in ENGS}
        self.semkeys = []
        self.semset = set()
        self.stack = ExitStack()
        self.sb_off = 16512
        self.sb_id = 0

    def sb(self, shape, dtype):
        n = 1
        for s in shape[1:]:
            n *= s
        nbytes = n * (4 if dtype == F32 else 2)
        off = (self.sb_off + 63) // 64 * 64
        assert off + nbytes <= 229000, ("SBUF overflow", off, nbytes)
        self.sb_off = off + nbytes
        self.peak = max(getattr(self, 'peak', 0), self.sb_off)
        self.sb_id += 1
        return self.nc.alloc_sbuf_tensor_at("t%d" % self.sb_id, list(shape), dtype, offset=off)

    def mark(self):
        return self.sb_off

    def release(self, m):
        self.barrier()
        self.sb_off = m

    def _key(self, key):
        if key not in self.semset:
            self.semset.add(key)
            self.semkeys.append(key)
        return key

    def _collect(self, eng, reads, writes):
        deps = {}

        def add(tok):
            if tok is None:
                return
            k, v = tok
            if deps.get(k, 0) < v:
                deps[k] = v
        for b in reads:
            add(b.w)
            if b.excl:
                for k, v in b.r.items():
                    if k[0] != eng:
                        add((k, v))
        for b in writes:
            add(b.w)
            for k, v in b.r.items():
                add((k, v))
        waits = []
        seen = self.seen[eng]
        for k, v in deps.items():
            if eng == "tensor" and k[0] == "tensor":
                continue
            if seen.get(k, 0) >= v:
                continue
            seen[k] = v
            waits.append((k, v))
        return waits

    def _update(self, tok, reads, writes):
        k, v = tok
        for b in reads:
            if b.r.get(k, 0) < v:
                b.r[k] = v
        for b in writes:
            b.w = tok
            b.r = {}

    def op(self, eng, emit, reads=(), writes=()):
        waits = self._collect(eng, reads, writes)
        self.cnt[eng] += 1
        if self.cnt[eng] > self.rot:
            self.epoch[eng] += 1
            self.cnt[eng] = 1
        tok = (self._key((eng, self.epoch[eng])), self.cnt[eng])
        self.prog[eng].append((waits, emit, tok, 1))
        self._update(tok, reads, writes)
        return tok

    def dma(self, q, emit, reads=(), writes=()):
        i = self.dma_rr[q]
        self.dma_rr[q] = (i + 1) % self.n_dma
        key = self._key(("dma", q, i))
        k = self.dma_uses.get(key, 0) + 1
        self.dma_uses[key] = k
        waits = self._collect(q, reads, writes)
        if k > 1 and self.seen[q].get(key, 0) < 16 * (k - 1):
            self.seen[q][key] = 16 * (k - 1)
            waits.append((key, 16 * (k - 1)))
        tok = (key, 16 * k)
        self.prog[q].append((waits, emit, tok, 16))
        self._update(tok, reads, writes)
        return tok

    def _all_tokens(self):
        toks = []
        for key, k in self.dma_uses.items():
            toks.append((key, 16 * k))
        for e in ENGS:
            if self.cnt[e] > 0:
                toks.append(((e, self.epoch[e]), self.cnt[e]))
        return toks

    def barrier(self):
        toks = self._all_tokens()
        for e in ENGS:
            waits = []
            for k, v in toks:
                if self.seen[e].get(k, 0) < v:
                    self.seen[e][k] = v
                    waits.append((k, v))
            if waits:
                self.prog[e].append((waits, None, None, 0))

    def finish(self):
        nc = self.nc
        self.barrier()
        sems = {}
        for key in self.semkeys:
            nm = "s_" + "_".join(str(x) for x in key)
            sems[key] = self.stack.enter_context(nc.semaphore(nm))
        prog = self.prog
        with nc.Block() as block:
            def mk(ename):
                def body(eng):
                    for waits, emit, tok, inc in prog[ename]:
                        for k, v in waits:
                            eng.wait_ge(sems[k], v)
                        if emit is not None:
                            emit(eng).then_inc(sems[tok[0]], inc)
                return body
            block.tensor(mk("tensor"))
            block.vector(mk("vector"))
            block.scalar(mk("scalar"))
            block.gpsimd(mk("gpsimd"))
            block.sync(mk("sync"))
        self.stack.close()


class Ring:
    def __init__(self, items):
        self.items = items
        self.i = 0

    def next(self):
        it = self.items[self.i]
        self.i = (self.i + 1) % len(self.items)
        return it


def _win_passes():
    P = {}
    off = 0
    perm = []

    def add(name, cols):
        nonlocal off
        P[name] = (off, len(cols))
        perm.extend(cols)
        off += len(cols)
    r = lambda a, n: list(range(a, a + n))
    for h in range(8):
        add("fox_qk%d" % h, r(2072 + h * 64, 64) + r(2584 + h * 64, 64))
    add("fox_f", r(3608, 8))
    for h in range(8):
        add("fox_v%d" % h, r(3096 + h * 64, 64))
    for i in range(4):
        add("swa_q%d" % i, r(1304 + i * 128, 128))
    add("swa_k", r(1816, 128))
    add("swa_v", r(1944, 128))
    for i in range(4):
        add("nsa_q%d" % i, r(i * 128, 128))
    for g in range(2):
        add("nsa_cmp%d" % g, r(512 + g * 64, 64) + r(640 + g * 64, 64))
        add("nsa_kk%d" % g, r(768 + g * 64, 64) + r(1024 + g * 64, 64))
        add("nsa_v%d" % g, r(896 + g * 64, 64) + r(1152 + g * 64, 64))
    add("nsa_gate", r(1280, 24))
    assert off == DIN and sorted(perm) == list(range(DIN))
    return P, np.array(perm)


WIN_P, WIN_PERM = _win_passes()


def _t5_bucket(n):
    n = np.maximum(n, 0)
    lr = np.log(np.maximum(n, 1).astype(np.float32) / np.float32(16)) / np.float32(np.log(128 / 16))
    large = 16 + (lr.astype(np.float32) * np.float32(16)).astype(np.int32)
    return np.where(n < 16, n, np.minimum(large, 31))


def build_program(NL, dbg=None):
    nc = bass.Bass("TRN2", target_bir_lowering=False)
    S = Sched(nc)

    def din(name, shape, dt=F32):
        return nc.dram_tensor(name, list(shape), dt, kind="ExternalInput").ap()

    def dscr(name, shape, dt):
        return nc.dram_tensor(name, list(shape), dt, kind="Internal").ap()

    x_in = din("x", [S_LEN, DM])
    y_out = nc.dram_tensor("y", [S_LEN, DM], F32, kind="ExternalOutput").ap()
    w1_d = [din("w1_%d" % i, [NL, NFC, 128, 2048]) for i in range(2)]
    w2_d = [din("w2_%d" % i, [NL, 128, NFC * DM]) for i in range(2)]
    win_d = din("win", [NL, 128, 8 * DIN])
    wgate_d = din("wgate", [NL, 128, 8 * 3072])
    bgate_d = din("bgate", [NL, 128, 24])
    wbr_d = din("wbr", [NL, 3, 128, 4 * DM])
    wout_d = din("wout", [NL, 128, 8 * DM])
    cw1_d = din("cw1", [NL, 2, 64, 32 * 256])
    cw2_d = din("cw2", [NL, 2, 128, 128])
    pet_d = din("pet", [NL, 2, 64, 32])
    lng_d = din("lng", [NL, 3, 128, DM])
    lnb_d = din("lnb", [NL, 3, 128, DM])
    sinks_d = din("sinks", [NL, 128, 8])
    bf_d = din("bf", [NL, 8, 1])
    ident_d = din("ident", [128, 128])
    onehot_d = din("onehot", [64, S_LEN])
    biasT_d = din("biasT", [16, 2, 128, 128])
    cfar_d = din("cfar", [128, 16])
    tc_d = din("tc", [8, 128, 512])
    ft_d = din("ft", [128, 128])
    cm_d = din("cm", [128, 128])
    lt_d = din("lt", [128, 128])
    sel24_d = din("sel24", [24, 1536])

    xs = [dscr("xs0", [S_LEN, DM], F32), dscr("xs1", [S_LEN, DM], F32)]
    w1s = [dscr("w1s%d" % i, [NFC, 128, 2048], BF16) for i in range(2)]
    w2s = [dscr("w2s%d" % i, [128, NFC * DM], BF16) for i in range(2)]
    wins = dscr("wins", [128, 8 * DIN], BF16)
    wgs = dscr("wgs", [128, 8 * 3072], BF16)
    wbrs = dscr("wbrs", [3, 128, 4 * DM], BF16)
    wouts = dscr("wouts", [128, 8 * DM], BF16)
    cw1s = dscr("cw1s", [2, 64, 32 * 256], BF16)
    ohs = dscr("ohs", [64, S_LEN], BF16)
    if dbg and "oT" in dbg:
        oT = nc.dram_tensor("oT", [3, 512, S_LEN], BF16, kind="ExternalOutput").ap()
    else:
        oT = dscr("oT", [3, 512, S_LEN], BF16)
    if dbg and "oT" in dbg:
        ocmp = nc.dram_tensor("ocmp", [8, 64, S_LEN], F32, kind="ExternalOutput").ap()
        dsel = nc.dram_tensor("dsel", [8, 64, S_LEN], F32, kind="ExternalOutput").ap()
        dwin = nc.dram_tensor("dwin", [8, 64, S_LEN], F32, kind="ExternalOutput").ap()
    else:
        ocmp = dscr("ocmp", [8, 64, S_LEN], F32)
        dsel = dwin = None
    caug = dscr("caug", [8, 6, S_LEN], BF16)
    B_xs = [[Buf() for _ in range(8)] for _ in range(2)]
    B_w1s = [Buf(), Buf()]
    B_w2s = [Buf(), Buf()]
    B_wins, B_wgs, B_wbrs, B_wouts, B_cw1s, B_ohs = Buf(), Buf(), Buf(), Buf(), Buf(), Buf()
    B_oT = [[Buf() for _ in range(8)] for _ in range(3)]
    B_ocmp = [[Buf() for _ in range(8)] for _ in range(8)]
    B_caug = Buf()
    B_y = Buf()
    dbg_d = {}
    if dbg:
        for nm in dbg:
            if nm == "oT":
                continue
            dbg_d[nm] = nc.dram_tensor("dbg_" + nm, [S_LEN, DM], F32, kind="ExternalOutput").ap()

    banks = [S.stack.enter_context(nc.psum_tensor("bank%d" % i, [128, 512], F32)) for i in range(8)]
    B_bank = [Buf(excl=True) for _ in range(8)]
    ring5 = Ring([0, 1, 2, 3, 4])

    def bank():
        i = ring5.next()
        return banks[i], B_bank[i]

    cpy_rr = [0]

    def mm(out, lhsT, rhs, start, stop, reads, writes):
        S.op("tensor", lambda e: e.matmul(out, lhsT=lhsT, rhs=rhs, start=start, stop=stop),
             reads=reads, writes=writes)

    def tr(out, in_, ident, reads, writes):
        S.op("tensor", lambda e: e.transpose(out, in_, ident), reads=reads, writes=writes)

    def act(out, in_, func, reads, writes, bias=None, scale=None, accum=None):
        kw = {}
        if bias is not None:
            kw["bias"] = bias
        if scale is not None:
            kw["scale"] = scale
        if accum is not None:
            kw["accum_out"] = accum
        S.op("scalar", lambda e: e.activation(out=out, in_=in_, func=func, **kw), reads=reads, writes=writes)

    def vcopy(eng, out, in_, reads, writes):
        S.op(eng, lambda e: e.tensor_copy(out=out, in_=in_), reads=reads, writes=writes)

    def copy_any(out, in_, reads, writes, psum=True):
        cpy_rr[0] ^= 1
        if cpy_rr[0]:
            vcopy("vector", out, in_, reads, writes)
        else:
            act(out, in_, AF.Copy, reads, writes)

    def tt(eng, out, in0, in1, op, reads, writes):
        S.op(eng, lambda e: e.tensor_tensor(out=out, in0=in0, in1=in1, op=op), reads=reads, writes=writes)

    def ts(eng, out, in0, s1, s2, op0, op1, reads, writes):
        if op1 is None:
            S.op(eng, lambda e: e.tensor_scalar(out=out, in0=in0, scalar1=s1, scalar2=None, op0=op0),
                 reads=reads, writes=writes)
        else:
            S.op(eng, lambda e: e.tensor_scalar(out=out, in0=in0, scalar1=s1, scalar2=s2, op0=op0, op1=op1),
                 reads=reads, writes=writes)

    def stt(eng, out, in0, scalar, in1, op0, op1, reads, writes):
        S.op(eng, lambda e: e.scalar_tensor_tensor(out=out, in0=in0, scalar=scalar, in1=in1, op0=op0, op1=op1),
             reads=reads, writes=writes)

    def memset(eng, ap, val, writes):
        S.op(eng, lambda e: e.memset(ap, val), writes=writes)

    def ld(out, in_, reads, writes):
        S.dma("sync", lambda e: e.dma_start(out=out, in_=in_), reads=reads, writes=writes)

    ident_f = S.sb([128, 128], F32)
    ident_b = S.sb([128, 128], BF16)
    nsaT = S.sb([128, 8, 2, 128], BF16)
    swaT = S.sb([128, 8, 2, 128], BF16)
    cm_b = S.sb([128, 128], BF16)
    lt_b = S.sb([128, 128], BF16)
    tc_b = S.sb([128, 8, 512], BF16)
    ft_f = S.sb([128, 128], F32)
    sel24 = S.sb([24, 1536], BF16)
    cfar = S.sb([128, 16], F32)
    B_const = Buf()
    CH = 1024
    stg = Ring([(S.sb([128, CH], F32), S.sb([128, CH], BF16), Buf(), Buf()) for _ in range(3)])
    stage_f, stage_b, B_stage_f, B_stage_b = stg.items[0]
    cwstage = S.sb([128, 160], F32)
    B_cwstage = Buf()

    def stD(out, in_, reads, writes):
        S.dma("gpsimd", lambda e: e.dma_start(out=out, in_=in_), reads=reads, writes=writes)

    epst = S.sb([128, 2], F32)
    memset("vector", epst[:, 0:1], LN_EPS, [B_const])
    memset("vector", epst[:, 1:2], 1.0, [B_const])
    ld(ident_f[:], ident_d, [], [B_const])
    vcopy("vector", ident_b[:], ident_f[:], [B_const], [B_const])
    ld(cfar[:], cfar_d, [], [B_const])
    ld(ft_f[:], ft_d, [], [B_const])
    for (dst, src) in ((cm_b, cm_d), (lt_b, lt_d)):
        ld(stage_f[:, 0:128], src, [], [B_stage_f])
        vcopy("vector", dst[:], stage_f[:, 0:128], [B_stage_f], [B_const])
    for c in range(2):
        ld(stage_f[0:24, 0:768], sel24_d[:, c * 768:(c + 1) * 768], [], [B_stage_f])
        vcopy("vector", sel24[:, c * 768:(c + 1) * 768], stage_f[0:24, 0:768], [B_stage_f], [B_const])
    for h in range(8):
        ld(stage_f[:, 0:512], tc_d[h], [], [B_stage_f])
        vcopy("vector", tc_b[:, h, :], stage_f[:, 0:512], [B_stage_f], [B_const])
    for h in range(16):
        for k in range(2):
            ld(stage_f[:, 0:128], biasT_d[h, k], [], [B_stage_f])
            if h < 8:
                ts("vector", nsaT[:, h, k, :], stage_f[:, 0:128], cfar[:, h:h + 1], None, ALU.subtract, None,
                   [B_stage_f, B_const], [B_const])
            else:
                vcopy("vector", swaT[:, h - 8, k, :], stage_f[:, 0:128], [B_stage_f], [B_const])
    for c in range(4):
        ld(stage_f[0:64, :], onehot_d[:, c * 1024:(c + 1) * 1024], [], [B_stage_f])
        vcopy("vector", stage_b[0:64, :], stage_f[0:64, :], [B_stage_f], [B_stage_b])
        ld(ohs[:, c * 1024:(c + 1) * 1024], stage_b[0:64, :], [B_stage_b], [B_ohs])

    PQ = []
    inflight = []
    prep_rr = [0]

    def prep(dst, src, n, rows, Bdst, scale=None):
        for c0 in range(0, n, CH):
            PQ.append((dst, src, c0, min(CH, n - c0), rows, Bdst, scale))

    def pump(k=1):
        for _ in range(k):
            if inflight and (len(inflight) >= 2 or not PQ):
                (dst, src, c0, w, rows, Bdst, scale), (sf, sbt, Bsf, Bsb) = inflight.pop(0)
                prep_rr[0] ^= 1
                ceng = "vector" if prep_rr[0] else "gpsimd"
                if scale is None:
                    vcopy(ceng, sbt[0:rows, 0:w], sf[0:rows, 0:w], [Bsf], [Bsb])
                else:
                    ts(ceng, sbt[0:rows, 0:w], sf[0:rows, 0:w], scale, None, ALU.mult, None, [Bsf], [Bsb])
                stD(dst[:, c0:c0 + w], sbt[0:rows, 0:w], [Bsb], [Bdst])
            if PQ:
                task = PQ.pop(0)
                bufs = stg.next()
                (dst, src, c0, w, rows, Bdst, scale) = task
                stD(bufs[0][0:rows, 0:w], src[:, c0:c0 + w], [], [bufs[2]])
                inflight.append((task, bufs))

    def prep_flush():
        while PQ or inflight:
            pump()

    def prep_ffn(l, which):
        prep(w2s[which], w2_d[which][l], NFC * DM, 128, B_w2s[which], scale=0.5)
        for fc in range(NFC):
            prep(w1s[which][fc], w1_d[which][l, fc], 2048, 128, B_w1s[which])

    def prep_mixer(l):
        prep(wins, win_d[l], 8 * DIN, 128, B_wins)
        prep(wgs, wgate_d[l], 8 * 3072, 128, B_wgs)
        for x in range(3):
            prep(wbrs[x], wbr_d[l, x], 4 * DM, 128, B_wbrs)
        prep(wouts, wout_d[l], 8 * DM, 128, B_wouts)
        for kv in range(2):
            prep(cw1s[kv], cw1_d[l, kv], 32 * 256, 64, B_cw1s)

    base_mark = S.mark()

    def ln_epilogue(xrow, b0, b1, Bb0, Bb1, Bx, lng, lnb, B_ln, tmp, dst_ap, Bdst, dbg_ap=None):
        z, junk, xo, st, B_z, B_junk, B_xo, B_st = tmp
        stt("vector", z[:, 0:512], xrow[:, 0:512], ALPHA, b0[:, :], ALU.mult, ALU.add, [Bx, Bb0], [B_z])
        stt("vector", z[:, 512:1024], xrow[:, 512:1024], ALPHA, b1[:, :], ALU.mult, ALU.add, [Bx, Bb1], [B_z])
        act(junk[:], z[:], AF.Copy, [B_z], [B_junk, B_st], accum=st[:, 0:1])
        act(junk[:], z[:], AF.Square, [B_z], [B_junk, B_st], accum=st[:, 1:2])
        ts("vector", st[:, 2:4], st[:, 0:2], 1.0 / DM, None, ALU.mult, None, [B_st], [B_st])
        stt("vector", st[:, 4:5], st[:, 2:3], -1.0, st[:, 2:3], ALU.mult, ALU.mult, [B_st], [B_st])
        tt("vector", st[:, 5:6], st[:, 4:5], st[:, 3:4], ALU.add, [B_st], [B_st])
        act(st[:, 8:9], st[:, 5:6], AF.Sqrt, [B_st, B_const], [B_st], bias=epst[:, 0:1])
        S.op("vector", lambda e: e.reciprocal(out=st[:, 6:7], in_=st[:, 8:9]), reads=[B_st], writes=[B_st])
        stt("vector", st[:, 7:8], st[:, 2:3], -1.0, st[:, 6:7], ALU.mult, ALU.mult, [B_st], [B_st])
        act(xo[:], z[:], AF.Identity, [B_z, B_st], [B_xo], bias=st[:, 7:8], scale=st[:, 6:7])
        tt("vector", xo[:], xo[:], lng[:], ALU.mult, [B_xo, B_ln], [B_xo])
        tt("gpsimd", xo[:], xo[:], lnb[:], ALU.add, [B_xo, B_ln], [B_xo])
        stD(dst_ap, xo[:], [B_xo], [Bdst])
        if dbg_ap is not None:
            ld(dbg_ap, xo[:], [B_xo], [Buf()])

    def ln_tmp():
        z = S.sb([128, DM], F32)
        junk = S.sb([128, DM], F32)
        xo = S.sb([128, DM], F32)
        st = S.sb([128, 10], F32)
        return (z, junk, xo, st, Buf(), Buf(), Buf(), Buf())

    def load_xT(xin, Bxin, xT, BxT, kcs=range(8)):
        for kc in kcs:
            bk, Bbk = bank()
            for n in range(4):
                tr(bk[:, n * 128:(n + 1) * 128], xin[:, n, kc * 128:(kc + 1) * 128], ident_f[:], [Bxin, B_const], [Bbk])
            copy_any(xT[:, kc, :], bk[:, :], [Bbk], [BxT])

    def ffn_stage(l, which, src, Bsrc, dst, Bdst, lnidx, dbg_ap=None):
        m0 = S.mark()
        prep_flush()
        if which == 0:
            prep_mixer(l)
        elif l + 1 < NL:
            prep_ffn(l + 1, 0)
        lng = S.sb([128, DM], F32)
        lnb = S.sb([128, DM], F32)
        B_ln = Buf()
        ld(lng[:], lng_d[l, lnidx], [], [B_ln])
        ld(lnb[:], lnb_d[l, lnidx], [], [B_ln])
        xin_r = Ring([(S.sb([128, 4, DM], F32), Buf()) for _ in range(2)])
        xT_r = Ring([(S.sb([128, 8, 512], BF16), Buf()) for _ in range(2)])
        w1p = Ring([(S.sb([128, 8, 256], BF16), Buf()) for _ in range(3)])
        hT = S.sb([128, NFC, 512], BF16)
        BhT = [Buf() for _ in range(NFC)]
        w2t = S.sb([128, NFC, DM], BF16)
        Bw2t = Buf()
        sgr = Ring([(S.sb([128, 512], F32), Buf()) for _ in range(2)])
        tmps = Ring([ln_tmp(), ln_tmp()])
        ring5.items = [0, 1, 2, 3]
        ring5.i = 0
        ypairs = [(5, 6), (7, 4)]
        ld(w2t[:], w2s[which].rearrange("p (f d) -> p f d", d=DM), [B_w2s[which]], [Bw2t])
        xin, Bxin = xin_r.next()
        ld(xin[:], src[0:512, :].rearrange("(n p) d -> p n d", p=128), [Bsrc[0]], [Bxin])
        xT, BxT = xT_r.next()
        load_xT(xin, Bxin, xT, BxT)
        for tg in range(8):
            for fc in range(NFC):
                pump()
                wp, Bwp = w1p.next()
                ld(wp[:], w1s[which][fc].rearrange("p (k c) -> p k c", c=256), [B_w1s[which]], [Bwp])
                bg, Bbg = bank()
                for kc in range(8):
                    mm(bg[:, :], wp[:, kc, 0:128], xT[:, kc, :], kc == 0, kc == 7, [Bwp, BxT], [Bbg])
                bu, Bbu = bank()
                for kc in range(8):
                    mm(bu[:, :], wp[:, kc, 128:256], xT[:, kc, :], kc == 0, kc == 7, [Bwp, BxT], [Bbu])
                sg, Bsg = sgr.next()
                act(sg[:], bg[:, :], AF.Silu, [Bbg], [Bsg])
                tt("vector", hT[:, fc, :], sg[:], bu[:, :], ALU.mult, [Bsg, Bbu], [BhT[fc]])
            if tg + 1 < 8:
                xin_n, Bxin_n = xin_r.next()
                ld(xin_n[:], src[(tg + 1) * 512:(tg + 2) * 512, :].rearrange("(n p) d -> p n d", p=128),
                   [Bsrc[tg + 1]], [Bxin_n])
                xT_n, BxT_n = xT_r.next()
            for n in range(4):
                i0, i1 = ypairs[n % 2]
                b0, b1 = banks[i0], banks[i1]
                for fc in range(NFC):
                    mm(b0[:, :], hT[:, fc, n * 128:(n + 1) * 128], w2t[:, fc, 0:512], fc == 0, fc == NFC - 1,
                       [BhT[fc], Bw2t], [B_bank[i0]])
                    mm(b1[:, :], hT[:, fc, n * 128:(n + 1) * 128], w2t[:, fc, 512:1024], fc == 0, fc == NFC - 1,
                       [BhT[fc], Bw2t], [B_bank[i1]])
                if tg + 1 < 8:
                    load_xT(xin_n, Bxin_n, xT_n, BxT_n, kcs=range(2 * n, 2 * n + 2))
                r0 = tg * 512 + n * 128
                ln_epilogue(xin[:, n, :], b0, b1, B_bank[i0], B_bank[i1], Bxin, lng, lnb, B_ln, tmps.next(),
                            dst[r0:r0 + 128, :], Bdst[tg] if isinstance(Bdst, list) else Bdst,
                            None if dbg_ap is None else dbg_ap[r0:r0 + 128, :])
            if tg + 1 < 8:
                xin, Bxin, xT, BxT = xin_n, Bxin_n, xT_n, BxT_n
        ring5.items = [0, 1, 2, 3, 4]
        ring5.i = 0
        S.release(m0)

    def mixer_stage(l, src, Bsrc, dst, Bdst, dbg_ap=None):
        m00 = S.mark()
        try:
            mixer_stage_(l, src, Bsrc, dst, Bdst, dbg_ap)
        except _Stop:
            S.release(m00)

    def chk(n):
        if _SUB <= n:
            raise _Stop()

    def mixer_stage_(l, src, Bsrc, dst, Bdst, dbg_ap=None):
        m0 = S.mark()
        prep_flush()
        prep_ffn(l, 1)
        wins3 = wins.rearrange("p (k c) -> p k c", c=DIN)

        hT = S.sb([128, 8, S_LEN], BF16)
        BhT = [Buf() for _ in range(8)]
        m_h = S.mark()
        xin = S.sb([128, 4, DM], F32)
        Bxin = Buf()
        xTt = S.sb([128, 8, 512], BF16)
        for tg in range(8):
            ld(xin[:], src[tg * 512:(tg + 1) * 512, :].rearrange("(n p) d -> p n d", p=128), [Bsrc[tg]], [Bxin])
            for kc in range(8):
                bk, Bbk = bank()
                for n in range(4):
                    tr(bk[:, n * 128:(n + 1) * 128], xin[:, n, kc * 128:(kc + 1) * 128], ident_f[:], [Bxin, B_const], [Bbk])
                copy_any(hT[:, kc, tg * 512:(tg + 1) * 512], bk[:, :], [Bbk], [BhT[tg]])
        S.release(m_h)

        Qa = [S.sb([128, S_LEN], BF16) for _ in range(2)]
        BQ = [[Buf() for _ in range(8)] for _ in range(2)]
        BQaug = [Buf(), Buf()]
        Ka = [S.sb([128, S_LEN], BF16) for _ in range(2)]
        BK = [[Buf() for _ in range(8)] for _ in range(2)]
        BKaug = [Buf(), Buf()]
        V1 = [S.sb([128, 32, 128], BF16) for _ in range(2)]
        BV = [[Buf() for _ in range(8)] for _ in range(2)]
        wpr = Ring([(S.sb([128, 1024], BF16), Buf()) for _ in range(2)])
        ptr = Ring([(S.sb([128, 512], BF16), Buf()) for _ in range(4)])
        den_g = S.sb([64, 512], F32)
        rec_g = S.sb([64, 512], F32)
        coef_g = S.sb([64, 512], F32)
        obf = S.sb([64, 512], BF16)
        B_den_g, B_rec_g, B_coef_g, B_obf = Buf(), Buf(), Buf(), Buf()
        gatesT = S.sb([24, S_LEN], BF16)
        B_gates = [Buf() for _ in range(8)]
        expsink = S.sb([128, 8], F32)
        B_sink = Buf()
        nbf = S.sb([8, 1], F32)
        B_nbf = Buf()
        if _MIX_STOP <= -1:
            S.release(m0)
            return
        for b in range(2):
            for tg in range(8):
                memset("gpsimd", V1[b][:, tg * 4:(tg + 1) * 4, 64:128], 1.0, [BV[b][tg]])
        ld(expsink[:], sinks_d[l], [], [B_sink])
        act(expsink[:], expsink[:], AF.Exp, [B_sink], [B_sink])
        ld(nbf[:], bf_d[l], [], [B_nbf])
        ts("vector", nbf[:], nbf[:], -1.0, None, ALU.mult, None, [B_nbf], [B_nbf])
        chk(1)

        def load_w(name, c0=0, wd=None):
            off, w = WIN_P[name]
            if wd is None:
                wd = w
            wpf, Bwp = wpr.next()
            wp = wpf[:, 0:8 * wd].rearrange("p (k c) -> p k c", c=wd)
            ld(wpf[:, 0:8 * wd], wins[:, 8 * off:8 * off + 8 * wd], [B_wins], [Bwp])
            return wp, Bwp, wd

        def proj_fm(name, evac, c0=0, wd=None, tgs=range(8)):
            wp, Bwp, wd = load_w(name, c0, wd)
            for tg in tgs:
                bk, Bbk = bank()
                for kc in range(8):
                    mm(bk[0:wd, :], wp[:, kc, 0:wd], hT[:, kc, tg * 512:(tg + 1) * 512], kc == 0, kc == 7,
                       [Bwp, BhT[tg]], [Bbk])
                evac(tg, bk, Bbk)

        def proj_tm(name, evac):
            wp, Bwp, wd = load_w(name)
            for t in range(32):
                bk, Bbk = bank()
                for kc in range(8):
                    mm(bk[:, 0:wd], hT[:, kc, t * 128:(t + 1) * 128], wp[:, kc, 0:wd], kc == 0, kc == 7,
                       [Bwp, BhT[t // 4]], [Bbk])
                evac(t, bk, Bbk)

        def ev_scaled(dst, r0, Bd, scale):
            def f(tg, bk, Bbk):
                if r0 == 0:
                    ts("vector", dst[0:64, tg * 512:(tg + 1) * 512], bk[0:64, :], scale, None, ALU.mult, None,
                       [Bbk], [Bd[tg]])
                else:
                    act(dst[0:64, tg * 512:(tg + 1) * 512], bk[r0:r0 + 64, :], AF.Copy, [Bbk], [Bd[tg]], scale=scale)
            return f

        def ev_plain(dst, r0, Bd):
            def f(tg, bk, Bbk):
                vcopy("vector", dst[0:64, tg * 512:(tg + 1) * 512], bk[r0:r0 + 64, :], [Bbk], [Bd[tg]])
            return f

        def ev_multi(*fs):
            def f(tg, bk, Bbk):
                for g in fs:
                    g(tg, bk, Bbk)
            return f

        AJ = []
        job_ctr = [0]
        LOOK = 3

        def attn_group(g, Q, Qreads, kdim, K, Kreads, Vt, BVt, plan, fin):
            AJ.append((g, Q, Qreads, kdim, K, Kreads, Vt, BVt, plan, fin))

        def attn_flush():
            pend = []

            def emit_qk(job, stt_, item):
                (g, Q, Qreads, kdim, K, Kreads, Vt, BVt, plan, fin) = job
                (kb, n_lo, n_hi, bias) = item
                bk, Bbk = bank()
                c0, c1 = n_lo * 128, n_hi * 128
                nb = len(bias)
                mm(bk[:, c0:c1], K[0:kdim, kb * 128:(kb + 1) * 128], Q[0:kdim, g * 512 + c0:g * 512 + c1],
                   True, nb == 0, Kreads(kb) + Qreads(g), [Bbk])
                for bi, (n, bap) in enumerate(sorted(bias.items())):
                    mm(bk[:, n * 128:(n + 1) * 128], ident_b[:], bap, False, bi == nb - 1, [B_const], [Bbk])
                return (job, stt_, item, bk, Bbk)

            def emit_pv(rec_):
                job, stt_, (kb, n_lo, n_hi, bias), bk, Bbk = rec_
                Vt, BVt = job[6], job[7]
                ob, Bob = stt_["ob"], stt_["Bob"]
                ntouch, total = stt_["ntouch"], stt_["total"]
                c0, c1 = n_lo * 128, n_hi * 128
                pt, Bpt = ptr.next()
                act(pt[:, c0:c1], bk[:, c0:c1], AF.Exp, [Bbk], [Bpt])
                n = n_lo
                while n < n_hi:
                    last = ntouch[n] == total[n] - 1
                    n2 = n + 1
                    while n2 < n_hi and (ntouch[n2] == total[n2] - 1) == last:
                        n2 += 1
                    S.op("tensor", lambda e, o_=ob[:, n * 128:n2 * 128], l_=Vt[:, kb, :], r_=pt[:, n * 128:n2 * 128],
                         st_=(stt_["npv"] == 0), sp_=last: e.matmul(o_, lhsT=l_, rhs=r_, start=st_, stop=sp_, skip_group_check=True),
                         reads=[BVt[kb // 4], Bpt], writes=[Bob])
                    stt_["npv"] += 1
                    for k in range(n, n2):
                        ntouch[k] += 1
                    n = n2
                stt_["left"] -= 1
                if stt_["left"] == 0:
                    stt_["fin"](ob, Bob)

            for job in AJ:
                plan, fin = job[8], job[9]
                jb = job_ctr[0]
                job_ctr[0] += 1
                stt_ = dict(ob=banks[5 + jb % 2], Bob=B_bank[5 + jb % 2], ntouch=[0] * 4, total=[0] * 4, npv=0,
                            left=len(plan), fin=fin)
                for (kb, n_lo, n_hi, bias) in plan:
                    for n in range(n_lo, n_hi):
                        stt_["total"][n] += 1
                pump()
                for item in plan:
                    pend.append(emit_qk(job, stt_, item))
                    if len(pend) > LOOK:
                        emit_pv(pend.pop(0))
            while pend:
                emit_pv(pend.pop(0))
            del AJ[:]

        def finalize(ob, Bob, ncols, gate_row=None, gcols=None, sink_h=None, clampden=False, out=None, Bout=None,
                     tmp=None):
            if tmp is None:
                tmp = (den_g, rec_g, coef_g, B_den_g, B_rec_g, B_coef_g)
            den, rec, coef, B_den, B_rec, B_coef = tmp
            if ncols >= 512:
                vcopy("vector", den[:, 0:ncols], ob[64:128, 0:ncols], [Bob], [B_den])
            else:
                act(den[:, 0:ncols], ob[64:128, 0:ncols], AF.Copy, [Bob], [B_den])
            if sink_h is not None:
                ts("vector", den[:, 0:ncols], den[:, 0:ncols], expsink[0:64, sink_h:sink_h + 1], None, ALU.add, None,
                   [B_den, B_sink], [B_den])
            if clampden:
                ts("vector", den[:, 0:ncols], den[:, 0:ncols], 1e-30, None, ALU.max, None, [B_den], [B_den])
            S.op("vector", lambda e: e.reciprocal(out=rec[:, 0:ncols], in_=den[:, 0:ncols]), reads=[B_den], writes=[B_rec])
            src_coef = rec
            Bsc = B_rec
            if gate_row is not None:
                gb, Bgb = banks[7], B_bank[7]
                mm(gb[0:64, 0:ncols], sel24[0:24, gate_row * 64:(gate_row + 1) * 64],
                   gatesT[0:24, gcols:gcols + ncols], True, True, [B_const, B_gates[gcols // 512]], [Bgb])
                tt("vector", coef[:, 0:ncols], rec[:, 0:ncols], gb[0:64, 0:ncols], ALU.mult, [B_rec, Bgb], [B_coef])
                src_coef = coef
                Bsc = B_coef
            tt("vector", out, ob[0:64, 0:ncols], src_coef[:, 0:ncols], ALU.mult, [Bob, Bsc], [Bout])

        def causal_plan(g, diag_bias, sub_bias=None):
            plan = []
            for kb in range(4 * g + 4):
                m = kb - 4 * g
                if m < 0:
                    b = {}
                    if sub_bias is not None and m == -1:
                        b[0] = sub_bias
                    plan.append((kb, 0, 4, b))
                else:
                    b = {m: diag_bias}
                    if sub_bias is not None and m + 1 < 4:
                        b[m + 1] = sub_bias
                    plan.append((kb, m, 4, b))
            return plan

        m1 = S.mark()
        spc = S.sb([8, 512], F32)
        Cc = S.sb([8, 512], F32)
        r1 = S.sb([8, 512], F32)
        r2 = S.sb([8, 512], F32)
        cb = [S.sb([8, 512], BF16) for _ in range(6)]
        carry = S.sb([8, 1], F32)
        B_f = Buf()
        memset("vector", carry[:], 0.0, [B_f])

        def f_evac(tg, bk, Bbk):
            act(spc[:], bk[0:8, :], AF.Exp, [Bbk, B_nbf], [B_f], bias=nbf[:, 0:1], scale=-1.0)
            act(spc[:], spc[:], AF.Ln, [B_f, B_const], [B_f], bias=epst[0:8, 1:2])
            S.op("vector", lambda e: e.tensor_tensor_scan(out=Cc[:], data0=spc[:], data1=spc[:], initial=carry[:, 0:1],
                                                           op0=ALU.add, op1=ALU.max), reads=[B_f], writes=[B_f])
            vcopy("vector", carry[:], Cc[:, 511:512], [B_f], [B_f])
            vcopy("vector", cb[3][:], Cc[:], [B_f], [B_f])
            tt("vector", r1[:], Cc[:], cb[3][:], ALU.subtract, [B_f], [B_f])
            vcopy("vector", cb[4][:], r1[:], [B_f], [B_f])
            tt("vector", r2[:], r1[:], cb[4][:], ALU.subtract, [B_f], [B_f])
            vcopy("vector", cb[5][:], r2[:], [B_f], [B_f])
            for j in range(3):
                ts("vector", cb[j][:], cb[3 + j][:], -1.0, None, ALU.mult, None, [B_f], [B_f])
            for j in range(6):
                ld(caug[:, j, tg * 512:(tg + 1) * 512], cb[j][:], [B_f], [B_caug])
        if _MIX_STOP >= 2:
            proj_fm("fox_f", f_evac)
        for b in range(2):
            memset("vector", Qa[b][64:70, :], 1.0, [BQaug[b]])
            memset("vector", Ka[b][64:70, :], 1.0, [BKaug[b]])
        chk(2)
        for h in range(8 if _MIX_STOP >= 3 else 0):
            b = h % 2
            proj_fm("fox_qk%d" % h, ev_multi(ev_scaled(Qa[b], 0, BQ[b], 0.125), ev_plain(Ka[b], 64, BK[b])))
            ld(Qa[b][64:67, :], caug[h, 0:3, :], [B_caug], [BQaug[b]])
            ld(Ka[b][67:70, :], caug[h, 3:6, :], [B_caug], [BKaug[b]])

            def v_evac(t, bk, Bbk, b=b):
                vcopy("vector", V1[b][:, t, 0:64], bk[:, 0:64], [Bbk], [BV[b][t // 4]])
            proj_tm("fox_v%d" % h, v_evac)
            for g in range(8):
                def fin(ob, Bob, g=g, h=h):
                    finalize(ob, Bob, 512, out=obf[:, :], Bout=B_obf)
                    stD(oT[2, h * 64:(h + 1) * 64, g * 512:(g + 1) * 512], obf[:, :], [B_obf], [B_oT[2][g]])
                attn_group(g, Qa[b], lambda gg, b=b: [BQ[b][gg], BQaug[b]], 70,
                           Ka[b], lambda kb, b=b: [BK[b][kb // 4], BKaug[b]], V1[b], BV[b],
                           causal_plan(g, cm_b[:]), fin)
            attn_flush()
        S.release(m1)

        m1 = S.mark()
        proj_fm("swa_k", ev_multi(ev_plain(Ka[0], 0, BK[0]), ev_plain(Ka[1], 64, BK[1])))

        def sv_evac(t, bk, Bbk):
            vcopy("vector", V1[0][:, t, 0:64], bk[:, 0:64], [Bbk], [BV[0][t // 4]])
            act(V1[1][:, t, 0:64], bk[:, 64:128], AF.Copy, [Bbk], [BV[1][t // 4]])
        proj_tm("swa_v", sv_evac)
        chk(3)
        for i in range(4 if _MIX_STOP >= 4 else 0):
            proj_fm("swa_q%d" % i, ev_multi(ev_scaled(Qa[0], 0, BQ[0], 0.125), ev_scaled(Qa[1], 64, BQ[1], 0.125)))
            for b in range(2):
                h = 2 * i + b
                gk = h // 4
                for g in range(8):
                    plan = []
                    for m in range(-1, 4):
                        kb = 4 * g + m
                        if kb < 0:
                            continue
                        n_lo, n_hi = max(0, m), min(4, m + 2)
                        bias = {}
                        for n in range(n_lo, n_hi):
                            bias[n] = swaT[:, h, n - m, :]
                        plan.append((kb, n_lo, n_hi, bias))

                    def fin(ob, Bob, g=g, h=h):
                        finalize(ob, Bob, 512, sink_h=h, out=obf[:, :], Bout=B_obf)
                        stD(oT[1, h * 64:(h + 1) * 64, g * 512:(g + 1) * 512], obf[:, :], [B_obf], [B_oT[1][g]])
                    attn_group(g, Qa[b], lambda gg, b=b: [BQ[b][gg]], 64,
                               Ka[gk], lambda kb, gk=gk: [BK[gk][kb // 4]], V1[gk], BV[gk], plan, fin)
            attn_flush()
        S.release(m1)

        m1 = S.mark()
        Ksel = Ka[0]
        BKsel = BK[0]
        B_oh = BKaug[0]
        Kwin = Ka[1]
        BKwin = BK[1]
        kcT = S.sb([64, 256], BF16)
        B_kcT = Buf()
        vc1 = S.sb([128, 2, 128], BF16)
        B_vc1 = Buf()
        m_cmp = S.mark()
        cwc = Ring([(S.sb([64, 8, 256], BF16), Buf()) for _ in range(2)])
        cw2t = S.sb([128, 2, 64], BF16)
        pet = S.sb([64, 32], BF16)
        B_cw2 = Buf()
        bvec = S.sb([128, 2], F32)
        B_bvec = Buf()
        gx = [S.sb([128, 256], F32) for _ in range(4)]
        B_gx = Buf()
        hid = [S.sb([128, 256], BF16) for _ in range(2)]
        B_hid = Buf()
        cmp_end = S.sb_off
        S.sb_off = m_cmp
        qt2 = [[S.sb([64, 128], BF16) for _ in range(4)] for _ in range(2)]
        B_qt2 = [[Buf() for _ in range(4)] for _ in range(2)]
        s4 = [S.sb([128, 256], F32) for _ in range(4)]
        e4 = [S.sb([128, 256], F32) for _ in range(4)]
        rs4 = [S.sb([128, 4], F32) for _ in range(4)]
        eT4 = [S.sb([128, 2, 128], BF16) for _ in range(4)]
        B_s4 = [Buf() for _ in range(4)]
        B_e4 = [Buf() for _ in range(4)]
        B_rs4 = [Buf() for _ in range(4)]
        B_eT4 = [Buf() for _ in range(4)]
        pacc2 = [S.sb([128, 260], F32) for _ in range(2)]
        B_pacc2 = [Buf(), Buf()]
        impt = S.sb([128, 64], F32)
        score = S.sb([128, 64], F32)
        top8 = S.sb([128, 8], F32)
        Mt = S.sb([128, 128], F32)
        ocst4 = [S.sb([64, 2, 128], F32) for _ in range(4)]
        B_ocst4 = [Buf() for _ in range(4)]
        fin_tmp = [(S.sb([64, 128], F32), S.sb([64, 128], F32), S.sb([64, 128], F32), Buf(), Buf(), Buf()) for _ in range(2)]
        B_imp, B_Mt = Buf(), Buf()
        S.sb_off = max(S.sb_off, cmp_end)
        osel = S.sb([64, 512], F32)
        B_osel = Buf()
        ocl = S.sb([64, 512], F32)
        B_ocl = Buf()
        wq2 = [S.sb([128, 8, 128], BF16) for _ in range(2)]
        B_wq2 = Buf()

        ld(Ksel[64:128, :], ohs, [B_ohs], [B_oh])
        memset("vector", vc1[:, :, 64:128], 1.0, [B_vc1])

        def gate_evac(tg, bk, Bbk):
            act(gatesT[0:24, tg * 512:(tg + 1) * 512], bk[0:24, :], AF.Sigmoid, [Bbk], [B_gates[tg]])
        chk(4)
        proj_fm("nsa_gate", gate_evac)
        chk(5)

        for g in range(2 if _MIX_STOP >= 5 else 0):
            S.barrier()
            proj_fm("nsa_cmp%d" % g, ev_multi(ev_plain(Ka[0], 0, BK[0]), ev_plain(Ka[1], 64, BK[1])))
            for kv in range(2):
                ld(cwstage[:, 0:128], cw2_d[l, kv], [], [B_cwstage])
                vcopy("vector", cw2t[:, :, :], cwstage[:, 0:128].rearrange("p (a b) -> p a b", b=64), [B_cwstage], [B_cw2])
                ld(cwstage[0:64, 128:160], pet_d[l, kv], [], [B_cwstage])
                vcopy("vector", pet[:], cwstage[0:64, 128:160], [B_cwstage], [B_cw2])
                bh = [bank(), bank()]
                bb = [bank(), bank()]
                src_t = Ka[kv]
                for c in range(4):
                    cw, Bcw = cwc.next()
                    ld(cw[:], cw1s[kv].rearrange("d (p c) -> d p c", c=256)[:, c * 8:(c + 1) * 8, :], [B_cw1s], [Bcw])
                    for pp in range(8):
                        p = c * 8 + pp
                        for hc in range(2):
                            mm(bh[hc][0][:, 0:255], cw[:, pp, hc * 128:(hc + 1) * 128],
                               src_t[0:64, p:p + 16 * 254 + 1:16], p == 0, p == 31,
                               [Bcw] + BK[kv], [bh[hc][1]])
                            mm(bb[hc][0][:, 0:1], cw[:, pp, hc * 128:(hc + 1) * 128], pet[:, p:p + 1], p == 0, p == 31,
                               [Bcw, B_cw2], [bb[hc][1]])
                for hc in range(2):
                    vcopy("vector", bvec[:, hc:hc + 1], bb[hc][0][:, 0:1], [bb[hc][1]], [B_bvec])
                    x1, sq, u, sg = gx
                    act(x1[:, 0:255], bh[hc][0][:, 0:255], AF.Identity, [bh[hc][1], B_bvec], [B_gx], bias=bvec[:, hc:hc + 1])
                    tt("vector", sq[:, 0:255], x1[:, 0:255], x1[:, 0:255], ALU.mult, [B_gx], [B_gx])
                    ts("vector", sq[:, 0:255], sq[:, 0:255], 0.044715, 1.0, ALU.mult, ALU.add, [B_gx], [B_gx])
                    tt("vector", u[:, 0:255], sq[:, 0:255], x1[:, 0:255], ALU.mult, [B_gx], [B_gx])
                    act(sg[:, 0:255], u[:, 0:255], AF.Sigmoid, [B_gx], [B_gx], scale=1.5957691216057308)
                    tt("vector", hid[hc][:, 0:255], x1[:, 0:255], sg[:, 0:255], ALU.mult, [B_gx], [B_hid])
                if kv == 0:
                    bk, Bbk = bank()
                    for hc in range(2):
                        mm(bk[0:64, 0:255], cw2t[:, hc, :], hid[hc][:, 0:255], hc == 0, hc == 1, [B_cw2, B_hid], [Bbk])
                    vcopy("vector", kcT[:, 0:255], bk[0:64, 0:255], [Bbk], [B_kcT])
                else:
                    for c in range(2):
                        nn = 128 if c == 0 else 127
                        bk, Bbk = bank()
                        for hc in range(2):
                            mm(bk[0:nn, 0:64], hid[hc][:, c * 128:c * 128 + nn], cw2t[:, hc, :], hc == 0, hc == 1,
                               [B_cw2, B_hid], [Bbk])
                        vcopy("vector", vc1[0:nn, c, 0:64], bk[0:nn, 0:64], [Bbk], [B_vc1])
            proj_fm("nsa_kk%d" % g, ev_multi(ev_plain(Ksel, 0, BKsel), ev_plain(Kwin, 64, BKwin)))

            def nv_evac(t, bk, Bbk):
                vcopy("vector", V1[0][:, t, 0:64], bk[:, 0:64], [Bbk], [BV[0][t // 4]])
                act(V1[1][:, t, 0:64], bk[:, 64:128], AF.Copy, [Bbk], [BV[1][t // 4]])
            proj_tm("nsa_v%d" % g, nv_evac)

            for i in range(2):
                off, w = WIN_P["nsa_q%d" % (2 * g + i)]
                ld(wq2[i][:], wins[:, 8 * off:8 * off + 1024].rearrange("p (k c) -> p k c", c=128), [B_wins], [B_wq2])
            S.barrier()
            memset("vector", Mt[:, 0:64], 0.0, [B_Mt])

            def q_stage(qb):
                par = qb % 2
                for i in range(2):
                    bk, Bbk = bank()
                    for kc in range(8):
                        mm(bk[:, 0:128], wq2[i][:, kc, :], hT[:, kc, qb * 128:(qb + 1) * 128], kc == 0, kc == 7,
                           [B_wq2, BhT[qb // 4]], [Bbk])
                    act(qt2[par][2 * i][:], bk[0:64, 0:128], AF.Copy, [Bbk], [B_qt2[par][2 * i]], scale=0.125)
                    act(qt2[par][2 * i + 1][:], bk[64:128, 0:128], AF.Copy, [Bbk], [B_qt2[par][2 * i + 1]], scale=0.125)

            q_stage(0)
            for qb in range(32):
                par = qb % 2
                tg = qb // 4
                ncols = min(255, 8 * qb + 7)
                nt = 1 if ncols <= 128 else 2
                pacc, B_pacc = pacc2[par], B_pacc2[par]
                memset("gpsimd", pacc[:], 0.0, [B_pacc])
                sbk = []
                for hp in range(4):
                    bk, Bbk = bank()
                    mm(bk[:, 0:ncols], qt2[par][hp][:], kcT[:, 0:ncols], True, True, [B_qt2[par][hp], B_kcT], [Bbk])
                    sbk.append((bk, Bbk))
                for hp in range(4):
                    h = 4 * g + hp
                    bk, Bbk = sbk[hp]
                    tt("vector", s4[hp][:, 0:ncols], bk[:, 0:ncols], tc_b[:, h, 256 - 8 * qb:256 - 8 * qb + ncols], ALU.add,
                       [Bbk, B_const], [B_s4[hp]])
                    act(e4[hp][:, 0:ncols], s4[hp][:, 0:ncols], AF.Exp, [B_s4[hp]], [B_e4[hp], B_rs4[hp]],
                        accum=rs4[hp][:, 0:1])
                if qb + 1 < 32:
                    q_stage(qb + 1)
                for hp in range(4):
                    rs = rs4[hp]
                    ts("vector", rs[:, 1:2], rs[:, 0:1], 1e-30, None, ALU.max, None, [B_rs4[hp]], [B_rs4[hp]])
                    S.op("vector", lambda e, rs=rs: e.reciprocal(out=rs[:, 2:3], in_=rs[:, 1:2]),
                         reads=[B_rs4[hp]], writes=[B_rs4[hp]])
                    if hp == 0:
                        ts("vector", pacc[:, 1:1 + ncols], e4[hp][:, 0:ncols], rs[:, 2:3], None, ALU.mult, None,
                           [B_e4[hp], B_rs4[hp]], [B_pacc])
                    else:
                        stt("vector", pacc[:, 1:1 + ncols], e4[hp][:, 0:ncols], rs[:, 2:3], pacc[:, 1:1 + ncols],
                            ALU.mult, ALU.add, [B_e4[hp], B_rs4[hp], B_pacc], [B_pacc])
                for hp in range(4):
                    h = 4 * g + hp
                    ob, Bob = banks[5 + (hp % 2)], B_bank[5 + (hp % 2)]
                    bk2, Bbk2 = bank()
                    for c in range(nt):
                        nn = min(128, ncols - c * 128)
                        tr(bk2[0:nn, c * 128:(c + 1) * 128], e4[hp][:, c * 128:c * 128 + nn], ident_f[:],
                           [B_e4[hp], B_const], [Bbk2])
                    for c in range(nt):
                        nn = min(128, ncols - c * 128)
                        act(eT4[hp][0:nn, c, :], bk2[0:nn, c * 128:(c + 1) * 128], AF.Copy, [Bbk2], [B_eT4[hp]])
                    for c in range(nt):
                        nn = min(128, ncols - c * 128)
                        mm(ob[:, 0:128], vc1[0:nn, c, :], eT4[hp][0:nn, c, :], c == 0, c == nt - 1,
                           [B_vc1, B_eT4[hp]], [Bob])
                    ocst = ocst4[hp]
                    finalize(ob, Bob, 128, gate_row=h * 3 + 0, gcols=qb * 128, clampden=True,
                             out=ocst[:, qb % 2, :], Bout=B_ocst4[hp], tmp=fin_tmp[hp % 2])
                    if qb % 2 == 1:
                        stD(ocmp[h, :, (qb - 1) * 128:(qb + 1) * 128], ocst[:, :, :].rearrange("p a b -> p (a b)"),
                            [B_ocst4[hp]], [B_ocmp[h][tg]])
                S.op("vector", lambda e, pacc=pacc: e.tensor_reduce(out=impt[:], in_=pacc[:, 0:256].rearrange("p (j m) -> p j m", m=4),
                                                                    axis=AX.X, op=ALU.add), reads=[B_pacc], writes=[B_imp])
                tt("vector", impt[:], impt[:], pacc[:, 4:260:4], ALU.add, [B_imp, B_pacc], [B_imp])
                tt("vector", score[:], impt[:], ft_f[:, 64 - 2 * qb:128 - 2 * qb], ALU.add, [B_imp, B_const], [B_imp])
                ts("vector", score[:, 0:1], score[:, 0:1], 100.0, None, ALU.add, None, [B_imp], [B_imp])
                S.op("vector", lambda e: e.max(out=top8[:], in_=score[:]), reads=[B_imp], writes=[B_imp])
                ts("vector", Mt[:, 64:128], score[:], top8[:, 7:8], None, ALU.is_ge, None, [B_imp, B_Mt], [B_Mt])
                ts("vector", Mt[:, 64:128], Mt[:, 64:128], -NEG, NEG, ALU.mult, ALU.add, [B_Mt], [B_Mt])
                bk, Bbk = bank()
                tr(bk[:, 0:128], Mt[:], ident_f[:], [B_Mt, B_const], [Bbk])
                vcopy("vector", Qa[0][64:128, qb * 128:(qb + 1) * 128], bk[64:128, 0:128], [Bbk], [BQaug[0]])
                act(Qa[1][64:128, qb * 128:(qb + 1) * 128], bk[64:128, 0:128], AF.Copy, [Bbk], [BQaug[1]])

            for i in range(2):
                proj_fm("nsa_q%d" % (2 * g + i),
                        ev_multi(ev_scaled(Qa[0], 0, BQ[0], 0.125), ev_scaled(Qa[1], 64, BQ[1], 0.125)))
                for b in range(2):
                    h = 4 * g + 2 * i + b
                    for gq in range(8):
                        def fin_sel(ob, Bob, gq=gq, h=h):
                            ld(ocl[:], ocmp[h, :, gq * 512:(gq + 1) * 512], [B_ocmp[h][gq]], [B_ocl])
                            finalize(ob, Bob, 512, gate_row=h * 3 + 1, gcols=gq * 512, out=osel[:, :], Bout=B_osel)
                            if dsel is not None:
                                ld(dsel[h, :, gq * 512:(gq + 1) * 512], osel[:, :], [B_osel], [Buf()])
                            tt("gpsimd", ocl[:], ocl[:], osel[:], ALU.add, [B_ocl, B_osel], [B_ocl])
                        attn_group(gq, Qa[b], lambda gg, b=b: [BQ[b][gg], BQaug[b]], 128,
                                   Ksel, lambda kb: [BKsel[kb // 4], B_oh], V1[0], BV[0],
                                   causal_plan(gq, nsaT[:, h, 0, :], nsaT[:, h, 1, :]), fin_sel)
                        plan = []
                        for m in range(-4, 4):
                            kb = 4 * gq + m
                            if kb < 0:
                                continue
                            n_lo, n_hi = max(0, m), min(4, m + 5)
                            bias = {}
                            for n in range(n_lo, n_hi):
                                if n - m == 0:
                                    bias[n] = nsaT[:, h, 0, :]
                                elif n - m == 1:
                                    bias[n] = nsaT[:, h, 1, :]
                                elif n - m == 4:
                                    bias[n] = lt_b[:]
                            plan.append((kb, n_lo, n_hi, bias))

                        def fin_win(ob, Bob, gq=gq, h=h):
                            finalize(ob, Bob, 512, gate_row=h * 3 + 2, gcols=gq * 512, out=osel[:, :], Bout=B_osel)
                            if dwin is not None:
                                ld(dwin[h, :, gq * 512:(gq + 1) * 512], osel[:, :], [B_osel], [Buf()])
                            tt("gpsimd", obf[:, :], ocl[:], osel[:], ALU.add, [B_ocl, B_osel], [B_obf])
                            stD(oT[0, h * 64:(h + 1) * 64, gq * 512:(gq + 1) * 512], obf[:, :], [B_obf], [B_oT[0][gq]])
                        attn_group(gq, Qa[b], lambda gg, b=b: [BQ[b][gg]], 64,
                                   Kwin, lambda kb: [BKwin[kb // 4]], V1[1], BV[1], plan, fin_win)
                attn_flush()
        S.release(m1)
        S.release(m_h)
        if not _MIX_OUT:
            S.release(m0)
            return

        lng = S.sb([128, DM], F32)
        lnb = S.sb([128, DM], F32)
        B_ln = Buf()
        ld(lng[:], lng_d[l, 1], [], [B_ln])
        ld(lnb[:], lnb_d[l, 1], [], [B_ln])
        bg_t = S.sb([128, 24], F32)
        ld(bg_t[:], bgate_d[l], [], [B_ln])
        woutt = S.sb([128, 8, DM], BF16)
        B_wo = Buf()
        ld(woutt[:], wouts.rearrange("p (k c) -> p k c", c=DM), [B_wouts], [B_wo])
        xin2 = S.sb([128, 4, DM], F32)
        Bxin2 = Buf()
        ot = [S.sb([128, 4, 512], BF16) for _ in range(3)]
        B_ot = [Buf() for _ in range(3)]
        wgr = Ring([(S.sb([128, 8, 128], BF16), Buf()) for _ in range(3)])
        wbrr = Ring([(S.sb([128, 4, 128], BF16), Buf()) for _ in range(3)])
        gsr = Ring([(S.sb([128, 512], F32), Buf()) for _ in range(2)])
        macc = S.sb([128, 512], F32)
        B_macc = Buf()
        mT = S.sb([128, 8, 512], BF16)
        BmT = [Buf() for _ in range(8)]
        tmps = Ring([ln_tmp(), ln_tmp()])
        ring5.items = [0, 1, 2, 3]
        ring5.i = 0
        ypairs = [(5, 6), (7, 4)]
        wgs3 = wgs.rearrange("p (k c) -> p k c", c=3072)
        for tg in range(8):
            ld(xin2[:], src[tg * 512:(tg + 1) * 512, :].rearrange("(n p) d -> p n d", p=128), [Bsrc[tg]], [Bxin2])
            for x in range(3):
                ld(ot[x][:], oT[x, :, tg * 512:(tg + 1) * 512].rearrange("(c p) t -> p c t", p=128), [B_oT[x][tg]], [B_ot[x]])
            for dc in range(8):
                for x in range(3):
                    pump()
                    wg, Bwg = wgr.next()
                    ld(wg[:], wgs3[:, :, x * DM + dc * 128:x * DM + (dc + 1) * 128], [B_wgs], [Bwg])
                    wb, Bwb = wbrr.next()
                    ld(wb[:], wbrs[x].rearrange("p (c d) -> p c d", d=DM)[:, :, dc * 128:(dc + 1) * 128], [B_wbrs], [Bwb])
                    bg, Bbg = bank()
                    for kc in range(8):
                        mm(bg[:, :], wg[:, kc, :], hT[:, kc, tg * 512:(tg + 1) * 512], kc == 0, kc == 7, [Bwg, BhT[tg]], [Bbg])
                    bb, Bbb = bank()
                    for c in range(4):
                        mm(bb[:, :], wb[:, c, :], ot[x][:, c, :], c == 0, c == 3, [Bwb, B_ot[x]], [Bbb])
                    gs, Bgs = gsr.next()
                    act(gs[:], bg[:, :], AF.Sigmoid, [Bbg, B_ln], [Bgs], bias=bg_t[:, x * 8 + dc:x * 8 + dc + 1])
                    if x == 0:
                        tt("vector", macc[:], gs[:], bb[:, :], ALU.mult, [Bgs, Bbb], [B_macc])
                    elif x == 1:
                        tt("vector", gs[:], gs[:], bb[:, :], ALU.mult, [Bgs, Bbb], [Bgs])
                        tt("gpsimd", macc[:], macc[:], gs[:], ALU.add, [Bgs, B_macc], [B_macc])
                    else:
                        tt("vector", gs[:], gs[:], bb[:, :], ALU.mult, [Bgs, Bbb], [Bgs])
                        tt("gpsimd", mT[:, dc, :], macc[:], gs[:], ALU.add, [Bgs, B_macc], [BmT[dc]])
            for n in range(4):
                i0, i1 = ypairs[n % 2]
                b0, b1 = banks[i0], banks[i1]
                for dc in range(8):
                    mm(b0[:, :], mT[:, dc, n * 128:(n + 1) * 128], woutt[:, dc, 0:512], dc == 0, dc == 7,
                       [BmT[dc], B_wo], [B_bank[i0]])
                    mm(b1[:, :], mT[:, dc, n * 128:(n + 1) * 128], woutt[:, dc, 512:1024], dc == 0, dc == 7,
                       [BmT[dc], B_wo], [B_bank[i1]])
                r0 = tg * 512 + n * 128
                ln_epilogue(xin2[:, n, :], b0, b1, B_bank[i0], B_bank[i1], Bxin2, lng, lnb, B_ln, tmps.next(),
                            dst[r0:r0 + 128, :], Bdst[tg] if isinstance(Bdst, list) else Bdst,
                            None if dbg_ap is None else dbg_ap[r0:r0 + 128, :])
        ring5.items = [0, 1, 2, 3, 4]
        ring5.i = 0
        S.release(m0)

    cur, Bcur = x_in, [Buf() for _ in range(8)]
    pp = 0
    prep_ffn(0, 0)
    for l in range(NL):
        for st in range(3):
            lastst = (l == NL - 1 and st == 2) or (_STOP_AFTER is not None and l * 3 + st == _STOP_AFTER - 1)
            if _STOP_AFTER is not None and l * 3 + st >= _STOP_AFTER:
                continue
            if lastst:
                dst, Bdst = y_out, B_y
            else:
                dst, Bdst = xs[pp], B_xs[pp]
            dbg_ap = dbg_d.get("l%ds%d" % (l, st))
            if st == 0:
                ffn_stage(l, 0, cur, Bcur, dst, Bdst, 0, dbg_ap)
            elif st == 1:
                mixer_stage(l, cur, Bcur, dst, Bdst, dbg_ap)
            else:
                ffn_stage(l, 1, cur, Bcur, dst, Bdst, 2, dbg_ap)
            cur, Bcur = dst, Bdst
            pp ^= 1
    S.finish()
    return nc


def _consts(rel_bias):
    c = {}
    c["ident"] = np.eye(128, dtype=np.float32)
    s = np.arange(S_LEN)
    c["onehot"] = (s[None, :] // 64 == np.arange(64)[:, None]).astype(np.float32)
    j = np.arange(128)[:, None]
    i = np.arange(128)[None, :]
    d0 = i - j
    d1 = 128 + i - j
    bk0 = _t5_bucket(d0)
    bk1 = _t5_bucket(d1)
    relT = np.ascontiguousarray(rel_bias.T)
    biasT = np.empty((16, 2, 128, 128), np.float32)
    for h in range(16):
        t0 = relT[h][bk0]
        t1 = relT[h][bk1]
        biasT[h, 0] = np.where(d0 >= 0, t0, np.float32(NEG))
        if h < 8:
            biasT[h, 1] = t1
        else:
            biasT[h, 1] = np.where(d1 < 128, t1, np.float32(NEG))
    c["biasT"] = biasT
    c["cfar"] = np.ascontiguousarray(np.broadcast_to(rel_bias[31][None, :], (128, 16))).astype(np.float32)
    ii = np.arange(128)[:, None]
    u = np.arange(512)[None, :]
    dist = ii - 16 * (u - 256) - 31
    bkc = _t5_bucket(dist)
    tc = np.empty((8, 128, 512), np.float32)
    for h in range(8):
        tc[h] = np.where(dist >= 0, relT[h][bkc], np.float32(NEG))
    c["tc"] = tc
    uu = np.arange(128)[None, :] - 64
    curb = (ii >= 64).astype(np.int64)
    ft = np.zeros((128, 128), np.float32)
    ft[np.broadcast_to(uu > curb, (128, 128))] = -100.0
    ft[np.broadcast_to((uu == curb) | (uu == curb - 1), (128, 128))] = 100.0
    c["ft"] = ft
    c["cm"] = np.where(d0 >= 0, 0.0, NEG).astype(np.float32)
    c["lt"] = np.where(i < j, 0.0, NEG).astype(np.float32)
    sel = np.zeros((24, 1536), np.float32)
    for r in range(24):
        sel[r, r * 64:(r + 1) * 64] = 1.0
    c["sel24"] = sel
    return c


def _layer_weights(inp, ls):
    L = len(ls)
    w = {}

    def stack(f):
        return np.ascontiguousarray(np.stack([f(l) for l in ls]))
    for i, (k1, k2) in enumerate((("ffn1_w1", "ffn1_w2"), ("ffn2_w1", "ffn2_w2"))):
        w["w1_%d" % i] = stack(lambda l: inp[k1][l].reshape(8, 128, 2, NFC, 128).transpose(3, 1, 0, 2, 4).reshape(NFC, 128, 2048))
        w["w2_%d" % i] = stack(lambda l: inp[k2][l].reshape(NFC, 128, DM).transpose(1, 0, 2).reshape(128, NFC * DM))
    def _win_layout(l):
        wp = inp["w_in"][l][:, WIN_PERM]
        blocks = []
        for name, (off, wd) in WIN_P.items():
            blocks.append(wp[:, off:off + wd].reshape(8, 128, wd).transpose(1, 0, 2).reshape(128, 8 * wd))
        return np.concatenate(blocks, axis=1)
    w["win"] = stack(_win_layout)
    w["wgate"] = stack(lambda l: inp["w_gate"][l].reshape(8, 128, 3072).transpose(1, 0, 2).reshape(128, 8 * 3072))
    w["bgate"] = stack(lambda l: inp["b_gate"][l].reshape(24, 128).T)
    w["wbr"] = stack(lambda l: np.stack([inp[k][l].reshape(4, 128, DM).transpose(1, 0, 2).reshape(128, 4 * DM)
                                         for k in ("w_br_a", "w_br_b", "w_br_c")]))
    w["wout"] = stack(lambda l: inp["w_out"][l].reshape(8, 128, DM).transpose(1, 0, 2).reshape(128, 8 * DM))
    w["cw1"] = stack(lambda l: np.stack([inp[k][l].reshape(32, 64, 256).transpose(1, 0, 2).reshape(64, 32 * 256)
                                         for k in ("cmp_k_w1", "cmp_v_w1")]))
    w["cw2"] = stack(lambda l: np.stack([inp[k][l].reshape(2, 128, 64).transpose(1, 0, 2).reshape(128, 128)
                                         for k in ("cmp_k_w2", "cmp_v_w2")]))
    w["pet"] = stack(lambda l: np.stack([inp[k][l].T for k in ("cmp_pe_k", "cmp_pe_v")]))
    w["lng"] = stack(lambda l: np.stack([np.broadcast_to(inp[k][l][None, :], (128, DM)) for k in ("ln1_g", "ln2_g", "ln3_g")]))
    w["lnb"] = stack(lambda l: np.stack([np.broadcast_to(inp[k][l][None, :], (128, DM)) for k in ("ln1_b", "ln2_b", "ln3_b")]))
    w["sinks"] = stack(lambda l: np.broadcast_to(inp["swa_sinks"][l][None, :], (128, 8)))
    w["bf"] = stack(lambda l: inp["fox_b_f"][l].reshape(8, 1))
    return {k: np.ascontiguousarray(v, dtype=np.float32) for k, v in w.items()}


_PROG = {}


def _get_prog(NL):
    if NL not in _PROG:
        _PROG[NL] = build_program(NL)
    return _PROG[NL]


def kernel(**inputs):
    inp = {k: np.asarray(v, dtype=np.float32) for k, v in inputs.items()}
    consts = _consts(inp["rel_bias"])
    x = inp["x"]
    nb = x.shape[0]
    NL = NLAYERS
    nc = _get_prog(NL)
    wl = _layer_weights(inp, list(range(NL)))
    in_maps = []
    for b in range(nb):
        m = {"x": np.ascontiguousarray(x[b])}
        m.update(wl)
        m.update(consts)
        in_maps.append(m)
    res = run_bass_kernel_spmd(nc, in_maps, core_ids=list(range(nb)))
    return np.stack([np.asarray(r["y"], dtype=np.float32) for r in res.results], axis=0)
```
